# Optimizing a Trainium2 kernel written in Bass

```python
import jax, jax.numpy as jnp
from jax import lax
import numpy as np

D_MODEL = 1024
BATCH = 16
SEQ = 256
DEPTH = 1
DEC_BATCH = 8
DEC_SEQ = 1024
PAST_LEN = 512

GRID_W = 64
MIX_W = D_MODEL
RET_W = MIX_W // 2
RET_HEADS = 4
RET_HEAD_DIM = RET_W // RET_HEADS
SSD_W = MIX_W - RET_W
SSD_HEAD_DIM = 64
SSD_HEADS = SSD_W // SSD_HEAD_DIM
SSD_GROUPS = 2
SSD_STATE = 128
HEADS_PER_GROUP = SSD_HEADS // SSD_GROUPS
CONV_W = 5
CONV_CH = SSD_W + 2 * SSD_GROUPS * SSD_STATE
CHUNK = 64
D_FF = -(-8 * D_MODEL // (3 * 256)) * 256
ROPE_BASE = 10000.0
EPS = 1e-6
ALPHA = (2.0 * DEPTH) ** 0.25
BETA = (8.0 * DEPTH) ** -0.25
IN_SPLITS = (RET_W, 2 * RET_W, 3 * RET_W, 4 * RET_W, 4 * RET_W + SSD_W, 4 * RET_W + SSD_W + CONV_CH)
IN_COLS = 4 * RET_W + SSD_W + CONV_CH + SSD_HEADS

kernel_name = "hymba_retention_ssd_prefix_flow_step"


def layer_norm(x, g, b):
    xf = x.astype(jnp.float32)
    mu = jnp.mean(xf, axis=-1, keepdims=True)
    var = jnp.mean(jnp.square(xf - mu), axis=-1, keepdims=True)
    y = (xf - mu) * lax.rsqrt(var + EPS) * g.astype(jnp.float32) + b.astype(jnp.float32)
    return y.astype(x.dtype)


def rms_norm(x):
    xf = x.astype(jnp.float32)
    return xf * lax.rsqrt(jnp.mean(jnp.square(xf), axis=-1, keepdims=True) + EPS)


def rope_2d(n_tokens):
    rows = n_tokens // GRID_W
    t_row = jnp.repeat(jnp.arange(rows, dtype=jnp.float32), GRID_W)
    t_col = jnp.tile(jnp.arange(GRID_W, dtype=jnp.float32), rows)
    nf = RET_HEAD_DIM // 4
    inv = ROPE_BASE ** (-jnp.arange(nf, dtype=jnp.float32) / nf)
    ang = jnp.concatenate([t_row[:, None] * inv, t_col[:, None] * inv], axis=-1)
    return jnp.cos(ang), jnp.sin(ang)


def apply_rope(x, cos, sin):
    half = x.shape[-1] // 2
    x1 = x[..., :half].astype(jnp.float32)
    x2 = x[..., half:].astype(jnp.float32)
    return jnp.concatenate([x1 * cos - x2 * sin, x1 * sin + x2 * cos], axis=-1).astype(x.dtype)


def chunked_scan(q, k, v, log_a, s0, inclusive):
    f32 = jnp.float32
    b, h, L, dk = q.shape
    dv = v.shape[-1]
    n = L // CHUNK
    qc = q.astype(f32).reshape(b, h, n, CHUNK, dk)
    kc = k.astype(f32).reshape(b, h, n, CHUNK, dk)
    vc = v.astype(f32).reshape(b, h, n, CHUNK, dv)
    cum = jnp.cumsum(log_a.astype(f32).reshape(b, h, n, CHUNK), axis=-1)
    idx = jnp.arange(CHUNK)
    mask = (idx[:, None] >= idx[None, :]) if inclusive else (idx[:, None] > idx[None, :])
    decay = jnp.exp(jnp.where(mask, cum[..., :, None] - cum[..., None, :], -jnp.inf))
    scores = jnp.einsum('bhnid,bhnjd->bhnij', qc, kc) * decay
    intra = jnp.einsum('bhnij,bhnje->bhnie', scores, vc)
    tail = jnp.exp(cum[..., -1:] - cum)
    chunk_state = jnp.einsum('bhnj,bhnjd,bhnje->bhnde', tail, kc, vc)
    chunk_decay = jnp.exp(cum[..., -1])

    def step(s, inp):
        dec, cs = inp
        return dec[..., None, None] * s + cs, s

    s_final, s_enter = lax.scan(step, s0.astype(f32),
                                (jnp.moveaxis(chunk_decay, 2, 0), jnp.moveaxis(chunk_state, 2, 0)))
    s_enter = jnp.moveaxis(s_enter, 0, 2)
    cross = jnp.einsum('bhnid,bhnde->bhnie', qc * jnp.exp(cum)[..., None], s_enter)
    return (intra + cross).reshape(b, h, L, dv), s_final


def bidir_scan(q, k, v_f, v_b, log_a_f, log_a_b, s0_f, s0_b):
    o_f, s_f = chunked_scan(q, k, v_f, log_a_f, s0_f, True)
    flip = lambda t: jnp.flip(t, axis=2)
    o_b, s_b = chunked_scan(flip(q), flip(k), flip(v_b), flip(log_a_b), s0_b, False)
    return o_f + flip(o_b), s_f, s_b


def depthwise_conv(x, w, b):
    y = lax.conv_general_dilated(x, w[:, None, :], window_strides=(1,),
                                 padding=[(CONV_W // 2, CONV_W // 2)],
                                 dimension_numbers=('NWC', 'WIO', 'NWC'),
                                 feature_group_count=x.shape[-1])
    return y + b


def adaln(cond, w_ada, b_ada):
    return jax.nn.silu(cond) @ w_ada + b_ada


def trunk_layer(x, mod, rope, s_ret0, s_ssd0, p):
    bsz, L, _ = x.shape
    sh1, sc1, g1, sh2, sc2, g2 = jnp.split(mod, 6, axis=-1)
    h = x * (1 + sc1) + sh1
    proj = h @ p['w_in']
    q, k, v, g, z, xbc, dt_raw = jnp.split(proj, IN_SPLITS, axis=-1)

    heads = lambda t: t.reshape(bsz, L, RET_HEADS, RET_HEAD_DIM).transpose(0, 2, 1, 3)
    q, k, v = heads(q), heads(k), heads(v)
    if rope is not None:
        q = apply_rope(q, rope[0], rope[1])
        k = apply_rope(k, rope[0], rope[1])
    k = k * (RET_HEAD_DIM ** -0.5)
    lg_f = jnp.broadcast_to(jax.nn.log_sigmoid(p['ret_decay_fwd'].astype(jnp.float32))[None, :, None], (bsz, RET_HEADS, L))
    lg_b = jnp.broadcast_to(jax.nn.log_sigmoid(p['ret_decay_bwd'].astype(jnp.float32))[None, :, None], (bsz, RET_HEADS, L))
    o_ret, sr_f, sr_b = bidir_scan(q, k, v, v, lg_f, lg_b, s_ret0[0], s_ret0[1])
    o_ret = rms_norm(o_ret).transpose(0, 2, 1, 3).reshape(bsz, L, RET_W).astype(x.dtype)
    o_ret = jax.nn.silu(g) * o_ret

    xbc = jax.nn.silu(depthwise_conv(xbc, p['conv_w'], p['conv_b']))
    xs, bm, cm = jnp.split(xbc, (SSD_W, SSD_W + SSD_GROUPS * SSD_STATE), axis=-1)
    xs = xs.reshape(bsz, L, SSD_HEADS, SSD_HEAD_DIM).transpose(0, 2, 1, 3)
    grp = lambda t: jnp.repeat(t.reshape(bsz, L, SSD_GROUPS, SSD_STATE), HEADS_PER_GROUP, axis=2).transpose(0, 2, 1, 3)
    bm, cm = grp(bm), grp(cm)
    dt_raw = dt_raw.astype(jnp.float32)
    dt_f = jax.nn.softplus(dt_raw + p['dt_bias_fwd'].astype(jnp.float32)).transpose(0, 2, 1)
    dt_b = jax.nn.softplus(dt_raw + p['dt_bias_bwd'].astype(jnp.float32)).transpose(0, 2, 1)
    la_f = dt_f * (-jnp.exp(p['a_log_fwd'].astype(jnp.float32)))[None, :, None]
    la_b = dt_b * (-jnp.exp(p['a_log_bwd'].astype(jnp.float32)))[None, :, None]
    xsf = xs.astype(jnp.float32)
    y, ss_f, ss_b = bidir_scan(cm, bm, xsf * dt_f[..., None], xsf * dt_b[..., None], la_f, la_b, s_ssd0[0], s_ssd0[1])
    y = y + p['d_skip'].astype(jnp.float32)[None, :, None, None] * xsf
    y = y.transpose(0, 2, 1, 3).reshape(bsz, L, SSD_W)
    o_ssd = (rms_norm(y * jax.nn.silu(z.astype(jnp.float32))) * p['ssd_norm_w'].astype(jnp.float32)).astype(x.dtype)

    mix = jnp.concatenate([o_ret, o_ssd], axis=-1) @ p['w_out']
    x = layer_norm(ALPHA * x + g1 * mix, p['ln1_g'], p['ln1_b'])
    h2 = x * (1 + sc2) + sh2
    ffn = (jax.nn.silu(h2 @ p['w_gate']) * (h2 @ p['w_up'])) @ p['w_down']
    x = layer_norm(ALPHA * x + g2 * ffn, p['ln2_g'], p['ln2_b'])
    s_ret = jnp.stack([sr_f, sr_b], axis=1).astype(x.dtype)
    s_ssd = jnp.stack([ss_f, ss_b], axis=1).astype(x.dtype)
    return x, s_ret, s_ssd


def setup_inputs(seed: int = 0) -> dict:
    key = jax.random.key(seed)
    ks = jax.random.split(key, 32)
    f32 = jnp.float32
    nrm = lambda k, shape, s: jax.random.normal(k, shape, f32) * s
    gamma0 = 1.0 - 2.0 ** (-5.0 - np.arange(RET_HEADS))
    logit0 = jnp.asarray(np.log(gamma0 / (1.0 - gamma0)), f32)
    dt_f = jnp.exp(jax.random.uniform(ks[12], (DEPTH, SSD_HEADS), f32, np.log(1e-3), np.log(1e-1)))
    dt_b = jnp.exp(jax.random.uniform(ks[13], (DEPTH, SSD_HEADS), f32, np.log(1e-3), np.log(1e-1)))
    inv_sp = lambda d: d + jnp.log(-jnp.expm1(-d))
    return {
        'x_prompt': nrm(ks[0], (BATCH, SEQ, D_MODEL), 1.0),
        'x_sample': nrm(ks[1], (DEC_BATCH, DEC_SEQ, D_MODEL), 1.0),
        'state_ret': nrm(ks[2], (DEC_BATCH, DEPTH, 2, RET_HEADS, RET_HEAD_DIM, RET_HEAD_DIM), 0.1),
        'state_ssd': nrm(ks[3], (DEC_BATCH, DEPTH, 2, SSD_HEADS, SSD_STATE, SSD_HEAD_DIM), 0.1),
        'c': nrm(ks[4], (DEC_BATCH, D_MODEL), 1.0),
        'c_ctx': nrm(ks[5], (D_MODEL,), 0.5),
        'w_in': nrm(ks[6], (DEPTH, D_MODEL, IN_COLS), D_MODEL ** -0.5),
        'ret_decay_fwd': logit0[None, :] + nrm(ks[7], (DEPTH, RET_HEADS), 0.05),
        'ret_decay_bwd': logit0[None, :] + nrm(ks[8], (DEPTH, RET_HEADS), 0.05),
        'conv_w': nrm(ks[9], (DEPTH, CONV_W, CONV_CH), CONV_W ** -0.5),
        'conv_b': nrm(ks[10], (DEPTH, CONV_CH), 0.02),
        'dt_bias_fwd': inv_sp(dt_f),
        'dt_bias_bwd': inv_sp(dt_b),
        'a_log_fwd': jnp.log(jax.random.uniform(ks[14], (DEPTH, SSD_HEADS), f32, 1.0, 16.0)),
        'a_log_bwd': jnp.log(jax.random.uniform(ks[15], (DEPTH, SSD_HEADS), f32, 1.0, 16.0)),
        'd_skip': 1.0 + nrm(ks[16], (DEPTH, SSD_HEADS), 0.1),
        'ssd_norm_w': 1.0 + nrm(ks[17], (DEPTH, SSD_W), 0.1),
        'w_out': nrm(ks[18], (DEPTH, MIX_W, D_MODEL), BETA * MIX_W ** -0.5),
        'ln1_g': 1.0 + nrm(ks[19], (DEPTH, D_MODEL), 0.05),
        'ln1_b': nrm(ks[20], (DEPTH, D_MODEL), 0.02),
        'w_gate': nrm(ks[21], (DEPTH, D_MODEL, D_FF), D_MODEL ** -0.5),
        'w_up': nrm(ks[22], (DEPTH, D_MODEL, D_FF), D_MODEL ** -0.5),
        'w_down': nrm(ks[23], (DEPTH, D_FF, D_MODEL), BETA * D_FF ** -0.5),
        'ln2_g': 1.0 + nrm(ks[24], (DEPTH, D_MODEL), 0.05),
        'ln2_b': nrm(ks[25], (DEPTH, D_MODEL), 0.02),
        'w_ada': nrm(ks[26], (DEPTH, D_MODEL, 6 * D_MODEL), 0.5 * D_MODEL ** -0.5),
        'b_ada': nrm(ks[27], (DEPTH, 6 * D_MODEL), 0.02),
    }


def reference(x_prompt, x_sample, state_ret, state_ssd, c, c_ctx, w_in, ret_decay_fwd, ret_decay_bwd,
              conv_w, conv_b, dt_bias_fwd, dt_bias_bwd, a_log_fwd, a_log_bwd, d_skip, ssd_norm_w,
              w_out, ln1_g, ln1_b, w_gate, w_up, w_down, ln2_g, ln2_b, w_ada, b_ada):
    rope = rope_2d(x_sample.shape[1])
    xp, xs = x_prompt, x_sample
    bp = x_prompt.shape[0]
    new_ret, new_ssd = [], []
    for l in range(DEPTH):
        p = {'w_in': w_in[l], 'ret_decay_fwd': ret_decay_fwd[l], 'ret_decay_bwd': ret_decay_bwd[l],
             'conv_w': conv_w[l], 'conv_b': conv_b[l], 'dt_bias_fwd': dt_bias_fwd[l], 'dt_bias_bwd': dt_bias_bwd[l],
             'a_log_fwd': a_log_fwd[l], 'a_log_bwd': a_log_bwd[l], 'd_skip': d_skip[l], 'ssd_norm_w': ssd_norm_w[l],
             'w_out': w_out[l], 'ln1_g': ln1_g[l], 'ln1_b': ln1_b[l], 'w_gate': w_gate[l], 'w_up': w_up[l],
             'w_down': w_down[l], 'ln2_g': ln2_g[l], 'ln2_b': ln2_b[l]}
        mod_ctx = adaln(c_ctx[None, :], w_ada[l], b_ada[l])[:, None, :]
        mod_lat = adaln(c, w_ada[l], b_ada[l])[:, None, :]
        zr = jnp.zeros((bp, RET_HEADS, RET_HEAD_DIM, RET_HEAD_DIM), xp.dtype)
        zs = jnp.zeros((bp, SSD_HEADS, SSD_STATE, SSD_HEAD_DIM), xp.dtype)
        xp, s_ret, s_ssd = trunk_layer(xp, mod_ctx, None, (zr, zr), (zs, zs), p)
        new_ret.append(s_ret)
        new_ssd.append(s_ssd)
        xs, _, _ = trunk_layer(xs, mod_lat, rope, (state_ret[:, l, 0], state_ret[:, l, 1]),
                               (state_ssd[:, l, 0], state_ssd[:, l, 1]), p)
    new_state_ret = jnp.stack(new_ret, axis=1)
    new_state_ssd = jnp.stack(new_ssd, axis=1)
    return (xp, xs, new_state_ret, new_state_ssd)
```

```python
import contextlib
import math
import numpy as np
import concourse.bass as bass
import concourse.mybir as mybir
from concourse.bass_utils import run_bass_kernel_spmd

F32 = mybir.dt.float32
BF16 = mybir.dt.bfloat16
U8 = mybir.dt.uint8
AF = mybir.ActivationFunctionType
ALU = mybir.AluOpType
SZ = {F32: 4, BF16: 2}

PE, ACT, DVE, POOL, SP = "tensor", "scalar", "vector", "gpsimd", "sync"
ENGS = [PE, ACT, DVE, POOL, SP]

D = 1024
NT = 1536
NTT = 12
INC = 3592
DFF = 2816
EPS = 1e-6
ALPHA = 2.0 ** 0.25
NEG = -32768.0
SEQS = [(0, 8), (8, 2), (10, 2)]


class Op:
    __slots__ = ("idx", "eng", "fn", "is_dma", "dsem", "dval", "deps", "inc", "cval", "cost", "fin", "npend", "succ", "start", "crit", "tag", "aps")

    def __init__(self, idx, eng, fn, is_dma):
        self.idx, self.eng, self.fn, self.is_dma = idx, eng, fn, is_dma
        self.dsem, self.dval, self.deps, self.inc, self.cval = None, 0, [], False, 0
        self.cost, self.fin, self.npend, self.succ = 0.3, 0.0, 0, []
        self.start, self.crit, self.tag, self.aps = 0.0, None, '', None


class Prog:
    def __init__(self, nc):
        self.nc = nc
        self.ops = []
        self.res = {}
        self.dma_sems = {}
        self.out_dma_ops = []
        self.cur_tag = ""
        self.last_dma = {}

    @staticmethod
    def _split(key):
        if isinstance(key, tuple):
            return key[0], tuple(key[1:])
        return key, ()

    @staticmethod
    def _related(p, q):
        n = min(len(p), len(q))
        return p[:n] == q[:n]

    def _record(self, op, reads, writes):
        deps = set()
        for k in reads:
            name, p = self._split(k)
            d = self.res.setdefault(name, {})
            for q, e in d.items():
                if self._related(p, q) and e[0] is not None:
                    deps.add(e[0])
        for k in writes:
            name, p = self._split(k)
            d = self.res.setdefault(name, {})
            for q, e in d.items():
                if self._related(p, q):
                    if e[0] is not None:
                        deps.add(e[0])
                    deps.update(e[1])
        deps.discard(op)
        op.deps = sorted(deps, key=lambda o: o.idx)
        for k in reads:
            name, p = self._split(k)
            d = self.res[name]
            if p not in d:
                d[p] = [None, []]
            d[p][1].append(op)
        for k in writes:
            name, p = self._split(k)
            d = self.res[name]
            for q in [q for q in d if len(q) >= len(p) and q[:len(p)] == p]:
                del d[q]
            d[p] = [op, []]

    def op(self, eng, fn, reads=(), writes=(), cost=0.3):
        o = Op(len(self.ops), eng, fn, False)
        o.cost = cost
        o.tag = self.cur_tag
        self.ops.append(o)
        self._record(o, list(reads), list(writes))
        return o

    def dma(self, eng, fn, semkey, reads=(), writes=(), is_output=False, cost=3.0, chain=True):
        o = Op(len(self.ops), eng, fn, True)
        o.cost = cost
        o.tag = self.cur_tag
        ent = self.dma_sems.setdefault(semkey, [None, 0])
        ent[1] += 16
        o.dsem, o.dval = semkey, ent[1]
        self.ops.append(o)
        self._record(o, list(reads), list(writes))
        prev = self.last_dma.get(semkey)
        if chain and prev is not None and prev not in o.deps:
            o.deps.append(prev)
            o.deps.sort(key=lambda q: q.idx)
        self.last_dma[semkey] = o
        if is_output:
            self.out_dma_ops.append(o)
        return o

    def schedule(self):
        import heapq
        ops = self.ops
        for o in ops:
            o.succ = []
        for o in ops:
            o.npend = len(o.deps)
            for d in o.deps:
                d.succ.append(o)

        def lat(d, o):
            return 0.05 if (d.eng == PE and o.eng == PE and not d.is_dma and not o.is_dma) else 0.25

        bl = [0.0] * len(ops)
        for o in reversed(ops):
            m = 0.0
            for s_ in o.succ:
                v = bl[s_.idx] + lat(o, s_)
                if v > m:
                    m = v
            bl[o.idx] = o.cost + m
        inorder = {SP: False, POOL: False, PE: False, ACT: False, DVE: False}
        pend = {e: [] for e in ENGS}
        avail = {e: [] for e in ENGS}
        tcur = {e: 0.0 for e in ENGS}
        ready_t = {}
        nxt = {e: 0 for e in ENGS}
        eng_ops = {e: [o for o in ops if o.eng == e] for e in ENGS}
        order = {e: [] for e in ENGS}

        def push(o):
            rt = 0.0
            for d in o.deps:
                rt = max(rt, d.fin + lat(d, o))
            ready_t[o.idx] = rt
            heapq.heappush(pend[o.eng], (rt, o.idx, o))

        for o in ops:
            if o.npend == 0:
                push(o)
        done, n = 0, len(ops)
        while done < n:
            best = None
            for e in ENGS:
                if inorder[e]:
                    if nxt[e] >= len(eng_ops[e]):
                        continue
                    o = eng_ops[e][nxt[e]]
                    if o.npend != 0 or o.idx not in ready_t:
                        continue
                    st_ = max(tcur[e], ready_t[o.idx])
                    cand = (st_, o.idx, e, o)
                else:
                    p, a = pend[e], avail[e]
                    while p and p[0][0] <= tcur[e]:
                        rt, idx, o = heapq.heappop(p)
                        heapq.heappush(a, (-bl[idx], idx, o))
                    if a:
                        o = a[0][2]
                        cand = (tcur[e], o.idx, e, o)
                    elif p:
                        rt, idx, o = p[0]
                        cand = (rt, idx, e, o)
                    else:
                        continue
                if best is None or cand[:2] < best[:2]:
                    best = cand
            assert best is not None, "scheduler deadlock"
            st_, _, e, o = best
            if inorder[e]:
                nxt[e] += 1
            else:
                if avail[e] and avail[e][0][2] is o:
                    heapq.heappop(avail[e])
                else:
                    heapq.heappop(pend[e])
            o.start = st_
            o.crit = ("eng", order[e][-1]) if (order[e] and tcur[e] >= ready_t[o.idx]) else \
                ("dep", max(o.deps, key=lambda d: d.fin) if o.deps else None)
            if o.is_dma:
                tcur[e] = st_ + (1.0 if e == POOL else 0.1)
                o.fin = st_ + o.cost
            else:
                tcur[e] = st_ + o.cost
                o.fin = tcur[e]
            order[e].append(o)
            done += 1
            for s_ in o.succ:
                s_.npend -= 1
                if s_.npend == 0:
                    push(s_)
        self.est_total = max(o.fin for o in ops)
        return order

    def emit(self, stack, reorder=True):
        nc = self.nc
        if reorder:
            order = self.schedule()
        else:
            order = {e: [o for o in self.ops if o.eng == e] for e in ENGS}
        esem = {e: stack.enter_context(nc.semaphore("c_" + e)) for e in ENGS}
        for i, (k, ent) in enumerate(self.dma_sems.items()):
            ent[0] = stack.enter_context(nc.semaphore("d%d" % i))
        final_waits = {}
        for o in self.out_dma_ops:
            final_waits[o.dsem] = max(final_waits.get(o.dsem, 0), o.dval)

        def skip(d, o):
            return (not d.is_dma) and d.eng == PE and o.eng == PE and not o.is_dma

        pos = {}
        for e in ENGS:
            for i_, o in enumerate(order[e]):
                pos[o.idx] = i_
        need = {}
        for e in ENGS:
            seenpos = {}
            for o in order[e]:
                last = {}
                for d in o.deps:
                    if d.is_dma or skip(d, o):
                        continue
                    m = last.get(d.eng)
                    if m is None or pos[d.idx] > pos[m.idx]:
                        last[d.eng] = d
                lst = []
                for pe_, m in last.items():
                    if seenpos.get(pe_, -1) >= pos[m.idx]:
                        continue
                    seenpos[pe_] = pos[m.idx]
                    m.inc = True
                    lst.append(m)
                need[o.idx] = lst
        cnt = {e: 0 for e in ENGS}
        for e in ENGS:
            for o in order[e]:
                if not o.is_dma and o.inc:
                    cnt[e] += 1
                    o.cval = cnt[e]
        seen = {e: {} for e in ENGS}
        per_eng = {e: [] for e in ENGS}
        for o in [o for e in ENGS for o in order[e]]:
            wl = [(esem[m.eng], m.cval) for m in need[o.idx]]
            waits = {}
            for d in o.deps:
                if d.is_dma:
                    key, val = d.dsem, d.dval
                    if val > waits.get(key, 0):
                        waits[key] = val
            s = seen[o.eng]
            for key, val in waits.items():
                if s.get(key, 0) >= val:
                    continue
                s[key] = val
                wl.append((self.dma_sems[key][0], val))
            per_eng[o.eng].append((o, wl))
        self.n_incs = dict(cnt)
        block = stack.enter_context(nc.Block())
        dma_sems = self.dma_sems

        def make(engname):
            def body(eng):
                for o, wl in per_eng[engname]:
                    for sem, val in wl:
                        eng.wait_ge(sem, val)
                    ins = o.fn(eng)
                    if o.is_dma:
                        ins.then_inc(dma_sems[o.dsem][0], 16)
                    elif o.inc:
                        ins.then_inc(esem[engname], 1)
                if engname == SP:
                    for k, v in final_waits.items():
                        eng.wait_ge(dma_sems[k][0], v)
            return body

        block.tensor(make(PE))
        block.scalar(make(ACT))
        block.vector(make(DVE))
        block.gpsimd(make(POOL))
        block.sync(make(SP))


CF_IDENT, CF_UTRI, CF_ONES = 0, 128, 256
CF_IOTAF, CF_IOTAB = 384, 1408
CF_COS, CF_SIN = 2432, 3456
CF_POS, CF_NEG = 4480, 6528
CF_TAILF, CF_TAILB = 8576, 8578
CF_N = 8580
CB_IDENT, CB_ONES, CB_MF, CB_MB, CB_SELROW, CB_SELBIAS, CB_N = 0, 128, 256, 768, 1280, 3328, 5376


def _host_consts():
    p = np.arange(128)[:, None].astype(np.float64)
    cf = np.zeros((128, CF_N), np.float32)
    j = np.arange(128)[None, :]
    cf[:, CF_IDENT:CF_IDENT + 128] = (p == j)
    cf[:, CF_UTRI:CF_UTRI + 128] = (p <= j)
    cf[:, CF_ONES:CF_ONES + 128] = 1.0
    t = np.arange(1024)[None, :]
    cf[:, CF_IOTAF:CF_IOTAF + 1024] = t + 1
    cf[:, CF_IOTAB:CF_IOTAB + 1024] = 1024 - t
    tt = np.arange(1024)
    t_row = (tt // 64).astype(np.float64)
    t_col = (tt % 64).astype(np.float64)
    inv = 10000.0 ** (-np.arange(32, dtype=np.float64) / 32.0)
    ang = np.concatenate([t_row[:, None] * inv[None, :], t_col[:, None] * inv[None, :]], axis=-1)
    cos = np.cos(ang).T
    sin = np.sin(ang).T
    cf[:, CF_COS:CF_COS + 1024] = np.concatenate([cos, cos], 0)
    cf[:, CF_SIN:CF_SIN + 1024] = np.concatenate([-sin, sin], 0)
    u = np.arange(2048)[None, :]
    delta = u - p - 1024
    cf[:, CF_POS:CF_POS + 2048] = np.maximum(delta, 0)
    cf[:, CF_NEG:CF_NEG + 2048] = np.minimum(delta, 0)
    jj = np.arange(2)[None, :]
    cf[:, CF_TAILF:CF_TAILF + 2] = 255 - 128 * jj - p
    cf[:, CF_TAILB:CF_TAILB + 2] = 128 * jj + p
    cb = np.zeros((128, CB_N), np.float32)
    cb[:, CB_IDENT:CB_IDENT + 128] = (p == j)
    cb[:, CB_ONES:CB_ONES + 128] = 1.0
    mf = np.where(p <= j, 0.0, NEG)
    mb = np.where(p > j, 0.0, NEG)
    cb[:, CB_MF:CB_MF + 512] = np.tile(mf, (1, 4))
    cb[:, CB_MB:CB_MB + 512] = np.tile(mb, (1, 4))
    k = np.arange(128)
    selrow = np.zeros((128, 16, 128), np.float32)
    for hd in range(16):
        selrow[(k < 96) & (k % 32 == hd), hd, :] = 1.0
    cb[:, CB_SELROW:CB_SELROW + 2048] = selrow.reshape(128, 2048)
    selb = np.zeros((128, 2, 8, 128), np.float32)
    for d in range(2):
        for h in range(8):
            selb[(k < 96) & (k % 32 == 16 + d * 8 + h), d, h, :] = 1.0
    cb[:, CB_SELBIAS:CB_SELBIAS + 2048] = selb.reshape(128, 2048)
    return cf, cb


R_LN1G, R_LN1B, R_LN2G, R_LN2B, R_NW, R_CONVB, R_BADA, R_SMALL, R_N = 0, 1024, 2048, 3072, 4096, 4608, 5632, 11776, 11824
C_CVEC, C_BADA, C_CONVW, C_CONVB, C_LN1G, C_LN1B, C_N = 0, 16, 64, 104, 112, 120, 128


def build(debug=()):
    nc = bass.Bass("TRN2", target_bir_lowering=False)
    P = Prog(nc)
    di = lambda name, shape: nc.dram_tensor(name, list(shape), F32, kind="ExternalInput").ap()
    do = lambda name, shape: nc.dram_tensor(name, list(shape), F32, kind="ExternalOutput").ap()
    x_d = di("x", [NT, D])
    sret_d = di("sret0", [128, 1024])
    sssd_d = di("sssd0", [128, 1024])
    w_in_d = di("w_in", [D, INC])
    w_out_d = di("w_out", [D, D])
    w_gate_d = di("w_gate", [D, DFF])
    w_up_d = di("w_up", [D, DFF])
    w_down_d = di("w_down", [DFF, D])
    w_ada_d = di("w_ada", [D, 6 * D])
    rows_d = di("rows", [1, R_N])
    cols_d = di("cols", [128, C_N])
    cf_d = di("cstf", [128, CF_N])
    cb_d = di("cstb", [128, CB_N])
    y_d = do("y", [NT, D])
    oret_d = do("oret", [2, 2, 4, 128, 128])
    ossd_d = do("ossd", [2, 2, 8, 128, 64])

    st = contextlib.ExitStack()
    ARENA = 212800
    arena = nc.alloc_sbuf_tensor("arena", [128, ARENA], U8)
    psum = nc.alloc_psum_tensor("psum", [128, 4096], F32)
    K = 1024

    def V(off, n, dt):
        off = int(off)
        assert off % 4 == 0 and off + n * SZ[dt] <= ARENA, (off, n)
        return arena[:, off:off + n * SZ[dt]].bitcast(dt)

    def PS(b, n=512, off=0):
        return psum[:, b * 512 + off:b * 512 + off + n]

    def pk(b):
        return "ps%d" % b

    o_const, o_scr, o_lnp, o_ring = 0, 5 * K, 15 * K, 31 * K
    SLOT = 8320
    o_A, o_B, o_C, o_D, o_E, o_F, o_G = 64 * K, 88 * K, 112 * K, 124 * K, 136 * K, 148 * K, 173 * K

    ident_f = V(0, 128, F32)
    utri_f = V(512, 128, F32)
    ones_f = V(1024, 128, F32)
    ident_b = V(1536, 128, BF16)
    ones_b = V(1792, 128, BF16)
    m = [2048]

    def small(n, dt=F32):
        a = V(m[0], n, dt)
        m[0] += (n * SZ[dt] + 3) // 4 * 4
        assert m[0] <= 5 * K, m[0]
        return a

    cols = small(C_N)
    smalls = small(48)
    modT = small(96)
    scv = small(16, BF16)
    lg = small(8)
    nlg = small(8)
    tmp8 = small(8)
    nA = small(16)
    tailpos = small(4)
    tailw = small(16)
    sc1p = small(16)
    sc2p = small(16)
    lscb = small(1)
    fencew = small(1)
    dtraw = small(96)
    dskb = small(8)

    def fs(ap):
        n = 1
        for d in ap.shape[1:]:
            n *= int(d)
        return n

    def inps(ap):
        return ap.tensor.name == "psum"

    def mm(out, lhsT, rhs, start, stop, reads, writes):
        n_ = fs(rhs)
        c = ((0.035 + n_ / 2560.0) if n_ >= 256 else (0.03 + n_ / 1400.0)) * (4.0 if rhs.dtype == F32 else 1.0)
        o_ = P.op(PE, lambda e: e.matmul(out, lhsT=lhsT, rhs=rhs, start=start, stop=stop), reads, writes, cost=c)
        o_.aps = ([lhsT, rhs], [out])
        return o_

    def tr(out, in_, reads, writes):
        o_ = P.op(PE, lambda e: e.transpose(out=out, in_=in_, identity=ident_f[:]), list(reads) + ["const"], writes, cost=0.1)
        o_.aps = ([in_, ident_f[:]], [out])
        return o_

    def act(out, in_, func, reads, writes, bias=None, scale=None, accum_out=None):
        kw = {}
        c = 0.2 + fs(in_) / 1400.0
        if fs(in_) <= 8:
            c = 0.6
        if bias is not None:
            kw["bias"] = bias
            c += 0.05
        if scale is not None:
            kw["scale"] = scale
        if accum_out is not None:
            kw["accum_out"] = accum_out
            c += 0.1
        o_ = P.op(ACT, lambda e: e.activation(out=out, in_=in_, func=func, **kw), reads, writes, cost=c)
        o_.aps = ([in_] + [v_ for v_ in (bias, scale) if v_ is not None and not isinstance(v_, float)], [out] + ([accum_out] if accum_out is not None else []))
        return o_

    def vcost(eng, n, f):
        if n <= 8:
            return 0.6
        return (0.1 + n / 490.0) if eng == POOL else (0.09 + n * f / 1060.0)

    def tt(out, in0, in1, op, reads, writes, eng=DVE):
        f = 1.0 if (inps(in0) or inps(in1)) else 2.0
        if in0.dtype == BF16 and in1.dtype == BF16 and out.dtype == BF16 and f == 2.0:
            f = 0.6
        o_ = P.op(eng, lambda e: e.tensor_tensor(out=out, in0=in0, in1=in1, op=op), reads, writes, cost=vcost(eng, fs(out), f))
        o_.aps = ([in0, in1], [out])
        return o_

    def ts(out, in0, s1, op0, reads, writes, s2=None, op1=None, eng=DVE):
        c = vcost(eng, fs(out), 1.0)
        if op1 is None:
            o_ = P.op(eng, lambda e: e.tensor_scalar(out=out, in0=in0, scalar1=s1, scalar2=None, op0=op0), reads, writes, cost=c)
            o_.aps = ([in0] + [v_ for v_ in (s1,) if not isinstance(v_, (float, int))], [out])
            return o_
        o_ = P.op(eng, lambda e: e.tensor_scalar(out=out, in0=in0, scalar1=s1, scalar2=s2, op0=op0, op1=op1), reads, writes, cost=c)
        o_.aps = ([in0] + [v_ for v_ in (s1, s2) if not isinstance(v_, (float, int))], [out])
        return o_

    def stt(out, in0, scalar, in1, op0, op1, reads, writes):
        f = 1.0 if (inps(in0) or inps(in1)) else 2.0
        o_ = P.op(DVE, lambda e: e.scalar_tensor_tensor(out=out, in0=in0, scalar=scalar, in1=in1, op0=op0, op1=op1), reads, writes,
                  cost=vcost(DVE, fs(out), f))
        o_.aps = ([in0, in1] + [v_ for v_ in (scalar,) if not isinstance(v_, (float, int))], [out])
        return o_

    def cp(out, in_, reads, writes, eng=DVE):
        o_ = P.op(eng, lambda e: e.tensor_copy(out=out, in_=in_), reads, writes, cost=vcost(eng, fs(out), 1.0))
        o_.aps = ([in_], [out])
        return o_

    def memset(ap, val, writes, eng=DVE):
        o_ = P.op(eng, lambda e: e.memset(ap, val), [], writes, cost=vcost(eng, fs(ap), 0.5))
        o_.aps = ([], [ap])
        return o_

    def dma(eng, out, in_, key, reads, writes, is_output=False, chain=True):
        nb = 128 * fs(out) * 4
        o_ = P.dma(eng, lambda e: e.dma_start(out=out, in_=in_), key, reads, writes, is_output, cost=2.0 + nb / 150e3, chain=chain)
        o_.aps = ([in_], [out])
        return o_

    P.marks = []

    def fence(regions, name=None):
        o = P.op(DVE, lambda e: e.memset(fencew[:], 0.0), [], list(regions), cost=0.1)
        P.marks.append((name or ("f%d" % len(P.marks)), o))

    def bcast_row(off, n):
        return rows_d[0:1, off:off + n].partition_broadcast(128).rearrange("p a n -> p (a n)")

    dbg_n = [0]

    def tap(name, ap, shape, reads):
        if name in debug:
            dd = do("dbg_" + name, shape)
            dbg_n[0] += 1
            dma(POOL, dd, ap, "dbg%d" % dbg_n[0], reads, [], True)

    ring_n = [0]

    def ring_load(parts, slot=None, after=()):
        if slot is None:
            s = ring_n[0] % 4
            ring_n[0] += 1
        else:
            s = slot
        base = o_ring + s * SLOT
        for ip, (src, dst_off_elems, nk, ncol) in enumerate(parts):
            dst = V(base + dst_off_elems * 2, nk * ncol, BF16).rearrange("p (k n) -> p k n", k=nk)
            dma(POOL, dst, src, "ring%d" % s, list(after), [("ring", s, ip)], chain=(ip == 0))
        return s, base

    def rstd_from(dst, src_ps_or_sb, scale, reads, writes):
        act(dst, src_ps_or_sb, AF.Ln, list(reads) + ["epsb"], writes, bias=epsb[:, 0:1], scale=scale)
        act(dst, dst, AF.Exp, writes, writes, scale=-0.5)

    epsb = small(1)
    dgs = [V(o_scr + 8 * K + i * 512, 128, F32) for i in range(2)]
    memset(epsb[:], EPS, ["epsb"])

    dma(SP, V(0, 384, F32), cf_d[:, 0:384], "c0", [], ["const"])
    dma(POOL, V(1536, 256, BF16), cb_d[:, 0:256], "c1", [], ["constb"])
    dma(SP, cols[:], cols_d[:, :], "c2", [], ["cols"])
    dma(SP, smalls[:], bcast_row(R_SMALL, 48), "c3", [], ["smalls"])
    dma(SP, tailpos[:], cf_d[:, CF_TAILF:CF_TAILF + 4], "c7", [], ["tailpos"])
    cos_t = V(o_lnp, 1024, F32)
    sin_t = V(o_lnp + 4 * K, 1024, F32)
    scb = V(o_scr + 4 * K, 2048, BF16)
    dma(SP, V(o_lnp, 2048, F32), cf_d[:, CF_COS:CF_COS + 2048], "c4", [], [("LNP", "rope")])
    act(scv[:], cols[:, C_CVEC:C_CVEC + 16], AF.Silu, ["cols"], ["scv"])
    scv3 = scv[:].rearrange("p (k c) -> p k c", c=2)

    def make_scb():
        cp(scb[:].rearrange("p (a m) -> p a m", m=128), scv[:].unsqueeze(2).to_broadcast([128, 16, 128]),
           ["scv"], [("SCR", "scb")])
    scb4 = scb[:].rearrange("p (k c m) -> p k c m", k=8, c=2)

    def mod_fm(ft0, col0, nslots):
        for sl in range(nslots):
            c0 = col0 + sl * 512
            s, base = ring_load([(w_ada_d[:, c0:c0 + 512].rearrange("(k p) n -> p k n", p=128), 0, 8, 512)])
            wv = V(base, 8 * 512, BF16).rearrange("p (k n) -> p k n", k=8)
            for j in range(4):
                ft = ft0 + sl * 4 + j
                for kt in range(8):
                    mm(PS(7, 2, ft * 2), wv[:, kt, j * 128:(j + 1) * 128], scv3[:, kt, :], kt == 0, kt == 7,
                       [("ring", s), "scv"], ["ps7"])
        n = nslots * 4
        tt(modT[:, ft0 * 2:(ft0 + n) * 2].rearrange("p (f c) -> p f c", c=2),
           PS(7, n * 2, ft0 * 2).rearrange("p (f c) -> p f c", c=2),
           cols[:, C_BADA + ft0:C_BADA + ft0 + n].unsqueeze(2).to_broadcast([128, n, 2]), ALU.add,
           ["ps7", "cols"], [("modT", ft0)])

    def gate_load(col0):
        slots = []
        for sl in range(2):
            c0 = col0 + sl * 512
            s, base = ring_load([(w_ada_d[:, c0:c0 + 512].rearrange("(k p) n -> p k n", p=128), 0, 8, 512)])
            slots.append((s, V(base, 8 * 512, BF16).rearrange("p (k n) -> p k n", k=8)))
        return slots

    def gate_compute(slots, dst_off, rowoff):
        make_scb()
        btmp = V(o_scr, 1024, F32)
        dma(SP, btmp[:], bcast_row(rowoff, 1024), "gb", [], [("SCR", "btmp")])
        for sl in range(2):
            s, wv = slots[sl]
            for c in range(2):
                b = 5 + c
                for kt in range(8):
                    mm(PS(b), scb4[:, kt, c, :], wv[:, kt, :], kt == 0, kt == 7, [("ring", s), ("SCR", "scb")], [pk(b)])
                tt(V(dst_off + (c * 1024 + sl * 512) * 4, 512, F32)[:], PS(b), btmp[:, sl * 512:(sl + 1) * 512], ALU.add,
                   [pk(b), ("SCR", "btmp")], [("LNP", "gate", c, sl)])

    def gate_bcast(dst_off, col0, rowoff):
        gate_compute(gate_load(col0), dst_off, rowoff)

    xTf = V(o_F, 8 * NT, F32).rearrange("p (k t) -> p k t", k=8)
    xst = [V(o_B + i * 4 * K, 1024, F32) for i in range(8)]
    x_t = x_d.rearrange("(n p) f -> n p f", p=128)

    def x_load(ck, after=()):
        h_ = ck % 2
        dst = V(o_B + h_ * 16 * K, 4096, F32).rearrange("p (j f) -> p j f", j=4)
        src = x_d[ck * 512:(ck + 1) * 512, :].rearrange("(j p) f -> p j f", p=128)
        dma(SP, dst, src, "xst%d" % h_, list(after), [("B", "xst", h_ * 4 + j) for j in range(4)])
    x_load(0)
    mod_fm(0, 0, 4)
    ts(sc1p[:], modT[:, 16:32], 1.0, ALU.add, [("modT", 0)], ["sc1p"])
    wada_done = [("ring", 0), ("ring", 1), ("ring", 2), ("ring", 3)]
    x_load(1, after=wada_done[:2])
    n_ = 0
    for ck in range(3):
        if ck == 1:
            x_load(2, after=wada_done)
        for kt in range(8):
            b = n_ % 4
            n_ += 1
            for j in range(4):
                jj = (ck % 2) * 4 + j
                tr(PS(b, 128, j * 128), xst[jj][:, kt * 128:(kt + 1) * 128], [("B", "xst", jj)], [pk(b)])
            if n_ % 2 == 0:
                act(xTf[:, kt, ck * 512:(ck + 1) * 512], PS(b), AF.Copy, [pk(b)], [("F", "xTf", kt, ck)])
            else:
                cp(xTf[:, kt, ck * 512:(ck + 1) * 512], PS(b), [pk(b)], [("F", "xTf", kt, ck)])
    sh1 = modT[:, 0:16].rearrange("p (k c) -> p k c", c=2)
    sc1p3 = sc1p[:].rearrange("p (k c) -> p k c", c=2)
    hT = V(o_A, 8 * NT, BF16).rearrange("p (k t) -> p k t", k=8)
    n_ = 0
    for ck in range(3):
        cond = 0 if ck < 2 else 1
        for kt in range(8):
            if n_ % 2 == 0:
                act(hT[:, kt, ck * 512:(ck + 1) * 512], xTf[:, kt, ck * 512:(ck + 1) * 512], AF.Identity,
                    [("F", "xTf", kt, ck), "sc1p", ("modT", 0)], [("A", ck, "hT", kt)],
                    bias=sh1[:, kt, cond:cond + 1], scale=sc1p3[:, kt, cond:cond + 1])
            else:
                ts(hT[:, kt, ck * 512:(ck + 1) * 512], xTf[:, kt, ck * 512:(ck + 1) * 512], sc1p3[:, kt, cond:cond + 1], ALU.mult,
                   [("F", "xTf", kt, ck), "sc1p", ("modT", 0)], [("A", ck, "hT", kt)], s2=sh1[:, kt, cond:cond + 1], op1=ALU.add)
            n_ += 1
    fence(["B", "C", "F", "G"])
    def w_in_chunk(c0, ncol, after=()):
        s, base = ring_load([(w_in_d[:, c0:c0 + ncol].rearrange("(k p) n -> p k n", p=128), 0, 8, ncol)], after=after)
        return s, V(base, 8 * ncol, BF16).rearrange("p (k n) -> p k n", k=8)

    pre_in = {1024: w_in_chunk(1024, 512, after=wada_done[:2])}
    for c0 in (0, 512):
        pre_in[c0] = w_in_chunk(c0, 512, after=[("A", 0, "hT", 7)])

    def w_in_get(c0, ncol):
        return pre_in.pop(c0) if c0 in pre_in else w_in_chunk(c0, ncol)

    delta = V(o_F, 2048, F32)
    E1 = V(o_F + 8 * K, 2048, F32)
    E2 = V(o_F + 16 * K, 2048, F32)
    P.op(POOL, lambda e: e.iota(delta[:], pattern=[[1, 2048]], base=-1024, channel_multiplier=-1,
                                allow_small_or_imprecise_dtypes=True), [], [("F", "delta")], cost=4.5)
    Th = V(o_G, 4 * 2048, BF16).rearrange("p (h u) -> p h u", h=4)
    iotaFB = V(o_G + 16 * K, 2048, F32)
    rowtab = V(o_G + 24 * K, 1024, BF16)
    mscr = [V(o_G + 26 * K, 512, F32), V(o_G + 32 * K, 512, F32)]
    ktm = V(o_G + 28 * K, 2048, BF16).rearrange("p (j d f) -> p j d f", j=2, d=2)
    P.op(POOL, lambda e: e.iota(iotaFB[:, 0:1024], pattern=[[1, 1024]], base=1, channel_multiplier=0,
                                allow_small_or_imprecise_dtypes=True), [], [("G", "iota", 0)], cost=2.3)
    P.op(POOL, lambda e: e.iota(iotaFB[:, 1024:2048], pattern=[[-1, 1024]], base=1024, channel_multiplier=0,
                                allow_small_or_imprecise_dtypes=True), [], [("G", "iota", 1)], cost=2.3)

    u8 = small(8)
    l8 = small(8)
    act(tmp8[:], smalls[:, 40:48], AF.Exp, ["smalls"], ["tmp8"], scale=-1.0)
    ts(u8[:], tmp8[:], 1.0, ALU.add, ["tmp8"], ["u8"])
    act(l8[:], u8[:], AF.Ln, ["u8"], ["l8"])
    ts(u8[:], u8[:], -1.0, ALU.add, ["u8"], ["u8"], s2=1e-30, op1=ALU.max)
    P.op(DVE, lambda e: e.reciprocal(out=u8[:], in_=u8[:]), ["u8"], ["u8"], cost=0.2)
    tt(l8[:], l8[:], u8[:], ALU.mult, ["l8", "u8"], ["l8"])
    tt(tmp8[:], tmp8[:], l8[:], ALU.mult, ["tmp8", "l8"], ["tmp8"])
    ts(lg[:], tmp8[:], -1.0, ALU.mult, ["tmp8"], ["lg"])
    cp(nlg[:], tmp8[:], ["tmp8"], ["nlg"])
    act(nA[:], smalls[:, 16:32], AF.Exp, ["smalls"], ["nA"])
    ts(nA[:], nA[:], -1.0, ALU.mult, ["nA"], ["nA"])
    cp(dskb[:], smalls[:, 32:40], ["smalls"], ["dskb"])
    memset(lscb[:], -0.5 * math.log(128.0), ["lscb"])
    for h in range(4):
        act(E1[:], delta[:], AF.Exp, [("F", "delta"), "lg", "lscb"], [("F", "E1")], bias=lscb[:, 0:1], scale=lg[:, h:h + 1])
        act(E2[:], delta[:], AF.Exp, [("F", "delta"), "nlg", "lscb"], [("F", "E2")], bias=lscb[:, 0:1], scale=nlg[:, 4 + h:5 + h])
        for q in range(4):
            qs = slice(q * 512, (q + 1) * 512)
            tt(Th[:, h, qs], E1[:, qs], E2[:, qs], ALU.min, [("F", "E1"), ("F", "E2")], [("G", "Th", h, q)])
    for d in range(2):
        for j in range(2):
            ts(tailw[:, d * 8 + j * 4:d * 8 + j * 4 + 4], lg[:, d * 4:d * 4 + 4], tailpos[:, d * 2 + j:d * 2 + j + 1],
               ALU.mult, ["lg", "tailpos"], [("tailw", d, j)])
    act(tailw[:], tailw[:], AF.Exp, ["tailw", "lscb"], ["tailw"], bias=lscb[:, 0:1])


    qT = V(o_B, 4 * NT, BF16).rearrange("p (h t) -> p h t", h=4)
    kT = V(o_B + 12 * K, 4 * NT, BF16).rearrange("p (h t) -> p h t", h=4)
    v_tm = V(o_C, NTT * 512, BF16).rearrange("p (n f) -> p n f", n=NTT)
    gT = V(o_D, 4 * NT, BF16).rearrange("p (h t) -> p h t", h=4)
    z_tm = V(o_E, NTT * 512, BF16).rearrange("p (n f) -> p n f", n=NTT)
    XW = 1548
    XOFF = [2, 1030, 1290]
    xbc = V(o_F, 8 * XW, BF16).rearrange("p (c t) -> p c t", c=8)
    rope_tmpB = [[V(o_scr, 512, F32), V(o_scr + 2 * K, 512, F32)], [V(o_lnp + 8 * K, 512, F32), V(o_lnp + 10 * K, 512, F32)]]
    nrope = [0]
    stS = V(o_scr + 4 * K, 1024, F32)

    def tok2xcol(ck):
        if ck < 2:
            return [(0, 512, XOFF[0] + ck * 512)]
        return [(0, 256, XOFF[1]), (256, 256, XOFF[2])]

    def fm_proj(s, wv, j, ck, bank):
        for kt in range(8):
            mm(PS(bank), wv[:, kt, j * 128:(j + 1) * 128], hT[:, kt, ck * 512:(ck + 1) * 512], kt == 0, kt == 7,
               [("ring", s), ("A", ck, "hT", kt)], [pk(bank)])

    def tm_proj(s, wv, t, bank, ncol=512, c0=0):
        for kt in range(8):
            mm(PS(bank, ncol), hT[:, kt, t * 128:(t + 1) * 128], wv[:, kt, c0:c0 + ncol], kt == 0, kt == 7,
               [("ring", s), ("A", t // 4, "hT", kt)], [pk(bank)])

    bk = [0]

    def nb4():
        b = bk[0] % 4
        bk[0] += 1
        return b

    s, wv = w_in_get(1024, 512)
    for t in range(NTT):
        b = nb4()
        tm_proj(s, wv, t, b)
        if t % 2 == 0:
            act(v_tm[:, t, :], PS(b), AF.Copy, [pk(b)], [("C", "v", t)])
        else:
            cp(v_tm[:, t, :], PS(b), [pk(b)], [("C", "v", t)])
    for which, dstT in ((0, qT), (1, kT)):
        s, wv = w_in_get(which * 512, 512)
        for j in range(4):
            for ck in range(3):
                b = nb4()
                fm_proj(s, wv, j, ck, b)
                dst = dstT[:, j, ck * 512:(ck + 1) * 512]
                wr = [("B", "qk", which, j, ck)]
                if ck == 2:
                    act(dst, PS(b), AF.Copy, [pk(b)], wr)
                else:
                    rq = nrope[0] % 2
                    nrope[0] += 1
                    tA, tB = rope_tmpB[rq]
                    rn = "SCR" if rq == 0 else "LNP"
                    tsl = slice(ck * 512, (ck + 1) * 512)
                    tt(tA[:], PS(b), cos_t[:, tsl], ALU.mult, [pk(b), ("LNP", "rope")], [(rn, "ropeA")])
                    tt(tB[0:64, :], PS(b)[64:128, :], sin_t[0:64, tsl], ALU.mult, [pk(b), ("LNP", "rope")], [(rn, "ropeB", 0)])
                    tt(tB[64:128, :], PS(b)[0:64, :], sin_t[64:128, tsl], ALU.mult, [pk(b), ("LNP", "rope")], [(rn, "ropeB", 1)])
                    tt(dst, tA[:], tB[:], ALU.add, [(rn, "ropeA"), (rn, "ropeB", 0), (rn, "ropeB", 1)], wr, eng=POOL)
        if which == 1:
            for sq in range(2):
                for jb in range(2):
                    t = 8 + sq * 2 + jb
                    b = nb4()
                    tm_proj(s, wv, t, b)
                    for d in range(2):
                        for h in range(4):
                            c = d * 8 + jb * 4 + h
                            act(ktm[:, jb, d, h * 128:(h + 1) * 128], PS(b, 128, h * 128), AF.Copy, [pk(b), "tailw"],
                                [("G", "ktm", jb, d, h)], scale=tailw[:, c:c + 1])
                for d in range(2):
                    for h in range(4):
                        for jb in range(2):
                            mm(PS(4 + d, 128, h * 128), ktm[:, jb, d, h * 128:(h + 1) * 128],
                               v_tm[:, 8 + sq * 2 + jb, h * 128:(h + 1) * 128], jb == 0, jb == 1,
                               [("G", "ktm", jb, d, h), ("C", "v", 8 + sq * 2 + jb)], [pk(4 + d)])
                    cp(stS[:, d * 512:(d + 1) * 512], PS(4 + d), [pk(4 + d)], [("SCR", "stS", d)])
                dma(SP, oret_d[sq].rearrange("d h p e -> p d h e"), stS[:].rearrange("p (d h e) -> p d h e", d=2, h=4),
                    "oret", [("SCR", "stS", 0), ("SCR", "stS", 1)], [], True)
    tap("qT", qT, [128, 4, NT], ["B"])
    tap("kT", kT, [128, 4, NT], ["B"])
    tap("v", v_tm, [128, NTT, 512], ["C"])
    s, wv = w_in_chunk(1536, 512)
    for j in range(4):
        for ck in range(3):
            b = nb4()
            fm_proj(s, wv, j, ck, b)
            act(gT[:, j, ck * 512:(ck + 1) * 512], PS(b), AF.Silu, [pk(b)], [("D", "g", j, ck)])
    mod_fm(24, 3072, 2)
    s, wv = w_in_chunk(2048, 512)
    for t in range(NTT):
        b = nb4()
        tm_proj(s, wv, t, b)
        act(z_tm[:, t, :], PS(b), AF.Silu, [pk(b)], [("E", "z", t)])
    mod_fm(32, 4096, 2)
    ts(sc2p[:], modT[:, 64:80], 1.0, ALU.add, [("modT", 32)], ["sc2p"])
    fence(["F"])
    for (c0_, c1_) in ((0, 2), (1026, 1030), (1286, 1290), (1546, 1548)):
        memset(xbc[:, :, c0_:c1_], 0.0, [("F", "xbc", "pad", c0_)], eng=POOL)
    for half in range(2):
        ncol = 512 if half == 0 else 520
        s, wv = w_in_chunk(2560 + half * 512, ncol)
        for j in range(4):
            ct = half * 4 + j
            for ck in range(3):
                b = nb4()
                fm_proj(s, wv, j, ck, b)
                for (o, n, xc) in tok2xcol(ck):
                    if (j + ck) % 2 == 0:
                        act(xbc[:, ct, xc:xc + n], PS(b, n, o), AF.Copy, [pk(b)], [("F", "xbc", ct, xc)])
                    else:
                        cp(xbc[:, ct, xc:xc + n], PS(b, n, o), [pk(b)], [("F", "xbc", ct, xc)])
        if half == 0:
            mod_fm(16, 2048, 2)
        if half == 1:
            for t in range(NTT):
                for kt in range(8):
                    mm(PS(7, 8, t * 8), hT[:, kt, t * 128:(t + 1) * 128], wv[:, kt, 512:520], kt == 0, kt == 7,
                       [("ring", s), ("A", t // 4, "hT", kt)], ["ps7"])
            cp(dtraw[:], PS(7, 96), ["ps7"], ["dtraw"])
    mod_fm(40, 5120, 2)
    pre_wo = []
    for half in range(2):
        s, base = ring_load([(w_out_d[:, half * 512:(half + 1) * 512].rearrange("(k p) n -> p k n", p=128), 0, 8, 512)], slot=2 + half)
        pre_wo.append((s, V(base, 8 * 512, BF16).rearrange("p (k n) -> p k n", k=8)))

    def ffn_slot(sl):
        c0 = sl * 256
        return ring_load([(w_gate_d[:, c0:c0 + 256].rearrange("(k p) n -> p k n", p=128), 0, 8, 256),
                          (w_up_d[:, c0:c0 + 256].rearrange("(k p) n -> p k n", p=128), 2048, 8, 256)], slot=sl % 2)
    ffn_pre = [ffn_slot(0), ffn_slot(1)]

    def gate_from_mod(ft0, dsts, dkeys):
        n_ = 0
        for c in range(2):
            for half in range(2):
                b = 4 + (n_ % 2)
                n_ += 1
                for k4 in range(4):
                    kt = half * 4 + k4
                    q = kt % 2
                    ts(dgs[q][:], ident_f[:], modT[:, (ft0 + kt) * 2 + c:(ft0 + kt) * 2 + c + 1], ALU.mult,
                       ["const", ("modT", ft0)], [("SCR", "dgs", q)])
                    mm(PS(b, 128, k4 * 128), ones_f[:], dgs[q][:], True, True, ["const", ("SCR", "dgs", q)], [pk(b)])
                cp(dsts[c][:, half * 512:(half + 1) * 512], PS(b), [pk(b)], [dkeys[c] + (half,)])

    fence(["A", "LNP", "SCR", ("G", "ktm")])
    oT = V(o_A, 8 * NT, BF16).rearrange("p (k t) -> p k t", k=8)
    PTb = [V(o_lnp + i * 8 * K, 8 * 512, BF16).rearrange("p (i t) -> p i t", i=8) for i in range(2)]
    rs_fB = [V(o_scr + 4 * K + i * 2 * K, 512, F32) for i in range(2)]
    sq_bB = [V(o_G + 28 * K + i * K, 512, BF16) for i in range(2)]
    qfb = [V(o_G + 30 * K + i * K, 512, BF16) for i in range(2)]
    S0b = V(o_scr + 8 * K, 1024, BF16).rearrange("p (d h e) -> p d h e", d=2, h=4)
    PTp = [V(o_scr + i * K, 512, BF16).rearrange("p (i t) -> p i t", i=2) for i in range(2)]
    rs_p = [V(o_scr + 2 * K + i * K, 256, F32) for i in range(2)]
    sq_p = [V(o_G + 34 * K, 256, BF16)] * 2
    dma(POOL, S0b, sret_d[:, :].rearrange("p (d h e) -> p d h e", d=2, h=4), "s0r", [], [("SCR", "S0b")])
    nmask = [0]
    ucount = {True: 0, False: 0}

    def ret_unit(t0, nb, h, r0, W):
        is_sample = nb == 8
        tok0 = t0 * 128
        q_ = ucount[is_sample] % 2
        ucount[is_sample] += 1
        if is_sample:
            pt, ptkey, ob, msb = PTb[q_], ("LNP", "PT", q_), 2 + q_, 4 + q_
            rsf_, sqb_, rkey, skey = rs_fB[q_], sq_bB[q_], ("SCR", "rs_f", q_), ("G", "ktm", "sq", q_)
        else:
            pt, ptkey, ob, msb = PTp[q_], ("SCR", "PTp", q_), 6, 7
            rsf_, sqb_, rkey, skey = rs_p[q_], sq_p[q_], ("SCR", "rs_p", q_), ("G", "ktm", "sqp")
        for i in range(nb):
            b = i % 2
            mm(PS(b, W), kT[:, h, tok0 + i * 128:tok0 + (i + 1) * 128], qT[:, h, tok0 + r0:tok0 + r0 + W], True, True,
               [("B", "qk")], [pk(b)])
            u0 = r0 - 128 * i + 1024
            if is_sample and i % 3 == 2:
                mq = nmask[0] % 2
                nmask[0] += 1
                act(mscr[mq][:, 0:W], PS(b, W), AF.Copy, [pk(b)], [("G", "mscr", mq)])
                tt(pt[:, i, 0:W], mscr[mq][:, 0:W], Th[:, h, u0:u0 + W], ALU.mult, [("G", "mscr", mq), ("G", "Th", h)],
                   [ptkey + (i,)], eng=POOL)
            else:
                tt(pt[:, i, 0:W], PS(b, W), Th[:, h, u0:u0 + W], ALU.mult, [pk(b), ("G", "Th", h)], [ptkey + (i,)])
        if is_sample:
            for d in range(2):
                act(rowtab[:, 0:W], iotaFB[:, d * 1024 + r0:d * 1024 + r0 + W], AF.Exp, [("G", "iota"), "lg"],
                    [("G", "rowtab")], scale=lg[:, d * 4 + h:d * 4 + h + 1])
                tt(qfb[d][:, 0:W], qT[:, h, tok0 + r0:tok0 + r0 + W], rowtab[:, 0:W], ALU.mult,
                   [("B", "qk"), ("G", "rowtab")], [("G", "ktm", "qf", d)])
        nmm = nb + (2 if is_sample else 0)
        for i in range(nb):
            mm(PS(ob, W), v_tm[:, t0 + i, h * 128:(h + 1) * 128], pt[:, i, 0:W], i == 0, i == nmm - 1,
               [("C", "v", t0 + i), ptkey + (i,)], [pk(ob)])
        if is_sample:
            for d in range(2):
                mm(PS(ob, W), S0b[:, d, h, :], qfb[d][:, 0:W], False, d == 1,
                   [("SCR", "S0b"), ("G", "ktm", "qf", d)], [pk(ob)])
        act(sqb_[:, 0:W], PS(ob, W), AF.Square, [pk(ob)], [skey])
        mm(PS(msb, W), ones_b[:], sqb_[:, 0:W], True, True, [skey, "constb"], [pk(msb)])
        rstd_from(rsf_[:, 0:W], PS(msb, W), 1.0 / 128.0, [pk(msb)], [rkey])
        tt(rsf_[:, 0:W], PS(ob, W), rsf_[:, 0:W], ALU.mult, [pk(ob), rkey], [rkey])
        tt(oT[:, h, tok0 + r0:tok0 + r0 + W], rsf_[:, 0:W], gT[:, h, tok0 + r0:tok0 + r0 + W], ALU.mult,
           [rkey, ("D", "g")], [("A", (tok0 + r0) // 512, "oT", h, t0, r0)], eng=POOL)

    s_units = [(0, 8, h, r0, 512) for h in range(4) for r0 in (0, 512)]
    p_units = [(t0, 2, h, 0, 256) for t0 in (8, 10) for h in range(4)]
    for su, pu in zip(s_units, p_units):
        ret_unit(*su)
        ret_unit(*pu)
    tap("oTr", oT[:, 0:4, :], [128, 4, NT], ["A"])

    fence(["B", "C", "D", "G", "SCR", "LNP"])
    cstS = V(o_lnp, 5120, BF16)
    dma(POOL, cstS[:], cb_d[:, CB_MF:CB_MF + 5120], "cstS", [], [("LNP", "cstS")])
    xs_tm = V(o_B, NTT * 512, BF16).rearrange("p (n f) -> p n f", n=NTT)
    B_tm = V(o_B + 12 * K, NTT * 256, BF16).rearrange("p (n f) -> p n f", n=NTT)
    BT = V(o_B + 18 * K, 2 * NT, BF16).rearrange("p (g t) -> p g t", g=2)
    CT = V(o_C, 2 * NT, BF16).rearrange("p (g t) -> p g t", g=2)
    diagw = V(o_G, 8 * 5 * 128, BF16).rearrange("p (c j m) -> p c j m", c=8, j=5)
    xsT = V(o_G + 12 * K, 4 * NT, BF16).rearrange("p (c t) -> p c t", c=4)
    nd_ = 0
    for ct in range(8):
        for j in range(5):
            sc_ = cols[:, C_CONVW + ct * 5 + j:C_CONVW + ct * 5 + j + 1]
            if nd_ % 3 == 0:
                ts(diagw[:, ct, j, :], ident_f[:], sc_, ALU.mult, ["const", "cols"], [("G", "diagw", ct, j)])
            elif nd_ % 3 == 1:
                act(diagw[:, ct, j, :], ident_f[:], AF.Copy, ["const", "cols"], [("G", "diagw", ct, j)], scale=sc_)
            else:
                ts(diagw[:, ct, j, :], ident_f[:], sc_, ALU.mult, ["const", "cols"], [("G", "diagw", ct, j)], eng=POOL)
            nd_ += 1
    nbk = 0
    for ct in range(8):
        for ck in range(3):
            for (o, n, xc) in tok2xcol(ck):
                bank = nbk % 4
                nbk += 1
                for j in range(5):
                    mm(PS(bank, n), diagw[:, ct, j, :], xbc[:, ct, xc + j - 2:xc + j - 2 + n], j == 0, j == 4,
                       [("F", "xbc"), ("G", "diagw", ct)], [pk(bank)])
                if ct < 4:
                    dstT, g, key = xsT, ct, ("G", "xsT", ct, ck, o)
                elif ct < 6:
                    dstT, g, key = BT, ct - 4, ("B", "BT", ct - 4, ck, o)
                else:
                    dstT, g, key = CT, ct - 6, ("C", "CT", ct - 6, ck, o)
                act(dstT[:, g, ck * 512 + o:ck * 512 + o + n], PS(bank, n), AF.Silu, [pk(bank), "cols"], [key],
                    bias=cols[:, C_CONVB + ct:C_CONVB + ct + 1])
    for t in range(NTT):
        bx = 4 + 2 * (t % 2)
        for c in range(4):
            mm(PS(bx, 128, c * 128), xsT[:, c, t * 128:(t + 1) * 128], ident_b[:], True, True,
               [("G", "xsT", c), "constb"], [pk(bx)])
        for c in range(2):
            mm(PS(bx + 1, 128, c * 128), BT[:, c, t * 128:(t + 1) * 128], ident_b[:], True, True,
               [("B", "BT", c), "constb"], [pk(bx + 1)])
        cp(xs_tm[:, t, :], PS(bx), [pk(bx)], [("B", "xs", t)])
        act(B_tm[:, t, :], PS(bx + 1, 256), AF.Copy, [pk(bx + 1)], [("B", "Btm", t)])
    tap("xsb", V(o_B, NTT * 768, BF16).rearrange("p (n f) -> p n f", n=NTT), [128, NTT, 768], ["B"]) if False else None
    tap("BCT", V(o_B + 18 * K, 4 * NT, BF16).rearrange("p (g t) -> p g t", g=4), [128, 4, NT], ["B", "C"])

    rsT = V(o_scr + 7 * K, NT, BF16)
    pool_off = [o_G + 24 * K]

    def pl(n, dt=F32):
        a = V(pool_off[0], n, dt)
        pool_off[0] += n * SZ[dt]
        assert pool_off[0] <= o_G + 35 * K - 1024
        return a
    X = pl(192).rearrange("p (b d) -> p b d", b=NTT)
    AX = pl(192).rearrange("p (b d) -> p b d", b=NTT)
    DT = pl(192).rearrange("p (b d) -> p b d", b=NTT)
    LA = pl(192).rearrange("p (b d) -> p b d", b=NTT)
    LNDT = pl(192).rearrange("p (b d) -> p b d", b=NTT)
    CUM = pl(192).rearrange("p (b d) -> p b d", b=NTT)
    TOT = pl(192).rearrange("p (b d) -> p b d", b=NTT)
    RSRC = pl(384).rearrange("p (b d) -> p b d", b=NTT)
    EXPO = V(o_scr + 4 * K, 576, F32).rearrange("p (b d) -> p b d", b=NTT)
    EXPIN = V(o_D, 576, F32).rearrange("p (b d) -> p b d", b=NTT)
    SPL3 = V(o_D + 2304, 1152, F32).rearrange("p (b d) -> p b d", b=NTT)
    R1 = V(o_D + 2304 + 4608, 384, F32).rearrange("p (b d) -> p b d", b=NTT)
    H1B = V(o_D + 2304 + 4608 + 1536, 384, BF16).rearrange("p (b d) -> p b d", b=NTT)
    PF = [("G", "pool")]
    PD = [("D", "pool")]
    dtr3 = dtraw[:].rearrange("p (b h) -> p b h", b=NTT)
    X4 = X.rearrange("p b (d h) -> p b d h", d=2)
    tt(X4, dtr3.unsqueeze(2).to_broadcast([128, NTT, 2, 8]),
       smalls[:, 0:16].rearrange("p (d h) -> p d h", d=2).unsqueeze(1).to_broadcast([128, NTT, 2, 8]), ALU.add,
       ["dtraw", "smalls"], PF)
    UU = pl(192).rearrange("p (b d) -> p b d", b=NTT)
    LL = pl(192).rearrange("p (b d) -> p b d", b=NTT)
    act(AX, X, AF.Abs, PF, PF)
    act(AX, AX, AF.Exp, PF, PF, scale=-1.0)
    ts(UU, AX, 1.0, ALU.add, PF, PF)
    act(LL, UU, AF.Ln, PF, PF)
    ts(UU, UU, -1.0, ALU.add, PF, PF, s2=1e-30, op1=ALU.max)
    P.op(DVE, lambda e: e.reciprocal(out=UU, in_=UU), PF, PF, cost=0.3)
    tt(LL, LL, UU, ALU.mult, PF, PF)
    tt(AX, AX, LL, ALU.mult, PF, PF)
    ts(X, X, 0.0, ALU.max, PF, PF)
    tt(DT, X, AX, ALU.add, PF, PF)
    ts(DT, DT, 1e-30, ALU.max, PF, PF)
    tt(LA, DT, nA[:].unsqueeze(1).to_broadcast([128, NTT, 16]), ALU.mult, PF + ["nA"], PF)
    act(LNDT, DT, AF.Ln, PF, PF)
    for b in range(NTT):
        mm(PS(0, 16, b * 16), utri_f[:], LA[:, b, :], True, True, PF + ["const"], ["ps0"])
        mm(PS(1, 16, b * 16), ones_f[:], LA[:, b, :], True, True, PF + ["const"], ["ps1"])
    cp(CUM, PS(0, 192).rearrange("p (b d) -> p b d", b=NTT), ["ps0"], PF)
    cp(TOT, PS(1, 192).rearrange("p (b d) -> p b d", b=NTT), ["ps1"], PF)
    cp(RSRC[:, :, 0:8], CUM[:, :, 0:8], PF, PF)
    tt(RSRC[:, :, 8:16], LA[:, :, 8:16], CUM[:, :, 8:16], ALU.subtract, PF, PF)
    tt(RSRC[:, :, 16:24], LNDT[:, :, 0:8], CUM[:, :, 0:8], ALU.subtract, PF, PF)
    tt(RSRC[:, :, 24:32], LNDT[:, :, 8:16], RSRC[:, :, 8:16], ALU.subtract, PF, PF)
    cp(EXPIN[:, :, 0:8], RSRC[:, :, 0:8], PF, PD)
    tt(EXPIN[:, :, 8:16], TOT[:, :, 8:16], RSRC[:, :, 8:16], ALU.add, PF, PD)
    tt(EXPIN[:, :, 16:24], TOT[:, :, 0:8], RSRC[:, :, 16:24], ALU.add, PF, PD)
    cp(EXPIN[:, :, 24:32], RSRC[:, :, 24:32], PF, PD)
    cp(EXPIN[:, :, 32:48], TOT, PF, PD)
    act(EXPO, EXPIN, AF.Exp, PD, [("SCR", "expo")])
    cp(H1B, RSRC, PF, PD)
    cp(SPL3[:, :, 0:32], H1B, PD, PD)
    tt(R1, RSRC, SPL3[:, :, 0:32], ALU.subtract, PF + PD, PD)
    cp(H1B, R1, PD, PD)
    cp(SPL3[:, :, 32:64], H1B, PD, PD)
    tt(R1, R1, SPL3[:, :, 32:64], ALU.subtract, PD, PD)
    cp(H1B, R1, PD, PD)
    cp(SPL3[:, :, 64:96], H1B, PD, PD)
    for ck in range(3):
        for j in range(4):
            tr(PS(2 + ck % 2, 128, j * 128)[0:96, :], SPL3[:, ck * 4 + j, :], PD, [pk(2 + ck % 2)])
        cp(rsT[0:96, ck * 512:(ck + 1) * 512], PS(2 + ck % 2)[0:96, :], [pk(2 + ck % 2)], [("SCR", "rsT", ck)])
    fence(["D", "F", "G", "SCR"])
    MF4, MB4 = cstS[:, 0:512], cstS[:, 512:1024]
    selrow = cstS[:, 1024:3072].rearrange("p (a m) -> p a m", a=16)
    selbias = cstS[:, 3072:5120].rearrange("p (d n) -> p d n", d=2)
    dI = V(o_G + 30 * K, 1024, BF16).rearrange("p (h m) -> p h m", h=8)
    for h in range(8):
        ts(dI[:, h, :], ident_f[:], dskb[:, h:h + 1], ALU.mult, ["const", "dskb"], [("G", "dI")])
    nwb = V(o_G + 32 * K, 512, F32)
    dma(SP, nwb[:], bcast_row(R_NW, 512), "nwb", [], [("G", "nwb")])
    WfB = [V(o_G + i * 11 * K, 1024, BF16) for i in range(2)]
    WbB = [V(o_G + i * 11 * K + 2 * K, 1024, BF16) for i in range(2)]
    PmB = [V(o_G + i * 11 * K + 4 * K, 1024, BF16).rearrange("p (h t) -> p h t", h=8) for i in range(2)]
    y1B = [V(o_G + i * 11 * K + 6 * K, 512, F32) for i in range(2)]
    y2B = [V(o_G + i * 11 * K + 8 * K, 512, F32) for i in range(2)]
    jkB = [V(o_G + i * 11 * K + 10 * K, 512, BF16) for i in range(2)]
    xswB = [[V(o_G + 22 * K + (d * 2 + i) * K, 512, BF16) for i in range(2)] for d in range(2)]
    Sst = [V(o_G + 26 * K, 512, F32), V(o_G + 28 * K, 512, F32)]
    ssqB = [small(1) for _ in range(2)]
    rstdB = [small(1) for _ in range(2)]
    fence(["D"], "ssdpre")
    Rall = V(o_D, 8 * 512, BF16).rearrange("p (b f) -> p b f", b=8)
    SallA = V(o_D + 8 * K, 4 * 512, BF16).rearrange("p (b f) -> p b f", b=4)
    SallB = V(o_C + 6 * K, 4 * 512, BF16).rearrange("p (b f) -> p b f", b=4)

    def Sall(i):
        return SallA[:, i, :] if i < 4 else SallB[:, i - 4, :]

    def Skey(i):
        return ("D", "S", i) if i < 4 else ("C", "S", i)
    psum2 = lambda b0: psum[:, b0 * 512:b0 * 512 + 1024]

    def bc8(ap8):
        return ap8.unsqueeze(2).to_broadcast([128, 8, 64])

    def v8(ap512):
        return ap512.rearrange("p (h e) -> p h e", h=8)

    nxsw = [0, 0]

    def state_update(d, b, bank):
        w = EXPO[:, b, 16 + d * 8:24 + d * 8]
        k = nxsw[d] % 2
        nxsw[d] += 1
        xw = xswB[d][k]
        tt(v8(xw[:]), v8(xs_tm[:, b, :]), bc8(w), ALU.mult, [("B", "xs", b), ("SCR", "expo")], [("G", "xsw", d, k)], eng=POOL)
        for g in range(2):
            mm(PS(bank, 256, g * 256), B_tm[:, b, g * 128:(g + 1) * 128], xw[:, g * 256:(g + 1) * 256], True, True,
               [("B", "Btm", b), ("G", "xsw", d, k)], [pk(bank)])
        tt(v8(Sst[d][:]), v8(Sst[d][:]), bc8(EXPO[:, b, 32 + d * 8:40 + d * 8]), ALU.mult, [("G", "S", d), ("SCR", "expo")],
           [("G", "S", d)], eng=POOL)
        tt(Sst[d][:], Sst[d][:], PS(bank), ALU.add, [("G", "S", d), pk(bank)], [("G", "S", d)])

    RallS = V(o_scr, 2 * 512, BF16).rearrange("p (b f) -> p b f", b=2)
    SallS = V(o_scr + 2 * K, 2 * 512, BF16).rearrange("p (b f) -> p b f", b=2)

    def Sv(big, i):
        return (Sall(i), Skey(i)) if big else (SallS[:, i, :], ("SCR", "S", i))

    def Rv(big, i):
        return (Rall[:, i, :], ("D", "R", i)) if big else (RallS[:, i, :], ("SCR", "R", i))

    def chain_steps(si):
        t0, nb = SEQS[si]
        big = nb == 8
        if big:
            dma(SP, Sst[0][:], sssd_d[:, 0:512], "s0s0", [], [("G", "S", 0)])
            dma(SP, Sst[1][:], sssd_d[:, 512:1024], "s0s1", [], [("G", "S", 1)])
        else:
            memset(Sst[0][:], 0.0, [("G", "S", 0)])
            memset(Sst[1][:], 0.0, [("G", "S", 1)])
        for step in range(nb):
            i_f = step
            i_b = nb - 1 - step
            sv, sk = Sv(big, i_f)
            rv, rk = Rv(big, i_b)
            act(sv, Sst[0][:], AF.Copy, [("G", "S", 0)], [sk])
            act(rv, Sst[1][:], AF.Copy, [("G", "S", 1)], [rk])
            if i_f < nb - 1 or not big:
                state_update(0, t0 + i_f, 6)
            if i_b > 0 or not big:
                state_update(1, t0 + i_b, 7)
            yield
        if not big:
            sq = si - 1
            for d in range(2):
                dma(SP, ossd_d[sq, d].rearrange("h n e -> n h e"), Sst[d][:].rearrange("p (h e) -> p h e", h=8), "ossd%d" % d,
                    [("G", "S", d)], [], True)
        yield

    nblk = [0]
    kbof = {}

    def stageA(si, i):
        t0, nb = SEQS[si]
        b = t0 + i
        tok = b * 128
        kb = nblk[0] % 2
        nblk[0] += 1
        kbof[b] = kb
        Wf, Wb, Pm = WfB[kb], WbB[kb], PmB[kb]
        bk_ = lambda n: ("G", n, kb)
        for g in range(2):
            mm(PS(4, 128, g * 128), BT[:, g, tok:tok + 128], CT[:, g, tok:tok + 128], True, True,
               [("B", "BT", g), ("C", "CT", g)], ["ps4"])
        for d in range(2):
            bank0 = 2 * d
            wr = [pk(bank0), pk(bank0 + 1)]
            Md = MF4 if d == 0 else MB4
            for half in range(2):
                mm(PS(bank0 + half), ident_b[:], Md, True, False, ["constb", ("LNP", "cstS")], wr)
                mm(PS(bank0 + half), rsT[0:96, tok:tok + 128], selbias[0:96, d, half * 512:(half + 1) * 512], False, False,
                   [("SCR", "rsT", b // 4), ("LNP", "cstS")], wr)
            for h in range(8):
                mm(PS(bank0 + h // 4, 128, (h % 4) * 128), selrow[0:96, d * 8 + h, :], rsT[0:96, tok:tok + 128], False, h % 4 == 3,
                   [("SCR", "rsT", b // 4), ("LNP", "cstS")], wr)
            act((Wf if d == 0 else Wb)[:], psum2(bank0), AF.Exp, wr, [bk_("W%d" % d)])
        tt(Wf[:], Wf[:], Wb[:], ALU.add, [bk_("W0"), bk_("W1")], [bk_("W0")])
        tt(Pm.rearrange("p (g q) t -> p g q t", g=2), Wf[:].rearrange("p (g q t) -> p g q t", g=2, q=4),
           PS(4, 256).rearrange("p (g t) -> p g t", g=2).unsqueeze(2).to_broadcast([128, 2, 4, 128]), ALU.mult,
           [bk_("W0"), "ps4"], [bk_("Pm")])

    def stageB(si, i):
        t0, nb = SEQS[si]
        big = nb == 8
        b = t0 + i
        tok = b * 128
        kb = kbof[b]
        Pm, y1, y2, jk = PmB[kb], y1B[kb], y2B[kb], jkB[kb]
        ssq_, rstd_ = ssqB[kb], rstdB[kb]
        bk_ = lambda n: ("G", n, kb)
        sv, sk = Sv(big, i)
        rv, rk = Rv(big, i)
        for h in range(8):
            mm(PS(5, 64, h * 64), Pm[:, h, :], xs_tm[:, b, h * 64:(h + 1) * 64], True, False,
               [bk_("Pm"), ("B", "xs", b)], ["ps5"])
            mm(PS(5, 64, h * 64), dI[:, h, :], xs_tm[:, b, h * 64:(h + 1) * 64], False, True,
               [("G", "dI"), ("B", "xs", b)], ["ps5"])
        for g in range(2):
            mm(PS(6, 256, g * 256), CT[:, g, tok:tok + 128], sv[:, g * 256:(g + 1) * 256], True, True,
               [("C", "CT", g), sk], ["ps6"])
            mm(PS(7, 256, g * 256), CT[:, g, tok:tok + 128], rv[:, g * 256:(g + 1) * 256], True, True,
               [("C", "CT", g), rk], ["ps7"])
        tt(v8(y1[:]), v8(PS(6)), bc8(EXPO[:, b, 0:8]), ALU.mult, ["ps6", ("SCR", "expo")], [bk_("y1")])
        tt(v8(y2[:]), v8(PS(7)), bc8(EXPO[:, b, 8:16]), ALU.mult, ["ps7", ("SCR", "expo")], [bk_("y2")])
        tt(y1[:], y1[:], PS(5), ALU.add, [bk_("y1"), "ps5"], [bk_("y1")])
        tt(y1[:], y1[:], y2[:], ALU.add, [bk_("y1"), bk_("y2")], [bk_("y1")])
        tt(y1[:], y1[:], z_tm[:, b, :], ALU.mult, [bk_("y1"), ("E", "z", b)], [bk_("y1")], eng=POOL)
        act(jk[:], y1[:], AF.Square, [bk_("y1")], [bk_("jk"), ("ssq", kb)], accum_out=ssq_[:, 0:1])
        rstd_from(rstd_[:, 0:1], ssq_[:, 0:1], 1.0 / 512.0, [("ssq", kb)], [("rstd", kb)])
        stt(y2[:], y1[:], rstd_[:, 0:1], nwb[:], ALU.mult, ALU.mult, [bk_("y1"), ("rstd", kb), ("G", "nwb")], [bk_("y2")])

    def stageC(si, i):
        t0, nb = SEQS[si]
        b = t0 + i
        tok = b * 128
        kb = kbof[b]
        y2 = y2B[kb]
        for j in range(4):
            tr(PS(4, 128, j * 128), y2[:, j * 128:(j + 1) * 128], [("G", "y2", kb)], ["ps4"])
        act(oT[:, 4:8, tok:tok + 128], PS(4).rearrange("p (j t) -> p j t", j=4), AF.Copy, ["ps4"], [("A", b // 4, "oT", 4, b)])

    seq_order = [1, 0, 2]
    blocks = [(si, i) for si in seq_order for i in range(SEQS[si][1])]
    first_of = {}
    for n_, (si, i) in enumerate(blocks):
        first_of.setdefault(si, n_)
    for _ in chain_steps(seq_order[0]):
        pass
    pending_chain = None
    nbk = len(blocks)
    for step in range(nbk + 2):
        if step < nbk:
            si, i = blocks[step]
            if i == 0:
                pos = seq_order.index(si)
                if pos + 1 < len(seq_order):
                    pending_chain = chain_steps(seq_order[pos + 1])
            stageA(si, i)
        if 1 <= step <= nbk:
            stageB(*blocks[step - 1])
        if 2 <= step:
            stageC(*blocks[step - 2])
        if pending_chain is not None:
            si_cur = blocks[min(step, nbk - 1)][0]
            nsteps = 4 if SEQS[si_cur][1] == 2 else 1
            for _ in range(nsteps):
                try:
                    next(pending_chain)
                except StopIteration:
                    pending_chain = None
                    break
    tap("oT", oT, [128, 8, NT], ["A"])

    fence(["B", "C", "D", "E", "F", "G", "LNP", "SCR"])
    x1 = [V(o_B + t * 4 * K, 1024, F32) for t in range(NTT)]
    xs2 = [V(o_G + j * 4 * K, 1024, F32) for j in range(2)]
    xs2k = [[("G", "xs2", 0)], [("G", "xs2", 1)]]
    lng = V(o_lnp, 1024, F32)
    lnb = V(o_lnp + 4 * K, 1024, F32)
    gateb = [V(o_lnp + 8 * K + c * 4 * K, 1024, F32) for c in range(2)]

    rstdL = [small(1) for _ in range(2)]
    nmrB = [small(1) for _ in range(2)]

    s1B = [small(1) for _ in range(2)]
    s2B = [small(1) for _ in range(2)]
    mB = [small(1) for _ in range(2)]
    vB = [small(1) for _ in range(2)]
    ljunk = V(o_scr, 1024, BF16)

    def ln_affine(t, dst, dst_key, lng, lnb, lkey):
        tt(dst[:], dst[:], lng[:], ALU.mult, [dst_key, lkey + ("g",)], [dst_key], eng=POOL)
        tt(dst[:], dst[:], lnb[:], ALU.add, [dst_key, lkey + ("b",)], [dst_key], eng=POOL)

    def layer_norm_tile(t, banks, xres, xres_key, dst, dst_key, lng, lnb, gateb, gkey, lkey, defer_affine=False):
        cond = 0 if t < 8 else 1
        q = t % 2
        s1, s2, m_, v_, rstd, nmr = s1B[q], s2B[q], mB[q], vB[q], rstdL[q], nmrB[q]
        u = dst
        xk = xres_key if isinstance(xres_key, list) else [xres_key]
        for half in range(2):
            hs = slice(half * 512, (half + 1) * 512)
            tt(u[:, hs], PS(banks[half]), gateb[cond][:, hs], ALU.mult, [pk(banks[half]), gkey + (cond, half)], [dst_key + (half,)])
            stt(u[:, hs], xres[:, hs], ALPHA, u[:, hs], ALU.mult, ALU.add, xk + [dst_key + (half,)], [dst_key + (half,)])
        act(ljunk[:], u[:], AF.Copy, [dst_key], [("SCR", "ljunk"), ("s1", q)], accum_out=s1[:, 0:1])
        act(ljunk[:], u[:], AF.Square, [dst_key], [("SCR", "ljunk"), ("s2", q)], accum_out=s2[:, 0:1])
        ts(m_[:, 0:1], s1[:, 0:1], 1.0 / 1024.0, ALU.mult, [("s1", q)], [("m", q)])
        tt(v_[:, 0:1], m_[:, 0:1], m_[:, 0:1], ALU.mult, [("m", q)], [("v", q)])
        stt(v_[:, 0:1], s2[:, 0:1], 1.0 / 1024.0, v_[:, 0:1], ALU.mult, ALU.subtract, [("s2", q), ("v", q)], [("v", q)])
        rstd_from(rstd[:, 0:1], v_[:, 0:1], 1.0, [("v", q)], [("rstdL", q)])
        ts(nmr[:, 0:1], m_[:, 0:1], rstd[:, 0:1], ALU.mult, [("m", q), ("rstdL", q)], [("nmr", q)], s2=-1.0, op1=ALU.mult)
        act(u[:], u[:], AF.Identity, [dst_key, ("rstdL", q), ("nmr", q)], [dst_key], bias=nmr[:, 0:1], scale=rstd[:, 0:1])
        if not defer_affine:
            ln_affine(t, dst, dst_key, lng, lnb, lkey)

    dma(SP, lng[:], bcast_row(R_LN1G, 1024), "lnpg", [], [("LNP", "ln", "g")])
    dma(SP, lnb[:], bcast_row(R_LN1B, 1024), "lnpb", [], [("LNP", "ln", "b")])
    gate_from_mod(16, gateb, [("LNP", "gate", 0), ("LNP", "gate", 1)])
    wo = pre_wo
    for t in range(NTT):
        j = t % 2
        dma(SP, xs2[j][:], x_t[t], "xs2%d" % j, [], xs2k[j])
        banks = (2 * (t % 4), 2 * (t % 4) + 1)
        for half in range(2):
            s, wv = wo[half]
            for kt in range(8):
                mm(PS(banks[half]), oT[:, kt, t * 128:(t + 1) * 128], wv[:, kt, :], kt == 0, kt == 7,
                   [("ring", s), ("A", t // 4)], [pk(banks[half])])
        layer_norm_tile(t, banks, xs2[j], xs2k[j], x1[t], ("B", "x1", t), lng, lnb, gateb, ("LNP", "gate"), ("LNP", "ln"), defer_affine=True)
    tap("x1", V(o_B, NTT * 1024, F32).rearrange("p (n f) -> p n f", n=NTT), [128, NTT, 1024], ["B"])

    sc2p3 = sc2p[:].rearrange("p (k c) -> p k c", c=2)
    sh2 = modT[:, 48:64].rearrange("p (k c) -> p k c", c=2)
    scale2 = small(16)
    bias2 = small(16)
    tt(scale2[:].rearrange("p (k c) -> p k c", c=2), sc2p3, cols[:, C_LN1G:C_LN1G + 8].unsqueeze(2).to_broadcast([128, 8, 2]), ALU.mult,
       ["sc2p", "cols"], ["scale2"])
    tt(bias2[:].rearrange("p (k c) -> p k c", c=2), sc2p3, cols[:, C_LN1B:C_LN1B + 8].unsqueeze(2).to_broadcast([128, 8, 2]), ALU.mult,
       ["sc2p", "cols"], ["bias2"])
    tt(bias2[:], bias2[:], modT[:, 48:64], ALU.add, ["bias2", ("modT", 24)], ["bias2"])
    scale23 = scale2[:].rearrange("p (k c) -> p k c", c=2)
    bias23 = bias2[:].rearrange("p (k c) -> p k c", c=2)
    h2T = V(o_A, 8 * NT, BF16).rearrange("p (k t) -> p k t", k=8)
    aT = V(o_E, 22 * NT, BF16).rearrange("p (f t) -> p f t", f=22)
    sg = [V(o_scr + 2 * K + i * K, 512, BF16) for i in range(2)]
    def h2t_chunk(ck, extra=()):
        cond = 0 if ck < 2 else 1
        fence([("A", ck)])
        for kt in range(8):
            b = kt % 4
            for j in range(4):
                tr(PS(b, 128, j * 128), x1[ck * 4 + j][:, kt * 128:(kt + 1) * 128], [("B", "x1", ck * 4 + j)] + list(extra), [pk(b)])
            if kt % 2 == 0:
                act(h2T[:, kt, ck * 512:(ck + 1) * 512], PS(b), AF.Identity, [pk(b), "scale2", "bias2"],
                    [("A", ck, "h2T", kt)], bias=bias23[:, kt, cond:cond + 1], scale=scale23[:, kt, cond:cond + 1])
            else:
                ts(h2T[:, kt, ck * 512:(ck + 1) * 512], PS(b), scale23[:, kt, cond:cond + 1], ALU.mult, [pk(b), "scale2", "bias2"],
                   [("A", ck, "h2T", kt)], s2=bias23[:, kt, cond:cond + 1], op1=ALU.add)

    npair = [0]

    def ffn_group(s, wg, wu, ft, jt, ck, order_key=None):
        bg = 2 * (npair[0] % 4)
        bu = bg + 1
        si_ = npair[0] % 2
        npair[0] += 1
        for kt in range(8):
            mm(PS(bg), wg[:, kt, jt * 128:(jt + 1) * 128], h2T[:, kt, ck * 512:(ck + 1) * 512], kt == 0, kt == 7,
               [("ring", s), ("A", ck, "h2T", kt)], [pk(bg)])
        for kt in range(8):
            mm(PS(bu), wu[:, kt, jt * 128:(jt + 1) * 128], h2T[:, kt, ck * 512:(ck + 1) * 512], kt == 0, kt == 7,
               [("ring", s), ("A", ck, "h2T", kt)], [pk(bu)] + ([order_key] if (order_key and kt == 7) else []))
        act(sg[si_][:], PS(bg), AF.Silu, [pk(bg)], [("SCR", "sg", si_)])
        tt(aT[:, ft, ck * 512:(ck + 1) * 512], sg[si_][:], PS(bu), ALU.mult, [("SCR", "sg", si_), pk(bu)], [("E", "aT", ft, ck)])

    def ffn_views(base):
        return (V(base, 2048, BF16).rearrange("p (k n) -> p k n", k=8), V(base + 4096, 2048, BF16).rearrange("p (k n) -> p k n", k=8))

    h2t_chunk(0)
    h2t_chunk(1)
    s0_, base0_ = ffn_pre[0]
    wg0_, wu0_ = ffn_views(base0_)
    for jt in range(2):
        for ck in range(2):
            ffn_group(s0_, wg0_, wu0_, jt, jt, ck, order_key=("ffnorder",))
    h2t_chunk(2, extra=[("ffnorder",)])
    assert o_E + 12 * 3072 <= o_G and o_scr + 2 * K >= o_scr + 2048
    for t in range(NTT):
        ln_affine(t, x1[t], ("B", "x1", t), lng, lnb, ("LNP", "ln"))
    fence(["LNP"])
    def wd_load(sl):
        nft = 4 if sl < 5 else 2
        src = w_down_d[sl * 512:sl * 512 + nft * 128, :].rearrange("(k p) n -> p k n", p=128)
        if sl in (2, 3):
            base = o_lnp + (sl - 2) * 8 * K
            key = ("LNP", "wd", sl)
            dma(POOL, V(base, nft * 1024, BF16).rearrange("p (k n) -> p k n", k=nft), src, "wd%d" % sl, [], [key])
        else:
            s, base = ring_load([(src, 0, nft, 1024)], slot={0: 2, 1: 3, 4: 0, 5: 1}[sl])
            key = ("ring", s)
        wvd = V(base, nft * 1024, BF16).rearrange("p (k n) -> p k n", k=nft)
        return [(key, wvd[:, k_, :]) for k_ in range(nft)]
    wd_part = {sl: wd_load(sl) for sl in (0, 1, 2, 3)}
    for jt in range(2):
        ffn_group(s0_, wg0_, wu0_, jt, jt, 2)
    for sl in range(1, 11):
        if sl == 6:
            fence(["E", "G"])
        s, base = ffn_pre[sl] if sl < 2 else ffn_slot(sl)
        wg, wu = ffn_views(base)
        for jt in range(2):
            for ck in range(3):
                ffn_group(s, wg, wu, sl * 2 + jt, jt, ck)

    fence(["A", "SCR"])
    lng2 = V(o_A + 8 * K, 1024, F32)
    lnb2 = V(o_A + 12 * K, 1024, F32)
    gateb2 = [V(o_A + 16 * K + c_ * 4 * K, 1024, F32) for c_ in range(2)]
    dma(SP, lng2[:], bcast_row(R_LN2G, 1024), "lnpg", [], [("A", "ln", "g")])
    dma(SP, lnb2[:], bcast_row(R_LN2B, 1024), "lnpb", [], [("A", "ln", "b")])
    gate_from_mod(40, gateb2, [("A", "gate", 0), ("A", "gate", 1)])
    ystage = [V(o_A + j * 4 * K, 1024, F32) for j in range(2)]
    for sl in (4, 5):
        wd_part[sl] = wd_load(sl)
    wd = []
    for sl in range(6):
        wd += wd_part[sl]
    y_t = y_d.rearrange("(n p) f -> n p f", p=128)
    for t in range(NTT):
        j = t % 2
        banks = (4 * j + 2, 4 * j + 3)
        for half in range(2):
            for ft in range(22):
                key, w = wd[ft]
                mm(PS(banks[half]), aT[:, ft, t * 128:(t + 1) * 128], w[:, half * 512:(half + 1) * 512], ft == 0, ft == 21,
                   [key, ("E", "aT", ft, t // 4)], [pk(banks[half])])
        if t < NTT - 1:
            layer_norm_tile(t, banks, x1[t], ("B", "x1", t), ystage[j], ("A", "ystage", j), lng2, lnb2, gateb2, ("A", "gate"), ("A", "ln"))
            dma(SP, y_t[t], ystage[j][:], "yout%d" % j, [("A", "ystage", j)], [], True)
        else:
            layer_norm_tile(t, banks, x1[t], ("B", "x1", t), ystage[j], ("A", "ystage", j), lng2, lnb2, gateb2, ("A", "gate"), ("A", "ln"),
                            defer_affine=True)
            for half in range(2):
                hs = slice(half * 512, (half + 1) * 512)
                eng_ = POOL if half == 0 else DVE
                kh = ("A", "ystage", j, half)
                tt(ystage[j][:, hs], ystage[j][:, hs], lng2[:, hs], ALU.mult, [kh, ("A", "ln", "g")], [kh], eng=eng_)
                tt(ystage[j][:, hs], ystage[j][:, hs], lnb2[:, hs], ALU.add, [kh, ("A", "ln", "b")], [kh], eng=eng_)
                dma(SP, y_t[t][:, hs], ystage[j][:, hs], "ylast%d" % half, [kh], [], True)
    P.emit(st)
    build.P = P
    return nc


_CACHE = {}


def _prep_inputs(inp):
    f = lambda a: np.ascontiguousarray(np.asarray(a, dtype=np.float32))
    cf, cb = _host_consts()
    rows = np.zeros((1, R_N), np.float32)
    rows[0, R_LN1G:R_LN1G + 1024] = f(inp["ln1_g"])[0]
    rows[0, R_LN1B:R_LN1B + 1024] = f(inp["ln1_b"])[0]
    rows[0, R_LN2G:R_LN2G + 1024] = f(inp["ln2_g"])[0]
    rows[0, R_LN2B:R_LN2B + 1024] = f(inp["ln2_b"])[0]
    rows[0, R_NW:R_NW + 512] = f(inp["ssd_norm_w"])[0]
    rows[0, R_CONVB:R_CONVB + 1024] = f(inp["conv_b"])[0]
    rows[0, R_BADA:R_BADA + 6144] = f(inp["b_ada"])[0]
    sm = np.concatenate([f(inp["dt_bias_fwd"])[0], f(inp["dt_bias_bwd"])[0], f(inp["a_log_fwd"])[0], f(inp["a_log_bwd"])[0],
                         f(inp["d_skip"])[0], f(inp["ret_decay_fwd"])[0], f(inp["ret_decay_bwd"])[0]])
    rows[0, R_SMALL:R_SMALL + 48] = sm
    shared = {
        "w_in": f(inp["w_in"])[0], "w_out": f(inp["w_out"])[0], "w_gate": f(inp["w_gate"])[0],
        "w_up": f(inp["w_up"])[0], "w_down": f(inp["w_down"])[0], "w_ada": f(inp["w_ada"])[0],
        "rows": rows, "cstf": cf, "cstb": cb,
    }
    xs = f(inp["x_sample"])
    xp = f(inp["x_prompt"])
    c = f(inp["c"])
    cc = f(inp["c_ctx"])
    sr = f(inp["state_ret"])
    ss = f(inp["state_ssd"])
    b_ada = f(inp["b_ada"])[0]
    conv_w = f(inp["conv_w"])[0]
    conv_b = f(inp["conv_b"])[0]
    maps = []
    for i in range(8):
        cols = np.zeros((128, C_N), np.float32)
        cv = np.stack([c[i], cc], 0).reshape(2, 8, 128).transpose(2, 1, 0)
        cols[:, C_CVEC:C_CVEC + 16] = cv.reshape(128, 16)
        cols[:, C_BADA:C_BADA + 48] = b_ada.reshape(48, 128).T
        cols[:, C_CONVW:C_CONVW + 40] = conv_w.reshape(5, 8, 128).transpose(2, 1, 0).reshape(128, 40)
        cols[:, C_CONVB:C_CONVB + 8] = conv_b.reshape(8, 128).T
        cols[:, C_LN1G:C_LN1G + 8] = f(inp["ln1_g"])[0].reshape(8, 128).T
        cols[:, C_LN1B:C_LN1B + 8] = f(inp["ln1_b"])[0].reshape(8, 128).T
        m = dict(shared)
        m["x"] = np.ascontiguousarray(np.concatenate([xs[i], xp[2 * i], xp[2 * i + 1]], 0))
        m["sret0"] = np.ascontiguousarray(sr[i, 0].transpose(2, 0, 1, 3).reshape(128, 1024))
        m["sssd0"] = np.ascontiguousarray(ss[i, 0].transpose(2, 0, 1, 3).reshape(128, 1024))
        m["cols"] = cols
        maps.append(m)
    return maps


def kernel(**inputs):
    if "nc" not in _CACHE:
        _CACHE["nc"] = build()
    nc = _CACHE["nc"]
    maps = _prep_inputs(inputs)
    res = run_bass_kernel_spmd(nc, maps, core_ids=list(range(8)))
    r = res.results
    y_s = np.stack([r[i]["y"][:1024] for i in range(8)], 0)
    y_p = np.concatenate([r[i]["y"][1024:].reshape(2, 256, 1024) for i in range(8)], 0)
    nret = np.concatenate([r[i]["oret"] for i in range(8)], 0)[:, None]
    nssd = np.concatenate([r[i]["ossd"] for i in range(8)], 0)[:, None]
    return (y_p.astype(np.float32), y_s.astype(np.float32), nret.astype(np.float32), nssd.astype(np.float32))
```

```python
import contextlib
import math
import numpy as np
import concourse.bass as bass
import concourse.mybir as mybir
from concourse.bass_utils import run_bass_kernel_spmd

F32 = mybir.dt.float32
BF16 = mybir.dt.bfloat16
U8 = mybir.dt.uint8
AF = mybir.ActivationFunctionType
ALU = mybir.AluOpType
SZ = {F32: 4, BF16: 2}

PE, ACT, DVE, POOL, SP = "tensor", "scalar", "vector", "gpsimd", "sync"
ENGS = [PE, ACT, DVE, POOL, SP]

D = 1024
NT = 1536
NTT = 12
INC = 3592
DFF = 2816
EPS = 1e-6
ALPHA = 2.0 ** 0.25
NEG = -32768.0
SEQS = [(0, 8), (8, 2), (10, 2)]


class Op:
    __slots__ = ("idx", "eng", "fn", "is_dma", "dsem", "dval", "deps", "inc", "cval", "cost", "fin", "npend", "succ", "start", "crit", "tag", "aps")

    def __init__(self, idx, eng, fn, is_dma):
        self.idx, self.eng, self.fn, self.is_dma = idx, eng, fn, is_dma
        self.dsem, self.dval, self.deps, self.inc, self.cval = None, 0, [], False, 0
        self.cost, self.fin, self.npend, self.succ = 0.3, 0.0, 0, []
        self.start, self.crit, self.tag, self.aps = 0.0, None, '', None


class Prog:
    def __init__(self, nc):
        self.nc = nc
        self.ops = []
        self.res = {}
        self.dma_sems = {}
        self.out_dma_ops = []
        self.cur_tag = ""
        self.last_dma = {}

    @staticmethod
    def _split(key):
        if isinstance(key, tuple):
            return key[0], tuple(key[1:])
        return key, ()

    @staticmethod
    def _related(p, q):
        n = min(len(p), len(q))
        return p[:n] == q[:n]

    def _record(self, op, reads, writes):
        deps = set()
        for k in reads:
            name, p = self._split(k)
            d = self.res.setdefault(name, {})
            for q, e in d.items():
                if self._related(p, q) and e[0] is not None:
                    deps.add(e[0])
        for k in writes:
            name, p = self._split(k)
            d = self.res.setdefault(name, {})
            for q, e in d.items():
                if self._related(p, q):
                    if e[0] is not None:
                        deps.add(e[0])
                    deps.update(e[1])
        deps.discard(op)
        op.deps = sorted(deps, key=lambda o: o.idx)
        for k in reads:
            name, p = self._split(k)
            d = self.res[name]
            if p not in d:
                d[p] = [None, []]
            d[p][1].append(op)
        for k in writes:
            name, p = self._split(k)
            d = self.res[name]
            for q in [q for q in d if len(q) >= len(p) and q[:len(p)] == p]:
                del d[q]
            d[p] = [op, []]

    def op(self, eng, fn, reads=(), writes=(), cost=0.3):
        o = Op(len(self.ops), eng, fn, False)
        o.cost = cost
        o.tag = self.cur_tag
        self.ops.append(o)
        self._record(o, list(reads), list(writes))
        return o

    def dma(self, eng, fn, semkey, reads=(), writes=(), is_output=False, cost=3.0, chain=True):
        o = Op(len(self.ops), eng, fn, True)
        o.cost = cost
        o.tag = self.cur_tag
        ent = self.dma_sems.setdefault(semkey, [None, 0])
        ent[1] += 16
        o.dsem, o.dval = semkey, ent[1]
        self.ops.append(o)
        self._record(o, list(reads), list(writes))
        prev = self.last_dma.get(semkey)
        if chain and prev is not None and prev not in o.deps:
            o.deps.append(prev)
            o.deps.sort(key=lambda q: q.idx)
        self.last_dma[semkey] = o
        if is_output:
            self.out_dma_ops.append(o)
        return o

    def schedule(self):
        import heapq
        ops = self.ops
        for o in ops:
            o.succ = []
        for o in ops:
            o.npend = len(o.deps)
            for d in o.deps:
                d.succ.append(o)

        def lat(d, o):
            return 0.05 if (d.eng == PE and o.eng == PE and not d.is_dma and not o.is_dma) else 0.25

        bl = [0.0] * len(ops)
        for o in reversed(ops):
            m = 0.0
            for s_ in o.succ:
                v = bl[s_.idx] + lat(o, s_)
                if v > m:
                    m = v
            bl[o.idx] = o.cost + m
        inorder = {SP: False, POOL: False, PE: False, ACT: False, DVE: False}
        pend = {e: [] for e in ENGS}
        avail = {e: [] for e in ENGS}
        tcur = {e: 0.0 for e in ENGS}
        ready_t = {}
        nxt = {e: 0 for e in ENGS}
        eng_ops = {e: [o for o in ops if o.eng == e] for e in ENGS}
        order = {e: [] for e in ENGS}

        def push(o):
            rt = 0.0
            for d in o.deps:
                rt = max(rt, d.fin + lat(d, o))
            ready_t[o.idx] = rt
            heapq.heappush(pend[o.eng], (rt, o.idx, o))

        for o in ops:
            if o.npend == 0:
                push(o)
        done, n = 0, len(ops)
        while done < n:
            best = None
            for e in ENGS:
                if inorder[e]:
                    if nxt[e] >= len(eng_ops[e]):
                        continue
                    o = eng_ops[e][nxt[e]]
                    if o.npend != 0 or o.idx not in ready_t:
                        continue
                    st_ = max(tcur[e], ready_t[o.idx])
                    cand = (st_, o.idx, e, o)
                else:
                    p, a = pend[e], avail[e]
                    while p and p[0][0] <= tcur[e]:
                        rt, idx, o = heapq.heappop(p)
                        heapq.heappush(a, (-bl[idx], idx, o))
                    if a:
                        o = a[0][2]
                        cand = (tcur[e], o.idx, e, o)
                    elif p:
                        rt, idx, o = p[0]
                        cand = (rt, idx, e, o)
                    else:
                        continue
                if best is None or cand[:2] < best[:2]:
                    best = cand
            assert best is not None, "scheduler deadlock"
            st_, _, e, o = best
            if inorder[e]:
                nxt[e] += 1
            else:
                if avail[e] and avail[e][0][2] is o:
                    heapq.heappop(avail[e])
                else:
                    heapq.heappop(pend[e])
            o.start = st_
            o.crit = ("eng", order[e][-1]) if (order[e] and tcur[e] >= ready_t[o.idx]) else \
                ("dep", max(o.deps, key=lambda d: d.fin) if o.deps else None)
            if o.is_dma:
                tcur[e] = st_ + (1.0 if e == POOL else 0.1)
                o.fin = st_ + o.cost
            else:
                tcur[e] = st_ + o.cost
                o.fin = tcur[e]
            order[e].append(o)
            done += 1
            for s_ in o.succ:
                s_.npend -= 1
                if s_.npend == 0:
                    push(s_)
        self.est_total = max(o.fin for o in ops)
        return order

    def emit(self, stack, reorder=True):
        nc = self.nc
        if reorder:
            order = self.schedule()
        else:
            order = {e: [o for o in self.ops if o.eng == e] for e in ENGS}
        esem = {e: stack.enter_context(nc.semaphore("c_" + e)) for e in ENGS}
        for i, (k, ent) in enumerate(self.dma_sems.items()):
            ent[0] = stack.enter_context(nc.semaphore("d%d" % i))
        final_waits = {}
        for o in self.out_dma_ops:
            final_waits[o.dsem] = max(final_waits.get(o.dsem, 0), o.dval)

        def skip(d, o):
            return (not d.is_dma) and d.eng == PE and o.eng == PE and not o.is_dma

        pos = {}
        for e in ENGS:
            for i_, o in enumerate(order[e]):
                pos[o.idx] = i_
        need = {}
        for e in ENGS:
            seenpos = {}
            for o in order[e]:
                last = {}
                for d in o.deps:
                    if d.is_dma or skip(d, o):
                        continue
                    m = last.get(d.eng)
                    if m is None or pos[d.idx] > pos[m.idx]:
                        last[d.eng] = d
                lst = []
                for pe_, m in last.items():
                    if seenpos.get(pe_, -1) >= pos[m.idx]:
                        continue
                    seenpos[pe_] = pos[m.idx]
                    m.inc = True
                    lst.append(m)
                need[o.idx] = lst
        cnt = {e: 0 for e in ENGS}
        for e in ENGS:
            for o in order[e]:
                if not o.is_dma and o.inc:
                    cnt[e] += 1
                    o.cval = cnt[e]
        seen = {e: {} for e in ENGS}
        per_eng = {e: [] for e in ENGS}
        for o in [o for e in ENGS for o in order[e]]:
            wl = [(esem[m.eng], m.cval) for m in need[o.idx]]
            waits = {}
            for d in o.deps:
                if d.is_dma:
                    key, val = d.dsem, d.dval
                    if val > waits.get(key, 0):
                        waits[key] = val
            s = seen[o.eng]
            for key, val in waits.items():
                if s.get(key, 0) >= val:
                    continue
                s[key] = val
                wl.append((self.dma_sems[key][0], val))
            per_eng[o.eng].append((o, wl))
        self.n_incs = dict(cnt)
        block = stack.enter_context(nc.Block())
        dma_sems = self.dma_sems

        def make(engname):
            def body(eng):
                for o, wl in per_eng[engname]:
                    for sem, val in wl:
                        eng.wait_ge(sem, val)
                    ins = o.fn(eng)
                    if o.is_dma:
                        ins.then_inc(dma_sems[o.dsem][0], 16)
                    elif o.inc:
                        ins.then_inc(esem[engname], 1)
                if engname == SP:
                    for k, v in final_waits.items():
                        eng.wait_ge(dma_sems[k][0], v)
            return body

        block.tensor(make(PE))
        block.scalar(make(ACT))
        block.vector(make(DVE))
        block.gpsimd(make(POOL))
        block.sync(make(SP))


CF_IDENT, CF_UTRI, CF_ONES = 0, 128, 256
CF_IOTAF, CF_IOTAB = 384, 1408
CF_COS, CF_SIN = 2432, 3456
CF_POS, CF_NEG = 4480, 6528
CF_TAILF, CF_TAILB = 8576, 8578
CF_N = 8580
CB_IDENT, CB_ONES, CB_MF, CB_MB, CB_SELROW, CB_SELBIAS, CB_N = 0, 128, 256, 768, 1280, 3328, 5376


def _host_consts():
    p = np.arange(128)[:, None].astype(np.float64)
    cf = np.zeros((128, CF_N), np.float32)
    j = np.arange(128)[None, :]
    cf[:, CF_IDENT:CF_IDENT + 128] = (p == j)
    cf[:, CF_UTRI:CF_UTRI + 128] = (p <= j)
    cf[:, CF_ONES:CF_ONES + 128] = 1.0
    t = np.arange(1024)[None, :]
    cf[:, CF_IOTAF:CF_IOTAF + 1024] = t + 1
    cf[:, CF_IOTAB:CF_IOTAB + 1024] = 1024 - t
    tt = np.arange(1024)
    t_row = (tt // 64).astype(np.float64)
    t_col = (tt % 64).astype(np.float64)
    inv = 10000.0 ** (-np.arange(32, dtype=np.float64) / 32.0)
    ang = np.concatenate([t_row[:, None] * inv[None, :], t_col[:, None] * inv[None, :]], axis=-1)
    cos = np.cos(ang).T
    sin = np.sin(ang).T
    cf[:, CF_COS:CF_COS + 1024] = np.concatenate([cos, cos], 0)
    cf[:, CF_SIN:CF_SIN + 1024] = np.concatenate([-sin, sin], 0)
    u = np.arange(2048)[None, :]
    delta = u - p - 1024
    cf[:, CF_POS:CF_POS + 2048] = np.maximum(delta, 0)
    cf[:, CF_NEG:CF_NEG + 2048] = np.minimum(delta, 0)
    jj = np.arange(2)[None, :]
    cf[:, CF_TAILF:CF_TAILF + 2] = 255 - 128 * jj - p
    cf[:, CF_TAILB:CF_TAILB + 2] = 128 * jj + p
    cb = np.zeros((128, CB_N), np.float32)
    cb[:, CB_IDENT:CB_IDENT + 128] = (p == j)
    cb[:, CB_ONES:CB_ONES + 128] = 1.0
    mf = np.where(p <= j, 0.0, NEG)
    mb = np.where(p > j, 0.0, NEG)
    cb[:, CB_MF:CB_MF + 512] = np.tile(mf, (1, 4))
    cb[:, CB_MB:CB_MB + 512] = np.tile(mb, (1, 4))
    k = np.arange(128)
    selrow = np.zeros((128, 16, 128), np.float32)
    for hd in range(16):
        selrow[(k < 96) & (k % 32 == hd), hd, :] = 1.0
    cb[:, CB_SELROW:CB_SELROW + 2048] = selrow.reshape(128, 2048)
    selb = np.zeros((128, 2, 8, 128), np.float32)
    for d in range(2):
        for h in range(8):
            selb[(k < 96) & (k % 32 == 16 + d * 8 + h), d, h, :] = 1.0
    cb[:, CB_SELBIAS:CB_SELBIAS + 2048] = selb.reshape(128, 2048)
    return cf, cb


R_LN1G, R_LN1B, R_LN2G, R_LN2B, R_NW, R_CONVB, R_BADA, R_SMALL, R_N = 0, 1024, 2048, 3072, 4096, 4608, 5632, 11776, 11824
C_CVEC, C_BADA, C_CONVW, C_CONVB, C_LN1G, C_LN1B, C_N = 0, 16, 64, 104, 112, 120, 128


def build(debug=()):
    nc = bass.Bass("TRN2", target_bir_lowering=False)
    P = Prog(nc)
    di = lambda name, shape: nc.dram_tensor(name, list(shape), F32, kind="ExternalInput").ap()
    do = lambda name, shape: nc.dram_tensor(name, list(shape), F32, kind="ExternalOutput").ap()
    x_d = di("x", [NT, D])
    sret_d = di("sret0", [128, 1024])
    sssd_d = di("sssd0", [128, 1024])
    w_in_d = di("w_in", [D, INC])
    w_out_d = di("w_out", [D, D])
    w_gate_d = di("w_gate", [D, DFF])
    w_up_d = di("w_up", [D, DFF])
    w_down_d = di("w_down", [DFF, D])
    w_ada_d = di("w_ada", [D, 6 * D])
    rows_d = di("rows", [1, R_N])
    cols_d = di("cols", [128, C_N])
    cf_d = di("cstf", [128, CF_N])
    cb_d = di("cstb", [128, CB_N])
    y_d = do("y", [NT, D])
    oret_d = do("oret", [2, 2, 4, 128, 128])
    ossd_d = do("ossd", [2, 2, 8, 128, 64])

    st = contextlib.ExitStack()
    ARENA = 212800
    arena = nc.alloc_sbuf_tensor("arena", [128, ARENA], U8)
    psum = nc.alloc_psum_tensor("psum", [128, 4096], F32)
    K = 1024

    def V(off, n, dt):
        off = int(off)
        assert off % 4 == 0 and off + n * SZ[dt] <= ARENA, (off, n)
        return arena[:, off:off + n * SZ[dt]].bitcast(dt)

    def PS(b, n=512, off=0):
        return psum[:, b * 512 + off:b * 512 + off + n]

    def pk(b):
        return "ps%d" % b

    o_const, o_scr, o_lnp, o_ring = 0, 5 * K, 15 * K, 31 * K
    SLOT = 8320
    o_A, o_B, o_C, o_D, o_E, o_F, o_G = 64 * K, 88 * K, 112 * K, 124 * K, 136 * K, 148 * K, 173 * K

    ident_f = V(0, 128, F32)
    utri_f = V(512, 128, F32)
    ones_f = V(1024, 128, F32)
    ident_b = V(1536, 128, BF16)
    ones_b = V(1792, 128, BF16)
    m = [2048]

    def small(n, dt=F32):
        a = V(m[0], n, dt)
        m[0] += (n * SZ[dt] + 3) // 4 * 4
        assert m[0] <= 5 * K, m[0]
        return a

    cols = small(C_N)
    smalls = small(48)
    modT = small(96)
    scv = small(16, BF16)
    lg = small(8)
    nlg = small(8)
    tmp8 = small(8)
    nA = small(16)
    tailpos = small(4)
    tailw = small(16)
    sc1p = small(16)
    sc2p = small(16)
    lscb = small(1)
    fencew = small(1)
    dtraw = small(96)
    dskb = small(8)

    def fs(ap):
        n = 1
        for d in ap.shape[1:]:
            n *= int(d)
        return n

    def inps(ap):
        return ap.tensor.name == "psum"

    def mm(out, lhsT, rhs, start, stop, reads, writes):
        n_ = fs(rhs)
        c = ((0.035 + n_ / 2560.0) if n_ >= 256 else (0.03 + n_ / 1400.0)) * (4.0 if rhs.dtype == F32 else 1.0)
        o_ = P.op(PE, lambda e: e.matmul(out, lhsT=lhsT, rhs=rhs, start=start, stop=stop), reads, writes, cost=c)
        o_.aps = ([lhsT, rhs], [out])
        return o_

    def tr(out, in_, reads, writes):
        o_ = P.op(PE, lambda e: e.transpose(out=out, in_=in_, identity=ident_f[:]), list(reads) + ["const"], writes, cost=0.1)
        o_.aps = ([in_, ident_f[:]], [out])
        return o_

    def act(out, in_, func, reads, writes, bias=None, scale=None, accum_out=None):
        kw = {}
        c = 0.2 + fs(in_) / 1400.0
        if fs(in_) <= 8:
            c = 0.6
        if bias is not None:
            kw["bias"] = bias
            c += 0.05
        if scale is not None:
            kw["scale"] = scale
        if accum_out is not None:
            kw["accum_out"] = accum_out
            c += 0.1
        o_ = P.op(ACT, lambda e: e.activation(out=out, in_=in_, func=func, **kw), reads, writes, cost=c)
        o_.aps = ([in_] + [v_ for v_ in (bias, scale) if v_ is not None and not isinstance(v_, float)], [out] + ([accum_out] if accum_out is not None else []))
        return o_

    def vcost(eng, n, f):
        if n <= 8:
            return 0.6
        return (0.1 + n / 490.0) if eng == POOL else (0.09 + n * f / 1060.0)

    def tt(out, in0, in1, op, reads, writes, eng=DVE):
        f = 1.0 if (inps(in0) or inps(in1)) else 2.0
        if in0.dtype == BF16 and in1.dtype == BF16 and out.dtype == BF16 and f == 2.0:
            f = 0.6
        o_ = P.op(eng, lambda e: e.tensor_tensor(out=out, in0=in0, in1=in1, op=op), reads, writes, cost=vcost(eng, fs(out), f))
        o_.aps = ([in0, in1], [out])
        return o_

    def ts(out, in0, s1, op0, reads, writes, s2=None, op1=None, eng=DVE):
        c = vcost(eng, fs(out), 1.0)
        if op1 is None:
            o_ = P.op(eng, lambda e: e.tensor_scalar(out=out, in0=in0, scalar1=s1, scalar2=None, op0=op0), reads, writes, cost=c)
            o_.aps = ([in0] + [v_ for v_ in (s1,) if not isinstance(v_, (float, int))], [out])
            return o_
        o_ = P.op(eng, lambda e: e.tensor_scalar(out=out, in0=in0, scalar1=s1, scalar2=s2, op0=op0, op1=op1), reads, writes, cost=c)
        o_.aps = ([in0] + [v_ for v_ in (s1, s2) if not isinstance(v_, (float, int))], [out])
        return o_

    def stt(out, in0, scalar, in1, op0, op1, reads, writes):
        f = 1.0 if (inps(in0) or inps(in1)) else 2.0
        o_ = P.op(DVE, lambda e: e.scalar_tensor_tensor(out=out, in0=in0, scalar=scalar, in1=in1, op0=op0, op1=op1), reads, writes,
                  cost=vcost(DVE, fs(out), f))
        o_.aps = ([in0, in1] + [v_ for v_ in (scalar,) if not isinstance(v_, (float, int))], [out])
        return o_

    def cp(out, in_, reads, writes, eng=DVE):
        o_ = P.op(eng, lambda e: e.tensor_copy(out=out, in_=in_), reads, writes, cost=vcost(eng, fs(out), 1.0))
        o_.aps = ([in_], [out])
        return o_

    def memset(ap, val, writes, eng=DVE):
        o_ = P.op(eng, lambda e: e.memset(ap, val), [], writes, cost=vcost(eng, fs(ap), 0.5))
        o_.aps = ([], [ap])
        return o_

    def dma(eng, out, in_, key, reads, writes, is_output=False, chain=True):
        nb = 128 * fs(out) * 4
        o_ = P.dma(eng, lambda e: e.dma_start(out=out, in_=in_), key, reads, writes, is_output, cost=2.0 + nb / 150e3, chain=chain)
        o_.aps = ([in_], [out])
        return o_

    P.marks = []

    def fence(regions, name=None):
        o = P.op(DVE, lambda e: e.memset(fencew[:], 0.0), [], list(regions), cost=0.1)
        P.marks.append((name or ("f%d" % len(P.marks)), o))

    def bcast_row(off, n):
        return rows_d[0:1, off:off + n].partition_broadcast(128).rearrange("p a n -> p (a n)")

    dbg_n = [0]

    def tap(name, ap, shape, reads):
        if name in debug:
            dd = do("dbg_" + name, shape)
            dbg_n[0] += 1
            dma(POOL, dd, ap, "dbg%d" % dbg_n[0], reads, [], True)

    ring_n = [0]

    def ring_load(parts, slot=None, after=()):
        if slot is None:
            s = ring_n[0] % 4
            ring_n[0] += 1
        else:
            s = slot
        base = o_ring + s * SLOT
        for ip, (src, dst_off_elems, nk, ncol) in enumerate(parts):
            dst = V(base + dst_off_elems * 2, nk * ncol, BF16).rearrange("p (k n) -> p k n", k=nk)
            dma(POOL, dst, src, "ring%d" % s, list(after), [("ring", s, ip)], chain=(ip == 0))
        return s, base

    def rstd_from(dst, src_ps_or_sb, scale, reads, writes):
        act(dst, src_ps_or_sb, AF.Ln, list(reads) + ["epsb"], writes, bias=epsb[:, 0:1], scale=scale)
        act(dst, dst, AF.Exp, writes, writes, scale=-0.5)

    epsb = small(1)
    dgs = [V(o_scr + 8 * K + i * 512, 128, F32) for i in range(2)]
    memset(epsb[:], EPS, ["epsb"])

    dma(SP, V(0, 384, F32), cf_d[:, 0:384], "c0", [], ["const"])
    dma(POOL, V(1536, 256, BF16), cb_d[:, 0:256], "c1", [], ["constb"])
    dma(SP, cols[:], cols_d[:, :], "c2", [], ["cols"])
    dma(SP, smalls[:], bcast_row(R_SMALL, 48), "c3", [], ["smalls"])
    dma(SP, tailpos[:], cf_d[:, CF_TAILF:CF_TAILF + 4], "c7", [], ["tailpos"])
    cos_t = V(o_lnp, 1024, F32)
    sin_t = V(o_lnp + 4 * K, 1024, F32)
    scb = V(o_scr + 4 * K, 2048, BF16)
    dma(SP, V(o_lnp, 2048, F32), cf_d[:, CF_COS:CF_COS + 2048], "c4", [], [("LNP", "rope")])
    act(scv[:], cols[:, C_CVEC:C_CVEC + 16], AF.Silu, ["cols"], ["scv"])
    scv3 = scv[:].rearrange("p (k c) -> p k c", c=2)

    def make_scb():
        cp(scb[:].rearrange("p (a m) -> p a m", m=128), scv[:].unsqueeze(2).to_broadcast([128, 16, 128]),
           ["scv"], [("SCR", "scb")])
    scb4 = scb[:].rearrange("p (k c m) -> p k c m", k=8, c=2)

    def mod_fm(ft0, col0, nslots):
        for sl in range(nslots):
            c0 = col0 + sl * 512
            s, base = ring_load([(w_ada_d[:, c0:c0 + 512].rearrange("(k p) n -> p k n", p=128), 0, 8, 512)])
            wv = V(base, 8 * 512, BF16).rearrange("p (k n) -> p k n", k=8)
            for j in range(4):
                ft = ft0 + sl * 4 + j
                for kt in range(8):
                    mm(PS(7, 2, ft * 2), wv[:, kt, j * 128:(j + 1) * 128], scv3[:, kt, :], kt == 0, kt == 7,
                       [("ring", s), "scv"], ["ps7"])
        n = nslots * 4
        tt(modT[:, ft0 * 2:(ft0 + n) * 2].rearrange("p (f c) -> p f c", c=2),
           PS(7, n * 2, ft0 * 2).rearrange("p (f c) -> p f c", c=2),
           cols[:, C_BADA + ft0:C_BADA + ft0 + n].unsqueeze(2).to_broadcast([128, n, 2]), ALU.add,
           ["ps7", "cols"], [("modT", ft0)])

    def gate_load(col0):
        slots = []
        for sl in range(2):
            c0 = col0 + sl * 512
            s, base = ring_load([(w_ada_d[:, c0:c0 + 512].rearrange("(k p) n -> p k n", p=128), 0, 8, 512)])
            slots.append((s, V(base, 8 * 512, BF16).rearrange("p (k n) -> p k n", k=8)))
        return slots

    def gate_compute(slots, dst_off, rowoff):
        make_scb()
        btmp = V(o_scr, 1024, F32)
        dma(SP, btmp[:], bcast_row(rowoff, 1024), "gb", [], [("SCR", "btmp")])
        for sl in range(2):
            s, wv = slots[sl]
            for c in range(2):
                b = 5 + c
                for kt in range(8):
                    mm(PS(b), scb4[:, kt, c, :], wv[:, kt, :], kt == 0, kt == 7, [("ring", s), ("SCR", "scb")], [pk(b)])
                tt(V(dst_off + (c * 1024 + sl * 512) * 4, 512, F32)[:], PS(b), btmp[:, sl * 512:(sl + 1) * 512], ALU.add,
                   [pk(b), ("SCR", "btmp")], [("LNP", "gate", c, sl)])

    def gate_bcast(dst_off, col0, rowoff):
        gate_compute(gate_load(col0), dst_off, rowoff)

    xTf = V(o_F, 8 * NT, F32).rearrange("p (k t) -> p k t", k=8)
    xst = [V(o_B + i * 4 * K, 1024, F32) for i in range(8)]
    x_t = x_d.rearrange("(n p) f -> n p f", p=128)

    def x_load(ck, after=()):
        h_ = ck % 2
        dst = V(o_B + h_ * 16 * K, 4096, F32).rearrange("p (j f) -> p j f", j=4)
        src = x_d[ck * 512:(ck + 1) * 512, :].rearrange("(j p) f -> p j f", p=128)
        dma(SP, dst, src, "xst%d" % h_, list(after), [("B", "xst", h_ * 4 + j) for j in range(4)])
    x_load(0)
    mod_fm(0, 0, 4)
    ts(sc1p[:], modT[:, 16:32], 1.0, ALU.add, [("modT", 0)], ["sc1p"])
    wada_done = [("ring", 0), ("ring", 1), ("ring", 2), ("ring", 3)]
    x_load(1, after=wada_done[:2])
    n_ = 0
    for ck in range(3):
        if ck == 1:
            x_load(2, after=wada_done)
        for kt in range(8):
            b = n_ % 4
            n_ += 1
            for j in range(4):
                jj = (ck % 2) * 4 + j
                tr(PS(b, 128, j * 128), xst[jj][:, kt * 128:(kt + 1) * 128], [("B", "xst", jj)], [pk(b)])
            if n_ % 2 == 0:
                act(xTf[:, kt, ck * 512:(ck + 1) * 512], PS(b), AF.Copy, [pk(b)], [("F", "xTf", kt, ck)])
            else:
                cp(xTf[:, kt, ck * 512:(ck + 1) * 512], PS(b), [pk(b)], [("F", "xTf", kt, ck)])
    sh1 = modT[:, 0:16].rearrange("p (k c) -> p k c", c=2)
    sc1p3 = sc1p[:].rearrange("p (k c) -> p k c", c=2)
    hT = V(o_A, 8 * NT, BF16).rearrange("p (k t) -> p k t", k=8)
    n_ = 0
    for ck in range(3):
        cond = 0 if ck < 2 else 1
        for kt in range(8):
            if n_ % 2 == 0:
                act(hT[:, kt, ck * 512:(ck + 1) * 512], xTf[:, kt, ck * 512:(ck + 1) * 512], AF.Identity,
                    [("F", "xTf", kt, ck), "sc1p", ("modT", 0)], [("A", ck, "hT", kt)],
                    bias=sh1[:, kt, cond:cond + 1], scale=sc1p3[:, kt, cond:cond + 1])
            else:
                ts(hT[:, kt, ck * 512:(ck + 1) * 512], xTf[:, kt, ck * 512:(ck + 1) * 512], sc1p3[:, kt, cond:cond + 1], ALU.mult,
                   [("F", "xTf", kt, ck), "sc1p", ("modT", 0)], [("A", ck, "hT", kt)], s2=sh1[:, kt, cond:cond + 1], op1=ALU.add)
            n_ += 1
    fence(["B", "C", "F", "G"])
    def w_in_chunk(c0, ncol, after=()):
        s, base = ring_load([(w_in_d[:, c0:c0 + ncol].rearrange("(k p) n -> p k n", p=128), 0, 8, ncol)], after=after)
        return s, V(base, 8 * ncol, BF16).rearrange("p (k n) -> p k n", k=8)

    pre_in = {1024: w_in_chunk(1024, 512)}
    for c0 in (0, 512):
        pre_in[c0] = w_in_chunk(c0, 512, after=[("A", 0, "hT", 7)])

    def w_in_get(c0, ncol):
        return pre_in.pop(c0) if c0 in pre_in else w_in_chunk(c0, ncol)

    delta = V(o_F, 2048, F32)
    E1 = V(o_F + 8 * K, 2048, F32)
    E2 = V(o_F + 16 * K, 2048, F32)
    P.op(POOL, lambda e: e.iota(delta[:], pattern=[[1, 2048]], base=-1024, channel_multiplier=-1,
                                allow_small_or_imprecise_dtypes=True), [], [("F", "delta")], cost=4.5)
    Th = V(o_G, 4 * 2048, BF16).rearrange("p (h u) -> p h u", h=4)
    iotaFB = V(o_G + 16 * K, 2048, F32)
    rowtab = V(o_G + 24 * K, 1024, BF16)
    mscr = [V(o_G + 26 * K, 512, F32), V(o_G + 32 * K, 512, F32)]
    ktm = V(o_G + 28 * K, 2048, BF16).rearrange("p (j d f) -> p j d f", j=2, d=2)
    P.op(POOL, lambda e: e.iota(iotaFB[:, 0:1024], pattern=[[1, 1024]], base=1, channel_multiplier=0,
                                allow_small_or_imprecise_dtypes=True), [], [("G", "iota", 0)], cost=2.3)
    P.op(POOL, lambda e: e.iota(iotaFB[:, 1024:2048], pattern=[[-1, 1024]], base=1024, channel_multiplier=0,
                                allow_small_or_imprecise_dtypes=True), [], [("G", "iota", 1)], cost=2.3)

    u8 = small(8)
    l8 = small(8)
    act(tmp8[:], smalls[:, 40:48], AF.Exp, ["smalls"], ["tmp8"], scale=-1.0)
    ts(u8[:], tmp8[:], 1.0, ALU.add, ["tmp8"], ["u8"])
    act(l8[:], u8[:], AF.Ln, ["u8"], ["l8"])
    ts(u8[:], u8[:], -1.0, ALU.add, ["u8"], ["u8"], s2=1e-30, op1=ALU.max)
    P.op(DVE, lambda e: e.reciprocal(out=u8[:], in_=u8[:]), ["u8"], ["u8"], cost=0.2)
    tt(l8[:], l8[:], u8[:], ALU.mult, ["l8", "u8"], ["l8"])
    tt(tmp8[:], tmp8[:], l8[:], ALU.mult, ["tmp8", "l8"], ["tmp8"])
    ts(lg[:], tmp8[:], -1.0, ALU.mult, ["tmp8"], ["lg"])
    cp(nlg[:], tmp8[:], ["tmp8"], ["nlg"])
    act(nA[:], smalls[:, 16:32], AF.Exp, ["smalls"], ["nA"])
    ts(nA[:], nA[:], -1.0, ALU.mult, ["nA"], ["nA"])
    cp(dskb[:], smalls[:, 32:40], ["smalls"], ["dskb"])
    memset(lscb[:], -0.5 * math.log(128.0), ["lscb"])
    for h in range(4):
        act(E1[:], delta[:], AF.Exp, [("F", "delta"), "lg", "lscb"], [("F", "E1")], bias=lscb[:, 0:1], scale=lg[:, h:h + 1])
        act(E2[:], delta[:], AF.Exp, [("F", "delta"), "nlg", "lscb"], [("F", "E2")], bias=lscb[:, 0:1], scale=nlg[:, 4 + h:5 + h])
        for q in range(4):
            qs = slice(q * 512, (q + 1) * 512)
            tt(Th[:, h, qs], E1[:, qs], E2[:, qs], ALU.min, [("F", "E1"), ("F", "E2")], [("G", "Th", h, q)])
    for d in range(2):
        for j in range(2):
            ts(tailw[:, d * 8 + j * 4:d * 8 + j * 4 + 4], lg[:, d * 4:d * 4 + 4], tailpos[:, d * 2 + j:d * 2 + j + 1],
               ALU.mult, ["lg", "tailpos"], [("tailw", d, j)])
    act(tailw[:], tailw[:], AF.Exp, ["tailw", "lscb"], ["tailw"], bias=lscb[:, 0:1])


    qT = V(o_B, 4 * NT, BF16).rearrange("p (h t) -> p h t", h=4)
    kT = V(o_B + 12 * K, 4 * NT, BF16).rearrange("p (h t) -> p h t", h=4)
    v_tm = V(o_C, NTT * 512, BF16).rearrange("p (n f) -> p n f", n=NTT)
    gT = V(o_D, 4 * NT, BF16).rearrange("p (h t) -> p h t", h=4)
    z_tm = V(o_E, NTT * 512, BF16).rearrange("p (n f) -> p n f", n=NTT)
    XW = 1548
    XOFF = [2, 1030, 1290]
    xbc = V(o_F, 8 * XW, BF16).rearrange("p (c t) -> p c t", c=8)
    rope_tmpB = [[V(o_scr, 512, F32), V(o_scr + 2 * K, 512, F32)], [V(o_lnp + 8 * K, 512, F32), V(o_lnp + 10 * K, 512, F32)]]
    nrope = [0]
    stS = V(o_scr + 4 * K, 1024, F32)

    def tok2xcol(ck):
        if ck < 2:
            return [(0, 512, XOFF[0] + ck * 512)]
        return [(0, 256, XOFF[1]), (256, 256, XOFF[2])]

    def fm_proj(s, wv, j, ck, bank):
        for kt in range(8):
            mm(PS(bank), wv[:, kt, j * 128:(j + 1) * 128], hT[:, kt, ck * 512:(ck + 1) * 512], kt == 0, kt == 7,
               [("ring", s), ("A", ck, "hT", kt)], [pk(bank)])

    def tm_proj(s, wv, t, bank, ncol=512, c0=0):
        for kt in range(8):
            mm(PS(bank, ncol), hT[:, kt, t * 128:(t + 1) * 128], wv[:, kt, c0:c0 + ncol], kt == 0, kt == 7,
               [("ring", s), ("A", t // 4, "hT", kt)], [pk(bank)])

    bk = [0]

    def nb4():
        b = bk[0] % 4
        bk[0] += 1
        return b

    s, wv = w_in_get(1024, 512)
    for t in range(NTT):
        b = nb4()
        tm_proj(s, wv, t, b)
        if t % 2 == 0:
            act(v_tm[:, t, :], PS(b), AF.Copy, [pk(b)], [("C", "v", t)])
        else:
            cp(v_tm[:, t, :], PS(b), [pk(b)], [("C", "v", t)])
    for which, dstT in ((0, qT), (1, kT)):
        s, wv = w_in_get(which * 512, 512)
        for j in range(4):
            for ck in range(3):
                b = nb4()
                fm_proj(s, wv, j, ck, b)
                dst = dstT[:, j, ck * 512:(ck + 1) * 512]
                wr = [("B", "qk", which, j, ck)]
                if ck == 2:
                    act(dst, PS(b), AF.Copy, [pk(b)], wr)
                else:
                    rq = nrope[0] % 2
                    nrope[0] += 1
                    tA, tB = rope_tmpB[rq]
                    rn = "SCR" if rq == 0 else "LNP"
                    tsl = slice(ck * 512, (ck + 1) * 512)
                    tt(tA[:], PS(b), cos_t[:, tsl], ALU.mult, [pk(b), ("LNP", "rope")], [(rn, "ropeA")])
                    tt(tB[0:64, :], PS(b)[64:128, :], sin_t[0:64, tsl], ALU.mult, [pk(b), ("LNP", "rope")], [(rn, "ropeB", 0)])
                    tt(tB[64:128, :], PS(b)[0:64, :], sin_t[64:128, tsl], ALU.mult, [pk(b), ("LNP", "rope")], [(rn, "ropeB", 1)])
                    tt(dst, tA[:], tB[:], ALU.add, [(rn, "ropeA"), (rn, "ropeB", 0), (rn, "ropeB", 1)], wr, eng=POOL)
        if which == 1:
            for sq in range(2):
                for jb in range(2):
                    t = 8 + sq * 2 + jb
                    b = nb4()
                    tm_proj(s, wv, t, b)
                    for d in range(2):
                        for h in range(4):
                            c = d * 8 + jb * 4 + h
                            act(ktm[:, jb, d, h * 128:(h + 1) * 128], PS(b, 128, h * 128), AF.Copy, [pk(b), "tailw"],
                                [("G", "ktm", jb, d, h)], scale=tailw[:, c:c + 1])
                for d in range(2):
                    for h in range(4):
                        for jb in range(2):
                            mm(PS(4 + d, 128, h * 128), ktm[:, jb, d, h * 128:(h + 1) * 128],
                               v_tm[:, 8 + sq * 2 + jb, h * 128:(h + 1) * 128], jb == 0, jb == 1,
                               [("G", "ktm", jb, d, h), ("C", "v", 8 + sq * 2 + jb)], [pk(4 + d)])
                    cp(stS[:, d * 512:(d + 1) * 512], PS(4 + d), [pk(4 + d)], [("SCR", "stS", d)])
                dma(SP, oret_d[sq].rearrange("d h p e -> p d h e"), stS[:].rearrange("p (d h e) -> p d h e", d=2, h=4),
                    "oret", [("SCR", "stS", 0), ("SCR", "stS", 1)], [], True)
    tap("qT", qT, [128, 4, NT], ["B"])
    tap("kT", kT, [128, 4, NT], ["B"])
    tap("v", v_tm, [128, NTT, 512], ["C"])
    s, wv = w_in_chunk(1536, 512)
    for j in range(4):
        for ck in range(3):
            b = nb4()
            fm_proj(s, wv, j, ck, b)
            act(gT[:, j, ck * 512:(ck + 1) * 512], PS(b), AF.Silu, [pk(b)], [("D", "g", j, ck)])
    mod_fm(24, 3072, 2)
    s, wv = w_in_chunk(2048, 512)
    for t in range(NTT):
        b = nb4()
        tm_proj(s, wv, t, b)
        act(z_tm[:, t, :], PS(b), AF.Silu, [pk(b)], [("E", "z", t)])
    mod_fm(32, 4096, 2)
    ts(sc2p[:], modT[:, 64:80], 1.0, ALU.add, [("modT", 32)], ["sc2p"])
    fence(["F"])
    for (c0_, c1_) in ((0, 2), (1026, 1030), (1286, 1290), (1546, 1548)):
        memset(xbc[:, :, c0_:c1_], 0.0, [("F", "xbc", "pad", c0_)], eng=POOL)
    for half in range(2):
        ncol = 512 if half == 0 else 520
        s, wv = w_in_chunk(2560 + half * 512, ncol)
        for j in range(4):
            ct = half * 4 + j
            for ck in range(3):
                b = nb4()
                fm_proj(s, wv, j, ck, b)
                for (o, n, xc) in tok2xcol(ck):
                    if (j + ck) % 2 == 0:
                        act(xbc[:, ct, xc:xc + n], PS(b, n, o), AF.Copy, [pk(b)], [("F", "xbc", ct, xc)])
                    else:
                        cp(xbc[:, ct, xc:xc + n], PS(b, n, o), [pk(b)], [("F", "xbc", ct, xc)])
        if half == 0:
            mod_fm(16, 2048, 2)
        if half == 1:
            for t in range(NTT):
                for kt in range(8):
                    mm(PS(7, 8, t * 8), hT[:, kt, t * 128:(t + 1) * 128], wv[:, kt, 512:520], kt == 0, kt == 7,
                       [("ring", s), ("A", t // 4, "hT", kt)], ["ps7"])
            cp(dtraw[:], PS(7, 96), ["ps7"], ["dtraw"])
    mod_fm(40, 5120, 2)
    pre_wo = []
    for half in range(2):
        s, base = ring_load([(w_out_d[:, half * 512:(half + 1) * 512].rearrange("(k p) n -> p k n", p=128), 0, 8, 512)], slot=2 + half)
        pre_wo.append((s, V(base, 8 * 512, BF16).rearrange("p (k n) -> p k n", k=8)))

    def ffn_slot(sl):
        c0 = sl * 256
        return ring_load([(w_gate_d[:, c0:c0 + 256].rearrange("(k p) n -> p k n", p=128), 0, 8, 256),
                          (w_up_d[:, c0:c0 + 256].rearrange("(k p) n -> p k n", p=128), 2048, 8, 256)], slot=sl % 2)
    ffn_pre = [ffn_slot(0), ffn_slot(1)]

    def gate_from_mod(ft0, dsts, dkeys):
        n_ = 0
        for c in range(2):
            for half in range(2):
                b = 4 + (n_ % 2)
                n_ += 1
                for k4 in range(4):
                    kt = half * 4 + k4
                    q = kt % 2
                    ts(dgs[q][:], ident_f[:], modT[:, (ft0 + kt) * 2 + c:(ft0 + kt) * 2 + c + 1], ALU.mult,
                       ["const", ("modT", ft0)], [("SCR", "dgs", q)])
                    mm(PS(b, 128, k4 * 128), ones_f[:], dgs[q][:], True, True, ["const", ("SCR", "dgs", q)], [pk(b)])
                cp(dsts[c][:, half * 512:(half + 1) * 512], PS(b), [pk(b)], [dkeys[c] + (half,)])

    fence(["A", "LNP", "SCR", ("G", "ktm")])
    oT = V(o_A, 8 * NT, BF16).rearrange("p (k t) -> p k t", k=8)
    PTb = [V(o_lnp + i * 8 * K, 8 * 512, BF16).rearrange("p (i t) -> p i t", i=8) for i in range(2)]
    rs_fB = [V(o_scr + 4 * K + i * 2 * K, 512, F32) for i in range(2)]
    sq_bB = [V(o_G + 28 * K + i * K, 512, BF16) for i in range(2)]
    qfb = [V(o_G + 30 * K + i * K, 512, BF16) for i in range(2)]
    S0b = V(o_scr + 8 * K, 1024, BF16).rearrange("p (d h e) -> p d h e", d=2, h=4)
    PTp = [V(o_scr + i * K, 512, BF16).rearrange("p (i t) -> p i t", i=2) for i in range(2)]
    rs_p = [V(o_scr + 2 * K + i * K, 256, F32) for i in range(2)]
    sq_p = [V(o_G + 34 * K, 256, BF16)] * 2
    dma(POOL, S0b, sret_d[:, :].rearrange("p (d h e) -> p d h e", d=2, h=4), "s0r", [], [("SCR", "S0b")])
    nmask = [0]
    ucount = {True: 0, False: 0}

    def ret_unit(t0, nb, h, r0, W):
        is_sample = nb == 8
        tok0 = t0 * 128
        q_ = ucount[is_sample] % 2
        ucount[is_sample] += 1
        if is_sample:
            pt, ptkey, ob, msb = PTb[q_], ("LNP", "PT", q_), 2 + q_, 4 + q_
            rsf_, sqb_, rkey, skey = rs_fB[q_], sq_bB[q_], ("SCR", "rs_f", q_), ("G", "ktm", "sq", q_)
        else:
            pt, ptkey, ob, msb = PTp[q_], ("SCR", "PTp", q_), 6, 7
            rsf_, sqb_, rkey, skey = rs_p[q_], sq_p[q_], ("SCR", "rs_p", q_), ("G", "ktm", "sqp")
        for i in range(nb):
            b = i % 2
            mm(PS(b, W), kT[:, h, tok0 + i * 128:tok0 + (i + 1) * 128], qT[:, h, tok0 + r0:tok0 + r0 + W], True, True,
               [("B", "qk")], [pk(b)])
            u0 = r0 - 128 * i + 1024
            if is_sample and i % 3 == 2:
                mq = nmask[0] % 2
                nmask[0] += 1
                act(mscr[mq][:, 0:W], PS(b, W), AF.Copy, [pk(b)], [("G", "mscr", mq)])
                tt(pt[:, i, 0:W], mscr[mq][:, 0:W], Th[:, h, u0:u0 + W], ALU.mult, [("G", "mscr", mq), ("G", "Th", h)],
                   [ptkey + (i,)], eng=POOL)
            else:
                tt(pt[:, i, 0:W], PS(b, W), Th[:, h, u0:u0 + W], ALU.mult, [pk(b), ("G", "Th", h)], [ptkey + (i,)])
        if is_sample:
            for d in range(2):
                act(rowtab[:, 0:W], iotaFB[:, d * 1024 + r0:d * 1024 + r0 + W], AF.Exp, [("G", "iota"), "lg"],
                    [("G", "rowtab")], scale=lg[:, d * 4 + h:d * 4 + h + 1])
                tt(qfb[d][:, 0:W], qT[:, h, tok0 + r0:tok0 + r0 + W], rowtab[:, 0:W], ALU.mult,
                   [("B", "qk"), ("G", "rowtab")], [("G", "ktm", "qf", d)])
        nmm = nb + (2 if is_sample else 0)
        for i in range(nb):
            mm(PS(ob, W), v_tm[:, t0 + i, h * 128:(h + 1) * 128], pt[:, i, 0:W], i == 0, i == nmm - 1,
               [("C", "v", t0 + i), ptkey + (i,)], [pk(ob)])
        if is_sample:
            for d in range(2):
                mm(PS(ob, W), S0b[:, d, h, :], qfb[d][:, 0:W], False, d == 1,
                   [("SCR", "S0b"), ("G", "ktm", "qf", d)], [pk(ob)])
        act(sqb_[:, 0:W], PS(ob, W), AF.Square, [pk(ob)], [skey])
        mm(PS(msb, W), ones_b[:], sqb_[:, 0:W], True, True, [skey, "constb"], [pk(msb)])
        rstd_from(rsf_[:, 0:W], PS(msb, W), 1.0 / 128.0, [pk(msb)], [rkey])
        tt(rsf_[:, 0:W], PS(ob, W), rsf_[:, 0:W], ALU.mult, [pk(ob), rkey], [rkey])
        tt(oT[:, h, tok0 + r0:tok0 + r0 + W], rsf_[:, 0:W], gT[:, h, tok0 + r0:tok0 + r0 + W], ALU.mult,
           [rkey, ("D", "g")], [("A", (tok0 + r0) // 512, "oT", h, t0, r0)], eng=POOL)

    s_units = [(0, 8, h, r0, 512) for h in range(4) for r0 in (0, 512)]
    p_units = [(t0, 2, h, 0, 256) for t0 in (8, 10) for h in range(4)]
    for su, pu in zip(s_units, p_units):
        ret_unit(*su)
        ret_unit(*pu)
    tap("oTr", oT[:, 0:4, :], [128, 4, NT], ["A"])

    fence([("G", "Th"), ("G", "iota")], "convA")
    fence(["B", "C", "D", "SCR", "LNP", ("G", "rowtab"), ("G", "mscr"), ("G", "ktm")], "convB")
    cstS = V(o_lnp, 5120, BF16)
    dma(POOL, cstS[:], cb_d[:, CB_MF:CB_MF + 5120], "cstS", [], [("LNP", "cstS")])
    xs_tm = V(o_B, NTT * 512, BF16).rearrange("p (n f) -> p n f", n=NTT)
    B_tm = V(o_B + 12 * K, NTT * 256, BF16).rearrange("p (n f) -> p n f", n=NTT)
    BT = V(o_B + 18 * K, 2 * NT, BF16).rearrange("p (g t) -> p g t", g=2)
    CT = V(o_C, 2 * NT, BF16).rearrange("p (g t) -> p g t", g=2)
    diagw = V(o_G, 8 * 5 * 128, BF16).rearrange("p (c j m) -> p c j m", c=8, j=5)
    xsT = V(o_G + 12 * K, 4 * NT, BF16).rearrange("p (c t) -> p c t", c=4)
    nd_ = 0
    for ct in range(8):
        for j in range(5):
            sc_ = cols[:, C_CONVW + ct * 5 + j:C_CONVW + ct * 5 + j + 1]
            if nd_ % 3 == 0:
                ts(diagw[:, ct, j, :], ident_f[:], sc_, ALU.mult, ["const", "cols"], [("G", "Th", "diagw", ct, j)])
            elif nd_ % 3 == 1:
                act(diagw[:, ct, j, :], ident_f[:], AF.Copy, ["const", "cols"], [("G", "Th", "diagw", ct, j)], scale=sc_)
            else:
                ts(diagw[:, ct, j, :], ident_f[:], sc_, ALU.mult, ["const", "cols"], [("G", "Th", "diagw", ct, j)], eng=POOL)
            nd_ += 1
    nbk = 0
    for ct in range(8):
        for ck in range(3):
            for (o, n, xc) in tok2xcol(ck):
                bank = nbk % 4
                nbk += 1
                for j in range(5):
                    mm(PS(bank, n), diagw[:, ct, j, :], xbc[:, ct, xc + j - 2:xc + j - 2 + n], j == 0, j == 4,
                       [("F", "xbc"), ("G", "Th", "diagw", ct)], [pk(bank)])
                if ct < 4:
                    dstT, g, key = xsT, ct, ("G", "Th", "xsT", ct, ck, o)
                elif ct < 6:
                    dstT, g, key = BT, ct - 4, ("B", "BT", ct - 4, ck, o)
                else:
                    dstT, g, key = CT, ct - 6, ("C", "CT", ct - 6, ck, o)
                act(dstT[:, g, ck * 512 + o:ck * 512 + o + n], PS(bank, n), AF.Silu, [pk(bank), "cols"], [key],
                    bias=cols[:, C_CONVB + ct:C_CONVB + ct + 1])
    for t in range(NTT):
        bx = 4 + 2 * (t % 2)
        for c in range(4):
            mm(PS(bx, 128, c * 128), xsT[:, c, t * 128:(t + 1) * 128], ident_b[:], True, True,
               [("G", "Th", "xsT", c), "constb"], [pk(bx)])
        for c in range(2):
            mm(PS(bx + 1, 128, c * 128), BT[:, c, t * 128:(t + 1) * 128], ident_b[:], True, True,
               [("B", "BT", c), "constb"], [pk(bx + 1)])
        cp(xs_tm[:, t, :], PS(bx), [pk(bx)], [("B", "xs", t)])
        act(B_tm[:, t, :], PS(bx + 1, 256), AF.Copy, [pk(bx + 1)], [("B", "Btm", t)])
    tap("xsb", V(o_B, NTT * 768, BF16).rearrange("p (n f) -> p n f", n=NTT), [128, NTT, 768], ["B"]) if False else None
    tap("BCT", V(o_B + 18 * K, 4 * NT, BF16).rearrange("p (g t) -> p g t", g=4), [128, 4, NT], ["B", "C"])

    rsT = V(o_scr + 7 * K, NT, BF16)
    pool_off = [o_G + 24 * K]

    def pl(n, dt=F32):
        a = V(pool_off[0], n, dt)
        pool_off[0] += n * SZ[dt]
        assert pool_off[0] <= o_G + 35 * K - 1024
        return a
    X = pl(192).rearrange("p (b d) -> p b d", b=NTT)
    AX = pl(192).rearrange("p (b d) -> p b d", b=NTT)
    DT = pl(192).rearrange("p (b d) -> p b d", b=NTT)
    LA = pl(192).rearrange("p (b d) -> p b d", b=NTT)
    LNDT = pl(192).rearrange("p (b d) -> p b d", b=NTT)
    CUM = pl(192).rearrange("p (b d) -> p b d", b=NTT)
    TOT = pl(192).rearrange("p (b d) -> p b d", b=NTT)
    RSRC = pl(384).rearrange("p (b d) -> p b d", b=NTT)
    EXPO = V(o_scr + 4 * K, 576, F32).rearrange("p (b d) -> p b d", b=NTT)
    EXPIN = V(o_D, 576, F32).rearrange("p (b d) -> p b d", b=NTT)
    SPL3 = V(o_D + 2304, 1152, F32).rearrange("p (b d) -> p b d", b=NTT)
    R1 = V(o_D + 2304 + 4608, 384, F32).rearrange("p (b d) -> p b d", b=NTT)
    H1B = V(o_D + 2304 + 4608 + 1536, 384, BF16).rearrange("p (b d) -> p b d", b=NTT)
    PF = [("G", "ktm", "pool")]
    PD = [("D", "pool")]
    dtr3 = dtraw[:].rearrange("p (b h) -> p b h", b=NTT)
    X4 = X.rearrange("p b (d h) -> p b d h", d=2)
    tt(X4, dtr3.unsqueeze(2).to_broadcast([128, NTT, 2, 8]),
       smalls[:, 0:16].rearrange("p (d h) -> p d h", d=2).unsqueeze(1).to_broadcast([128, NTT, 2, 8]), ALU.add,
       ["dtraw", "smalls"], PF)
    UU = pl(192).rearrange("p (b d) -> p b d", b=NTT)
    LL = pl(192).rearrange("p (b d) -> p b d", b=NTT)
    act(AX, X, AF.Abs, PF, PF)
    act(AX, AX, AF.Exp, PF, PF, scale=-1.0)
    ts(UU, AX, 1.0, ALU.add, PF, PF)
    act(LL, UU, AF.Ln, PF, PF)
    ts(UU, UU, -1.0, ALU.add, PF, PF, s2=1e-30, op1=ALU.max)
    P.op(DVE, lambda e: e.reciprocal(out=UU, in_=UU), PF, PF, cost=0.3)
    tt(LL, LL, UU, ALU.mult, PF, PF)
    tt(AX, AX, LL, ALU.mult, PF, PF)
    ts(X, X, 0.0, ALU.max, PF, PF)
    tt(DT, X, AX, ALU.add, PF, PF)
    ts(DT, DT, 1e-30, ALU.max, PF, PF)
    tt(LA, DT, nA[:].unsqueeze(1).to_broadcast([128, NTT, 16]), ALU.mult, PF + ["nA"], PF)
    act(LNDT, DT, AF.Ln, PF, PF)
    for b in range(NTT):
        mm(PS(0, 16, b * 16), utri_f[:], LA[:, b, :], True, True, PF + ["const"], ["ps0"])
        mm(PS(1, 16, b * 16), ones_f[:], LA[:, b, :], True, True, PF + ["const"], ["ps1"])
    cp(CUM, PS(0, 192).rearrange("p (b d) -> p b d", b=NTT), ["ps0"], PF)
    cp(TOT, PS(1, 192).rearrange("p (b d) -> p b d", b=NTT), ["ps1"], PF)
    cp(RSRC[:, :, 0:8], CUM[:, :, 0:8], PF, PF)
    tt(RSRC[:, :, 8:16], LA[:, :, 8:16], CUM[:, :, 8:16], ALU.subtract, PF, PF)
    tt(RSRC[:, :, 16:24], LNDT[:, :, 0:8], CUM[:, :, 0:8], ALU.subtract, PF, PF)
    tt(RSRC[:, :, 24:32], LNDT[:, :, 8:16], RSRC[:, :, 8:16], ALU.subtract, PF, PF)
    cp(EXPIN[:, :, 0:8], RSRC[:, :, 0:8], PF, PD)
    tt(EXPIN[:, :, 8:16], TOT[:, :, 8:16], RSRC[:, :, 8:16], ALU.add, PF, PD)
    tt(EXPIN[:, :, 16:24], TOT[:, :, 0:8], RSRC[:, :, 16:24], ALU.add, PF, PD)
    cp(EXPIN[:, :, 24:32], RSRC[:, :, 24:32], PF, PD)
    cp(EXPIN[:, :, 32:48], TOT, PF, PD)
    act(EXPO, EXPIN, AF.Exp, PD, [("SCR", "expo")])
    cp(H1B, RSRC, PF, PD)
    cp(SPL3[:, :, 0:32], H1B, PD, PD)
    tt(R1, RSRC, SPL3[:, :, 0:32], ALU.subtract, PF + PD, PD)
    cp(H1B, R1, PD, PD)
    cp(SPL3[:, :, 32:64], H1B, PD, PD)
    tt(R1, R1, SPL3[:, :, 32:64], ALU.subtract, PD, PD)
    cp(H1B, R1, PD, PD)
    cp(SPL3[:, :, 64:96], H1B, PD, PD)
    for ck in range(3):
        for j in range(4):
            tr(PS(2 + ck % 2, 128, j * 128)[0:96, :], SPL3[:, ck * 4 + j, :], PD, [pk(2 + ck % 2)])
        cp(rsT[0:96, ck * 512:(ck + 1) * 512], PS(2 + ck % 2)[0:96, :], [pk(2 + ck % 2)], [("SCR", "rsT", ck)])
    fence(["D", "F", "G", "SCR"])
    MF4, MB4 = cstS[:, 0:512], cstS[:, 512:1024]
    selrow = cstS[:, 1024:3072].rearrange("p (a m) -> p a m", a=16)
    selbias = cstS[:, 3072:5120].rearrange("p (d n) -> p d n", d=2)
    dI = V(o_G + 30 * K, 1024, BF16).rearrange("p (h m) -> p h m", h=8)
    for h in range(8):
        ts(dI[:, h, :], ident_f[:], dskb[:, h:h + 1], ALU.mult, ["const", "dskb"], [("G", "dI")])
    nwb = V(o_G + 32 * K, 512, F32)
    dma(SP, nwb[:], bcast_row(R_NW, 512), "nwb", [], [("G", "nwb")])
    WfB = [V(o_G + i * 11 * K, 1024, BF16) for i in range(2)]
    WbB = [V(o_G + i * 11 * K + 2 * K, 1024, BF16) for i in range(2)]
    PmB = [V(o_G + i * 11 * K + 4 * K, 1024, BF16).rearrange("p (h t) -> p h t", h=8) for i in range(2)]
    y1B = [V(o_G + i * 11 * K + 6 * K, 512, F32) for i in range(2)]
    y2B = [V(o_G + i * 11 * K + 8 * K, 512, F32) for i in range(2)]
    jkB = [V(o_G + i * 11 * K + 10 * K, 512, BF16) for i in range(2)]
    xswB = [[V(o_G + 22 * K + (d * 2 + i) * K, 512, BF16) for i in range(2)] for d in range(2)]
    Sst = [V(o_G + 26 * K, 512, F32), V(o_G + 28 * K, 512, F32)]
    ssqB = [small(1) for _ in range(2)]
    rstdB = [small(1) for _ in range(2)]
    fence(["D"], "ssdpre")
    Rall = V(o_D, 8 * 512, BF16).rearrange("p (b f) -> p b f", b=8)
    SallA = V(o_D + 8 * K, 4 * 512, BF16).rearrange("p (b f) -> p b f", b=4)
    SallB = V(o_C + 6 * K, 4 * 512, BF16).rearrange("p (b f) -> p b f", b=4)

    def Sall(i):
        return SallA[:, i, :] if i < 4 else SallB[:, i - 4, :]

    def Skey(i):
        return ("D", "S", i) if i < 4 else ("C", "S", i)
    psum2 = lambda b0: psum[:, b0 * 512:b0 * 512 + 1024]

    def bc8(ap8):
        return ap8.unsqueeze(2).to_broadcast([128, 8, 64])

    def v8(ap512):
        return ap512.rearrange("p (h e) -> p h e", h=8)

    nxsw = [0, 0]

    def state_update(d, b, bank):
        w = EXPO[:, b, 16 + d * 8:24 + d * 8]
        k = nxsw[d] % 2
        nxsw[d] += 1
        xw = xswB[d][k]
        tt(v8(xw[:]), v8(xs_tm[:, b, :]), bc8(w), ALU.mult, [("B", "xs", b), ("SCR", "expo")], [("G", "xsw", d, k)], eng=POOL)
        for g in range(2):
            mm(PS(bank, 256, g * 256), B_tm[:, b, g * 128:(g + 1) * 128], xw[:, g * 256:(g + 1) * 256], True, True,
               [("B", "Btm", b), ("G", "xsw", d, k)], [pk(bank)])
        tt(v8(Sst[d][:]), v8(Sst[d][:]), bc8(EXPO[:, b, 32 + d * 8:40 + d * 8]), ALU.mult, [("G", "S", d), ("SCR", "expo")],
           [("G", "S", d)], eng=POOL)
        tt(Sst[d][:], Sst[d][:], PS(bank), ALU.add, [("G", "S", d), pk(bank)], [("G", "S", d)])

    RallS = V(o_scr, 2 * 512, BF16).rearrange("p (b f) -> p b f", b=2)
    SallS = V(o_scr + 2 * K, 2 * 512, BF16).rearrange("p (b f) -> p b f", b=2)

    def Sv(big, i):
        return (Sall(i), Skey(i)) if big else (SallS[:, i, :], ("SCR", "S", i))

    def Rv(big, i):
        return (Rall[:, i, :], ("D", "R", i)) if big else (RallS[:, i, :], ("SCR", "R", i))

    def chain_steps(si):
        t0, nb = SEQS[si]
        big = nb == 8
        if big:
            dma(SP, Sst[0][:], sssd_d[:, 0:512], "s0s0", [], [("G", "S", 0)])
            dma(SP, Sst[1][:], sssd_d[:, 512:1024], "s0s1", [], [("G", "S", 1)])
        else:
            memset(Sst[0][:], 0.0, [("G", "S", 0)])
            memset(Sst[1][:], 0.0, [("G", "S", 1)])
        for step in range(nb):
            i_f = step
            i_b = nb - 1 - step
            sv, sk = Sv(big, i_f)
            rv, rk = Rv(big, i_b)
            act(sv, Sst[0][:], AF.Copy, [("G", "S", 0)], [sk])
            act(rv, Sst[1][:], AF.Copy, [("G", "S", 1)], [rk])
            if i_f < nb - 1 or not big:
                state_update(0, t0 + i_f, 6)
            if i_b > 0 or not big:
                state_update(1, t0 + i_b, 7)
            yield
        if not big:
            sq = si - 1
            for d in range(2):
                dma(SP, ossd_d[sq, d].rearrange("h n e -> n h e"), Sst[d][:].rearrange("p (h e) -> p h e", h=8), "ossd%d" % d,
                    [("G", "S", d)], [], True)
        yield

    nblk = [0]
    kbof = {}

    def stageA(si, i):
        t0, nb = SEQS[si]
        b = t0 + i
        tok = b * 128
        kb = nblk[0] % 2
        nblk[0] += 1
        kbof[b] = kb
        Wf, Wb, Pm = WfB[kb], WbB[kb], PmB[kb]
        bk_ = lambda n: ("G", n, kb)
        for g in range(2):
            mm(PS(4, 128, g * 128), BT[:, g, tok:tok + 128], CT[:, g, tok:tok + 128], True, True,
               [("B", "BT", g), ("C", "CT", g)], ["ps4"])
        for d in range(2):
            bank0 = 2 * d
            wr = [pk(bank0), pk(bank0 + 1)]
            Md = MF4 if d == 0 else MB4
            for half in range(2):
                mm(PS(bank0 + half), ident_b[:], Md, True, False, ["constb", ("LNP", "cstS")], wr)
                mm(PS(bank0 + half), rsT[0:96, tok:tok + 128], selbias[0:96, d, half * 512:(half + 1) * 512], False, False,
                   [("SCR", "rsT", b // 4), ("LNP", "cstS")], wr)
            for h in range(8):
                mm(PS(bank0 + h // 4, 128, (h % 4) * 128), selrow[0:96, d * 8 + h, :], rsT[0:96, tok:tok + 128], False, h % 4 == 3,
                   [("SCR", "rsT", b // 4), ("LNP", "cstS")], wr)
            act((Wf if d == 0 else Wb)[:], psum2(bank0), AF.Exp, wr, [bk_("W%d" % d)])
        tt(Wf[:], Wf[:], Wb[:], ALU.add, [bk_("W0"), bk_("W1")], [bk_("W0")])
        tt(Pm.rearrange("p (g q) t -> p g q t", g=2), Wf[:].rearrange("p (g q t) -> p g q t", g=2, q=4),
           PS(4, 256).rearrange("p (g t) -> p g t", g=2).unsqueeze(2).to_broadcast([128, 2, 4, 128]), ALU.mult,
           [bk_("W0"), "ps4"], [bk_("Pm")])

    def stageB(si, i):
        t0, nb = SEQS[si]
        big = nb == 8
        b = t0 + i
        tok = b * 128
        kb = kbof[b]
        Pm, y1, y2, jk = PmB[kb], y1B[kb], y2B[kb], jkB[kb]
        ssq_, rstd_ = ssqB[kb], rstdB[kb]
        bk_ = lambda n: ("G", n, kb)
        sv, sk = Sv(big, i)
        rv, rk = Rv(big, i)
        for h in range(8):
            mm(PS(5, 64, h * 64), Pm[:, h, :], xs_tm[:, b, h * 64:(h + 1) * 64], True, False,
               [bk_("Pm"), ("B", "xs", b)], ["ps5"])
            mm(PS(5, 64, h * 64), dI[:, h, :], xs_tm[:, b, h * 64:(h + 1) * 64], False, True,
               [("G", "dI"), ("B", "xs", b)], ["ps5"])
        for g in range(2):
            mm(PS(6, 256, g * 256), CT[:, g, tok:tok + 128], sv[:, g * 256:(g + 1) * 256], True, True,
               [("C", "CT", g), sk], ["ps6"])
            mm(PS(7, 256, g * 256), CT[:, g, tok:tok + 128], rv[:, g * 256:(g + 1) * 256], True, True,
               [("C", "CT", g), rk], ["ps7"])
        tt(v8(y1[:]), v8(PS(6)), bc8(EXPO[:, b, 0:8]), ALU.mult, ["ps6", ("SCR", "expo")], [bk_("y1")])
        tt(v8(y2[:]), v8(PS(7)), bc8(EXPO[:, b, 8:16]), ALU.mult, ["ps7", ("SCR", "expo")], [bk_("y2")])
        tt(y1[:], y1[:], PS(5), ALU.add, [bk_("y1"), "ps5"], [bk_("y1")])
        tt(y1[:], y1[:], y2[:], ALU.add, [bk_("y1"), bk_("y2")], [bk_("y1")])
        tt(y1[:], y1[:], z_tm[:, b, :], ALU.mult, [bk_("y1"), ("E", "z", b)], [bk_("y1")], eng=POOL)
        act(jk[:], y1[:], AF.Square, [bk_("y1")], [bk_("jk"), ("ssq", kb)], accum_out=ssq_[:, 0:1])
        rstd_from(rstd_[:, 0:1], ssq_[:, 0:1], 1.0 / 512.0, [("ssq", kb)], [("rstd", kb)])
        stt(y2[:], y1[:], rstd_[:, 0:1], nwb[:], ALU.mult, ALU.mult, [bk_("y1"), ("rstd", kb), ("G", "nwb")], [bk_("y2")])

    def stageC(si, i):
        t0, nb = SEQS[si]
        b = t0 + i
        tok = b * 128
        kb = kbof[b]
        y2 = y2B[kb]
        for j in range(4):
            tr(PS(4, 128, j * 128), y2[:, j * 128:(j + 1) * 128], [("G", "y2", kb)], ["ps4"])
        act(oT[:, 4:8, tok:tok + 128], PS(4).rearrange("p (j t) -> p j t", j=4), AF.Copy, ["ps4"], [("A", b // 4, "oT", 4, b)])

    seq_order = [1, 0, 2]
    blocks = [(si, i) for si in seq_order for i in range(SEQS[si][1])]
    first_of = {}
    for n_, (si, i) in enumerate(blocks):
        first_of.setdefault(si, n_)
    for _ in chain_steps(seq_order[0]):
        pass
    pending_chain = None
    nbk = len(blocks)
    for step in range(nbk + 2):
        if step < nbk:
            si, i = blocks[step]
            if i == 0:
                pos = seq_order.index(si)
                if pos + 1 < len(seq_order):
                    pending_chain = chain_steps(seq_order[pos + 1])
            stageA(si, i)
        if 1 <= step <= nbk:
            stageB(*blocks[step - 1])
        if 2 <= step:
            stageC(*blocks[step - 2])
        if pending_chain is not None:
            si_cur = blocks[min(step, nbk - 1)][0]
            nsteps = 4 if SEQS[si_cur][1] == 2 else 1
            for _ in range(nsteps):
                try:
                    next(pending_chain)
                except StopIteration:
                    pending_chain = None
                    break
    tap("oT", oT, [128, 8, NT], ["A"])

    fence(["B", "C", "D", "E", "F", "G", "LNP", "SCR"])
    x1 = [V(o_B + t * 4 * K, 1024, F32) for t in range(NTT)]
    xs2 = [V(o_G + j * 4 * K, 1024, F32) for j in range(2)]
    xs2k = [[("G", "xs2", 0)], [("G", "xs2", 1)]]
    lng = V(o_lnp, 1024, F32)
    lnb = V(o_lnp + 4 * K, 1024, F32)
    gateb = [V(o_lnp + 8 * K + c * 4 * K, 1024, F32) for c in range(2)]

    rstdL = [small(1) for _ in range(2)]
    nmrB = [small(1) for _ in range(2)]

    s1B = [small(1) for _ in range(2)]
    s2B = [small(1) for _ in range(2)]
    mB = [small(1) for _ in range(2)]
    vB = [small(1) for _ in range(2)]
    ljunk = V(o_scr, 1024, BF16)

    def ln_affine(t, dst, dst_key, lng, lnb, lkey):
        tt(dst[:], dst[:], lng[:], ALU.mult, [dst_key, lkey + ("g",)], [dst_key], eng=POOL)
        tt(dst[:], dst[:], lnb[:], ALU.add, [dst_key, lkey + ("b",)], [dst_key], eng=POOL)

    def layer_norm_tile(t, banks, xres, xres_key, dst, dst_key, lng, lnb, gateb, gkey, lkey, defer_affine=False):
        cond = 0 if t < 8 else 1
        q = t % 2
        s1, s2, m_, v_, rstd, nmr = s1B[q], s2B[q], mB[q], vB[q], rstdL[q], nmrB[q]
        u = dst
        xk = xres_key if isinstance(xres_key, list) else [xres_key]
        for half in range(2):
            hs = slice(half * 512, (half + 1) * 512)
            tt(u[:, hs], PS(banks[half]), gateb[cond][:, hs], ALU.mult, [pk(banks[half]), gkey + (cond, half)], [dst_key + (half,)])
            stt(u[:, hs], xres[:, hs], ALPHA, u[:, hs], ALU.mult, ALU.add, xk + [dst_key + (half,)], [dst_key + (half,)])
        act(ljunk[:], u[:], AF.Copy, [dst_key], [("SCR", "ljunk"), ("s1", q)], accum_out=s1[:, 0:1])
        act(ljunk[:], u[:], AF.Square, [dst_key], [("SCR", "ljunk"), ("s2", q)], accum_out=s2[:, 0:1])
        ts(m_[:, 0:1], s1[:, 0:1], 1.0 / 1024.0, ALU.mult, [("s1", q)], [("m", q)])
        tt(v_[:, 0:1], m_[:, 0:1], m_[:, 0:1], ALU.mult, [("m", q)], [("v", q)])
        stt(v_[:, 0:1], s2[:, 0:1], 1.0 / 1024.0, v_[:, 0:1], ALU.mult, ALU.subtract, [("s2", q), ("v", q)], [("v", q)])
        rstd_from(rstd[:, 0:1], v_[:, 0:1], 1.0, [("v", q)], [("rstdL", q)])
        ts(nmr[:, 0:1], m_[:, 0:1], rstd[:, 0:1], ALU.mult, [("m", q), ("rstdL", q)], [("nmr", q)], s2=-1.0, op1=ALU.mult)
        act(u[:], u[:], AF.Identity, [dst_key, ("rstdL", q), ("nmr", q)], [dst_key], bias=nmr[:, 0:1], scale=rstd[:, 0:1])
        if not defer_affine:
            ln_affine(t, dst, dst_key, lng, lnb, lkey)

    dma(SP, lng[:], bcast_row(R_LN1G, 1024), "lnpg", [], [("LNP", "ln", "g")])
    dma(SP, lnb[:], bcast_row(R_LN1B, 1024), "lnpb", [], [("LNP", "ln", "b")])
    gate_from_mod(16, gateb, [("LNP", "gate", 0), ("LNP", "gate", 1)])
    wo = pre_wo
    for t in range(NTT):
        j = t % 2
        dma(SP, xs2[j][:], x_t[t], "xs2%d" % j, [], xs2k[j])
        banks = (2 * (t % 4), 2 * (t % 4) + 1)
        for half in range(2):
            s, wv = wo[half]
            for kt in range(8):
                mm(PS(banks[half]), oT[:, kt, t * 128:(t + 1) * 128], wv[:, kt, :], kt == 0, kt == 7,
                   [("ring", s), ("A", t // 4)], [pk(banks[half])])
        layer_norm_tile(t, banks, xs2[j], xs2k[j], x1[t], ("B", "x1", t), lng, lnb, gateb, ("LNP", "gate"), ("LNP", "ln"), defer_affine=True)
    tap("x1", V(o_B, NTT * 1024, F32).rearrange("p (n f) -> p n f", n=NTT), [128, NTT, 1024], ["B"])

    sc2p3 = sc2p[:].rearrange("p (k c) -> p k c", c=2)
    sh2 = modT[:, 48:64].rearrange("p (k c) -> p k c", c=2)
    scale2 = small(16)
    bias2 = small(16)
    tt(scale2[:].rearrange("p (k c) -> p k c", c=2), sc2p3, cols[:, C_LN1G:C_LN1G + 8].unsqueeze(2).to_broadcast([128, 8, 2]), ALU.mult,
       ["sc2p", "cols"], ["scale2"])
    tt(bias2[:].rearrange("p (k c) -> p k c", c=2), sc2p3, cols[:, C_LN1B:C_LN1B + 8].unsqueeze(2).to_broadcast([128, 8, 2]), ALU.mult,
       ["sc2p", "cols"], ["bias2"])
    tt(bias2[:], bias2[:], modT[:, 48:64], ALU.add, ["bias2", ("modT", 24)], ["bias2"])
    scale23 = scale2[:].rearrange("p (k c) -> p k c", c=2)
    bias23 = bias2[:].rearrange("p (k c) -> p k c", c=2)
    h2T = V(o_A, 8 * NT, BF16).rearrange("p (k t) -> p k t", k=8)
    aT = V(o_E, 22 * NT, BF16).rearrange("p (f t) -> p f t", f=22)
    sg = [V(o_scr + 2 * K + i * K, 512, BF16) for i in range(2)]
    def h2t_chunk(ck, extra=()):
        cond = 0 if ck < 2 else 1
        fence([("A", ck)])
        for kt in range(8):
            b = kt % 4
            for j in range(4):
                tr(PS(b, 128, j * 128), x1[ck * 4 + j][:, kt * 128:(kt + 1) * 128], [("B", "x1", ck * 4 + j)] + list(extra), [pk(b)])
            if kt % 2 == 0:
                act(h2T[:, kt, ck * 512:(ck + 1) * 512], PS(b), AF.Identity, [pk(b), "scale2", "bias2"],
                    [("A", ck, "h2T", kt)], bias=bias23[:, kt, cond:cond + 1], scale=scale23[:, kt, cond:cond + 1])
            else:
                ts(h2T[:, kt, ck * 512:(ck + 1) * 512], PS(b), scale23[:, kt, cond:cond + 1], ALU.mult, [pk(b), "scale2", "bias2"],
                   [("A", ck, "h2T", kt)], s2=bias23[:, kt, cond:cond + 1], op1=ALU.add)

    npair = [0]

    def ffn_group(s, wg, wu, ft, jt, ck, order_key=None):
        bg = 2 * (npair[0] % 4)
        bu = bg + 1
        si_ = npair[0] % 2
        npair[0] += 1
        for kt in range(8):
            mm(PS(bg), wg[:, kt, jt * 128:(jt + 1) * 128], h2T[:, kt, ck * 512:(ck + 1) * 512], kt == 0, kt == 7,
               [("ring", s), ("A", ck, "h2T", kt)], [pk(bg)])
        for kt in range(8):
            mm(PS(bu), wu[:, kt, jt * 128:(jt + 1) * 128], h2T[:, kt, ck * 512:(ck + 1) * 512], kt == 0, kt == 7,
               [("ring", s), ("A", ck, "h2T", kt)], [pk(bu)] + ([order_key] if (order_key and kt == 7) else []))
        act(sg[si_][:], PS(bg), AF.Silu, [pk(bg)], [("SCR", "sg", si_)])
        tt(aT[:, ft, ck * 512:(ck + 1) * 512], sg[si_][:], PS(bu), ALU.mult, [("SCR", "sg", si_), pk(bu)], [("E", "aT", ft, ck)])

    def ffn_views(base):
        return (V(base, 2048, BF16).rearrange("p (k n) -> p k n", k=8), V(base + 4096, 2048, BF16).rearrange("p (k n) -> p k n", k=8))

    h2t_chunk(0)
    h2t_chunk(1)
    s0_, base0_ = ffn_pre[0]
    wg0_, wu0_ = ffn_views(base0_)
    for jt in range(2):
        for ck in range(2):
            ffn_group(s0_, wg0_, wu0_, jt, jt, ck, order_key=("ffnorder",))
    h2t_chunk(2, extra=[("ffnorder",)])
    assert o_E + 12 * 3072 <= o_G and o_scr + 2 * K >= o_scr + 2048
    for t in range(NTT):
        ln_affine(t, x1[t], ("B", "x1", t), lng, lnb, ("LNP", "ln"))
    fence(["LNP"])
    def wd_load(sl):
        nft = 4 if sl < 5 else 2
        src = w_down_d[sl * 512:sl * 512 + nft * 128, :].rearrange("(k p) n -> p k n", p=128)
        if sl in (2, 3):
            base = o_lnp + (sl - 2) * 8 * K
            key = ("LNP", "wd", sl)
            dma(POOL, V(base, nft * 1024, BF16).rearrange("p (k n) -> p k n", k=nft), src, "wd%d" % sl, [], [key])
        else:
            s, base = ring_load([(src, 0, nft, 1024)], slot={0: 2, 1: 3, 4: 0, 5: 1}[sl])
            key = ("ring", s)
        wvd = V(base, nft * 1024, BF16).rearrange("p (k n) -> p k n", k=nft)
        return [(key, wvd[:, k_, :]) for k_ in range(nft)]
    wd_part = {sl: wd_load(sl) for sl in (0, 1, 2, 3)}
    for jt in range(2):
        ffn_group(s0_, wg0_, wu0_, jt, jt, 2)
    for sl in range(1, 11):
        if sl == 6:
            fence(["E", "G"])
        s, base = ffn_pre[sl] if sl < 2 else ffn_slot(sl)
        wg, wu = ffn_views(base)
        for jt in range(2):
            for ck in range(3):
                ffn_group(s, wg, wu, sl * 2 + jt, jt, ck)

    fence(["A", "SCR"])
    lng2 = V(o_A + 8 * K, 1024, F32)
    lnb2 = V(o_A + 12 * K, 1024, F32)
    gateb2 = [V(o_A + 16 * K + c_ * 4 * K, 1024, F32) for c_ in range(2)]
    dma(SP, lng2[:], bcast_row(R_LN2G, 1024), "lnpg", [], [("A", "ln", "g")])
    dma(SP, lnb2[:], bcast_row(R_LN2B, 1024), "lnpb", [], [("A", "ln", "b")])
    gate_from_mod(40, gateb2, [("A", "gate", 0), ("A", "gate", 1)])
    ystage = [V(o_A + j * 4 * K, 1024, F32) for j in range(2)]
    for sl in (4, 5):
        wd_part[sl] = wd_load(sl)
    wd = []
    for sl in range(6):
        wd += wd_part[sl]
    y_t = y_d.rearrange("(n p) f -> n p f", p=128)
    for t in range(NTT):
        j = t % 2
        banks = (4 * j + 2, 4 * j + 3)
        for half in range(2):
            for ft in range(22):
                key, w = wd[ft]
                mm(PS(banks[half]), aT[:, ft, t * 128:(t + 1) * 128], w[:, half * 512:(half + 1) * 512], ft == 0, ft == 21,
                   [key, ("E", "aT", ft, t // 4)], [pk(banks[half])])
        if t < NTT - 1:
            layer_norm_tile(t, banks, x1[t], ("B", "x1", t), ystage[j], ("A", "ystage", j), lng2, lnb2, gateb2, ("A", "gate"), ("A", "ln"))
            dma(SP, y_t[t], ystage[j][:], "yout%d" % j, [("A", "ystage", j)], [], True)
        else:
            layer_norm_tile(t, banks, x1[t], ("B", "x1", t), ystage[j], ("A", "ystage", j), lng2, lnb2, gateb2, ("A", "gate"), ("A", "ln"),
                            defer_affine=True)
            for half in range(2):
                hs = slice(half * 512, (half + 1) * 512)
                eng_ = POOL if half == 0 else DVE
                kh = ("A", "ystage", j, half)
                tt(ystage[j][:, hs], ystage[j][:, hs], lng2[:, hs], ALU.mult, [kh, ("A", "ln", "g")], [kh], eng=eng_)
                tt(ystage[j][:, hs], ystage[j][:, hs], lnb2[:, hs], ALU.add, [kh, ("A", "ln", "b")], [kh], eng=eng_)
                dma(SP, y_t[t][:, hs], ystage[j][:, hs], "ylast%d" % half, [kh], [], True)
    P.emit(st)
    build.P = P
    return nc


_CACHE = {}


def _prep_inputs(inp):
    f = lambda a: np.ascontiguousarray(np.asarray(a, dtype=np.float32))
    cf, cb = _host_consts()
    rows = np.zeros((1, R_N), np.float32)
    rows[0, R_LN1G:R_LN1G + 1024] = f(inp["ln1_g"])[0]
    rows[0, R_LN1B:R_LN1B + 1024] = f(inp["ln1_b"])[0]
    rows[0, R_LN2G:R_LN2G + 1024] = f(inp["ln2_g"])[0]
    rows[0, R_LN2B:R_LN2B + 1024] = f(inp["ln2_b"])[0]
    rows[0, R_NW:R_NW + 512] = f(inp["ssd_norm_w"])[0]
    rows[0, R_CONVB:R_CONVB + 1024] = f(inp["conv_b"])[0]
    rows[0, R_BADA:R_BADA + 6144] = f(inp["b_ada"])[0]
    sm = np.concatenate([f(inp["dt_bias_fwd"])[0], f(inp["dt_bias_bwd"])[0], f(inp["a_log_fwd"])[0], f(inp["a_log_bwd"])[0],
                         f(inp["d_skip"])[0], f(inp["ret_decay_fwd"])[0], f(inp["ret_decay_bwd"])[0]])
    rows[0, R_SMALL:R_SMALL + 48] = sm
    shared = {
        "w_in": f(inp["w_in"])[0], "w_out": f(inp["w_out"])[0], "w_gate": f(inp["w_gate"])[0],
        "w_up": f(inp["w_up"])[0], "w_down": f(inp["w_down"])[0], "w_ada": f(inp["w_ada"])[0],
        "rows": rows, "cstf": cf, "cstb": cb,
    }
    xs = f(inp["x_sample"])
    xp = f(inp["x_prompt"])
    c = f(inp["c"])
    cc = f(inp["c_ctx"])
    sr = f(inp["state_ret"])
    ss = f(inp["state_ssd"])
    b_ada = f(inp["b_ada"])[0]
    conv_w = f(inp["conv_w"])[0]
    conv_b = f(inp["conv_b"])[0]
    maps = []
    for i in range(8):
        cols = np.zeros((128, C_N), np.float32)
        cv = np.stack([c[i], cc], 0).reshape(2, 8, 128).transpose(2, 1, 0)
        cols[:, C_CVEC:C_CVEC + 16] = cv.reshape(128, 16)
        cols[:, C_BADA:C_BADA + 48] = b_ada.reshape(48, 128).T
        cols[:, C_CONVW:C_CONVW + 40] = conv_w.reshape(5, 8, 128).transpose(2, 1, 0).reshape(128, 40)
        cols[:, C_CONVB:C_CONVB + 8] = conv_b.reshape(8, 128).T
        cols[:, C_LN1G:C_LN1G + 8] = f(inp["ln1_g"])[0].reshape(8, 128).T
        cols[:, C_LN1B:C_LN1B + 8] = f(inp["ln1_b"])[0].reshape(8, 128).T
        m = dict(shared)
        m["x"] = np.ascontiguousarray(np.concatenate([xs[i], xp[2 * i], xp[2 * i + 1]], 0))
        m["sret0"] = np.ascontiguousarray(sr[i, 0].transpose(2, 0, 1, 3).reshape(128, 1024))
        m["sssd0"] = np.ascontiguousarray(ss[i, 0].transpose(2, 0, 1, 3).reshape(128, 1024))
        m["cols"] = cols
        maps.append(m)
    return maps


def kernel(**inputs):
    if "nc" not in _CACHE:
        _CACHE["nc"] = build()
    nc = _CACHE["nc"]
    maps = _prep_inputs(inputs)
    res = run_bass_kernel_spmd(nc, maps, core_ids=list(range(8)))
    r = res.results
    y_s = np.stack([r[i]["y"][:1024] for i in range(8)], 0)
    y_p = np.concatenate([r[i]["y"][1024:].reshape(2, 256, 1024) for i in range(8)], 0)
    nret = np.concatenate([r[i]["oret"] for i in range(8)], 0)[:, None]
    nssd = np.concatenate([r[i]["ossd"] for i in range(8)], 0)[:, None]
    return (y_p.astype(np.float32), y_s.astype(np.float32), nret.astype(np.float32), nssd.astype(np.float32))
```

```python
import contextlib
import math
import numpy as np
import concourse.bass as bass
import concourse.mybir as mybir
from concourse.bass_utils import run_bass_kernel_spmd

F32 = mybir.dt.float32
BF16 = mybir.dt.bfloat16
U8 = mybir.dt.uint8
AF = mybir.ActivationFunctionType
ALU = mybir.AluOpType
SZ = {F32: 4, BF16: 2}

PE, ACT, DVE, POOL, SP = "tensor", "scalar", "vector", "gpsimd", "sync"
ENGS = [PE, ACT, DVE, POOL, SP]

D = 1024
NT = 1536
NTT = 12
INC = 3592
DFF = 2816
EPS = 1e-6
ALPHA = 2.0 ** 0.25
NEG = -32768.0
SEQS = [(0, 8), (8, 2), (10, 2)]


class Op:
    __slots__ = ("idx", "eng", "fn", "is_dma", "dsem", "dval", "deps", "inc", "cval", "cost", "fin", "npend", "succ", "start", "crit", "tag", "aps")

    def __init__(self, idx, eng, fn, is_dma):
        self.idx, self.eng, self.fn, self.is_dma = idx, eng, fn, is_dma
        self.dsem, self.dval, self.deps, self.inc, self.cval = None, 0, [], False, 0
        self.cost, self.fin, self.npend, self.succ = 0.3, 0.0, 0, []
        self.start, self.crit, self.tag, self.aps = 0.0, None, '', None


class Prog:
    def __init__(self, nc):
        self.nc = nc
        self.ops = []
        self.res = {}
        self.dma_sems = {}
        self.out_dma_ops = []
        self.cur_tag = ""
        self.last_dma = {}

    @staticmethod
    def _split(key):
        if isinstance(key, tuple):
            return key[0], tuple(key[1:])
        return key, ()

    @staticmethod
    def _related(p, q):
        n = min(len(p), len(q))
        return p[:n] == q[:n]

    def _record(self, op, reads, writes):
        deps = set()
        for k in reads:
            name, p = self._split(k)
            d = self.res.setdefault(name, {})
            for q, e in d.items():
                if self._related(p, q) and e[0] is not None:
                    deps.add(e[0])
        for k in writes:
            name, p = self._split(k)
            d = self.res.setdefault(name, {})
            for q, e in d.items():
                if self._related(p, q):
                    if e[0] is not None:
                        deps.add(e[0])
                    deps.update(e[1])
        deps.discard(op)
        op.deps = sorted(deps, key=lambda o: o.idx)
        for k in reads:
            name, p = self._split(k)
            d = self.res[name]
            if p not in d:
                d[p] = [None, []]
            d[p][1].append(op)
        for k in writes:
            name, p = self._split(k)
            d = self.res[name]
            for q in [q for q in d if len(q) >= len(p) and q[:len(p)] == p]:
                del d[q]
            d[p] = [op, []]

    def op(self, eng, fn, reads=(), writes=(), cost=0.3):
        o = Op(len(self.ops), eng, fn, False)
        o.cost = cost
        o.tag = self.cur_tag
        self.ops.append(o)
        self._record(o, list(reads), list(writes))
        return o

    def dma(self, eng, fn, semkey, reads=(), writes=(), is_output=False, cost=3.0, chain=True):
        o = Op(len(self.ops), eng, fn, True)
        o.cost = cost
        o.tag = self.cur_tag
        ent = self.dma_sems.setdefault(semkey, [None, 0])
        ent[1] += 16
        o.dsem, o.dval = semkey, ent[1]
        self.ops.append(o)
        self._record(o, list(reads), list(writes))
        prev = self.last_dma.get(semkey)
        if chain and prev is not None and prev not in o.deps:
            o.deps.append(prev)
            o.deps.sort(key=lambda q: q.idx)
        self.last_dma[semkey] = o
        if is_output:
            self.out_dma_ops.append(o)
        return o

    def schedule(self):
        import heapq
        ops = self.ops
        for o in ops:
            o.succ = []
        for o in ops:
            o.npend = len(o.deps)
            for d in o.deps:
                d.succ.append(o)

        def lat(d, o):
            return 0.05 if (d.eng == PE and o.eng == PE and not d.is_dma and not o.is_dma) else 0.25

        bl = [0.0] * len(ops)
        for o in reversed(ops):
            m = 0.0
            for s_ in o.succ:
                v = bl[s_.idx] + lat(o, s_)
                if v > m:
                    m = v
            bl[o.idx] = o.cost + m
        inorder = {SP: False, POOL: False, PE: False, ACT: False, DVE: False}
        pend = {e: [] for e in ENGS}
        avail = {e: [] for e in ENGS}
        tcur = {e: 0.0 for e in ENGS}
        ready_t = {}
        nxt = {e: 0 for e in ENGS}
        eng_ops = {e: [o for o in ops if o.eng == e] for e in ENGS}
        order = {e: [] for e in ENGS}

        def push(o):
            rt = 0.0
            for d in o.deps:
                rt = max(rt, d.fin + lat(d, o))
            ready_t[o.idx] = rt
            heapq.heappush(pend[o.eng], (rt, o.idx, o))

        for o in ops:
            if o.npend == 0:
                push(o)
        done, n = 0, len(ops)
        while done < n:
            best = None
            for e in ENGS:
                if inorder[e]:
                    if nxt[e] >= len(eng_ops[e]):
                        continue
                    o = eng_ops[e][nxt[e]]
                    if o.npend != 0 or o.idx not in ready_t:
                        continue
                    st_ = max(tcur[e], ready_t[o.idx])
                    cand = (st_, o.idx, e, o)
                else:
                    p, a = pend[e], avail[e]
                    while p and p[0][0] <= tcur[e]:
                        rt, idx, o = heapq.heappop(p)
                        heapq.heappush(a, (-bl[idx], idx, o))
                    if a:
                        o = a[0][2]
                        cand = (tcur[e], o.idx, e, o)
                    elif p:
                        rt, idx, o = p[0]
                        cand = (rt, idx, e, o)
                    else:
                        continue
                if best is None or cand[:2] < best[:2]:
                    best = cand
            assert best is not None, "scheduler deadlock"
            st_, _, e, o = best
            if inorder[e]:
                nxt[e] += 1
            else:
                if avail[e] and avail[e][0][2] is o:
                    heapq.heappop(avail[e])
                else:
                    heapq.heappop(pend[e])
            o.start = st_
            o.crit = ("eng", order[e][-1]) if (order[e] and tcur[e] >= ready_t[o.idx]) else \
                ("dep", max(o.deps, key=lambda d: d.fin) if o.deps else None)
            if o.is_dma:
                tcur[e] = st_ + (1.0 if e == POOL else 0.1)
                o.fin = st_ + o.cost
            else:
                tcur[e] = st_ + o.cost
                o.fin = tcur[e]
            order[e].append(o)
            done += 1
            for s_ in o.succ:
                s_.npend -= 1
                if s_.npend == 0:
                    push(s_)
        self.est_total = max(o.fin for o in ops)
        return order

    def emit(self, stack, reorder=True):
        nc = self.nc
        if reorder:
            order = self.schedule()
        else:
            order = {e: [o for o in self.ops if o.eng == e] for e in ENGS}
        esem = {e: stack.enter_context(nc.semaphore("c_" + e)) for e in ENGS}
        for i, (k, ent) in enumerate(self.dma_sems.items()):
            ent[0] = stack.enter_context(nc.semaphore("d%d" % i))
        final_waits = {}
        for o in self.out_dma_ops:
            final_waits[o.dsem] = max(final_waits.get(o.dsem, 0), o.dval)

        def skip(d, o):
            return (not d.is_dma) and d.eng == PE and o.eng == PE and not o.is_dma

        pos = {}
        for e in ENGS:
            for i_, o in enumerate(order[e]):
                pos[o.idx] = i_
        need = {}
        for e in ENGS:
            seenpos = {}
            for o in order[e]:
                last = {}
                for d in o.deps:
                    if d.is_dma or skip(d, o):
                        continue
                    m = last.get(d.eng)
                    if m is None or pos[d.idx] > pos[m.idx]:
                        last[d.eng] = d
                lst = []
                for pe_, m in last.items():
                    if seenpos.get(pe_, -1) >= pos[m.idx]:
                        continue
                    seenpos[pe_] = pos[m.idx]
                    m.inc = True
                    lst.append(m)
                need[o.idx] = lst
        cnt = {e: 0 for e in ENGS}
        for e in ENGS:
            for o in order[e]:
                if not o.is_dma and o.inc:
                    cnt[e] += 1
                    o.cval = cnt[e]
        seen = {e: {} for e in ENGS}
        per_eng = {e: [] for e in ENGS}
        for o in [o for e in ENGS for o in order[e]]:
            wl = [(esem[m.eng], m.cval) for m in need[o.idx]]
            waits = {}
            for d in o.deps:
                if d.is_dma:
                    key, val = d.dsem, d.dval
                    if val > waits.get(key, 0):
                        waits[key] = val
            s = seen[o.eng]
            for key, val in waits.items():
                if s.get(key, 0) >= val:
                    continue
                s[key] = val
                wl.append((self.dma_sems[key][0], val))
            per_eng[o.eng].append((o, wl))
        self.n_incs = dict(cnt)
        block = stack.enter_context(nc.Block())
        dma_sems = self.dma_sems

        def make(engname):
            def body(eng):
                for o, wl in per_eng[engname]:
                    for sem, val in wl:
                        eng.wait_ge(sem, val)
                    ins = o.fn(eng)
                    if o.is_dma:
                        ins.then_inc(dma_sems[o.dsem][0], 16)
                    elif o.inc:
                        ins.then_inc(esem[engname], 1)
                if engname == SP:
                    for k, v in final_waits.items():
                        eng.wait_ge(dma_sems[k][0], v)
            return body

        block.tensor(make(PE))
        block.scalar(make(ACT))
        block.vector(make(DVE))
        block.gpsimd(make(POOL))
        block.sync(make(SP))


CF_IDENT, CF_UTRI, CF_ONES = 0, 128, 256
CF_IOTAF, CF_IOTAB = 384, 1408
CF_COS, CF_SIN = 2432, 3456
CF_POS, CF_NEG = 4480, 6528
CF_TAILF, CF_TAILB = 8576, 8578
CF_N = 8580
CB_IDENT, CB_ONES, CB_MF, CB_MB, CB_SELROW, CB_SELBIAS, CB_N = 0, 128, 256, 768, 1280, 3328, 5376


def _host_consts():
    p = np.arange(128)[:, None].astype(np.float64)
    cf = np.zeros((128, CF_N), np.float32)
    j = np.arange(128)[None, :]
    cf[:, CF_IDENT:CF_IDENT + 128] = (p == j)
    cf[:, CF_UTRI:CF_UTRI + 128] = (p <= j)
    cf[:, CF_ONES:CF_ONES + 128] = 1.0
    t = np.arange(1024)[None, :]
    cf[:, CF_IOTAF:CF_IOTAF + 1024] = t + 1
    cf[:, CF_IOTAB:CF_IOTAB + 1024] = 1024 - t
    tt = np.arange(1024)
    t_row = (tt // 64).astype(np.float64)
    t_col = (tt % 64).astype(np.float64)
    inv = 10000.0 ** (-np.arange(32, dtype=np.float64) / 32.0)
    ang = np.concatenate([t_row[:, None] * inv[None, :], t_col[:, None] * inv[None, :]], axis=-1)
    cos = np.cos(ang).T
    sin = np.sin(ang).T
    cf[:, CF_COS:CF_COS + 1024] = np.concatenate([cos, cos], 0)
    cf[:, CF_SIN:CF_SIN + 1024] = np.concatenate([-sin, sin], 0)
    u = np.arange(2048)[None, :]
    delta = u - p - 1024
    cf[:, CF_POS:CF_POS + 2048] = np.maximum(delta, 0)
    cf[:, CF_NEG:CF_NEG + 2048] = np.minimum(delta, 0)
    jj = np.arange(2)[None, :]
    cf[:, CF_TAILF:CF_TAILF + 2] = 255 - 128 * jj - p
    cf[:, CF_TAILB:CF_TAILB + 2] = 128 * jj + p
    cb = np.zeros((128, CB_N), np.float32)
    cb[:, CB_IDENT:CB_IDENT + 128] = (p == j)
    cb[:, CB_ONES:CB_ONES + 128] = 1.0
    mf = np.where(p <= j, 0.0, NEG)
    mb = np.where(p > j, 0.0, NEG)
    cb[:, CB_MF:CB_MF + 512] = np.tile(mf, (1, 4))
    cb[:, CB_MB:CB_MB + 512] = np.tile(mb, (1, 4))
    k = np.arange(128)
    selrow = np.zeros((128, 16, 128), np.float32)
    for hd in range(16):
        selrow[(k < 96) & (k % 32 == hd), hd, :] = 1.0
    cb[:, CB_SELROW:CB_SELROW + 2048] = selrow.reshape(128, 2048)
    selb = np.zeros((128, 2, 8, 128), np.float32)
    for d in range(2):
        for h in range(8):
            selb[(k < 96) & (k % 32 == 16 + d * 8 + h), d, h, :] = 1.0
    cb[:, CB_SELBIAS:CB_SELBIAS + 2048] = selb.reshape(128, 2048)
    return cf, cb


R_LN1G, R_LN1B, R_LN2G, R_LN2B, R_NW, R_CONVB, R_BADA, R_SMALL, R_N = 0, 1024, 2048, 3072, 4096, 4608, 5632, 11776, 11824
C_CVEC, C_BADA, C_CONVW, C_CONVB, C_LN1G, C_LN1B, C_N = 0, 16, 64, 104, 112, 120, 128


def build(debug=()):
    nc = bass.Bass("TRN2", target_bir_lowering=False)
    P = Prog(nc)
    di = lambda name, shape: nc.dram_tensor(name, list(shape), F32, kind="ExternalInput").ap()
    do = lambda name, shape: nc.dram_tensor(name, list(shape), F32, kind="ExternalOutput").ap()
    x_d = di("x", [NT, D])
    sret_d = di("sret0", [128, 1024])
    sssd_d = di("sssd0", [128, 1024])
    w_in_d = di("w_in", [D, INC])
    w_out_d = di("w_out", [D, D])
    w_gate_d = di("w_gate", [D, DFF])
    w_up_d = di("w_up", [D, DFF])
    w_down_d = di("w_down", [DFF, D])
    w_ada_d = di("w_ada", [D, 6 * D])
    rows_d = di("rows", [1, R_N])
    cols_d = di("cols", [128, C_N])
    cf_d = di("cstf", [128, CF_N])
    cb_d = di("cstb", [128, CB_N])
    y_d = do("y", [NT, D])
    oret_d = do("oret", [2, 2, 4, 128, 128])
    ossd_d = do("ossd", [2, 2, 8, 128, 64])

    st = contextlib.ExitStack()
    ARENA = 212800
    arena = nc.alloc_sbuf_tensor("arena", [128, ARENA], U8)
    psum = nc.alloc_psum_tensor("psum", [128, 4096], F32)
    K = 1024

    def V(off, n, dt):
        off = int(off)
        assert off % 4 == 0 and off + n * SZ[dt] <= ARENA, (off, n)
        return arena[:, off:off + n * SZ[dt]].bitcast(dt)

    def PS(b, n=512, off=0):
        return psum[:, b * 512 + off:b * 512 + off + n]

    def pk(b):
        return "ps%d" % b

    o_const, o_scr, o_lnp, o_ring = 0, 5 * K, 15 * K, 31 * K
    SLOT = 8320
    o_A, o_B, o_C, o_D, o_E, o_F, o_G = 64 * K, 88 * K, 112 * K, 124 * K, 136 * K, 148 * K, 173 * K

    ident_f = V(0, 128, F32)
    utri_f = V(512, 128, F32)
    ones_f = V(1024, 128, F32)
    ident_b = V(1536, 128, BF16)
    ones_b = V(1792, 128, BF16)
    m = [2048]

    def small(n, dt=F32):
        a = V(m[0], n, dt)
        m[0] += (n * SZ[dt] + 3) // 4 * 4
        assert m[0] <= 5 * K, m[0]
        return a

    cols = small(C_N)
    smalls = small(48)
    modT = small(96)
    scv = small(16, BF16)
    lg = small(8)
    nlg = small(8)
    tmp8 = small(8)
    nA = small(16)
    tailpos = small(4)
    tailw = small(16)
    sc1p = small(16)
    sc2p = small(16)
    lscb = small(1)
    fencew = small(1)
    dtraw = small(96)
    dskb = small(8)

    def fs(ap):
        n = 1
        for d in ap.shape[1:]:
            n *= int(d)
        return n

    def inps(ap):
        return ap.tensor.name == "psum"

    def mm(out, lhsT, rhs, start, stop, reads, writes):
        n_ = fs(rhs)
        c = ((0.035 + n_ / 2560.0) if n_ >= 256 else (0.03 + n_ / 1400.0)) * (4.0 if rhs.dtype == F32 else 1.0)
        o_ = P.op(PE, lambda e: e.matmul(out, lhsT=lhsT, rhs=rhs, start=start, stop=stop), reads, writes, cost=c)
        o_.aps = ([lhsT, rhs], [out])
        return o_

    def tr(out, in_, reads, writes):
        o_ = P.op(PE, lambda e: e.transpose(out=out, in_=in_, identity=ident_f[:]), list(reads) + ["const"], writes, cost=0.1)
        o_.aps = ([in_, ident_f[:]], [out])
        return o_

    def act(out, in_, func, reads, writes, bias=None, scale=None, accum_out=None):
        kw = {}
        c = 0.2 + fs(in_) / 1400.0
        if fs(in_) <= 8:
            c = 0.6
        if bias is not None:
            kw["bias"] = bias
            c += 0.05
        if scale is not None:
            kw["scale"] = scale
        if accum_out is not None:
            kw["accum_out"] = accum_out
            c += 0.1
        o_ = P.op(ACT, lambda e: e.activation(out=out, in_=in_, func=func, **kw), reads, writes, cost=c)
        o_.aps = ([in_] + [v_ for v_ in (bias, scale) if v_ is not None and not isinstance(v_, float)], [out] + ([accum_out] if accum_out is not None else []))
        return o_

    def vcost(eng, n, f):
        if n <= 8:
            return 0.6
        return (0.1 + n / 490.0) if eng == POOL else (0.09 + n * f / 1060.0)

    def tt(out, in0, in1, op, reads, writes, eng=DVE):
        f = 1.0 if (inps(in0) or inps(in1)) else 2.0
        if in0.dtype == BF16 and in1.dtype == BF16 and out.dtype == BF16 and f == 2.0:
            f = 0.6
        o_ = P.op(eng, lambda e: e.tensor_tensor(out=out, in0=in0, in1=in1, op=op), reads, writes, cost=vcost(eng, fs(out), f))
        o_.aps = ([in0, in1], [out])
        return o_

    def ts(out, in0, s1, op0, reads, writes, s2=None, op1=None, eng=DVE):
        c = vcost(eng, fs(out), 1.0)
        if op1 is None:
            o_ = P.op(eng, lambda e: e.tensor_scalar(out=out, in0=in0, scalar1=s1, scalar2=None, op0=op0), reads, writes, cost=c)
            o_.aps = ([in0] + [v_ for v_ in (s1,) if not isinstance(v_, (float, int))], [out])
            return o_
        o_ = P.op(eng, lambda e: e.tensor_scalar(out=out, in0=in0, scalar1=s1, scalar2=s2, op0=op0, op1=op1), reads, writes, cost=c)
        o_.aps = ([in0] + [v_ for v_ in (s1, s2) if not isinstance(v_, (float, int))], [out])
        return o_

    def stt(out, in0, scalar, in1, op0, op1, reads, writes):
        f = 1.0 if (inps(in0) or inps(in1)) else 2.0
        o_ = P.op(DVE, lambda e: e.scalar_tensor_tensor(out=out, in0=in0, scalar=scalar, in1=in1, op0=op0, op1=op1), reads, writes,
                  cost=vcost(DVE, fs(out), f))
        o_.aps = ([in0, in1] + [v_ for v_ in (scalar,) if not isinstance(v_, (float, int))], [out])
        return o_

    def cp(out, in_, reads, writes, eng=DVE):
        o_ = P.op(eng, lambda e: e.tensor_copy(out=out, in_=in_), reads, writes, cost=vcost(eng, fs(out), 1.0))
        o_.aps = ([in_], [out])
        return o_

    def memset(ap, val, writes, eng=DVE):
        o_ = P.op(eng, lambda e: e.memset(ap, val), [], writes, cost=vcost(eng, fs(ap), 0.5))
        o_.aps = ([], [ap])
        return o_

    def dma(eng, out, in_, key, reads, writes, is_output=False, chain=True):
        nb = 128 * fs(out) * 4
        o_ = P.dma(eng, lambda e: e.dma_start(out=out, in_=in_), key, reads, writes, is_output, cost=2.0 + nb / 150e3, chain=chain)
        o_.aps = ([in_], [out])
        return o_

    P.marks = []

    def fence(regions, name=None):
        o = P.op(DVE, lambda e: e.memset(fencew[:], 0.0), [], list(regions), cost=0.1)
        P.marks.append((name or ("f%d" % len(P.marks)), o))

    def bcast_row(off, n):
        return rows_d[0:1, off:off + n].partition_broadcast(128).rearrange("p a n -> p (a n)")

    dbg_n = [0]

    def tap(name, ap, shape, reads):
        if name in debug:
            dd = do("dbg_" + name, shape)
            dbg_n[0] += 1
            dma(POOL, dd, ap, "dbg%d" % dbg_n[0], reads, [], True)

    ring_n = [0]

    def ring_load(parts, slot=None, after=()):
        if slot is None:
            s = ring_n[0] % 4
            ring_n[0] += 1
        else:
            s = slot
        base = o_ring + s * SLOT
        for ip, (src, dst_off_elems, nk, ncol) in enumerate(parts):
            dst = V(base + dst_off_elems * 2, nk * ncol, BF16).rearrange("p (k n) -> p k n", k=nk)
            dma(POOL, dst, src, "ring%d" % s, list(after), [("ring", s, ip)], chain=(ip == 0))
        return s, base

    def rstd_from(dst, src_ps_or_sb, scale, reads, writes):
        act(dst, src_ps_or_sb, AF.Ln, list(reads) + ["epsb"], writes, bias=epsb[:, 0:1], scale=scale)
        act(dst, dst, AF.Exp, writes, writes, scale=-0.5)

    epsb = small(1)
    dgs = [V(o_scr + 8 * K + i * 512, 128, F32) for i in range(2)]
    memset(epsb[:], EPS, ["epsb"])

    dma(SP, V(0, 384, F32), cf_d[:, 0:384], "c0", [], ["const"])
    dma(POOL, V(1536, 256, BF16), cb_d[:, 0:256], "c1", [], ["constb"])
    dma(SP, cols[:], cols_d[:, :], "c2", [], ["cols"])
    dma(SP, smalls[:], bcast_row(R_SMALL, 48), "c3", [], ["smalls"])
    dma(SP, tailpos[:], cf_d[:, CF_TAILF:CF_TAILF + 4], "c7", [], ["tailpos"])
    cos_t = V(o_lnp, 1024, F32)
    sin_t = V(o_lnp + 4 * K, 1024, F32)
    scb = V(o_scr + 4 * K, 2048, BF16)
    act(scv[:], cols[:, C_CVEC:C_CVEC + 16], AF.Silu, ["cols"], ["scv"])
    scv3 = scv[:].rearrange("p (k c) -> p k c", c=2)

    def make_scb():
        cp(scb[:].rearrange("p (a m) -> p a m", m=128), scv[:].unsqueeze(2).to_broadcast([128, 16, 128]),
           ["scv"], [("SCR", "scb")])
    scb4 = scb[:].rearrange("p (k c m) -> p k c m", k=8, c=2)

    def mod_fm(ft0, col0, nslots):
        for sl in range(nslots):
            c0 = col0 + sl * 512
            s, base = ring_load([(w_ada_d[:, c0:c0 + 512].rearrange("(k p) n -> p k n", p=128), 0, 8, 512)])
            wv = V(base, 8 * 512, BF16).rearrange("p (k n) -> p k n", k=8)
            for j in range(4):
                ft = ft0 + sl * 4 + j
                for kt in range(8):
                    mm(PS(7, 2, ft * 2), wv[:, kt, j * 128:(j + 1) * 128], scv3[:, kt, :], kt == 0, kt == 7,
                       [("ring", s), "scv"], ["ps7"])
        n = nslots * 4
        tt(modT[:, ft0 * 2:(ft0 + n) * 2].rearrange("p (f c) -> p f c", c=2),
           PS(7, n * 2, ft0 * 2).rearrange("p (f c) -> p f c", c=2),
           cols[:, C_BADA + ft0:C_BADA + ft0 + n].unsqueeze(2).to_broadcast([128, n, 2]), ALU.add,
           ["ps7", "cols"], [("modT", ft0)])

    def gate_load(col0):
        slots = []
        for sl in range(2):
            c0 = col0 + sl * 512
            s, base = ring_load([(w_ada_d[:, c0:c0 + 512].rearrange("(k p) n -> p k n", p=128), 0, 8, 512)])
            slots.append((s, V(base, 8 * 512, BF16).rearrange("p (k n) -> p k n", k=8)))
        return slots

    def gate_compute(slots, dst_off, rowoff):
        make_scb()
        btmp = V(o_scr, 1024, F32)
        dma(SP, btmp[:], bcast_row(rowoff, 1024), "gb", [], [("SCR", "btmp")])
        for sl in range(2):
            s, wv = slots[sl]
            for c in range(2):
                b = 5 + c
                for kt in range(8):
                    mm(PS(b), scb4[:, kt, c, :], wv[:, kt, :], kt == 0, kt == 7, [("ring", s), ("SCR", "scb")], [pk(b)])
                tt(V(dst_off + (c * 1024 + sl * 512) * 4, 512, F32)[:], PS(b), btmp[:, sl * 512:(sl + 1) * 512], ALU.add,
                   [pk(b), ("SCR", "btmp")], [("LNP", "gate", c, sl)])

    def gate_bcast(dst_off, col0, rowoff):
        gate_compute(gate_load(col0), dst_off, rowoff)

    xTf = V(o_F, 8 * NT, F32).rearrange("p (k t) -> p k t", k=8)
    xst = [V(o_B + i * 4 * K, 1024, F32) for i in range(8)]
    x_t = x_d.rearrange("(n p) f -> n p f", p=128)

    def x_load(ck, after=()):
        h_ = ck % 2
        dst = V(o_B + h_ * 16 * K, 4096, F32).rearrange("p (j f) -> p j f", j=4)
        src = x_d[ck * 512:(ck + 1) * 512, :].rearrange("(j p) f -> p j f", p=128)
        dma(SP, dst, src, "xst%d" % h_, list(after), [("B", "xst", h_ * 4 + j) for j in range(4)])
    x_load(0)
    mod_fm(0, 0, 4)
    ts(sc1p[:], modT[:, 16:32], 1.0, ALU.add, [("modT", 0)], ["sc1p"])
    wada_done = [("ring", 0), ("ring", 1), ("ring", 2), ("ring", 3)]
    x_load(1, after=wada_done[:2])
    dma(SP, V(o_lnp, 2048, F32), cf_d[:, CF_COS:CF_COS + 2048], "c4", [], [("LNP", "rope")])
    n_ = 0
    for ck in range(3):
        if ck == 1:
            x_load(2, after=wada_done)
        for kt in range(8):
            b = n_ % 4
            n_ += 1
            for j in range(4):
                jj = (ck % 2) * 4 + j
                tr(PS(b, 128, j * 128), xst[jj][:, kt * 128:(kt + 1) * 128], [("B", "xst", jj)], [pk(b)])
            if n_ % 2 == 0:
                act(xTf[:, kt, ck * 512:(ck + 1) * 512], PS(b), AF.Copy, [pk(b)], [("F", "xTf", kt, ck)])
            else:
                cp(xTf[:, kt, ck * 512:(ck + 1) * 512], PS(b), [pk(b)], [("F", "xTf", kt, ck)])
    sh1 = modT[:, 0:16].rearrange("p (k c) -> p k c", c=2)
    sc1p3 = sc1p[:].rearrange("p (k c) -> p k c", c=2)
    hT = V(o_A, 8 * NT, BF16).rearrange("p (k t) -> p k t", k=8)
    n_ = 0
    for ck in range(3):
        cond = 0 if ck < 2 else 1
        for kt in range(8):
            if n_ % 2 == 0:
                act(hT[:, kt, ck * 512:(ck + 1) * 512], xTf[:, kt, ck * 512:(ck + 1) * 512], AF.Identity,
                    [("F", "xTf", kt, ck), "sc1p", ("modT", 0)], [("A", ck, "hT", kt)],
                    bias=sh1[:, kt, cond:cond + 1], scale=sc1p3[:, kt, cond:cond + 1])
            else:
                ts(hT[:, kt, ck * 512:(ck + 1) * 512], xTf[:, kt, ck * 512:(ck + 1) * 512], sc1p3[:, kt, cond:cond + 1], ALU.mult,
                   [("F", "xTf", kt, ck), "sc1p", ("modT", 0)], [("A", ck, "hT", kt)], s2=sh1[:, kt, cond:cond + 1], op1=ALU.add)
            n_ += 1
    fence(["B", "C", "F", "G"])
    def w_in_chunk(c0, ncol, after=()):
        s, base = ring_load([(w_in_d[:, c0:c0 + ncol].rearrange("(k p) n -> p k n", p=128), 0, 8, ncol)], after=after)
        return s, V(base, 8 * ncol, BF16).rearrange("p (k n) -> p k n", k=8)

    pre_in = {1024: w_in_chunk(1024, 512)}
    for c0 in (0, 512):
        pre_in[c0] = w_in_chunk(c0, 512, after=[("A", 0, "hT", 7)])

    def w_in_get(c0, ncol):
        return pre_in.pop(c0) if c0 in pre_in else w_in_chunk(c0, ncol)

    delta = V(o_F, 2048, F32)
    E1 = V(o_F + 8 * K, 2048, F32)
    E2 = V(o_F + 16 * K, 2048, F32)
    P.op(POOL, lambda e: e.iota(delta[:], pattern=[[1, 2048]], base=-1024, channel_multiplier=-1,
                                allow_small_or_imprecise_dtypes=True), [], [("F", "delta")], cost=4.5)
    Th = V(o_G, 4 * 2048, BF16).rearrange("p (h u) -> p h u", h=4)
    iotaFB = V(o_G + 16 * K, 2048, F32)
    rowtab = V(o_G + 24 * K, 1024, BF16)
    mscr = [V(o_G + 26 * K, 512, F32), V(o_G + 32 * K, 512, F32)]
    ktm = V(o_G + 28 * K, 2048, BF16).rearrange("p (j d f) -> p j d f", j=2, d=2)
    P.op(POOL, lambda e: e.iota(iotaFB[:, 0:1024], pattern=[[1, 1024]], base=1, channel_multiplier=0,
                                allow_small_or_imprecise_dtypes=True), [], [("G", "iota", 0)], cost=2.3)
    P.op(POOL, lambda e: e.iota(iotaFB[:, 1024:2048], pattern=[[-1, 1024]], base=1024, channel_multiplier=0,
                                allow_small_or_imprecise_dtypes=True), [], [("G", "iota", 1)], cost=2.3)

    u8 = small(8)
    l8 = small(8)
    act(tmp8[:], smalls[:, 40:48], AF.Exp, ["smalls"], ["tmp8"], scale=-1.0)
    ts(u8[:], tmp8[:], 1.0, ALU.add, ["tmp8"], ["u8"])
    act(l8[:], u8[:], AF.Ln, ["u8"], ["l8"])
    ts(u8[:], u8[:], -1.0, ALU.add, ["u8"], ["u8"], s2=1e-30, op1=ALU.max)
    P.op(DVE, lambda e: e.reciprocal(out=u8[:], in_=u8[:]), ["u8"], ["u8"], cost=0.2)
    tt(l8[:], l8[:], u8[:], ALU.mult, ["l8", "u8"], ["l8"])
    tt(tmp8[:], tmp8[:], l8[:], ALU.mult, ["tmp8", "l8"], ["tmp8"])
    ts(lg[:], tmp8[:], -1.0, ALU.mult, ["tmp8"], ["lg"])
    cp(nlg[:], tmp8[:], ["tmp8"], ["nlg"])
    act(nA[:], smalls[:, 16:32], AF.Exp, ["smalls"], ["nA"])
    ts(nA[:], nA[:], -1.0, ALU.mult, ["nA"], ["nA"])
    cp(dskb[:], smalls[:, 32:40], ["smalls"], ["dskb"])
    memset(lscb[:], -0.5 * math.log(128.0), ["lscb"])
    for h in range(4):
        act(E1[:], delta[:], AF.Exp, [("F", "delta"), "lg", "lscb"], [("F", "E1")], bias=lscb[:, 0:1], scale=lg[:, h:h + 1])
        act(E2[:], delta[:], AF.Exp, [("F", "delta"), "nlg", "lscb"], [("F", "E2")], bias=lscb[:, 0:1], scale=nlg[:, 4 + h:5 + h])
        for q in range(4):
            qs = slice(q * 512, (q + 1) * 512)
            tt(Th[:, h, qs], E1[:, qs], E2[:, qs], ALU.min, [("F", "E1"), ("F", "E2")], [("G", "Th", h, q)])
    for d in range(2):
        for j in range(2):
            ts(tailw[:, d * 8 + j * 4:d * 8 + j * 4 + 4], lg[:, d * 4:d * 4 + 4], tailpos[:, d * 2 + j:d * 2 + j + 1],
               ALU.mult, ["lg", "tailpos"], [("tailw", d, j)])
    act(tailw[:], tailw[:], AF.Exp, ["tailw", "lscb"], ["tailw"], bias=lscb[:, 0:1])


    qT = V(o_B, 4 * NT, BF16).rearrange("p (h t) -> p h t", h=4)
    kT = V(o_B + 12 * K, 4 * NT, BF16).rearrange("p (h t) -> p h t", h=4)
    v_tm = V(o_C, NTT * 512, BF16).rearrange("p (n f) -> p n f", n=NTT)
    gT = V(o_D, 4 * NT, BF16).rearrange("p (h t) -> p h t", h=4)
    z_tm = V(o_E, NTT * 512, BF16).rearrange("p (n f) -> p n f", n=NTT)
    XW = 1548
    XOFF = [2, 1030, 1290]
    xbc = V(o_F, 8 * XW, BF16).rearrange("p (c t) -> p c t", c=8)
    rope_tmpB = [[V(o_scr, 512, F32), V(o_scr + 2 * K, 512, F32)], [V(o_lnp + 8 * K, 512, F32), V(o_lnp + 10 * K, 512, F32)]]
    nrope = [0]
    stS = V(o_scr + 4 * K, 1024, F32)

    def tok2xcol(ck):
        if ck < 2:
            return [(0, 512, XOFF[0] + ck * 512)]
        return [(0, 256, XOFF[1]), (256, 256, XOFF[2])]

    def fm_proj(s, wv, j, ck, bank):
        for kt in range(8):
            mm(PS(bank), wv[:, kt, j * 128:(j + 1) * 128], hT[:, kt, ck * 512:(ck + 1) * 512], kt == 0, kt == 7,
               [("ring", s), ("A", ck, "hT", kt)], [pk(bank)])

    def tm_proj(s, wv, t, bank, ncol=512, c0=0):
        for kt in range(8):
            mm(PS(bank, ncol), hT[:, kt, t * 128:(t + 1) * 128], wv[:, kt, c0:c0 + ncol], kt == 0, kt == 7,
               [("ring", s), ("A", t // 4, "hT", kt)], [pk(bank)])

    bk = [0]

    def nb4():
        b = bk[0] % 4
        bk[0] += 1
        return b

    s, wv = w_in_get(1024, 512)
    for t in range(NTT):
        b = nb4()
        tm_proj(s, wv, t, b)
        if t % 2 == 0:
            act(v_tm[:, t, :], PS(b), AF.Copy, [pk(b)], [("C", "v", t)])
        else:
            cp(v_tm[:, t, :], PS(b), [pk(b)], [("C", "v", t)])
    for which, dstT in ((0, qT), (1, kT)):
        s, wv = w_in_get(which * 512, 512)
        for j in range(4):
            for ck in range(3):
                b = nb4()
                fm_proj(s, wv, j, ck, b)
                dst = dstT[:, j, ck * 512:(ck + 1) * 512]
                wr = [("B", "qk", which, j, ck)]
                if ck == 2:
                    act(dst, PS(b), AF.Copy, [pk(b)], wr)
                else:
                    rq = nrope[0] % 2
                    nrope[0] += 1
                    tA, tB = rope_tmpB[rq]
                    rn = "SCR" if rq == 0 else "LNP"
                    tsl = slice(ck * 512, (ck + 1) * 512)
                    tt(tA[:], PS(b), cos_t[:, tsl], ALU.mult, [pk(b), ("LNP", "rope")], [(rn, "ropeA")])
                    tt(tB[0:64, :], PS(b)[64:128, :], sin_t[0:64, tsl], ALU.mult, [pk(b), ("LNP", "rope")], [(rn, "ropeB", 0)])
                    tt(tB[64:128, :], PS(b)[0:64, :], sin_t[64:128, tsl], ALU.mult, [pk(b), ("LNP", "rope")], [(rn, "ropeB", 1)])
                    tt(dst, tA[:], tB[:], ALU.add, [(rn, "ropeA"), (rn, "ropeB", 0), (rn, "ropeB", 1)], wr, eng=POOL)
        if which == 1:
            for sq in range(2):
                for jb in range(2):
                    t = 8 + sq * 2 + jb
                    b = nb4()
                    tm_proj(s, wv, t, b)
                    for d in range(2):
                        for h in range(4):
                            c = d * 8 + jb * 4 + h
                            act(ktm[:, jb, d, h * 128:(h + 1) * 128], PS(b, 128, h * 128), AF.Copy, [pk(b), "tailw"],
                                [("G", "ktm", jb, d, h)], scale=tailw[:, c:c + 1])
                for d in range(2):
                    for h in range(4):
                        for jb in range(2):
                            mm(PS(4 + d, 128, h * 128), ktm[:, jb, d, h * 128:(h + 1) * 128],
                               v_tm[:, 8 + sq * 2 + jb, h * 128:(h + 1) * 128], jb == 0, jb == 1,
                               [("G", "ktm", jb, d, h), ("C", "v", 8 + sq * 2 + jb)], [pk(4 + d)])
                    cp(stS[:, d * 512:(d + 1) * 512], PS(4 + d), [pk(4 + d)], [("SCR", "stS", d)])
                dma(SP, oret_d[sq].rearrange("d h p e -> p d h e"), stS[:].rearrange("p (d h e) -> p d h e", d=2, h=4),
                    "oret", [("SCR", "stS", 0), ("SCR", "stS", 1)], [], True)
    tap("qT", qT, [128, 4, NT], ["B"])
    tap("kT", kT, [128, 4, NT], ["B"])
    tap("v", v_tm, [128, NTT, 512], ["C"])
    s, wv = w_in_chunk(1536, 512)
    for j in range(4):
        for ck in range(3):
            b = nb4()
            fm_proj(s, wv, j, ck, b)
            act(gT[:, j, ck * 512:(ck + 1) * 512], PS(b), AF.Silu, [pk(b)], [("D", "g", j, ck)])
    mod_fm(24, 3072, 2)
    s, wv = w_in_chunk(2048, 512)
    for t in range(NTT):
        b = nb4()
        tm_proj(s, wv, t, b)
        act(z_tm[:, t, :], PS(b), AF.Silu, [pk(b)], [("E", "z", t)])
    mod_fm(32, 4096, 2)
    ts(sc2p[:], modT[:, 64:80], 1.0, ALU.add, [("modT", 32)], ["sc2p"])
    fence(["F"])
    for (c0_, c1_) in ((0, 2), (1026, 1030), (1286, 1290), (1546, 1548)):
        memset(xbc[:, :, c0_:c1_], 0.0, [("F", "xbc", "pad", c0_)], eng=POOL)
    for half in range(2):
        ncol = 512 if half == 0 else 520
        s, wv = w_in_chunk(2560 + half * 512, ncol)
        for j in range(4):
            ct = half * 4 + j
            for ck in range(3):
                b = nb4()
                fm_proj(s, wv, j, ck, b)
                for (o, n, xc) in tok2xcol(ck):
                    if (j + ck) % 2 == 0:
                        act(xbc[:, ct, xc:xc + n], PS(b, n, o), AF.Copy, [pk(b)], [("F", "xbc", ct, xc)])
                    else:
                        cp(xbc[:, ct, xc:xc + n], PS(b, n, o), [pk(b)], [("F", "xbc", ct, xc)])
        if half == 0:
            mod_fm(16, 2048, 2)
        if half == 1:
            for t in range(NTT):
                for kt in range(8):
                    mm(PS(7, 8, t * 8), hT[:, kt, t * 128:(t + 1) * 128], wv[:, kt, 512:520], kt == 0, kt == 7,
                       [("ring", s), ("A", t // 4, "hT", kt)], ["ps7"])
            cp(dtraw[:], PS(7, 96), ["ps7"], ["dtraw"])
    mod_fm(40, 5120, 2)
    pre_wo = []
    for half in range(2):
        s, base = ring_load([(w_out_d[:, half * 512:(half + 1) * 512].rearrange("(k p) n -> p k n", p=128), 0, 8, 512)], slot=2 + half)
        pre_wo.append((s, V(base, 8 * 512, BF16).rearrange("p (k n) -> p k n", k=8)))

    def ffn_slot(sl):
        c0 = sl * 256
        return ring_load([(w_gate_d[:, c0:c0 + 256].rearrange("(k p) n -> p k n", p=128), 0, 8, 256),
                          (w_up_d[:, c0:c0 + 256].rearrange("(k p) n -> p k n", p=128), 2048, 8, 256)], slot=sl % 2)
    ffn_pre = [ffn_slot(0), ffn_slot(1)]

    def gate_from_mod(ft0, dsts, dkeys):
        n_ = 0
        for c in range(2):
            for half in range(2):
                b = 4 + (n_ % 2)
                n_ += 1
                for k4 in range(4):
                    kt = half * 4 + k4
                    q = kt % 2
                    ts(dgs[q][:], ident_f[:], modT[:, (ft0 + kt) * 2 + c:(ft0 + kt) * 2 + c + 1], ALU.mult,
                       ["const", ("modT", ft0)], [("SCR", "dgs", q)])
                    mm(PS(b, 128, k4 * 128), ones_f[:], dgs[q][:], True, True, ["const", ("SCR", "dgs", q)], [pk(b)])
                cp(dsts[c][:, half * 512:(half + 1) * 512], PS(b), [pk(b)], [dkeys[c] + (half,)])

    fence(["A", "LNP", "SCR", ("G", "ktm")])
    oT = V(o_A, 8 * NT, BF16).rearrange("p (k t) -> p k t", k=8)
    PTb = [V(o_lnp + i * 8 * K, 8 * 512, BF16).rearrange("p (i t) -> p i t", i=8) for i in range(2)]
    rs_fB = [V(o_scr + 4 * K + i * 2 * K, 512, F32) for i in range(2)]
    sq_bB = [V(o_G + 28 * K + i * K, 512, BF16) for i in range(2)]
    qfb = [V(o_G + 30 * K + i * K, 512, BF16) for i in range(2)]
    S0b = V(o_scr + 8 * K, 1024, BF16).rearrange("p (d h e) -> p d h e", d=2, h=4)
    PTp = [V(o_scr + i * K, 512, BF16).rearrange("p (i t) -> p i t", i=2) for i in range(2)]
    rs_p = [V(o_scr + 2 * K + i * K, 256, F32) for i in range(2)]
    sq_p = [V(o_G + 34 * K, 256, BF16)] * 2
    dma(POOL, S0b, sret_d[:, :].rearrange("p (d h e) -> p d h e", d=2, h=4), "s0r", [], [("SCR", "S0b")])
    nmask = [0]
    ucount = {True: 0, False: 0}

    def ret_unit(t0, nb, h, r0, W):
        is_sample = nb == 8
        tok0 = t0 * 128
        q_ = ucount[is_sample] % 2
        ucount[is_sample] += 1
        if is_sample:
            pt, ptkey, ob, msb = PTb[q_], ("LNP", "PT", q_), 2 + q_, 4 + q_
            rsf_, sqb_, rkey, skey = rs_fB[q_], sq_bB[q_], ("SCR", "rs_f", q_), ("G", "ktm", "sq", q_)
        else:
            pt, ptkey, ob, msb = PTp[q_], ("SCR", "PTp", q_), 6, 7
            rsf_, sqb_, rkey, skey = rs_p[q_], sq_p[q_], ("SCR", "rs_p", q_), ("G", "ktm", "sqp")
        for i in range(nb):
            b = i % 2
            mm(PS(b, W), kT[:, h, tok0 + i * 128:tok0 + (i + 1) * 128], qT[:, h, tok0 + r0:tok0 + r0 + W], True, True,
               [("B", "qk")], [pk(b)])
            u0 = r0 - 128 * i + 1024
            if is_sample and i % 3 == 2:
                mq = nmask[0] % 2
                nmask[0] += 1
                act(mscr[mq][:, 0:W], PS(b, W), AF.Copy, [pk(b)], [("G", "mscr", mq)])
                tt(pt[:, i, 0:W], mscr[mq][:, 0:W], Th[:, h, u0:u0 + W], ALU.mult, [("G", "mscr", mq), ("G", "Th", h)],
                   [ptkey + (i,)], eng=POOL)
            else:
                tt(pt[:, i, 0:W], PS(b, W), Th[:, h, u0:u0 + W], ALU.mult, [pk(b), ("G", "Th", h)], [ptkey + (i,)])
        if is_sample:
            for d in range(2):
                act(rowtab[:, 0:W], iotaFB[:, d * 1024 + r0:d * 1024 + r0 + W], AF.Exp, [("G", "iota"), "lg"],
                    [("G", "rowtab")], scale=lg[:, d * 4 + h:d * 4 + h + 1])
                tt(qfb[d][:, 0:W], qT[:, h, tok0 + r0:tok0 + r0 + W], rowtab[:, 0:W], ALU.mult,
                   [("B", "qk"), ("G", "rowtab")], [("G", "ktm", "qf", d)])
        nmm = nb + (2 if is_sample else 0)
        for i in range(nb):
            mm(PS(ob, W), v_tm[:, t0 + i, h * 128:(h + 1) * 128], pt[:, i, 0:W], i == 0, i == nmm - 1,
               [("C", "v", t0 + i), ptkey + (i,)], [pk(ob)])
        if is_sample:
            for d in range(2):
                mm(PS(ob, W), S0b[:, d, h, :], qfb[d][:, 0:W], False, d == 1,
                   [("SCR", "S0b"), ("G", "ktm", "qf", d)], [pk(ob)])
        act(sqb_[:, 0:W], PS(ob, W), AF.Square, [pk(ob)], [skey])
        mm(PS(msb, W), ones_b[:], sqb_[:, 0:W], True, True, [skey, "constb"], [pk(msb)])
        rstd_from(rsf_[:, 0:W], PS(msb, W), 1.0 / 128.0, [pk(msb)], [rkey])
        tt(rsf_[:, 0:W], PS(ob, W), rsf_[:, 0:W], ALU.mult, [pk(ob), rkey], [rkey])
        tt(oT[:, h, tok0 + r0:tok0 + r0 + W], rsf_[:, 0:W], gT[:, h, tok0 + r0:tok0 + r0 + W], ALU.mult,
           [rkey, ("D", "g")], [("A", (tok0 + r0) // 512, "oT", h, t0, r0)], eng=POOL)

    s_units = [(0, 8, h, r0, 512) for h in range(4) for r0 in (0, 512)]
    p_units = [(t0, 2, h, 0, 256) for t0 in (8, 10) for h in range(4)]
    for su, pu in zip(s_units, p_units):
        ret_unit(*su)
        ret_unit(*pu)
    tap("oTr", oT[:, 0:4, :], [128, 4, NT], ["A"])

    fence(["B", "C", "D", "G", "SCR", "LNP"])
    cstS = V(o_lnp, 5120, BF16)
    dma(POOL, cstS[:], cb_d[:, CB_MF:CB_MF + 5120], "cstS", [], [("LNP", "cstS")])
    xs_tm = V(o_B, NTT * 512, BF16).rearrange("p (n f) -> p n f", n=NTT)
    B_tm = V(o_B + 12 * K, NTT * 256, BF16).rearrange("p (n f) -> p n f", n=NTT)
    BT = V(o_B + 18 * K, 2 * NT, BF16).rearrange("p (g t) -> p g t", g=2)
    CT = V(o_C, 2 * NT, BF16).rearrange("p (g t) -> p g t", g=2)
    diagw = V(o_G, 8 * 5 * 128, BF16).rearrange("p (c j m) -> p c j m", c=8, j=5)
    xsT = V(o_G + 12 * K, 4 * NT, BF16).rearrange("p (c t) -> p c t", c=4)
    nd_ = 0
    for ct in range(8):
        for j in range(5):
            sc_ = cols[:, C_CONVW + ct * 5 + j:C_CONVW + ct * 5 + j + 1]
            if nd_ % 3 == 0:
                ts(diagw[:, ct, j, :], ident_f[:], sc_, ALU.mult, ["const", "cols"], [("G", "diagw", ct, j)])
            elif nd_ % 3 == 1:
                act(diagw[:, ct, j, :], ident_f[:], AF.Copy, ["const", "cols"], [("G", "diagw", ct, j)], scale=sc_)
            else:
                ts(diagw[:, ct, j, :], ident_f[:], sc_, ALU.mult, ["const", "cols"], [("G", "diagw", ct, j)], eng=POOL)
            nd_ += 1
    nbk = 0
    for ct in range(8):
        for ck in range(3):
            for (o, n, xc) in tok2xcol(ck):
                bank = nbk % 4
                nbk += 1
                for j in range(5):
                    mm(PS(bank, n), diagw[:, ct, j, :], xbc[:, ct, xc + j - 2:xc + j - 2 + n], j == 0, j == 4,
                       [("F", "xbc"), ("G", "diagw", ct)], [pk(bank)])
                if ct < 4:
                    dstT, g, key = xsT, ct, ("G", "xsT", ct, ck, o)
                elif ct < 6:
                    dstT, g, key = BT, ct - 4, ("B", "BT", ct - 4, ck, o)
                else:
                    dstT, g, key = CT, ct - 6, ("C", "CT", ct - 6, ck, o)
                act(dstT[:, g, ck * 512 + o:ck * 512 + o + n], PS(bank, n), AF.Silu, [pk(bank), "cols"], [key],
                    bias=cols[:, C_CONVB + ct:C_CONVB + ct + 1])
    for t in range(NTT):
        bx = 4 + 2 * (t % 2)
        for c in range(4):
            mm(PS(bx, 128, c * 128), xsT[:, c, t * 128:(t + 1) * 128], ident_b[:], True, True,
               [("G", "xsT", c), "constb"], [pk(bx)])
        for c in range(2):
            mm(PS(bx + 1, 128, c * 128), BT[:, c, t * 128:(t + 1) * 128], ident_b[:], True, True,
               [("B", "BT", c), "constb"], [pk(bx + 1)])
        cp(xs_tm[:, t, :], PS(bx), [pk(bx)], [("B", "xs", t)])
        act(B_tm[:, t, :], PS(bx + 1, 256), AF.Copy, [pk(bx + 1)], [("B", "Btm", t)])
    tap("xsb", V(o_B, NTT * 768, BF16).rearrange("p (n f) -> p n f", n=NTT), [128, NTT, 768], ["B"]) if False else None
    tap("BCT", V(o_B + 18 * K, 4 * NT, BF16).rearrange("p (g t) -> p g t", g=4), [128, 4, NT], ["B", "C"])

    rsT = V(o_scr + 7 * K, NT, BF16)
    pool_off = [o_G + 24 * K]

    def pl(n, dt=F32):
        a = V(pool_off[0], n, dt)
        pool_off[0] += n * SZ[dt]
        assert pool_off[0] <= o_G + 35 * K - 1024
        return a
    X = pl(192).rearrange("p (b d) -> p b d", b=NTT)
    AX = pl(192).rearrange("p (b d) -> p b d", b=NTT)
    DT = pl(192).rearrange("p (b d) -> p b d", b=NTT)
    LA = pl(192).rearrange("p (b d) -> p b d", b=NTT)
    LNDT = pl(192).rearrange("p (b d) -> p b d", b=NTT)
    CUM = pl(192).rearrange("p (b d) -> p b d", b=NTT)
    TOT = pl(192).rearrange("p (b d) -> p b d", b=NTT)
    RSRC = pl(384).rearrange("p (b d) -> p b d", b=NTT)
    EXPO = V(o_scr + 4 * K, 576, F32).rearrange("p (b d) -> p b d", b=NTT)
    EXPIN = V(o_D, 576, F32).rearrange("p (b d) -> p b d", b=NTT)
    SPL3 = V(o_D + 2304, 1152, F32).rearrange("p (b d) -> p b d", b=NTT)
    R1 = V(o_D + 2304 + 4608, 384, F32).rearrange("p (b d) -> p b d", b=NTT)
    H1B = V(o_D + 2304 + 4608 + 1536, 384, BF16).rearrange("p (b d) -> p b d", b=NTT)
    PF = [("G", "pool")]
    PD = [("D", "pool")]
    dtr3 = dtraw[:].rearrange("p (b h) -> p b h", b=NTT)
    X4 = X.rearrange("p b (d h) -> p b d h", d=2)
    tt(X4, dtr3.unsqueeze(2).to_broadcast([128, NTT, 2, 8]),
       smalls[:, 0:16].rearrange("p (d h) -> p d h", d=2).unsqueeze(1).to_broadcast([128, NTT, 2, 8]), ALU.add,
       ["dtraw", "smalls"], PF)
    UU = pl(192).rearrange("p (b d) -> p b d", b=NTT)
    LL = pl(192).rearrange("p (b d) -> p b d", b=NTT)
    act(AX, X, AF.Abs, PF, PF)
    act(AX, AX, AF.Exp, PF, PF, scale=-1.0)
    ts(UU, AX, 1.0, ALU.add, PF, PF)
    act(LL, UU, AF.Ln, PF, PF)
    ts(UU, UU, -1.0, ALU.add, PF, PF, s2=1e-30, op1=ALU.max)
    P.op(DVE, lambda e: e.reciprocal(out=UU, in_=UU), PF, PF, cost=0.3)
    tt(LL, LL, UU, ALU.mult, PF, PF)
    tt(AX, AX, LL, ALU.mult, PF, PF)
    ts(X, X, 0.0, ALU.max, PF, PF)
    tt(DT, X, AX, ALU.add, PF, PF)
    ts(DT, DT, 1e-30, ALU.max, PF, PF)
    tt(LA, DT, nA[:].unsqueeze(1).to_broadcast([128, NTT, 16]), ALU.mult, PF + ["nA"], PF)
    act(LNDT, DT, AF.Ln, PF, PF)
    for b in range(NTT):
        mm(PS(0, 16, b * 16), utri_f[:], LA[:, b, :], True, True, PF + ["const"], ["ps0"])
        mm(PS(1, 16, b * 16), ones_f[:], LA[:, b, :], True, True, PF + ["const"], ["ps1"])
    cp(CUM, PS(0, 192).rearrange("p (b d) -> p b d", b=NTT), ["ps0"], PF)
    cp(TOT, PS(1, 192).rearrange("p (b d) -> p b d", b=NTT), ["ps1"], PF)
    cp(RSRC[:, :, 0:8], CUM[:, :, 0:8], PF, PF)
    tt(RSRC[:, :, 8:16], LA[:, :, 8:16], CUM[:, :, 8:16], ALU.subtract, PF, PF)
    tt(RSRC[:, :, 16:24], LNDT[:, :, 0:8], CUM[:, :, 0:8], ALU.subtract, PF, PF)
    tt(RSRC[:, :, 24:32], LNDT[:, :, 8:16], RSRC[:, :, 8:16], ALU.subtract, PF, PF)
    cp(EXPIN[:, :, 0:8], RSRC[:, :, 0:8], PF, PD)
    tt(EXPIN[:, :, 8:16], TOT[:, :, 8:16], RSRC[:, :, 8:16], ALU.add, PF, PD)
    tt(EXPIN[:, :, 16:24], TOT[:, :, 0:8], RSRC[:, :, 16:24], ALU.add, PF, PD)
    cp(EXPIN[:, :, 24:32], RSRC[:, :, 24:32], PF, PD)
    cp(EXPIN[:, :, 32:48], TOT, PF, PD)
    act(EXPO, EXPIN, AF.Exp, PD, [("SCR", "expo")])
    cp(H1B, RSRC, PF, PD)
    cp(SPL3[:, :, 0:32], H1B, PD, PD)
    tt(R1, RSRC, SPL3[:, :, 0:32], ALU.subtract, PF + PD, PD)
    cp(H1B, R1, PD, PD)
    cp(SPL3[:, :, 32:64], H1B, PD, PD)
    tt(R1, R1, SPL3[:, :, 32:64], ALU.subtract, PD, PD)
    cp(H1B, R1, PD, PD)
    cp(SPL3[:, :, 64:96], H1B, PD, PD)
    for ck in range(3):
        for j in range(4):
            tr(PS(2 + ck % 2, 128, j * 128)[0:96, :], SPL3[:, ck * 4 + j, :], PD, [pk(2 + ck % 2)])
        cp(rsT[0:96, ck * 512:(ck + 1) * 512], PS(2 + ck % 2)[0:96, :], [pk(2 + ck % 2)], [("SCR", "rsT", ck)])
    fence(["D", "F", "G", "SCR"])
    MF4, MB4 = cstS[:, 0:512], cstS[:, 512:1024]
    selrow = cstS[:, 1024:3072].rearrange("p (a m) -> p a m", a=16)
    selbias = cstS[:, 3072:5120].rearrange("p (d n) -> p d n", d=2)
    dI = V(o_G + 30 * K, 1024, BF16).rearrange("p (h m) -> p h m", h=8)
    for h in range(8):
        ts(dI[:, h, :], ident_f[:], dskb[:, h:h + 1], ALU.mult, ["const", "dskb"], [("G", "dI")])
    nwb = V(o_G + 32 * K, 512, F32)
    dma(SP, nwb[:], bcast_row(R_NW, 512), "nwb", [], [("G", "nwb")])
    WfB = [V(o_G + i * 11 * K, 1024, BF16) for i in range(2)]
    WbB = [V(o_G + i * 11 * K + 2 * K, 1024, BF16) for i in range(2)]
    PmB = [V(o_G + i * 11 * K + 4 * K, 1024, BF16).rearrange("p (h t) -> p h t", h=8) for i in range(2)]
    y1B = [V(o_G + i * 11 * K + 6 * K, 512, F32) for i in range(2)]
    y2B = [V(o_G + i * 11 * K + 8 * K, 512, F32) for i in range(2)]
    jkB = [V(o_G + i * 11 * K + 10 * K, 512, BF16) for i in range(2)]
    xswB = [[V(o_G + 22 * K + (d * 2 + i) * K, 512, BF16) for i in range(2)] for d in range(2)]
    Sst = [V(o_G + 26 * K, 512, F32), V(o_G + 28 * K, 512, F32)]
    ssqB = [small(1) for _ in range(2)]
    rstdB = [small(1) for _ in range(2)]
    fence(["D"], "ssdpre")
    Rall = V(o_D, 8 * 512, BF16).rearrange("p (b f) -> p b f", b=8)
    SallA = V(o_D + 8 * K, 4 * 512, BF16).rearrange("p (b f) -> p b f", b=4)
    SallB = V(o_C + 6 * K, 4 * 512, BF16).rearrange("p (b f) -> p b f", b=4)

    def Sall(i):
        return SallA[:, i, :] if i < 4 else SallB[:, i - 4, :]

    def Skey(i):
        return ("D", "S", i) if i < 4 else ("C", "S", i)
    psum2 = lambda b0: psum[:, b0 * 512:b0 * 512 + 1024]

    def bc8(ap8):
        return ap8.unsqueeze(2).to_broadcast([128, 8, 64])

    def v8(ap512):
        return ap512.rearrange("p (h e) -> p h e", h=8)

    nxsw = [0, 0]

    def state_update(d, b, bank):
        w = EXPO[:, b, 16 + d * 8:24 + d * 8]
        k = nxsw[d] % 2
        nxsw[d] += 1
        xw = xswB[d][k]
        tt(v8(xw[:]), v8(xs_tm[:, b, :]), bc8(w), ALU.mult, [("B", "xs", b), ("SCR", "expo")], [("G", "xsw", d, k)], eng=POOL)
        for g in range(2):
            mm(PS(bank, 256, g * 256), B_tm[:, b, g * 128:(g + 1) * 128], xw[:, g * 256:(g + 1) * 256], True, True,
               [("B", "Btm", b), ("G", "xsw", d, k)], [pk(bank)])
        tt(v8(Sst[d][:]), v8(Sst[d][:]), bc8(EXPO[:, b, 32 + d * 8:40 + d * 8]), ALU.mult, [("G", "S", d), ("SCR", "expo")],
           [("G", "S", d)], eng=POOL)
        tt(Sst[d][:], Sst[d][:], PS(bank), ALU.add, [("G", "S", d), pk(bank)], [("G", "S", d)])

    RallS = V(o_scr, 2 * 512, BF16).rearrange("p (b f) -> p b f", b=2)
    SallS = V(o_scr + 2 * K, 2 * 512, BF16).rearrange("p (b f) -> p b f", b=2)

    def Sv(big, i):
        return (Sall(i), Skey(i)) if big else (SallS[:, i, :], ("SCR", "S", i))

    def Rv(big, i):
        return (Rall[:, i, :], ("D", "R", i)) if big else (RallS[:, i, :], ("SCR", "R", i))

    def chain_steps(si):
        t0, nb = SEQS[si]
        big = nb == 8
        if big:
            dma(SP, Sst[0][:], sssd_d[:, 0:512], "s0s0", [], [("G", "S", 0)])
            dma(SP, Sst[1][:], sssd_d[:, 512:1024], "s0s1", [], [("G", "S", 1)])
        else:
            memset(Sst[0][:], 0.0, [("G", "S", 0)])
            memset(Sst[1][:], 0.0, [("G", "S", 1)])
        for step in range(nb):
            i_f = step
            i_b = nb - 1 - step
            sv, sk = Sv(big, i_f)
            rv, rk = Rv(big, i_b)
            act(sv, Sst[0][:], AF.Copy, [("G", "S", 0)], [sk])
            act(rv, Sst[1][:], AF.Copy, [("G", "S", 1)], [rk])
            if i_f < nb - 1 or not big:
                state_update(0, t0 + i_f, 6)
            if i_b > 0 or not big:
                state_update(1, t0 + i_b, 7)
            yield
        if not big:
            sq = si - 1
            for d in range(2):
                dma(SP, ossd_d[sq, d].rearrange("h n e -> n h e"), Sst[d][:].rearrange("p (h e) -> p h e", h=8), "ossd%d" % d,
                    [("G", "S", d)], [], True)
        yield

    nblk = [0]
    kbof = {}

    def stageA(si, i):
        t0, nb = SEQS[si]
        b = t0 + i
        tok = b * 128
        kb = nblk[0] % 2
        nblk[0] += 1
        kbof[b] = kb
        Wf, Wb, Pm = WfB[kb], WbB[kb], PmB[kb]
        bk_ = lambda n: ("G", n, kb)
        for g in range(2):
            mm(PS(4, 128, g * 128), BT[:, g, tok:tok + 128], CT[:, g, tok:tok + 128], True, True,
               [("B", "BT", g), ("C", "CT", g)], ["ps4"])
        for d in range(2):
            bank0 = 2 * d
            wr = [pk(bank0), pk(bank0 + 1)]
            Md = MF4 if d == 0 else MB4
            for half in range(2):
                mm(PS(bank0 + half), ident_b[:], Md, True, False, ["constb", ("LNP", "cstS")], wr)
                mm(PS(bank0 + half), rsT[0:96, tok:tok + 128], selbias[0:96, d, half * 512:(half + 1) * 512], False, False,
                   [("SCR", "rsT", b // 4), ("LNP", "cstS")], wr)
            for h in range(8):
                mm(PS(bank0 + h // 4, 128, (h % 4) * 128), selrow[0:96, d * 8 + h, :], rsT[0:96, tok:tok + 128], False, h % 4 == 3,
                   [("SCR", "rsT", b // 4), ("LNP", "cstS")], wr)
            act((Wf if d == 0 else Wb)[:], psum2(bank0), AF.Exp, wr, [bk_("W%d" % d)])
        tt(Wf[:], Wf[:], Wb[:], ALU.add, [bk_("W0"), bk_("W1")], [bk_("W0")])
        tt(Pm.rearrange("p (g q) t -> p g q t", g=2), Wf[:].rearrange("p (g q t) -> p g q t", g=2, q=4),
           PS(4, 256).rearrange("p (g t) -> p g t", g=2).unsqueeze(2).to_broadcast([128, 2, 4, 128]), ALU.mult,
           [bk_("W0"), "ps4"], [bk_("Pm")])

    def stageB(si, i):
        t0, nb = SEQS[si]
        big = nb == 8
        b = t0 + i
        tok = b * 128
        kb = kbof[b]
        Pm, y1, y2, jk = PmB[kb], y1B[kb], y2B[kb], jkB[kb]
        ssq_, rstd_ = ssqB[kb], rstdB[kb]
        bk_ = lambda n: ("G", n, kb)
        sv, sk = Sv(big, i)
        rv, rk = Rv(big, i)
        for h in range(8):
            mm(PS(5, 64, h * 64), Pm[:, h, :], xs_tm[:, b, h * 64:(h + 1) * 64], True, False,
               [bk_("Pm"), ("B", "xs", b)], ["ps5"])
            mm(PS(5, 64, h * 64), dI[:, h, :], xs_tm[:, b, h * 64:(h + 1) * 64], False, True,
               [("G", "dI"), ("B", "xs", b)], ["ps5"])
        for g in range(2):
            mm(PS(6, 256, g * 256), CT[:, g, tok:tok + 128], sv[:, g * 256:(g + 1) * 256], True, True,
               [("C", "CT", g), sk], ["ps6"])
            mm(PS(7, 256, g * 256), CT[:, g, tok:tok + 128], rv[:, g * 256:(g + 1) * 256], True, True,
               [("C", "CT", g), rk], ["ps7"])
        tt(v8(y1[:]), v8(PS(6)), bc8(EXPO[:, b, 0:8]), ALU.mult, ["ps6", ("SCR", "expo")], [bk_("y1")])
        tt(v8(y2[:]), v8(PS(7)), bc8(EXPO[:, b, 8:16]), ALU.mult, ["ps7", ("SCR", "expo")], [bk_("y2")])
        tt(y1[:], y1[:], PS(5), ALU.add, [bk_("y1"), "ps5"], [bk_("y1")])
        tt(y1[:], y1[:], y2[:], ALU.add, [bk_("y1"), bk_("y2")], [bk_("y1")])
        tt(y1[:], y1[:], z_tm[:, b, :], ALU.mult, [bk_("y1"), ("E", "z", b)], [bk_("y1")], eng=POOL)
        act(jk[:], y1[:], AF.Square, [bk_("y1")], [bk_("jk"), ("ssq", kb)], accum_out=ssq_[:, 0:1])
        rstd_from(rstd_[:, 0:1], ssq_[:, 0:1], 1.0 / 512.0, [("ssq", kb)], [("rstd", kb)])
        stt(y2[:], y1[:], rstd_[:, 0:1], nwb[:], ALU.mult, ALU.mult, [bk_("y1"), ("rstd", kb), ("G", "nwb")], [bk_("y2")])

    def stageC(si, i):
        t0, nb = SEQS[si]
        b = t0 + i
        tok = b * 128
        kb = kbof[b]
        y2 = y2B[kb]
        for j in range(4):
            tr(PS(4, 128, j * 128), y2[:, j * 128:(j + 1) * 128], [("G", "y2", kb)], ["ps4"])
        act(oT[:, 4:8, tok:tok + 128], PS(4).rearrange("p (j t) -> p j t", j=4), AF.Copy, ["ps4"], [("A", b // 4, "oT", 4, b)])

    seq_order = [1, 0, 2]
    blocks = [(si, i) for si in seq_order for i in range(SEQS[si][1])]
    first_of = {}
    for n_, (si, i) in enumerate(blocks):
        first_of.setdefault(si, n_)
    for _ in chain_steps(seq_order[0]):
        pass
    pending_chain = None
    nbk = len(blocks)
    for step in range(nbk + 2):
        if step < nbk:
            si, i = blocks[step]
            if i == 0:
                pos = seq_order.index(si)
                if pos + 1 < len(seq_order):
                    pending_chain = chain_steps(seq_order[pos + 1])
            stageA(si, i)
        if 1 <= step <= nbk:
            stageB(*blocks[step - 1])
        if 2 <= step:
            stageC(*blocks[step - 2])
        if pending_chain is not None:
            si_cur = blocks[min(step, nbk - 1)][0]
            nsteps = 4 if SEQS[si_cur][1] == 2 else 1
            for _ in range(nsteps):
                try:
                    next(pending_chain)
                except StopIteration:
                    pending_chain = None
                    break
    tap("oT", oT, [128, 8, NT], ["A"])

    fence(["B", "C", "D", "E", "F", "G", "LNP", "SCR"])
    x1 = [V(o_B + t * 4 * K, 1024, F32) for t in range(NTT)]
    xs2 = [V(o_G + j * 4 * K, 1024, F32) for j in range(2)]
    xs2k = [[("G", "xs2", 0)], [("G", "xs2", 1)]]
    lng = V(o_lnp, 1024, F32)
    lnb = V(o_lnp + 4 * K, 1024, F32)
    gateb = [V(o_lnp + 8 * K + c * 4 * K, 1024, F32) for c in range(2)]

    rstdL = [small(1) for _ in range(2)]
    nmrB = [small(1) for _ in range(2)]

    s1B = [small(1) for _ in range(2)]
    s2B = [small(1) for _ in range(2)]
    mB = [small(1) for _ in range(2)]
    vB = [small(1) for _ in range(2)]
    ljunk = V(o_scr, 1024, BF16)

    def ln_affine(t, dst, dst_key, lng, lnb, lkey):
        tt(dst[:], dst[:], lng[:], ALU.mult, [dst_key, lkey + ("g",)], [dst_key], eng=POOL)
        tt(dst[:], dst[:], lnb[:], ALU.add, [dst_key, lkey + ("b",)], [dst_key], eng=POOL)

    def layer_norm_tile(t, banks, xres, xres_key, dst, dst_key, lng, lnb, gateb, gkey, lkey, defer_affine=False):
        cond = 0 if t < 8 else 1
        q = t % 2
        s1, s2, m_, v_, rstd, nmr = s1B[q], s2B[q], mB[q], vB[q], rstdL[q], nmrB[q]
        u = dst
        xk = xres_key if isinstance(xres_key, list) else [xres_key]
        for half in range(2):
            hs = slice(half * 512, (half + 1) * 512)
            tt(u[:, hs], PS(banks[half]), gateb[cond][:, hs], ALU.mult, [pk(banks[half]), gkey + (cond, half)], [dst_key + (half,)])
            stt(u[:, hs], xres[:, hs], ALPHA, u[:, hs], ALU.mult, ALU.add, xk + [dst_key + (half,)], [dst_key + (half,)])
        act(ljunk[:], u[:], AF.Copy, [dst_key], [("SCR", "ljunk"), ("s1", q)], accum_out=s1[:, 0:1])
        act(ljunk[:], u[:], AF.Square, [dst_key], [("SCR", "ljunk"), ("s2", q)], accum_out=s2[:, 0:1])
        ts(m_[:, 0:1], s1[:, 0:1], 1.0 / 1024.0, ALU.mult, [("s1", q)], [("m", q)])
        tt(v_[:, 0:1], m_[:, 0:1], m_[:, 0:1], ALU.mult, [("m", q)], [("v", q)])
        stt(v_[:, 0:1], s2[:, 0:1], 1.0 / 1024.0, v_[:, 0:1], ALU.mult, ALU.subtract, [("s2", q), ("v", q)], [("v", q)])
        rstd_from(rstd[:, 0:1], v_[:, 0:1], 1.0, [("v", q)], [("rstdL", q)])
        ts(nmr[:, 0:1], m_[:, 0:1], rstd[:, 0:1], ALU.mult, [("m", q), ("rstdL", q)], [("nmr", q)], s2=-1.0, op1=ALU.mult)
        act(u[:], u[:], AF.Identity, [dst_key, ("rstdL", q), ("nmr", q)], [dst_key], bias=nmr[:, 0:1], scale=rstd[:, 0:1])
        if not defer_affine:
            ln_affine(t, dst, dst_key, lng, lnb, lkey)

    dma(SP, lng[:], bcast_row(R_LN1G, 1024), "lnpg", [], [("LNP", "ln", "g")])
    dma(SP, lnb[:], bcast_row(R_LN1B, 1024), "lnpb", [], [("LNP", "ln", "b")])
    gate_from_mod(16, gateb, [("LNP", "gate", 0), ("LNP", "gate", 1)])
    wo = pre_wo
    for t in range(NTT):
        j = t % 2
        dma(SP, xs2[j][:], x_t[t], "xs2%d" % j, [], xs2k[j])
        banks = (2 * (t % 4), 2 * (t % 4) + 1)
        for half in range(2):
            s, wv = wo[half]
            for kt in range(8):
                mm(PS(banks[half]), oT[:, kt, t * 128:(t + 1) * 128], wv[:, kt, :], kt == 0, kt == 7,
                   [("ring", s), ("A", t // 4)], [pk(banks[half])])
        layer_norm_tile(t, banks, xs2[j], xs2k[j], x1[t], ("B", "x1", t), lng, lnb, gateb, ("LNP", "gate"), ("LNP", "ln"), defer_affine=True)
    tap("x1", V(o_B, NTT * 1024, F32).rearrange("p (n f) -> p n f", n=NTT), [128, NTT, 1024], ["B"])

    sc2p3 = sc2p[:].rearrange("p (k c) -> p k c", c=2)
    sh2 = modT[:, 48:64].rearrange("p (k c) -> p k c", c=2)
    scale2 = small(16)
    bias2 = small(16)
    tt(scale2[:].rearrange("p (k c) -> p k c", c=2), sc2p3, cols[:, C_LN1G:C_LN1G + 8].unsqueeze(2).to_broadcast([128, 8, 2]), ALU.mult,
       ["sc2p", "cols"], ["scale2"])
    tt(bias2[:].rearrange("p (k c) -> p k c", c=2), sc2p3, cols[:, C_LN1B:C_LN1B + 8].unsqueeze(2).to_broadcast([128, 8, 2]), ALU.mult,
       ["sc2p", "cols"], ["bias2"])
    tt(bias2[:], bias2[:], modT[:, 48:64], ALU.add, ["bias2", ("modT", 24)], ["bias2"])
    scale23 = scale2[:].rearrange("p (k c) -> p k c", c=2)
    bias23 = bias2[:].rearrange("p (k c) -> p k c", c=2)
    h2T = V(o_A, 8 * NT, BF16).rearrange("p (k t) -> p k t", k=8)
    aT = V(o_E, 22 * NT, BF16).rearrange("p (f t) -> p f t", f=22)
    sg = [V(o_scr + 2 * K + i * K, 512, BF16) for i in range(2)]
    def h2t_chunk(ck, extra=()):
        cond = 0 if ck < 2 else 1
        fence([("A", ck)])
        for kt in range(8):
            b = kt % 4
            for j in range(4):
                tr(PS(b, 128, j * 128), x1[ck * 4 + j][:, kt * 128:(kt + 1) * 128], [("B", "x1", ck * 4 + j)] + list(extra), [pk(b)])
            if kt % 2 == 0:
                act(h2T[:, kt, ck * 512:(ck + 1) * 512], PS(b), AF.Identity, [pk(b), "scale2", "bias2"],
                    [("A", ck, "h2T", kt)], bias=bias23[:, kt, cond:cond + 1], scale=scale23[:, kt, cond:cond + 1])
            else:
                ts(h2T[:, kt, ck * 512:(ck + 1) * 512], PS(b), scale23[:, kt, cond:cond + 1], ALU.mult, [pk(b), "scale2", "bias2"],
                   [("A", ck, "h2T", kt)], s2=bias23[:, kt, cond:cond + 1], op1=ALU.add)

    npair = [0]

    def ffn_group(s, wg, wu, ft, jt, ck, order_key=None):
        bg = 2 * (npair[0] % 4)
        bu = bg + 1
        si_ = npair[0] % 2
        npair[0] += 1
        for kt in range(8):
            mm(PS(bg), wg[:, kt, jt * 128:(jt + 1) * 128], h2T[:, kt, ck * 512:(ck + 1) * 512], kt == 0, kt == 7,
               [("ring", s), ("A", ck, "h2T", kt)], [pk(bg)])
        for kt in range(8):
            mm(PS(bu), wu[:, kt, jt * 128:(jt + 1) * 128], h2T[:, kt, ck * 512:(ck + 1) * 512], kt == 0, kt == 7,
               [("ring", s), ("A", ck, "h2T", kt)], [pk(bu)] + ([order_key] if (order_key and kt == 7) else []))
        act(sg[si_][:], PS(bg), AF.Silu, [pk(bg)], [("SCR", "sg", si_)])
        tt(aT[:, ft, ck * 512:(ck + 1) * 512], sg[si_][:], PS(bu), ALU.mult, [("SCR", "sg", si_), pk(bu)], [("E", "aT", ft, ck)])

    def ffn_views(base):
        return (V(base, 2048, BF16).rearrange("p (k n) -> p k n", k=8), V(base + 4096, 2048, BF16).rearrange("p (k n) -> p k n", k=8))

    h2t_chunk(0)
    h2t_chunk(1)
    s0_, base0_ = ffn_pre[0]
    wg0_, wu0_ = ffn_views(base0_)
    for jt in range(2):
        for ck in range(2):
            ffn_group(s0_, wg0_, wu0_, jt, jt, ck, order_key=("ffnorder",))
    h2t_chunk(2, extra=[("ffnorder",)])
    assert o_E + 12 * 3072 <= o_G and o_scr + 2 * K >= o_scr + 2048
    for t in range(NTT):
        ln_affine(t, x1[t], ("B", "x1", t), lng, lnb, ("LNP", "ln"))
    fence(["LNP"])
    def wd_load(sl):
        nft = 4 if sl < 5 else 2
        src = w_down_d[sl * 512:sl * 512 + nft * 128, :].rearrange("(k p) n -> p k n", p=128)
        if sl in (2, 3):
            base = o_lnp + (sl - 2) * 8 * K
            key = ("LNP", "wd", sl)
            dma(POOL, V(base, nft * 1024, BF16).rearrange("p (k n) -> p k n", k=nft), src, "wd%d" % sl, [], [key])
        else:
            s, base = ring_load([(src, 0, nft, 1024)], slot={0: 2, 1: 3, 4: 0, 5: 1}[sl])
            key = ("ring", s)
        wvd = V(base, nft * 1024, BF16).rearrange("p (k n) -> p k n", k=nft)
        return [(key, wvd[:, k_, :]) for k_ in range(nft)]
    wd_part = {sl: wd_load(sl) for sl in (0, 1, 2, 3)}
    for jt in range(2):
        ffn_group(s0_, wg0_, wu0_, jt, jt, 2)
    for sl in range(1, 11):
        if sl == 6:
            fence(["E", "G"])
        s, base = ffn_pre[sl] if sl < 2 else ffn_slot(sl)
        wg, wu = ffn_views(base)
        for jt in range(2):
            for ck in range(3):
                ffn_group(s, wg, wu, sl * 2 + jt, jt, ck)

    fence(["A", "SCR"])
    lng2 = V(o_A + 8 * K, 1024, F32)
    lnb2 = V(o_A + 12 * K, 1024, F32)
    gateb2 = [V(o_A + 16 * K + c_ * 4 * K, 1024, F32) for c_ in range(2)]
    dma(SP, lng2[:], bcast_row(R_LN2G, 1024), "lnpg", [], [("A", "ln", "g")])
    dma(SP, lnb2[:], bcast_row(R_LN2B, 1024), "lnpb", [], [("A", "ln", "b")])
    gate_from_mod(40, gateb2, [("A", "gate", 0), ("A", "gate", 1)])
    ystage = [V(o_A + j * 4 * K, 1024, F32) for j in range(2)]
    for sl in (4, 5):
        wd_part[sl] = wd_load(sl)
    wd = []
    for sl in range(6):
        wd += wd_part[sl]
    y_t = y_d.rearrange("(n p) f -> n p f", p=128)
    for t in range(NTT):
        j = t % 2
        banks = (4 * j + 2, 4 * j + 3)
        for half in range(2):
            for ft in range(22):
                key, w = wd[ft]
                mm(PS(banks[half]), aT[:, ft, t * 128:(t + 1) * 128], w[:, half * 512:(half + 1) * 512], ft == 0, ft == 21,
                   [key, ("E", "aT", ft, t // 4)], [pk(banks[half])])
        if t < NTT - 1:
            layer_norm_tile(t, banks, x1[t], ("B", "x1", t), ystage[j], ("A", "ystage", j), lng2, lnb2, gateb2, ("A", "gate"), ("A", "ln"))
            dma(SP, y_t[t], ystage[j][:], "yout%d" % j, [("A", "ystage", j)], [], True)
        else:
            layer_norm_tile(t, banks, x1[t], ("B", "x1", t), ystage[j], ("A", "ystage", j), lng2, lnb2, gateb2, ("A", "gate"), ("A", "ln"),
                            defer_affine=True)
            for half in range(2):
                hs = slice(half * 512, (half + 1) * 512)
                eng_ = POOL if half == 0 else DVE
                kh = ("A", "ystage", j, half)
                tt(ystage[j][:, hs], ystage[j][:, hs], lng2[:, hs], ALU.mult, [kh, ("A", "ln", "g")], [kh], eng=eng_)
                tt(ystage[j][:, hs], ystage[j][:, hs], lnb2[:, hs], ALU.add, [kh, ("A", "ln", "b")], [kh], eng=eng_)
                dma(SP, y_t[t][:, hs], ystage[j][:, hs], "ylast%d" % half, [kh], [], True)
    P.emit(st)
    build.P = P
    return nc


_CACHE = {}


def _prep_inputs(inp):
    f = lambda a: np.ascontiguousarray(np.asarray(a, dtype=np.float32))
    cf, cb = _host_consts()
    rows = np.zeros((1, R_N), np.float32)
    rows[0, R_LN1G:R_LN1G + 1024] = f(inp["ln1_g"])[0]
    rows[0, R_LN1B:R_LN1B + 1024] = f(inp["ln1_b"])[0]
    rows[0, R_LN2G:R_LN2G + 1024] = f(inp["ln2_g"])[0]
    rows[0, R_LN2B:R_LN2B + 1024] = f(inp["ln2_b"])[0]
    rows[0, R_NW:R_NW + 512] = f(inp["ssd_norm_w"])[0]
    rows[0, R_CONVB:R_CONVB + 1024] = f(inp["conv_b"])[0]
    rows[0, R_BADA:R_BADA + 6144] = f(inp["b_ada"])[0]
    sm = np.concatenate([f(inp["dt_bias_fwd"])[0], f(inp["dt_bias_bwd"])[0], f(inp["a_log_fwd"])[0], f(inp["a_log_bwd"])[0],
                         f(inp["d_skip"])[0], f(inp["ret_decay_fwd"])[0], f(inp["ret_decay_bwd"])[0]])
    rows[0, R_SMALL:R_SMALL + 48] = sm
    shared = {
        "w_in": f(inp["w_in"])[0], "w_out": f(inp["w_out"])[0], "w_gate": f(inp["w_gate"])[0],
        "w_up": f(inp["w_up"])[0], "w_down": f(inp["w_down"])[0], "w_ada": f(inp["w_ada"])[0],
        "rows": rows, "cstf": cf, "cstb": cb,
    }
    xs = f(inp["x_sample"])
    xp = f(inp["x_prompt"])
    c = f(inp["c"])
    cc = f(inp["c_ctx"])
    sr = f(inp["state_ret"])
    ss = f(inp["state_ssd"])
    b_ada = f(inp["b_ada"])[0]
    conv_w = f(inp["conv_w"])[0]
    conv_b = f(inp["conv_b"])[0]
    maps = []
    for i in range(8):
        cols = np.zeros((128, C_N), np.float32)
        cv = np.stack([c[i], cc], 0).reshape(2, 8, 128).transpose(2, 1, 0)
        cols[:, C_CVEC:C_CVEC + 16] = cv.reshape(128, 16)
        cols[:, C_BADA:C_BADA + 48] = b_ada.reshape(48, 128).T
        cols[:, C_CONVW:C_CONVW + 40] = conv_w.reshape(5, 8, 128).transpose(2, 1, 0).reshape(128, 40)
        cols[:, C_CONVB:C_CONVB + 8] = conv_b.reshape(8, 128).T
        cols[:, C_LN1G:C_LN1G + 8] = f(inp["ln1_g"])[0].reshape(8, 128).T
        cols[:, C_LN1B:C_LN1B + 8] = f(inp["ln1_b"])[0].reshape(8, 128).T
        m = dict(shared)
        m["x"] = np.ascontiguousarray(np.concatenate([xs[i], xp[2 * i], xp[2 * i + 1]], 0))
        m["sret0"] = np.ascontiguousarray(sr[i, 0].transpose(2, 0, 1, 3).reshape(128, 1024))
        m["sssd0"] = np.ascontiguousarray(ss[i, 0].transpose(2, 0, 1, 3).reshape(128, 1024))
        m["cols"] = cols
        maps.append(m)
    return maps


def kernel(**inputs):
    if "nc" not in _CACHE:
        _CACHE["nc"] = build()
    nc = _CACHE["nc"]
    maps = _prep_inputs(inputs)
    res = run_bass_kernel_spmd(nc, maps, core_ids=list(range(8)))
    r = res.results
    y_s = np.stack([r[i]["y"][:1024] for i in range(8)], 0)
    y_p = np.concatenate([r[i]["y"][1024:].reshape(2, 256, 1024) for i in range(8)], 0)
    nret = np.concatenate([r[i]["oret"] for i in range(8)], 0)[:, None]
    nssd = np.concatenate([r[i]["ossd"] for i in range(8)], 0)[:, None]
    return (y_p.astype(np.float32), y_s.astype(np.float32), nret.astype(np.float32), nssd.astype(np.float32))
```

```python
import contextlib
import math
import numpy as np
import concourse.bass as bass
import concourse.mybir as mybir
from concourse.bass_utils import run_bass_kernel_spmd

F32 = mybir.dt.float32
BF16 = mybir.dt.bfloat16
U8 = mybir.dt.uint8
AF = mybir.ActivationFunctionType
ALU = mybir.AluOpType
SZ = {F32: 4, BF16: 2}

PE, ACT, DVE, POOL, SP = "tensor", "scalar", "vector", "gpsimd", "sync"
ENGS = [PE, ACT, DVE, POOL, SP]

D = 1024
NT = 1536
NTT = 12
INC = 3592
DFF = 2816
EPS = 1e-6
ALPHA = 2.0 ** 0.25
NEG = -32768.0
SEQS = [(0, 8), (8, 2), (10, 2)]


class Op:
    __slots__ = ("idx", "eng", "fn", "is_dma", "dsem", "dval", "deps", "inc", "cval", "cost", "fin", "npend", "succ", "start", "crit", "tag", "aps")

    def __init__(self, idx, eng, fn, is_dma):
        self.idx, self.eng, self.fn, self.is_dma = idx, eng, fn, is_dma
        self.dsem, self.dval, self.deps, self.inc, self.cval = None, 0, [], False, 0
        self.cost, self.fin, self.npend, self.succ = 0.3, 0.0, 0, []
        self.start, self.crit, self.tag, self.aps = 0.0, None, '', None


class Prog:
    def __init__(self, nc):
        self.nc = nc
        self.ops = []
        self.res = {}
        self.dma_sems = {}
        self.out_dma_ops = []
        self.cur_tag = ""
        self.last_dma = {}

    @staticmethod
    def _split(key):
        if isinstance(key, tuple):
            return key[0], tuple(key[1:])
        return key, ()

    @staticmethod
    def _related(p, q):
        n = min(len(p), len(q))
        return p[:n] == q[:n]

    def _record(self, op, reads, writes):
        deps = set()
        for k in reads:
            name, p = self._split(k)
            d = self.res.setdefault(name, {})
            for q, e in d.items():
                if self._related(p, q) and e[0] is not None:
                    deps.add(e[0])
        for k in writes:
            name, p = self._split(k)
            d = self.res.setdefault(name, {})
            for q, e in d.items():
                if self._related(p, q):
                    if e[0] is not None:
                        deps.add(e[0])
                    deps.update(e[1])
        deps.discard(op)
        op.deps = sorted(deps, key=lambda o: o.idx)
        for k in reads:
            name, p = self._split(k)
            d = self.res[name]
            if p not in d:
                d[p] = [None, []]
            d[p][1].append(op)
        for k in writes:
            name, p = self._split(k)
            d = self.res[name]
            for q in [q for q in d if len(q) >= len(p) and q[:len(p)] == p]:
                del d[q]
            d[p] = [op, []]

    def op(self, eng, fn, reads=(), writes=(), cost=0.3):
        o = Op(len(self.ops), eng, fn, False)
        o.cost = cost
        o.tag = self.cur_tag
        self.ops.append(o)
        self._record(o, list(reads), list(writes))
        return o

    def dma(self, eng, fn, semkey, reads=(), writes=(), is_output=False, cost=3.0, chain=True):
        o = Op(len(self.ops), eng, fn, True)
        o.cost = cost
        o.tag = self.cur_tag
        ent = self.dma_sems.setdefault(semkey, [None, 0])
        ent[1] += 16
        o.dsem, o.dval = semkey, ent[1]
        self.ops.append(o)
        self._record(o, list(reads), list(writes))
        prev = self.last_dma.get(semkey)
        if chain and prev is not None and prev not in o.deps:
            o.deps.append(prev)
            o.deps.sort(key=lambda q: q.idx)
        self.last_dma[semkey] = o
        if is_output:
            self.out_dma_ops.append(o)
        return o

    def schedule(self):
        import heapq
        ops = self.ops
        for o in ops:
            o.succ = []
        for o in ops:
            o.npend = len(o.deps)
            for d in o.deps:
                d.succ.append(o)

        def lat(d, o):
            return 0.05 if (d.eng == PE and o.eng == PE and not d.is_dma and not o.is_dma) else 0.25

        bl = [0.0] * len(ops)
        for o in reversed(ops):
            m = 0.0
            for s_ in o.succ:
                v = bl[s_.idx] + lat(o, s_)
                if v > m:
                    m = v
            bl[o.idx] = o.cost + m
        inorder = {SP: False, POOL: False, PE: False, ACT: False, DVE: False}
        pend = {e: [] for e in ENGS}
        avail = {e: [] for e in ENGS}
        tcur = {e: 0.0 for e in ENGS}
        ready_t = {}
        nxt = {e: 0 for e in ENGS}
        eng_ops = {e: [o for o in ops if o.eng == e] for e in ENGS}
        order = {e: [] for e in ENGS}

        def push(o):
            rt = 0.0
            for d in o.deps:
                rt = max(rt, d.fin + lat(d, o))
            ready_t[o.idx] = rt
            heapq.heappush(pend[o.eng], (rt, o.idx, o))

        for o in ops:
            if o.npend == 0:
                push(o)
        done, n = 0, len(ops)
        while done < n:
            best = None
            for e in ENGS:
                if inorder[e]:
                    if nxt[e] >= len(eng_ops[e]):
                        continue
                    o = eng_ops[e][nxt[e]]
                    if o.npend != 0 or o.idx not in ready_t:
                        continue
                    st_ = max(tcur[e], ready_t[o.idx])
                    cand = (st_, o.idx, e, o)
                else:
                    p, a = pend[e], avail[e]
                    while p and p[0][0] <= tcur[e]:
                        rt, idx, o = heapq.heappop(p)
                        heapq.heappush(a, (-bl[idx], idx, o))
                    if a:
                        o = a[0][2]
                        cand = (tcur[e], o.idx, e, o)
                    elif p:
                        rt, idx, o = p[0]
                        cand = (rt, idx, e, o)
                    else:
                        continue
                if best is None or cand[:2] < best[:2]:
                    best = cand
            assert best is not None, "scheduler deadlock"
            st_, _, e, o = best
            if inorder[e]:
                nxt[e] += 1
            else:
                if avail[e] and avail[e][0][2] is o:
                    heapq.heappop(avail[e])
                else:
                    heapq.heappop(pend[e])
            o.start = st_
            o.crit = ("eng", order[e][-1]) if (order[e] and tcur[e] >= ready_t[o.idx]) else \
                ("dep", max(o.deps, key=lambda d: d.fin) if o.deps else None)
            if o.is_dma:
                tcur[e] = st_ + (1.0 if e == POOL else 0.1)
                o.fin = st_ + o.cost
            else:
                tcur[e] = st_ + o.cost
                o.fin = tcur[e]
            order[e].append(o)
            done += 1
            for s_ in o.succ:
                s_.npend -= 1
                if s_.npend == 0:
                    push(s_)
        self.est_total = max(o.fin for o in ops)
        return order

    def emit(self, stack, reorder=True):
        nc = self.nc
        if reorder:
            order = self.schedule()
        else:
            order = {e: [o for o in self.ops if o.eng == e] for e in ENGS}
        esem = {e: stack.enter_context(nc.semaphore("c_" + e)) for e in ENGS}
        for i, (k, ent) in enumerate(self.dma_sems.items()):
            ent[0] = stack.enter_context(nc.semaphore("d%d" % i))
        final_waits = {}
        for o in self.out_dma_ops:
            final_waits[o.dsem] = max(final_waits.get(o.dsem, 0), o.dval)

        def skip(d, o):
            return (not d.is_dma) and d.eng == PE and o.eng == PE and not o.is_dma

        pos = {}
        for e in ENGS:
            for i_, o in enumerate(order[e]):
                pos[o.idx] = i_
        need = {}
        for e in ENGS:
            seenpos = {}
            for o in order[e]:
                last = {}
                for d in o.deps:
                    if d.is_dma or skip(d, o):
                        continue
                    m = last.get(d.eng)
                    if m is None or pos[d.idx] > pos[m.idx]:
                        last[d.eng] = d
                lst = []
                for pe_, m in last.items():
                    if seenpos.get(pe_, -1) >= pos[m.idx]:
                        continue
                    seenpos[pe_] = pos[m.idx]
                    m.inc = True
                    lst.append(m)
                need[o.idx] = lst
        cnt = {e: 0 for e in ENGS}
        for e in ENGS:
            for o in order[e]:
                if not o.is_dma and o.inc:
                    cnt[e] += 1
                    o.cval = cnt[e]
        seen = {e: {} for e in ENGS}
        per_eng = {e: [] for e in ENGS}
        for o in [o for e in ENGS for o in order[e]]:
            wl = [(esem[m.eng], m.cval) for m in need[o.idx]]
            waits = {}
            for d in o.deps:
                if d.is_dma:
                    key, val = d.dsem, d.dval
                    if val > waits.get(key, 0):
                        waits[key] = val
            s = seen[o.eng]
            for key, val in waits.items():
                if s.get(key, 0) >= val:
                    continue
                s[key] = val
                wl.append((self.dma_sems[key][0], val))
            per_eng[o.eng].append((o, wl))
        self.n_incs = dict(cnt)
        block = stack.enter_context(nc.Block())
        dma_sems = self.dma_sems

        def make(engname):
            def body(eng):
                for o, wl in per_eng[engname]:
                    for sem, val in wl:
                        eng.wait_ge(sem, val)
                    ins = o.fn(eng)
                    if o.is_dma:
                        ins.then_inc(dma_sems[o.dsem][0], 16)
                    elif o.inc:
                        ins.then_inc(esem[engname], 1)
                if engname == SP:
                    for k, v in final_waits.items():
                        eng.wait_ge(dma_sems[k][0], v)
            return body

        block.tensor(make(PE))
        block.scalar(make(ACT))
        block.vector(make(DVE))
        block.gpsimd(make(POOL))
        block.sync(make(SP))


CF_IDENT, CF_UTRI, CF_ONES = 0, 128, 256
CF_IOTAF, CF_IOTAB = 384, 1408
CF_COS, CF_SIN = 2432, 3456
CF_POS, CF_NEG = 4480, 6528
CF_TAILF, CF_TAILB = 8576, 8578
CF_N = 8580
CB_IDENT, CB_ONES, CB_MF, CB_MB, CB_SELROW, CB_SELBIAS, CB_N = 0, 128, 256, 768, 1280, 3328, 5376


def _host_consts():
    p = np.arange(128)[:, None].astype(np.float64)
    cf = np.zeros((128, CF_N), np.float32)
    j = np.arange(128)[None, :]
    cf[:, CF_IDENT:CF_IDENT + 128] = (p == j)
    cf[:, CF_UTRI:CF_UTRI + 128] = (p <= j)
    cf[:, CF_ONES:CF_ONES + 128] = 1.0
    t = np.arange(1024)[None, :]
    cf[:, CF_IOTAF:CF_IOTAF + 1024] = t + 1
    cf[:, CF_IOTAB:CF_IOTAB + 1024] = 1024 - t
    tt = np.arange(1024)
    t_row = (tt // 64).astype(np.float64)
    t_col = (tt % 64).astype(np.float64)
    inv = 10000.0 ** (-np.arange(32, dtype=np.float64) / 32.0)
    ang = np.concatenate([t_row[:, None] * inv[None, :], t_col[:, None] * inv[None, :]], axis=-1)
    cos = np.cos(ang).T
    sin = np.sin(ang).T
    cf[:, CF_COS:CF_COS + 1024] = np.concatenate([cos, cos], 0)
    cf[:, CF_SIN:CF_SIN + 1024] = np.concatenate([-sin, sin], 0)
    u = np.arange(2048)[None, :]
    delta = u - p - 1024
    cf[:, CF_POS:CF_POS + 2048] = np.maximum(delta, 0)
    cf[:, CF_NEG:CF_NEG + 2048] = np.minimum(delta, 0)
    jj = np.arange(2)[None, :]
    cf[:, CF_TAILF:CF_TAILF + 2] = 255 - 128 * jj - p
    cf[:, CF_TAILB:CF_TAILB + 2] = 128 * jj + p
    cb = np.zeros((128, CB_N), np.float32)
    cb[:, CB_IDENT:CB_IDENT + 128] = (p == j)
    cb[:, CB_ONES:CB_ONES + 128] = 1.0
    mf = np.where(p <= j, 0.0, NEG)
    mb = np.where(p > j, 0.0, NEG)
    cb[:, CB_MF:CB_MF + 512] = np.tile(mf, (1, 4))
    cb[:, CB_MB:CB_MB + 512] = np.tile(mb, (1, 4))
    k = np.arange(128)
    selrow = np.zeros((128, 16, 128), np.float32)
    for hd in range(16):
        selrow[(k < 96) & (k % 32 == hd), hd, :] = 1.0
    cb[:, CB_SELROW:CB_SELROW + 2048] = selrow.reshape(128, 2048)
    selb = np.zeros((128, 2, 8, 128), np.float32)
    for d in range(2):
        for h in range(8):
            selb[(k < 96) & (k % 32 == 16 + d * 8 + h), d, h, :] = 1.0
    cb[:, CB_SELBIAS:CB_SELBIAS + 2048] = selb.reshape(128, 2048)
    return cf, cb


R_LN1G, R_LN1B, R_LN2G, R_LN2B, R_NW, R_CONVB, R_BADA, R_SMALL, R_N = 0, 1024, 2048, 3072, 4096, 4608, 5632, 11776, 11824
C_CVEC, C_BADA, C_CONVW, C_CONVB, C_LN1G, C_LN1B, C_N = 0, 16, 64, 104, 112, 120, 128


def build(debug=()):
    nc = bass.Bass("TRN2", target_bir_lowering=False)
    P = Prog(nc)
    di = lambda name, shape: nc.dram_tensor(name, list(shape), F32, kind="ExternalInput").ap()
    do = lambda name, shape: nc.dram_tensor(name, list(shape), F32, kind="ExternalOutput").ap()
    x_d = di("x", [NT, D])
    sret_d = di("sret0", [128, 1024])
    sssd_d = di("sssd0", [128, 1024])
    w_in_d = di("w_in", [D, INC])
    w_out_d = di("w_out", [D, D])
    w_gate_d = di("w_gate", [D, DFF])
    w_up_d = di("w_up", [D, DFF])
    w_down_d = di("w_down", [DFF, D])
    w_ada_d = di("w_ada", [D, 6 * D])
    rows_d = di("rows", [1, R_N])
    cols_d = di("cols", [128, C_N])
    cf_d = di("cstf", [128, CF_N])
    cb_d = di("cstb", [128, CB_N])
    y_d = do("y", [NT, D])
    oret_d = do("oret", [2, 2, 4, 128, 128])
    ossd_d = do("ossd", [2, 2, 8, 128, 64])

    st = contextlib.ExitStack()
    ARENA = 212800
    arena = nc.alloc_sbuf_tensor("arena", [128, ARENA], U8)
    psum = nc.alloc_psum_tensor("psum", [128, 4096], F32)
    K = 1024

    def V(off, n, dt):
        off = int(off)
        assert off % 4 == 0 and off + n * SZ[dt] <= ARENA, (off, n)
        return arena[:, off:off + n * SZ[dt]].bitcast(dt)

    def PS(b, n=512, off=0):
        return psum[:, b * 512 + off:b * 512 + off + n]

    def pk(b):
        return "ps%d" % b

    o_const, o_scr, o_lnp, o_ring = 0, 5 * K, 15 * K, 31 * K
    SLOT = 8320
    o_A, o_B, o_C, o_D, o_E, o_F, o_G = 64 * K, 88 * K, 112 * K, 124 * K, 136 * K, 148 * K, 173 * K

    ident_f = V(0, 128, F32)
    utri_f = V(512, 128, F32)
    ones_f = V(1024, 128, F32)
    ident_b = V(1536, 128, BF16)
    ones_b = V(1792, 128, BF16)
    m = [2048]

    def small(n, dt=F32):
        a = V(m[0], n, dt)
        m[0] += (n * SZ[dt] + 3) // 4 * 4
        assert m[0] <= 5 * K, m[0]
        return a

    cols = small(C_N)
    smalls = small(48)
    modT = small(96)
    scv = small(16, BF16)
    lg = small(8)
    nlg = small(8)
    tmp8 = small(8)
    nA = small(16)
    tailpos = small(4)
    tailw = small(16)
    sc1p = small(16)
    sc2p = small(16)
    lscb = small(1)
    fencew = small(1)
    dtraw = small(96)
    dskb = small(8)

    def fs(ap):
        n = 1
        for d in ap.shape[1:]:
            n *= int(d)
        return n

    def inps(ap):
        return ap.tensor.name == "psum"

    def mm(out, lhsT, rhs, start, stop, reads, writes):
        n_ = fs(rhs)
        c = ((0.035 + n_ / 2560.0) if n_ >= 256 else (0.03 + n_ / 1400.0)) * (4.0 if rhs.dtype == F32 else 1.0)
        o_ = P.op(PE, lambda e: e.matmul(out, lhsT=lhsT, rhs=rhs, start=start, stop=stop), reads, writes, cost=c)
        o_.aps = ([lhsT, rhs], [out])
        return o_

    def tr(out, in_, reads, writes):
        o_ = P.op(PE, lambda e: e.transpose(out=out, in_=in_, identity=ident_f[:]), list(reads) + ["const"], writes, cost=0.1)
        o_.aps = ([in_, ident_f[:]], [out])
        return o_

    def act(out, in_, func, reads, writes, bias=None, scale=None, accum_out=None):
        kw = {}
        c = 0.2 + fs(in_) / 1400.0
        if fs(in_) <= 8:
            c = 0.6
        if bias is not None:
            kw["bias"] = bias
            c += 0.05
        if scale is not None:
            kw["scale"] = scale
        if accum_out is not None:
            kw["accum_out"] = accum_out
            c += 0.1
        o_ = P.op(ACT, lambda e: e.activation(out=out, in_=in_, func=func, **kw), reads, writes, cost=c)
        o_.aps = ([in_] + [v_ for v_ in (bias, scale) if v_ is not None and not isinstance(v_, float)], [out] + ([accum_out] if accum_out is not None else []))
        return o_

    def vcost(eng, n, f):
        if n <= 8:
            return 0.6
        return (0.1 + n / 490.0) if eng == POOL else (0.09 + n * f / 1060.0)

    def tt(out, in0, in1, op, reads, writes, eng=DVE):
        f = 1.0 if (inps(in0) or inps(in1)) else 2.0
        if in0.dtype == BF16 and in1.dtype == BF16 and out.dtype == BF16 and f == 2.0:
            f = 0.6
        o_ = P.op(eng, lambda e: e.tensor_tensor(out=out, in0=in0, in1=in1, op=op), reads, writes, cost=vcost(eng, fs(out), f))
        o_.aps = ([in0, in1], [out])
        return o_

    def ts(out, in0, s1, op0, reads, writes, s2=None, op1=None, eng=DVE):
        c = vcost(eng, fs(out), 1.0)
        if op1 is None:
            o_ = P.op(eng, lambda e: e.tensor_scalar(out=out, in0=in0, scalar1=s1, scalar2=None, op0=op0), reads, writes, cost=c)
            o_.aps = ([in0] + [v_ for v_ in (s1,) if not isinstance(v_, (float, int))], [out])
            return o_
        o_ = P.op(eng, lambda e: e.tensor_scalar(out=out, in0=in0, scalar1=s1, scalar2=s2, op0=op0, op1=op1), reads, writes, cost=c)
        o_.aps = ([in0] + [v_ for v_ in (s1, s2) if not isinstance(v_, (float, int))], [out])
        return o_

    def stt(out, in0, scalar, in1, op0, op1, reads, writes):
        f = 1.0 if (inps(in0) or inps(in1)) else 2.0
        o_ = P.op(DVE, lambda e: e.scalar_tensor_tensor(out=out, in0=in0, scalar=scalar, in1=in1, op0=op0, op1=op1), reads, writes,
                  cost=vcost(DVE, fs(out), f))
        o_.aps = ([in0, in1] + [v_ for v_ in (scalar,) if not isinstance(v_, (float, int))], [out])
        return o_

    def cp(out, in_, reads, writes, eng=DVE):
        o_ = P.op(eng, lambda e: e.tensor_copy(out=out, in_=in_), reads, writes, cost=vcost(eng, fs(out), 1.0))
        o_.aps = ([in_], [out])
        return o_

    def memset(ap, val, writes, eng=DVE):
        o_ = P.op(eng, lambda e: e.memset(ap, val), [], writes, cost=vcost(eng, fs(ap), 0.5))
        o_.aps = ([], [ap])
        return o_

    def dma(eng, out, in_, key, reads, writes, is_output=False, chain=True):
        nb = 128 * fs(out) * 4
        o_ = P.dma(eng, lambda e: e.dma_start(out=out, in_=in_), key, reads, writes, is_output, cost=2.0 + nb / 150e3, chain=chain)
        o_.aps = ([in_], [out])
        return o_

    P.marks = []

    def fence(regions, name=None):
        o = P.op(DVE, lambda e: e.memset(fencew[:], 0.0), [], list(regions), cost=0.1)
        P.marks.append((name or ("f%d" % len(P.marks)), o))

    def bcast_row(off, n):
        return rows_d[0:1, off:off + n].partition_broadcast(128).rearrange("p a n -> p (a n)")

    dbg_n = [0]

    def tap(name, ap, shape, reads):
        if name in debug:
            dd = do("dbg_" + name, shape)
            dbg_n[0] += 1
            dma(POOL, dd, ap, "dbg%d" % dbg_n[0], reads, [], True)

    ring_n = [0]

    def ring_load(parts, slot=None, after=()):
        if slot is None:
            s = ring_n[0] % 4
            ring_n[0] += 1
        else:
            s = slot
        base = o_ring + s * SLOT
        for ip, (src, dst_off_elems, nk, ncol) in enumerate(parts):
            dst = V(base + dst_off_elems * 2, nk * ncol, BF16).rearrange("p (k n) -> p k n", k=nk)
            dma(POOL, dst, src, "ring%d" % s, list(after), [("ring", s, ip)], chain=(ip == 0))
        return s, base

    def rstd_from(dst, src_ps_or_sb, scale, reads, writes):
        act(dst, src_ps_or_sb, AF.Ln, list(reads) + ["epsb"], writes, bias=epsb[:, 0:1], scale=scale)
        act(dst, dst, AF.Exp, writes, writes, scale=-0.5)

    epsb = small(1)
    dgs = [V(o_scr + 8 * K + i * 512, 128, F32) for i in range(2)]
    memset(epsb[:], EPS, ["epsb"])

    dma(SP, V(0, 384, F32), cf_d[:, 0:384], "c0", [], ["const"])
    dma(POOL, V(1536, 256, BF16), cb_d[:, 0:256], "c1", [], ["constb"])
    dma(SP, cols[:], cols_d[:, :], "c2", [], ["cols"])
    dma(SP, smalls[:], bcast_row(R_SMALL, 48), "c3", [], ["smalls"])
    dma(SP, tailpos[:], cf_d[:, CF_TAILF:CF_TAILF + 4], "c7", [], ["tailpos"])
    cos_t = V(o_lnp, 1024, F32)
    sin_t = V(o_lnp + 4 * K, 1024, F32)
    scb = V(o_scr + 4 * K, 2048, BF16)
    dma(SP, V(o_lnp, 2048, F32), cf_d[:, CF_COS:CF_COS + 2048], "c4", [], [("LNP", "rope")])
    act(scv[:], cols[:, C_CVEC:C_CVEC + 16], AF.Silu, ["cols"], ["scv"])
    scv3 = scv[:].rearrange("p (k c) -> p k c", c=2)

    def make_scb():
        cp(scb[:].rearrange("p (a m) -> p a m", m=128), scv[:].unsqueeze(2).to_broadcast([128, 16, 128]),
           ["scv"], [("SCR", "scb")])
    scb4 = scb[:].rearrange("p (k c m) -> p k c m", k=8, c=2)

    def mod_fm(ft0, col0, nslots):
        for sl in range(nslots):
            c0 = col0 + sl * 512
            s, base = ring_load([(w_ada_d[:, c0:c0 + 512].rearrange("(k p) n -> p k n", p=128), 0, 8, 512)])
            wv = V(base, 8 * 512, BF16).rearrange("p (k n) -> p k n", k=8)
            for j in range(4):
                ft = ft0 + sl * 4 + j
                for kt in range(8):
                    mm(PS(7, 2, ft * 2), wv[:, kt, j * 128:(j + 1) * 128], scv3[:, kt, :], kt == 0, kt == 7,
                       [("ring", s), "scv"], ["ps7"])
        n = nslots * 4
        tt(modT[:, ft0 * 2:(ft0 + n) * 2].rearrange("p (f c) -> p f c", c=2),
           PS(7, n * 2, ft0 * 2).rearrange("p (f c) -> p f c", c=2),
           cols[:, C_BADA + ft0:C_BADA + ft0 + n].unsqueeze(2).to_broadcast([128, n, 2]), ALU.add,
           ["ps7", "cols"], [("modT", ft0)])

    def gate_load(col0):
        slots = []
        for sl in range(2):
            c0 = col0 + sl * 512
            s, base = ring_load([(w_ada_d[:, c0:c0 + 512].rearrange("(k p) n -> p k n", p=128), 0, 8, 512)])
            slots.append((s, V(base, 8 * 512, BF16).rearrange("p (k n) -> p k n", k=8)))
        return slots

    def gate_compute(slots, dst_off, rowoff):
        make_scb()
        btmp = V(o_scr, 1024, F32)
        dma(SP, btmp[:], bcast_row(rowoff, 1024), "gb", [], [("SCR", "btmp")])
        for sl in range(2):
            s, wv = slots[sl]
            for c in range(2):
                b = 5 + c
                for kt in range(8):
                    mm(PS(b), scb4[:, kt, c, :], wv[:, kt, :], kt == 0, kt == 7, [("ring", s), ("SCR", "scb")], [pk(b)])
                tt(V(dst_off + (c * 1024 + sl * 512) * 4, 512, F32)[:], PS(b), btmp[:, sl * 512:(sl + 1) * 512], ALU.add,
                   [pk(b), ("SCR", "btmp")], [("LNP", "gate", c, sl)])

    def gate_bcast(dst_off, col0, rowoff):
        gate_compute(gate_load(col0), dst_off, rowoff)

    xTf = V(o_F, 8 * NT, F32).rearrange("p (k t) -> p k t", k=8)
    xst = [V(o_B + i * 4 * K, 1024, F32) for i in range(8)]
    x_t = x_d.rearrange("(n p) f -> n p f", p=128)

    def x_load(ck, after=()):
        h_ = ck % 2
        dst = V(o_B + h_ * 16 * K, 4096, F32).rearrange("p (j f) -> p j f", j=4)
        src = x_d[ck * 512:(ck + 1) * 512, :].rearrange("(j p) f -> p j f", p=128)
        dma(SP, dst, src, "xst%d" % h_, list(after), [("B", "xst", h_ * 4 + j) for j in range(4)])
    x_load(0)
    for (ft0_, c0_) in ((0, 0), (8, 1024), (4, 512), (12, 1536)):
        s_, base_ = ring_load([(w_ada_d[:, c0_:c0_ + 512].rearrange("(k p) n -> p k n", p=128), 0, 8, 512)])
        wv_ = V(base_, 8 * 512, BF16).rearrange("p (k n) -> p k n", k=8)
        for j in range(4):
            ft = ft0_ + j
            for kt in range(8):
                mm(PS(7, 2, ft * 2), wv_[:, kt, j * 128:(j + 1) * 128], scv3[:, kt, :], kt == 0, kt == 7,
                   [("ring", s_), "scv"], ["ps7"])
        tt(modT[:, ft0_ * 2:(ft0_ + 4) * 2].rearrange("p (f c) -> p f c", c=2),
           PS(7, 8, ft0_ * 2).rearrange("p (f c) -> p f c", c=2),
           cols[:, C_BADA + ft0_:C_BADA + ft0_ + 4].unsqueeze(2).to_broadcast([128, 4, 2]), ALU.add,
           ["ps7", "cols"], [("modT", 0, ft0_)])
        if ft0_ >= 8:
            ts(sc1p[:, (ft0_ - 8) * 2:(ft0_ - 4) * 2], modT[:, ft0_ * 2:(ft0_ + 4) * 2], 1.0, ALU.add,
               [("modT", 0, ft0_)], [("sc1p", ft0_ - 8)])
    wada_done = [("ring", 0), ("ring", 1), ("ring", 2), ("ring", 3)]
    x_load(1, after=wada_done[:2])
    n_ = 0
    for ck in range(3):
        if ck == 1:
            x_load(2, after=wada_done)
        for kt in range(8):
            b = n_ % 4
            n_ += 1
            for j in range(4):
                jj = (ck % 2) * 4 + j
                tr(PS(b, 128, j * 128), xst[jj][:, kt * 128:(kt + 1) * 128], [("B", "xst", jj)], [pk(b)])
            if n_ % 2 == 0:
                act(xTf[:, kt, ck * 512:(ck + 1) * 512], PS(b), AF.Copy, [pk(b)], [("F", "xTf", kt, ck)])
            else:
                cp(xTf[:, kt, ck * 512:(ck + 1) * 512], PS(b), [pk(b)], [("F", "xTf", kt, ck)])
    sh1 = modT[:, 0:16].rearrange("p (k c) -> p k c", c=2)
    sc1p3 = sc1p[:].rearrange("p (k c) -> p k c", c=2)
    hT = V(o_A, 8 * NT, BF16).rearrange("p (k t) -> p k t", k=8)
    n_ = 0
    for ck in range(3):
        cond = 0 if ck < 2 else 1
        for kt in range(8):
            if n_ % 2 == 0:
                act(hT[:, kt, ck * 512:(ck + 1) * 512], xTf[:, kt, ck * 512:(ck + 1) * 512], AF.Identity,
                    [("F", "xTf", kt, ck), ("sc1p", (kt // 4) * 4), ("modT", 0, (kt // 4) * 4)], [("A", ck, "hT", kt)],
                    bias=sh1[:, kt, cond:cond + 1], scale=sc1p3[:, kt, cond:cond + 1])
            else:
                ts(hT[:, kt, ck * 512:(ck + 1) * 512], xTf[:, kt, ck * 512:(ck + 1) * 512], sc1p3[:, kt, cond:cond + 1], ALU.mult,
                   [("F", "xTf", kt, ck), ("sc1p", (kt // 4) * 4), ("modT", 0, (kt // 4) * 4)], [("A", ck, "hT", kt)], s2=sh1[:, kt, cond:cond + 1], op1=ALU.add)
            n_ += 1
    fence(["B", "C", "F", "G"])
    def w_in_chunk(c0, ncol, after=()):
        s, base = ring_load([(w_in_d[:, c0:c0 + ncol].rearrange("(k p) n -> p k n", p=128), 0, 8, ncol)], after=after)
        return s, V(base, 8 * ncol, BF16).rearrange("p (k n) -> p k n", k=8)

    pre_in = {1024: w_in_chunk(1024, 512)}
    for c0 in (0, 512):
        pre_in[c0] = w_in_chunk(c0, 512, after=[("A", 0, "hT", 7)])

    def w_in_get(c0, ncol):
        return pre_in.pop(c0) if c0 in pre_in else w_in_chunk(c0, ncol)

    delta = V(o_F, 2048, F32)
    E1 = V(o_F + 8 * K, 2048, F32)
    E2 = V(o_F + 16 * K, 2048, F32)
    P.op(POOL, lambda e: e.iota(delta[:], pattern=[[1, 2048]], base=-1024, channel_multiplier=-1,
                                allow_small_or_imprecise_dtypes=True), [], [("F", "delta")], cost=4.5)
    Th = V(o_G, 4 * 2048, BF16).rearrange("p (h u) -> p h u", h=4)
    iotaFB = V(o_G + 16 * K, 2048, F32)
    rowtab = V(o_G + 24 * K, 1024, BF16)
    mscr = [V(o_G + 26 * K, 512, F32), V(o_G + 32 * K, 512, F32)]
    ktm = V(o_G + 28 * K, 2048, BF16).rearrange("p (j d f) -> p j d f", j=2, d=2)
    P.op(POOL, lambda e: e.iota(iotaFB[:, 0:1024], pattern=[[1, 1024]], base=1, channel_multiplier=0,
                                allow_small_or_imprecise_dtypes=True), [], [("G", "iota", 0)], cost=2.3)
    P.op(POOL, lambda e: e.iota(iotaFB[:, 1024:2048], pattern=[[-1, 1024]], base=1024, channel_multiplier=0,
                                allow_small_or_imprecise_dtypes=True), [], [("G", "iota", 1)], cost=2.3)

    u8 = small(8)
    l8 = small(8)
    act(tmp8[:], smalls[:, 40:48], AF.Exp, ["smalls"], ["tmp8"], scale=-1.0)
    ts(u8[:], tmp8[:], 1.0, ALU.add, ["tmp8"], ["u8"])
    act(l8[:], u8[:], AF.Ln, ["u8"], ["l8"])
    ts(u8[:], u8[:], -1.0, ALU.add, ["u8"], ["u8"], s2=1e-30, op1=ALU.max)
    P.op(DVE, lambda e: e.reciprocal(out=u8[:], in_=u8[:]), ["u8"], ["u8"], cost=0.2)
    tt(l8[:], l8[:], u8[:], ALU.mult, ["l8", "u8"], ["l8"])
    tt(tmp8[:], tmp8[:], l8[:], ALU.mult, ["tmp8", "l8"], ["tmp8"])
    ts(lg[:], tmp8[:], -1.0, ALU.mult, ["tmp8"], ["lg"])
    cp(nlg[:], tmp8[:], ["tmp8"], ["nlg"])
    act(nA[:], smalls[:, 16:32], AF.Exp, ["smalls"], ["nA"])
    ts(nA[:], nA[:], -1.0, ALU.mult, ["nA"], ["nA"])
    cp(dskb[:], smalls[:, 32:40], ["smalls"], ["dskb"])
    memset(lscb[:], -0.5 * math.log(128.0), ["lscb"])
    for h in range(4):
        act(E1[:], delta[:], AF.Exp, [("F", "delta"), "lg", "lscb"], [("F", "E1")], bias=lscb[:, 0:1], scale=lg[:, h:h + 1])
        act(E2[:], delta[:], AF.Exp, [("F", "delta"), "nlg", "lscb"], [("F", "E2")], bias=lscb[:, 0:1], scale=nlg[:, 4 + h:5 + h])
        for q in range(4):
            qs = slice(q * 512, (q + 1) * 512)
            tt(Th[:, h, qs], E1[:, qs], E2[:, qs], ALU.min, [("F", "E1"), ("F", "E2")], [("G", "Th", h, q)])
    for d in range(2):
        for j in range(2):
            ts(tailw[:, d * 8 + j * 4:d * 8 + j * 4 + 4], lg[:, d * 4:d * 4 + 4], tailpos[:, d * 2 + j:d * 2 + j + 1],
               ALU.mult, ["lg", "tailpos"], [("tailw", d, j)])
    act(tailw[:], tailw[:], AF.Exp, ["tailw", "lscb"], ["tailw"], bias=lscb[:, 0:1])


    qT = V(o_B, 4 * NT, BF16).rearrange("p (h t) -> p h t", h=4)
    kT = V(o_B + 12 * K, 4 * NT, BF16).rearrange("p (h t) -> p h t", h=4)
    v_tm = V(o_C, NTT * 512, BF16).rearrange("p (n f) -> p n f", n=NTT)
    gT = V(o_D, 4 * NT, BF16).rearrange("p (h t) -> p h t", h=4)
    z_tm = V(o_E, NTT * 512, BF16).rearrange("p (n f) -> p n f", n=NTT)
    XW = 1548
    XOFF = [2, 1030, 1290]
    xbc = V(o_F, 8 * XW, BF16).rearrange("p (c t) -> p c t", c=8)
    rope_tmpB = [[V(o_scr, 512, F32), V(o_scr + 2 * K, 512, F32)], [V(o_lnp + 8 * K, 512, F32), V(o_lnp + 10 * K, 512, F32)]]
    nrope = [0]
    stS = V(o_scr + 4 * K, 1024, F32)

    def tok2xcol(ck):
        if ck < 2:
            return [(0, 512, XOFF[0] + ck * 512)]
        return [(0, 256, XOFF[1]), (256, 256, XOFF[2])]

    def fm_proj(s, wv, j, ck, bank):
        for kt in range(8):
            mm(PS(bank), wv[:, kt, j * 128:(j + 1) * 128], hT[:, kt, ck * 512:(ck + 1) * 512], kt == 0, kt == 7,
               [("ring", s), ("A", ck, "hT", kt)], [pk(bank)])

    def tm_proj(s, wv, t, bank, ncol=512, c0=0):
        for kt in range(8):
            mm(PS(bank, ncol), hT[:, kt, t * 128:(t + 1) * 128], wv[:, kt, c0:c0 + ncol], kt == 0, kt == 7,
               [("ring", s), ("A", t // 4, "hT", kt)], [pk(bank)])

    bk = [0]

    def nb4():
        b = bk[0] % 4
        bk[0] += 1
        return b

    s, wv = w_in_get(1024, 512)
    for t in range(NTT):
        b = nb4()
        tm_proj(s, wv, t, b)
        if t % 2 == 0:
            act(v_tm[:, t, :], PS(b), AF.Copy, [pk(b)], [("C", "v", t)])
        else:
            cp(v_tm[:, t, :], PS(b), [pk(b)], [("C", "v", t)])
    for which, dstT in ((0, qT), (1, kT)):
        s, wv = w_in_get(which * 512, 512)
        for j in range(4):
            for ck in range(3):
                b = nb4()
                fm_proj(s, wv, j, ck, b)
                dst = dstT[:, j, ck * 512:(ck + 1) * 512]
                wr = [("B", "qk", which, j, ck)]
                if ck == 2:
                    act(dst, PS(b), AF.Copy, [pk(b)], wr)
                else:
                    rq = nrope[0] % 2
                    nrope[0] += 1
                    tA, tB = rope_tmpB[rq]
                    rn = "SCR" if rq == 0 else "LNP"
                    tsl = slice(ck * 512, (ck + 1) * 512)
                    tt(tA[:], PS(b), cos_t[:, tsl], ALU.mult, [pk(b), ("LNP", "rope")], [(rn, "ropeA")])
                    tt(tB[0:64, :], PS(b)[64:128, :], sin_t[0:64, tsl], ALU.mult, [pk(b), ("LNP", "rope")], [(rn, "ropeB", 0)])
                    tt(tB[64:128, :], PS(b)[0:64, :], sin_t[64:128, tsl], ALU.mult, [pk(b), ("LNP", "rope")], [(rn, "ropeB", 1)])
                    tt(dst, tA[:], tB[:], ALU.add, [(rn, "ropeA"), (rn, "ropeB", 0), (rn, "ropeB", 1)], wr, eng=POOL)
        if which == 1:
            for sq in range(2):
                for jb in range(2):
                    t = 8 + sq * 2 + jb
                    b = nb4()
                    tm_proj(s, wv, t, b)
                    for d in range(2):
                        for h in range(4):
                            c = d * 8 + jb * 4 + h
                            act(ktm[:, jb, d, h * 128:(h + 1) * 128], PS(b, 128, h * 128), AF.Copy, [pk(b), "tailw"],
                                [("G", "ktm", jb, d, h)], scale=tailw[:, c:c + 1])
                for d in range(2):
                    for h in range(4):
                        for jb in range(2):
                            mm(PS(4 + d, 128, h * 128), ktm[:, jb, d, h * 128:(h + 1) * 128],
                               v_tm[:, 8 + sq * 2 + jb, h * 128:(h + 1) * 128], jb == 0, jb == 1,
                               [("G", "ktm", jb, d, h), ("C", "v", 8 + sq * 2 + jb)], [pk(4 + d)])
                    cp(stS[:, d * 512:(d + 1) * 512], PS(4 + d), [pk(4 + d)], [("SCR", "stS", d)])
                dma(SP, oret_d[sq].rearrange("d h p e -> p d h e"), stS[:].rearrange("p (d h e) -> p d h e", d=2, h=4),
                    "oret", [("SCR", "stS", 0), ("SCR", "stS", 1)], [], True)
    tap("qT", qT, [128, 4, NT], ["B"])
    tap("kT", kT, [128, 4, NT], ["B"])
    tap("v", v_tm, [128, NTT, 512], ["C"])
    s, wv = w_in_chunk(1536, 512)
    for j in range(4):
        for ck in range(3):
            b = nb4()
            fm_proj(s, wv, j, ck, b)
            act(gT[:, j, ck * 512:(ck + 1) * 512], PS(b), AF.Silu, [pk(b)], [("D", "g", j, ck)])
    mod_fm(24, 3072, 2)
    s, wv = w_in_chunk(2048, 512)
    for t in range(NTT):
        b = nb4()
        tm_proj(s, wv, t, b)
        act(z_tm[:, t, :], PS(b), AF.Silu, [pk(b)], [("E", "z", t)])
    mod_fm(32, 4096, 2)
    ts(sc2p[:], modT[:, 64:80], 1.0, ALU.add, [("modT", 32)], ["sc2p"])
    fence(["F"])
    for (c0_, c1_) in ((0, 2), (1026, 1030), (1286, 1290), (1546, 1548)):
        memset(xbc[:, :, c0_:c1_], 0.0, [("F", "xbc", "pad", c0_)], eng=POOL)
    for half in range(2):
        ncol = 512 if half == 0 else 520
        s, wv = w_in_chunk(2560 + half * 512, ncol)
        for j in range(4):
            ct = half * 4 + j
            for ck in range(3):
                b = nb4()
                fm_proj(s, wv, j, ck, b)
                for (o, n, xc) in tok2xcol(ck):
                    if (j + ck) % 2 == 0:
                        act(xbc[:, ct, xc:xc + n], PS(b, n, o), AF.Copy, [pk(b)], [("F", "xbc", ct, xc)])
                    else:
                        cp(xbc[:, ct, xc:xc + n], PS(b, n, o), [pk(b)], [("F", "xbc", ct, xc)])
        if half == 0:
            mod_fm(16, 2048, 2)
        if half == 1:
            for t in range(NTT):
                for kt in range(8):
                    mm(PS(7, 8, t * 8), hT[:, kt, t * 128:(t + 1) * 128], wv[:, kt, 512:520], kt == 0, kt == 7,
                       [("ring", s), ("A", t // 4, "hT", kt)], ["ps7"])
            cp(dtraw[:], PS(7, 96), ["ps7"], ["dtraw"])
    mod_fm(40, 5120, 2)
    pre_wo = []
    for half in range(2):
        s, base = ring_load([(w_out_d[:, half * 512:(half + 1) * 512].rearrange("(k p) n -> p k n", p=128), 0, 8, 512)], slot=2 + half)
        pre_wo.append((s, V(base, 8 * 512, BF16).rearrange("p (k n) -> p k n", k=8)))

    def ffn_slot(sl):
        c0 = sl * 256
        return ring_load([(w_gate_d[:, c0:c0 + 256].rearrange("(k p) n -> p k n", p=128), 0, 8, 256),
                          (w_up_d[:, c0:c0 + 256].rearrange("(k p) n -> p k n", p=128), 2048, 8, 256)], slot=sl % 2)
    ffn_pre = [ffn_slot(0), ffn_slot(1)]

    def gate_from_mod(ft0, dsts, dkeys):
        n_ = 0
        for c in range(2):
            for half in range(2):
                b = 4 + (n_ % 2)
                n_ += 1
                for k4 in range(4):
                    kt = half * 4 + k4
                    q = kt % 2
                    ts(dgs[q][:], ident_f[:], modT[:, (ft0 + kt) * 2 + c:(ft0 + kt) * 2 + c + 1], ALU.mult,
                       ["const", ("modT", ft0)], [("SCR", "dgs", q)])
                    mm(PS(b, 128, k4 * 128), ones_f[:], dgs[q][:], True, True, ["const", ("SCR", "dgs", q)], [pk(b)])
                cp(dsts[c][:, half * 512:(half + 1) * 512], PS(b), [pk(b)], [dkeys[c] + (half,)])

    fence(["A", "LNP", "SCR", ("G", "ktm")])
    oT = V(o_A, 8 * NT, BF16).rearrange("p (k t) -> p k t", k=8)
    PTb = [V(o_lnp + i * 8 * K, 8 * 512, BF16).rearrange("p (i t) -> p i t", i=8) for i in range(2)]
    rs_fB = [V(o_scr + 4 * K + i * 2 * K, 512, F32) for i in range(2)]
    sq_bB = [V(o_G + 28 * K + i * K, 512, BF16) for i in range(2)]
    qfb = [V(o_G + 30 * K + i * K, 512, BF16) for i in range(2)]
    S0b = V(o_scr + 8 * K, 1024, BF16).rearrange("p (d h e) -> p d h e", d=2, h=4)
    PTp = [V(o_scr + i * K, 512, BF16).rearrange("p (i t) -> p i t", i=2) for i in range(2)]
    rs_p = [V(o_scr + 2 * K + i * K, 256, F32) for i in range(2)]
    sq_p = [V(o_G + 34 * K, 256, BF16)] * 2
    dma(POOL, S0b, sret_d[:, :].rearrange("p (d h e) -> p d h e", d=2, h=4), "s0r", [], [("SCR", "S0b")])
    nmask = [0]
    ucount = {True: 0, False: 0}

    def ret_unit(t0, nb, h, r0, W):
        is_sample = nb == 8
        tok0 = t0 * 128
        q_ = ucount[is_sample] % 2
        ucount[is_sample] += 1
        if is_sample:
            pt, ptkey, ob, msb = PTb[q_], ("LNP", "PT", q_), 2 + q_, 4 + q_
            rsf_, sqb_, rkey, skey = rs_fB[q_], sq_bB[q_], ("SCR", "rs_f", q_), ("G", "ktm", "sq", q_)
        else:
            pt, ptkey, ob, msb = PTp[q_], ("SCR", "PTp", q_), 6, 7
            rsf_, sqb_, rkey, skey = rs_p[q_], sq_p[q_], ("SCR", "rs_p", q_), ("G", "ktm", "sqp")
        for i in range(nb):
            b = i % 2
            mm(PS(b, W), kT[:, h, tok0 + i * 128:tok0 + (i + 1) * 128], qT[:, h, tok0 + r0:tok0 + r0 + W], True, True,
               [("B", "qk")], [pk(b)])
            u0 = r0 - 128 * i + 1024
            if is_sample and i % 3 == 2:
                mq = nmask[0] % 2
                nmask[0] += 1
                act(mscr[mq][:, 0:W], PS(b, W), AF.Copy, [pk(b)], [("G", "mscr", mq)])
                tt(pt[:, i, 0:W], mscr[mq][:, 0:W], Th[:, h, u0:u0 + W], ALU.mult, [("G", "mscr", mq), ("G", "Th", h)],
                   [ptkey + (i,)], eng=POOL)
            else:
                tt(pt[:, i, 0:W], PS(b, W), Th[:, h, u0:u0 + W], ALU.mult, [pk(b), ("G", "Th", h)], [ptkey + (i,)])
        if is_sample:
            for d in range(2):
                act(rowtab[:, 0:W], iotaFB[:, d * 1024 + r0:d * 1024 + r0 + W], AF.Exp, [("G", "iota"), "lg"],
                    [("G", "rowtab")], scale=lg[:, d * 4 + h:d * 4 + h + 1])
                tt(qfb[d][:, 0:W], qT[:, h, tok0 + r0:tok0 + r0 + W], rowtab[:, 0:W], ALU.mult,
                   [("B", "qk"), ("G", "rowtab")], [("G", "ktm", "qf", d)])
        nmm = nb + (2 if is_sample else 0)
        for i in range(nb):
            mm(PS(ob, W), v_tm[:, t0 + i, h * 128:(h + 1) * 128], pt[:, i, 0:W], i == 0, i == nmm - 1,
               [("C", "v", t0 + i), ptkey + (i,)], [pk(ob)])
        if is_sample:
            for d in range(2):
                mm(PS(ob, W), S0b[:, d, h, :], qfb[d][:, 0:W], False, d == 1,
                   [("SCR", "S0b"), ("G", "ktm", "qf", d)], [pk(ob)])
        act(sqb_[:, 0:W], PS(ob, W), AF.Square, [pk(ob)], [skey])
        mm(PS(msb, W), ones_b[:], sqb_[:, 0:W], True, True, [skey, "constb"], [pk(msb)])
        rstd_from(rsf_[:, 0:W], PS(msb, W), 1.0 / 128.0, [pk(msb)], [rkey])
        tt(rsf_[:, 0:W], PS(ob, W), rsf_[:, 0:W], ALU.mult, [pk(ob), rkey], [rkey])
        tt(oT[:, h, tok0 + r0:tok0 + r0 + W], rsf_[:, 0:W], gT[:, h, tok0 + r0:tok0 + r0 + W], ALU.mult,
           [rkey, ("D", "g")], [("A", (tok0 + r0) // 512, "oT", h, t0, r0)], eng=POOL)

    s_units = [(0, 8, h, r0, 512) for h in range(4) for r0 in (0, 512)]
    p_units = [(t0, 2, h, 0, 256) for t0 in (8, 10) for h in range(4)]
    for su, pu in zip(s_units, p_units):
        ret_unit(*su)
        ret_unit(*pu)
    tap("oTr", oT[:, 0:4, :], [128, 4, NT], ["A"])

    fence(["B", "C", "D", "G", "SCR", "LNP"])
    cstS = V(o_lnp, 5120, BF16)
    dma(POOL, cstS[:], cb_d[:, CB_MF:CB_MF + 5120], "cstS", [], [("LNP", "cstS")])
    xs_tm = V(o_B, NTT * 512, BF16).rearrange("p (n f) -> p n f", n=NTT)
    B_tm = V(o_B + 12 * K, NTT * 256, BF16).rearrange("p (n f) -> p n f", n=NTT)
    BT = V(o_B + 18 * K, 2 * NT, BF16).rearrange("p (g t) -> p g t", g=2)
    CT = V(o_C, 2 * NT, BF16).rearrange("p (g t) -> p g t", g=2)
    diagw = V(o_G, 8 * 5 * 128, BF16).rearrange("p (c j m) -> p c j m", c=8, j=5)
    xsT = V(o_G + 12 * K, 4 * NT, BF16).rearrange("p (c t) -> p c t", c=4)
    nd_ = 0
    for ct in range(8):
        for j in range(5):
            sc_ = cols[:, C_CONVW + ct * 5 + j:C_CONVW + ct * 5 + j + 1]
            if nd_ % 3 == 0:
                ts(diagw[:, ct, j, :], ident_f[:], sc_, ALU.mult, ["const", "cols"], [("G", "diagw", ct, j)])
            elif nd_ % 3 == 1:
                act(diagw[:, ct, j, :], ident_f[:], AF.Copy, ["const", "cols"], [("G", "diagw", ct, j)], scale=sc_)
            else:
                ts(diagw[:, ct, j, :], ident_f[:], sc_, ALU.mult, ["const", "cols"], [("G", "diagw", ct, j)], eng=POOL)
            nd_ += 1
    nbk = 0
    for ct in range(8):
        for ck in range(3):
            for (o, n, xc) in tok2xcol(ck):
                bank = nbk % 4
                nbk += 1
                for j in range(5):
                    mm(PS(bank, n), diagw[:, ct, j, :], xbc[:, ct, xc + j - 2:xc + j - 2 + n], j == 0, j == 4,
                       [("F", "xbc"), ("G", "diagw", ct)], [pk(bank)])
                if ct < 4:
                    dstT, g, key = xsT, ct, ("G", "xsT", ct, ck, o)
                elif ct < 6:
                    dstT, g, key = BT, ct - 4, ("B", "BT", ct - 4, ck, o)
                else:
                    dstT, g, key = CT, ct - 6, ("C", "CT", ct - 6, ck, o)
                act(dstT[:, g, ck * 512 + o:ck * 512 + o + n], PS(bank, n), AF.Silu, [pk(bank), "cols"], [key],
                    bias=cols[:, C_CONVB + ct:C_CONVB + ct + 1])
    for t in range(NTT):
        bx = 4 + 2 * (t % 2)
        for c in range(4):
            mm(PS(bx, 128, c * 128), xsT[:, c, t * 128:(t + 1) * 128], ident_b[:], True, True,
               [("G", "xsT", c), "constb"], [pk(bx)])
        for c in range(2):
            mm(PS(bx + 1, 128, c * 128), BT[:, c, t * 128:(t + 1) * 128], ident_b[:], True, True,
               [("B", "BT", c), "constb"], [pk(bx + 1)])
        cp(xs_tm[:, t, :], PS(bx), [pk(bx)], [("B", "xs", t)])
        act(B_tm[:, t, :], PS(bx + 1, 256), AF.Copy, [pk(bx + 1)], [("B", "Btm", t)])
    tap("xsb", V(o_B, NTT * 768, BF16).rearrange("p (n f) -> p n f", n=NTT), [128, NTT, 768], ["B"]) if False else None
    tap("BCT", V(o_B + 18 * K, 4 * NT, BF16).rearrange("p (g t) -> p g t", g=4), [128, 4, NT], ["B", "C"])

    rsT = V(o_scr + 7 * K, NT, BF16)
    pool_off = [o_G + 24 * K]

    def pl(n, dt=F32):
        a = V(pool_off[0], n, dt)
        pool_off[0] += n * SZ[dt]
        assert pool_off[0] <= o_G + 35 * K - 1024
        return a
    X = pl(192).rearrange("p (b d) -> p b d", b=NTT)
    AX = pl(192).rearrange("p (b d) -> p b d", b=NTT)
    DT = pl(192).rearrange("p (b d) -> p b d", b=NTT)
    LA = pl(192).rearrange("p (b d) -> p b d", b=NTT)
    LNDT = pl(192).rearrange("p (b d) -> p b d", b=NTT)
    CUM = pl(192).rearrange("p (b d) -> p b d", b=NTT)
    TOT = pl(192).rearrange("p (b d) -> p b d", b=NTT)
    RSRC = pl(384).rearrange("p (b d) -> p b d", b=NTT)
    EXPO = V(o_scr + 4 * K, 576, F32).rearrange("p (b d) -> p b d", b=NTT)
    EXPIN = V(o_D, 576, F32).rearrange("p (b d) -> p b d", b=NTT)
    SPL3 = V(o_D + 2304, 1152, F32).rearrange("p (b d) -> p b d", b=NTT)
    R1 = V(o_D + 2304 + 4608, 384, F32).rearrange("p (b d) -> p b d", b=NTT)
    H1B = V(o_D + 2304 + 4608 + 1536, 384, BF16).rearrange("p (b d) -> p b d", b=NTT)
    PF = [("G", "pool")]
    PD = [("D", "pool")]
    dtr3 = dtraw[:].rearrange("p (b h) -> p b h", b=NTT)
    X4 = X.rearrange("p b (d h) -> p b d h", d=2)
    tt(X4, dtr3.unsqueeze(2).to_broadcast([128, NTT, 2, 8]),
       smalls[:, 0:16].rearrange("p (d h) -> p d h", d=2).unsqueeze(1).to_broadcast([128, NTT, 2, 8]), ALU.add,
       ["dtraw", "smalls"], PF)
    UU = pl(192).rearrange("p (b d) -> p b d", b=NTT)
    LL = pl(192).rearrange("p (b d) -> p b d", b=NTT)
    act(AX, X, AF.Abs, PF, PF)
    act(AX, AX, AF.Exp, PF, PF, scale=-1.0)
    ts(UU, AX, 1.0, ALU.add, PF, PF)
    act(LL, UU, AF.Ln, PF, PF)
    ts(UU, UU, -1.0, ALU.add, PF, PF, s2=1e-30, op1=ALU.max)
    P.op(DVE, lambda e: e.reciprocal(out=UU, in_=UU), PF, PF, cost=0.3)
    tt(LL, LL, UU, ALU.mult, PF, PF)
    tt(AX, AX, LL, ALU.mult, PF, PF)
    ts(X, X, 0.0, ALU.max, PF, PF)
    tt(DT, X, AX, ALU.add, PF, PF)
    ts(DT, DT, 1e-30, ALU.max, PF, PF)
    tt(LA, DT, nA[:].unsqueeze(1).to_broadcast([128, NTT, 16]), ALU.mult, PF + ["nA"], PF)
    act(LNDT, DT, AF.Ln, PF, PF)
    for b in range(NTT):
        mm(PS(0, 16, b * 16), utri_f[:], LA[:, b, :], True, True, PF + ["const"], ["ps0"])
        mm(PS(1, 16, b * 16), ones_f[:], LA[:, b, :], True, True, PF + ["const"], ["ps1"])
    cp(CUM, PS(0, 192).rearrange("p (b d) -> p b d", b=NTT), ["ps0"], PF)
    cp(TOT, PS(1, 192).rearrange("p (b d) -> p b d", b=NTT), ["ps1"], PF)
    cp(RSRC[:, :, 0:8], CUM[:, :, 0:8], PF, PF)
    tt(RSRC[:, :, 8:16], LA[:, :, 8:16], CUM[:, :, 8:16], ALU.subtract, PF, PF)
    tt(RSRC[:, :, 16:24], LNDT[:, :, 0:8], CUM[:, :, 0:8], ALU.subtract, PF, PF)
    tt(RSRC[:, :, 24:32], LNDT[:, :, 8:16], RSRC[:, :, 8:16], ALU.subtract, PF, PF)
    cp(EXPIN[:, :, 0:8], RSRC[:, :, 0:8], PF, PD)
    tt(EXPIN[:, :, 8:16], TOT[:, :, 8:16], RSRC[:, :, 8:16], ALU.add, PF, PD)
    tt(EXPIN[:, :, 16:24], TOT[:, :, 0:8], RSRC[:, :, 16:24], ALU.add, PF, PD)
    cp(EXPIN[:, :, 24:32], RSRC[:, :, 24:32], PF, PD)
    cp(EXPIN[:, :, 32:48], TOT, PF, PD)
    act(EXPO, EXPIN, AF.Exp, PD, [("SCR", "expo")])
    cp(H1B, RSRC, PF, PD)
    cp(SPL3[:, :, 0:32], H1B, PD, PD)
    tt(R1, RSRC, SPL3[:, :, 0:32], ALU.subtract, PF + PD, PD)
    cp(H1B, R1, PD, PD)
    cp(SPL3[:, :, 32:64], H1B, PD, PD)
    tt(R1, R1, SPL3[:, :, 32:64], ALU.subtract, PD, PD)
    cp(H1B, R1, PD, PD)
    cp(SPL3[:, :, 64:96], H1B, PD, PD)
    for ck in range(3):
        for j in range(4):
            tr(PS(2 + ck % 2, 128, j * 128)[0:96, :], SPL3[:, ck * 4 + j, :], PD, [pk(2 + ck % 2)])
        cp(rsT[0:96, ck * 512:(ck + 1) * 512], PS(2 + ck % 2)[0:96, :], [pk(2 + ck % 2)], [("SCR", "rsT", ck)])
    fence(["D", "F", "G", "SCR"])
    MF4, MB4 = cstS[:, 0:512], cstS[:, 512:1024]
    selrow = cstS[:, 1024:3072].rearrange("p (a m) -> p a m", a=16)
    selbias = cstS[:, 3072:5120].rearrange("p (d n) -> p d n", d=2)
    dI = V(o_G + 30 * K, 1024, BF16).rearrange("p (h m) -> p h m", h=8)
    for h in range(8):
        ts(dI[:, h, :], ident_f[:], dskb[:, h:h + 1], ALU.mult, ["const", "dskb"], [("G", "dI")])
    nwb = V(o_G + 32 * K, 512, F32)
    dma(SP, nwb[:], bcast_row(R_NW, 512), "nwb", [], [("G", "nwb")])
    WfB = [V(o_G + i * 11 * K, 1024, BF16) for i in range(2)]
    WbB = [V(o_G + i * 11 * K + 2 * K, 1024, BF16) for i in range(2)]
    PmB = [V(o_G + i * 11 * K + 4 * K, 1024, BF16).rearrange("p (h t) -> p h t", h=8) for i in range(2)]
    y1B = [V(o_G + i * 11 * K + 6 * K, 512, F32) for i in range(2)]
    y2B = [V(o_G + i * 11 * K + 8 * K, 512, F32) for i in range(2)]
    jkB = [V(o_G + i * 11 * K + 10 * K, 512, BF16) for i in range(2)]
    xswB = [[V(o_G + 22 * K + (d * 2 + i) * K, 512, BF16) for i in range(2)] for d in range(2)]
    Sst = [V(o_G + 26 * K, 512, F32), V(o_G + 28 * K, 512, F32)]
    ssqB = [small(1) for _ in range(2)]
    rstdB = [small(1) for _ in range(2)]
    fence(["D"], "ssdpre")
    Rall = V(o_D, 8 * 512, BF16).rearrange("p (b f) -> p b f", b=8)
    SallA = V(o_D + 8 * K, 4 * 512, BF16).rearrange("p (b f) -> p b f", b=4)
    SallB = V(o_C + 6 * K, 4 * 512, BF16).rearrange("p (b f) -> p b f", b=4)

    def Sall(i):
        return SallA[:, i, :] if i < 4 else SallB[:, i - 4, :]

    def Skey(i):
        return ("D", "S", i) if i < 4 else ("C", "S", i)
    psum2 = lambda b0: psum[:, b0 * 512:b0 * 512 + 1024]

    def bc8(ap8):
        return ap8.unsqueeze(2).to_broadcast([128, 8, 64])

    def v8(ap512):
        return ap512.rearrange("p (h e) -> p h e", h=8)

    nxsw = [0, 0]

    def state_update(d, b, bank):
        w = EXPO[:, b, 16 + d * 8:24 + d * 8]
        k = nxsw[d] % 2
        nxsw[d] += 1
        xw = xswB[d][k]
        tt(v8(xw[:]), v8(xs_tm[:, b, :]), bc8(w), ALU.mult, [("B", "xs", b), ("SCR", "expo")], [("G", "xsw", d, k)], eng=POOL)
        for g in range(2):
            mm(PS(bank, 256, g * 256), B_tm[:, b, g * 128:(g + 1) * 128], xw[:, g * 256:(g + 1) * 256], True, True,
               [("B", "Btm", b), ("G", "xsw", d, k)], [pk(bank)])
        tt(v8(Sst[d][:]), v8(Sst[d][:]), bc8(EXPO[:, b, 32 + d * 8:40 + d * 8]), ALU.mult, [("G", "S", d), ("SCR", "expo")],
           [("G", "S", d)], eng=POOL)
        tt(Sst[d][:], Sst[d][:], PS(bank), ALU.add, [("G", "S", d), pk(bank)], [("G", "S", d)])

    RallS = V(o_scr, 2 * 512, BF16).rearrange("p (b f) -> p b f", b=2)
    SallS = V(o_scr + 2 * K, 2 * 512, BF16).rearrange("p (b f) -> p b f", b=2)

    def Sv(big, i):
        return (Sall(i), Skey(i)) if big else (SallS[:, i, :], ("SCR", "S", i))

    def Rv(big, i):
        return (Rall[:, i, :], ("D", "R", i)) if big else (RallS[:, i, :], ("SCR", "R", i))

    def chain_steps(si):
        t0, nb = SEQS[si]
        big = nb == 8
        if big:
            dma(SP, Sst[0][:], sssd_d[:, 0:512], "s0s0", [], [("G", "S", 0)])
            dma(SP, Sst[1][:], sssd_d[:, 512:1024], "s0s1", [], [("G", "S", 1)])
        else:
            memset(Sst[0][:], 0.0, [("G", "S", 0)])
            memset(Sst[1][:], 0.0, [("G", "S", 1)])
        for step in range(nb):
            i_f = step
            i_b = nb - 1 - step
            sv, sk = Sv(big, i_f)
            rv, rk = Rv(big, i_b)
            act(sv, Sst[0][:], AF.Copy, [("G", "S", 0)], [sk])
            act(rv, Sst[1][:], AF.Copy, [("G", "S", 1)], [rk])
            if i_f < nb - 1 or not big:
                state_update(0, t0 + i_f, 6)
            if i_b > 0 or not big:
                state_update(1, t0 + i_b, 7)
            yield
        if not big:
            sq = si - 1
            for d in range(2):
                dma(SP, ossd_d[sq, d].rearrange("h n e -> n h e"), Sst[d][:].rearrange("p (h e) -> p h e", h=8), "ossd%d" % d,
                    [("G", "S", d)], [], True)
        yield

    nblk = [0]
    kbof = {}

    def stageA(si, i):
        t0, nb = SEQS[si]
        b = t0 + i
        tok = b * 128
        kb = nblk[0] % 2
        nblk[0] += 1
        kbof[b] = kb
        Wf, Wb, Pm = WfB[kb], WbB[kb], PmB[kb]
        bk_ = lambda n: ("G", n, kb)
        for g in range(2):
            mm(PS(4, 128, g * 128), BT[:, g, tok:tok + 128], CT[:, g, tok:tok + 128], True, True,
               [("B", "BT", g), ("C", "CT", g)], ["ps4"])
        for d in range(2):
            bank0 = 2 * d
            wr = [pk(bank0), pk(bank0 + 1)]
            Md = MF4 if d == 0 else MB4
            for half in range(2):
                mm(PS(bank0 + half), ident_b[:], Md, True, False, ["constb", ("LNP", "cstS")], wr)
                mm(PS(bank0 + half), rsT[0:96, tok:tok + 128], selbias[0:96, d, half * 512:(half + 1) * 512], False, False,
                   [("SCR", "rsT", b // 4), ("LNP", "cstS")], wr)
            for h in range(8):
                mm(PS(bank0 + h // 4, 128, (h % 4) * 128), selrow[0:96, d * 8 + h, :], rsT[0:96, tok:tok + 128], False, h % 4 == 3,
                   [("SCR", "rsT", b // 4), ("LNP", "cstS")], wr)
            act((Wf if d == 0 else Wb)[:], psum2(bank0), AF.Exp, wr, [bk_("W%d" % d)])
        tt(Wf[:], Wf[:], Wb[:], ALU.add, [bk_("W0"), bk_("W1")], [bk_("W0")])
        tt(Pm.rearrange("p (g q) t -> p g q t", g=2), Wf[:].rearrange("p (g q t) -> p g q t", g=2, q=4),
           PS(4, 256).rearrange("p (g t) -> p g t", g=2).unsqueeze(2).to_broadcast([128, 2, 4, 128]), ALU.mult,
           [bk_("W0"), "ps4"], [bk_("Pm")])

    def stageB(si, i):
        t0, nb = SEQS[si]
        big = nb == 8
        b = t0 + i
        tok = b * 128
        kb = kbof[b]
        Pm, y1, y2, jk = PmB[kb], y1B[kb], y2B[kb], jkB[kb]
        ssq_, rstd_ = ssqB[kb], rstdB[kb]
        bk_ = lambda n: ("G", n, kb)
        sv, sk = Sv(big, i)
        rv, rk = Rv(big, i)
        for h in range(8):
            mm(PS(5, 64, h * 64), Pm[:, h, :], xs_tm[:, b, h * 64:(h + 1) * 64], True, False,
               [bk_("Pm"), ("B", "xs", b)], ["ps5"])
            mm(PS(5, 64, h * 64), dI[:, h, :], xs_tm[:, b, h * 64:(h + 1) * 64], False, True,
               [("G", "dI"), ("B", "xs", b)], ["ps5"])
        for g in range(2):
            mm(PS(6, 256, g * 256), CT[:, g, tok:tok + 128], sv[:, g * 256:(g + 1) * 256], True, True,
               [("C", "CT", g), sk], ["ps6"])
            mm(PS(7, 256, g * 256), CT[:, g, tok:tok + 128], rv[:, g * 256:(g + 1) * 256], True, True,
               [("C", "CT", g), rk], ["ps7"])
        tt(v8(y1[:]), v8(PS(6)), bc8(EXPO[:, b, 0:8]), ALU.mult, ["ps6", ("SCR", "expo")], [bk_("y1")])
        tt(v8(y2[:]), v8(PS(7)), bc8(EXPO[:, b, 8:16]), ALU.mult, ["ps7", ("SCR", "expo")], [bk_("y2")])
        tt(y1[:], y1[:], PS(5), ALU.add, [bk_("y1"), "ps5"], [bk_("y1")])
        tt(y1[:], y1[:], y2[:], ALU.add, [bk_("y1"), bk_("y2")], [bk_("y1")])
        tt(y1[:], y1[:], z_tm[:, b, :], ALU.mult, [bk_("y1"), ("E", "z", b)], [bk_("y1")], eng=POOL)
        act(jk[:], y1[:], AF.Square, [bk_("y1")], [bk_("jk"), ("ssq", kb)], accum_out=ssq_[:, 0:1])
        rstd_from(rstd_[:, 0:1], ssq_[:, 0:1], 1.0 / 512.0, [("ssq", kb)], [("rstd", kb)])
        stt(y2[:], y1[:], rstd_[:, 0:1], nwb[:], ALU.mult, ALU.mult, [bk_("y1"), ("rstd", kb), ("G", "nwb")], [bk_("y2")])

    def stageC(si, i):
        t0, nb = SEQS[si]
        b = t0 + i
        tok = b * 128
        kb = kbof[b]
        y2 = y2B[kb]
        for j in range(4):
            tr(PS(4, 128, j * 128), y2[:, j * 128:(j + 1) * 128], [("G", "y2", kb)], ["ps4"])
        act(oT[:, 4:8, tok:tok + 128], PS(4).rearrange("p (j t) -> p j t", j=4), AF.Copy, ["ps4"], [("A", b // 4, "oT", 4, b)])

    seq_order = [1, 0, 2]
    blocks = [(si, i) for si in seq_order for i in range(SEQS[si][1])]
    first_of = {}
    for n_, (si, i) in enumerate(blocks):
        first_of.setdefault(si, n_)
    for _ in chain_steps(seq_order[0]):
        pass
    pending_chain = None
    nbk = len(blocks)
    for step in range(nbk + 2):
        if step < nbk:
            si, i = blocks[step]
            if i == 0:
                pos = seq_order.index(si)
                if pos + 1 < len(seq_order):
                    pending_chain = chain_steps(seq_order[pos + 1])
            stageA(si, i)
        if 1 <= step <= nbk:
            stageB(*blocks[step - 1])
        if 2 <= step:
            stageC(*blocks[step - 2])
        if pending_chain is not None:
            si_cur = blocks[min(step, nbk - 1)][0]
            nsteps = 4 if SEQS[si_cur][1] == 2 else 1
            for _ in range(nsteps):
                try:
                    next(pending_chain)
                except StopIteration:
                    pending_chain = None
                    break
    tap("oT", oT, [128, 8, NT], ["A"])

    fence(["B", "C", "D", "E", "F", "G", "LNP", "SCR"])
    x1 = [V(o_B + t * 4 * K, 1024, F32) for t in range(NTT)]
    xs2 = [V(o_G + j * 4 * K, 1024, F32) for j in range(2)]
    xs2k = [[("G", "xs2", 0)], [("G", "xs2", 1)]]
    lng = V(o_lnp, 1024, F32)
    lnb = V(o_lnp + 4 * K, 1024, F32)
    gateb = [V(o_lnp + 8 * K + c * 4 * K, 1024, F32) for c in range(2)]

    rstdL = [small(1) for _ in range(2)]
    nmrB = [small(1) for _ in range(2)]

    s1B = [small(1) for _ in range(2)]
    s2B = [small(1) for _ in range(2)]
    mB = [small(1) for _ in range(2)]
    vB = [small(1) for _ in range(2)]
    ljunk = V(o_scr, 1024, BF16)

    def ln_affine(t, dst, dst_key, lng, lnb, lkey):
        tt(dst[:], dst[:], lng[:], ALU.mult, [dst_key, lkey + ("g",)], [dst_key], eng=POOL)
        tt(dst[:], dst[:], lnb[:], ALU.add, [dst_key, lkey + ("b",)], [dst_key], eng=POOL)

    def layer_norm_tile(t, banks, xres, xres_key, dst, dst_key, lng, lnb, gateb, gkey, lkey, defer_affine=False):
        cond = 0 if t < 8 else 1
        q = t % 2
        s1, s2, m_, v_, rstd, nmr = s1B[q], s2B[q], mB[q], vB[q], rstdL[q], nmrB[q]
        u = dst
        xk = xres_key if isinstance(xres_key, list) else [xres_key]
        for half in range(2):
            hs = slice(half * 512, (half + 1) * 512)
            tt(u[:, hs], PS(banks[half]), gateb[cond][:, hs], ALU.mult, [pk(banks[half]), gkey + (cond, half)], [dst_key + (half,)])
            stt(u[:, hs], xres[:, hs], ALPHA, u[:, hs], ALU.mult, ALU.add, xk + [dst_key + (half,)], [dst_key + (half,)])
        act(ljunk[:], u[:], AF.Copy, [dst_key], [("SCR", "ljunk"), ("s1", q)], accum_out=s1[:, 0:1])
        act(ljunk[:], u[:], AF.Square, [dst_key], [("SCR", "ljunk"), ("s2", q)], accum_out=s2[:, 0:1])
        ts(m_[:, 0:1], s1[:, 0:1], 1.0 / 1024.0, ALU.mult, [("s1", q)], [("m", q)])
        tt(v_[:, 0:1], m_[:, 0:1], m_[:, 0:1], ALU.mult, [("m", q)], [("v", q)])
        stt(v_[:, 0:1], s2[:, 0:1], 1.0 / 1024.0, v_[:, 0:1], ALU.mult, ALU.subtract, [("s2", q), ("v", q)], [("v", q)])
        rstd_from(rstd[:, 0:1], v_[:, 0:1], 1.0, [("v", q)], [("rstdL", q)])
        ts(nmr[:, 0:1], m_[:, 0:1], rstd[:, 0:1], ALU.mult, [("m", q), ("rstdL", q)], [("nmr", q)], s2=-1.0, op1=ALU.mult)
        act(u[:], u[:], AF.Identity, [dst_key, ("rstdL", q), ("nmr", q)], [dst_key], bias=nmr[:, 0:1], scale=rstd[:, 0:1])
        if not defer_affine:
            ln_affine(t, dst, dst_key, lng, lnb, lkey)

    dma(SP, lng[:], bcast_row(R_LN1G, 1024), "lnpg", [], [("LNP", "ln", "g")])
    dma(SP, lnb[:], bcast_row(R_LN1B, 1024), "lnpb", [], [("LNP", "ln", "b")])
    gate_from_mod(16, gateb, [("LNP", "gate", 0), ("LNP", "gate", 1)])
    wo = pre_wo
    for t in range(NTT):
        j = t % 2
        dma(SP, xs2[j][:], x_t[t], "xs2%d" % j, [], xs2k[j])
        banks = (2 * (t % 4), 2 * (t % 4) + 1)
        for half in range(2):
            s, wv = wo[half]
            for kt in range(8):
                mm(PS(banks[half]), oT[:, kt, t * 128:(t + 1) * 128], wv[:, kt, :], kt == 0, kt == 7,
                   [("ring", s), ("A", t // 4)], [pk(banks[half])])
        layer_norm_tile(t, banks, xs2[j], xs2k[j], x1[t], ("B", "x1", t), lng, lnb, gateb, ("LNP", "gate"), ("LNP", "ln"), defer_affine=True)
    tap("x1", V(o_B, NTT * 1024, F32).rearrange("p (n f) -> p n f", n=NTT), [128, NTT, 1024], ["B"])

    sc2p3 = sc2p[:].rearrange("p (k c) -> p k c", c=2)
    sh2 = modT[:, 48:64].rearrange("p (k c) -> p k c", c=2)
    scale2 = small(16)
    bias2 = small(16)
    tt(scale2[:].rearrange("p (k c) -> p k c", c=2), sc2p3, cols[:, C_LN1G:C_LN1G + 8].unsqueeze(2).to_broadcast([128, 8, 2]), ALU.mult,
       ["sc2p", "cols"], ["scale2"])
    tt(bias2[:].rearrange("p (k c) -> p k c", c=2), sc2p3, cols[:, C_LN1B:C_LN1B + 8].unsqueeze(2).to_broadcast([128, 8, 2]), ALU.mult,
       ["sc2p", "cols"], ["bias2"])
    tt(bias2[:], bias2[:], modT[:, 48:64], ALU.add, ["bias2", ("modT", 24)], ["bias2"])
    scale23 = scale2[:].rearrange("p (k c) -> p k c", c=2)
    bias23 = bias2[:].rearrange("p (k c) -> p k c", c=2)
    h2T = V(o_A, 8 * NT, BF16).rearrange("p (k t) -> p k t", k=8)
    aT = V(o_E, 22 * NT, BF16).rearrange("p (f t) -> p f t", f=22)
    sg = [V(o_scr + 2 * K + i * K, 512, BF16) for i in range(2)]
    def h2t_chunk(ck, extra=()):
        cond = 0 if ck < 2 else 1
        fence([("A", ck)])
        for kt in range(8):
            b = kt % 4
            for j in range(4):
                tr(PS(b, 128, j * 128), x1[ck * 4 + j][:, kt * 128:(kt + 1) * 128], [("B", "x1", ck * 4 + j)] + list(extra), [pk(b)])
            if kt % 2 == 0:
                act(h2T[:, kt, ck * 512:(ck + 1) * 512], PS(b), AF.Identity, [pk(b), "scale2", "bias2"],
                    [("A", ck, "h2T", kt)], bias=bias23[:, kt, cond:cond + 1], scale=scale23[:, kt, cond:cond + 1])
            else:
                ts(h2T[:, kt, ck * 512:(ck + 1) * 512], PS(b), scale23[:, kt, cond:cond + 1], ALU.mult, [pk(b), "scale2", "bias2"],
                   [("A", ck, "h2T", kt)], s2=bias23[:, kt, cond:cond + 1], op1=ALU.add)

    npair = [0]

    def ffn_group(s, wg, wu, ft, jt, ck, order_key=None):
        bg = 2 * (npair[0] % 4)
        bu = bg + 1
        si_ = npair[0] % 2
        npair[0] += 1
        for kt in range(8):
            mm(PS(bg), wg[:, kt, jt * 128:(jt + 1) * 128], h2T[:, kt, ck * 512:(ck + 1) * 512], kt == 0, kt == 7,
               [("ring", s), ("A", ck, "h2T", kt)], [pk(bg)])
        for kt in range(8):
            mm(PS(bu), wu[:, kt, jt * 128:(jt + 1) * 128], h2T[:, kt, ck * 512:(ck + 1) * 512], kt == 0, kt == 7,
               [("ring", s), ("A", ck, "h2T", kt)], [pk(bu)] + ([order_key] if (order_key and kt == 7) else []))
        act(sg[si_][:], PS(bg), AF.Silu, [pk(bg)], [("SCR", "sg", si_)])
        tt(aT[:, ft, ck * 512:(ck + 1) * 512], sg[si_][:], PS(bu), ALU.mult, [("SCR", "sg", si_), pk(bu)], [("E", "aT", ft, ck)])

    def ffn_views(base):
        return (V(base, 2048, BF16).rearrange("p (k n) -> p k n", k=8), V(base + 4096, 2048, BF16).rearrange("p (k n) -> p k n", k=8))

    h2t_chunk(0)
    h2t_chunk(1)
    s0_, base0_ = ffn_pre[0]
    wg0_, wu0_ = ffn_views(base0_)
    for jt in range(2):
        for ck in range(2):
            ffn_group(s0_, wg0_, wu0_, jt, jt, ck, order_key=("ffnorder",))
    h2t_chunk(2, extra=[("ffnorder",)])
    assert o_E + 12 * 3072 <= o_G and o_scr + 2 * K >= o_scr + 2048
    for t in range(NTT):
        ln_affine(t, x1[t], ("B", "x1", t), lng, lnb, ("LNP", "ln"))
    fence(["LNP"])
    def wd_load(sl):
        nft = 4 if sl < 5 else 2
        src = w_down_d[sl * 512:sl * 512 + nft * 128, :].rearrange("(k p) n -> p k n", p=128)
        if sl in (2, 3):
            base = o_lnp + (sl - 2) * 8 * K
            key = ("LNP", "wd", sl)
            dma(POOL, V(base, nft * 1024, BF16).rearrange("p (k n) -> p k n", k=nft), src, "wd%d" % sl, [], [key])
        else:
            s, base = ring_load([(src, 0, nft, 1024)], slot={0: 2, 1: 3, 4: 0, 5: 1}[sl])
            key = ("ring", s)
        wvd = V(base, nft * 1024, BF16).rearrange("p (k n) -> p k n", k=nft)
        return [(key, wvd[:, k_, :]) for k_ in range(nft)]
    wd_part = {sl: wd_load(sl) for sl in (0, 1, 2, 3)}
    for jt in range(2):
        ffn_group(s0_, wg0_, wu0_, jt, jt, 2)
    for sl in range(1, 11):
        if sl == 6:
            fence(["E", "G"])
        s, base = ffn_pre[sl] if sl < 2 else ffn_slot(sl)
        wg, wu = ffn_views(base)
        for jt in range(2):
            for ck in range(3):
                ffn_group(s, wg, wu, sl * 2 + jt, jt, ck)

    fence(["A", "SCR"])
    lng2 = V(o_A + 8 * K, 1024, F32)
    lnb2 = V(o_A + 12 * K, 1024, F32)
    gateb2 = [V(o_A + 16 * K + c_ * 4 * K, 1024, F32) for c_ in range(2)]
    dma(SP, lng2[:], bcast_row(R_LN2G, 1024), "lnpg", [], [("A", "ln", "g")])
    dma(SP, lnb2[:], bcast_row(R_LN2B, 1024), "lnpb", [], [("A", "ln", "b")])
    gate_from_mod(40, gateb2, [("A", "gate", 0), ("A", "gate", 1)])
    ystage = [V(o_A + j * 4 * K, 1024, F32) for j in range(2)]
    for sl in (4, 5):
        wd_part[sl] = wd_load(sl)
    wd = []
    for sl in range(6):
        wd += wd_part[sl]
    y_t = y_d.rearrange("(n p) f -> n p f", p=128)
    for t in range(NTT):
        j = t % 2
        banks = (4 * j + 2, 4 * j + 3)
        for half in range(2):
            for ft in range(22):
                key, w = wd[ft]
                mm(PS(banks[half]), aT[:, ft, t * 128:(t + 1) * 128], w[:, half * 512:(half + 1) * 512], ft == 0, ft == 21,
                   [key, ("E", "aT", ft, t // 4)], [pk(banks[half])])
        if t < NTT - 1:
            layer_norm_tile(t, banks, x1[t], ("B", "x1", t), ystage[j], ("A", "ystage", j), lng2, lnb2, gateb2, ("A", "gate"), ("A", "ln"))
            dma(SP, y_t[t], ystage[j][:], "yout%d" % j, [("A", "ystage", j)], [], True)
        else:
            layer_norm_tile(t, banks, x1[t], ("B", "x1", t), ystage[j], ("A", "ystage", j), lng2, lnb2, gateb2, ("A", "gate"), ("A", "ln"),
                            defer_affine=True)
            for half in range(2):
                hs = slice(half * 512, (half + 1) * 512)
                eng_ = POOL if half == 0 else DVE
                kh = ("A", "ystage", j, half)
                tt(ystage[j][:, hs], ystage[j][:, hs], lng2[:, hs], ALU.mult, [kh, ("A", "ln", "g")], [kh], eng=eng_)
                tt(ystage[j][:, hs], ystage[j][:, hs], lnb2[:, hs], ALU.add, [kh, ("A", "ln", "b")], [kh], eng=eng_)
                dma(SP, y_t[t][:, hs], ystage[j][:, hs], "ylast%d" % half, [kh], [], True)
    P.emit(st)
    build.P = P
    return nc


_CACHE = {}


def _prep_inputs(inp):
    f = lambda a: np.ascontiguousarray(np.asarray(a, dtype=np.float32))
    cf, cb = _host_consts()
    rows = np.zeros((1, R_N), np.float32)
    rows[0, R_LN1G:R_LN1G + 1024] = f(inp["ln1_g"])[0]
    rows[0, R_LN1B:R_LN1B + 1024] = f(inp["ln1_b"])[0]
    rows[0, R_LN2G:R_LN2G + 1024] = f(inp["ln2_g"])[0]
    rows[0, R_LN2B:R_LN2B + 1024] = f(inp["ln2_b"])[0]
    rows[0, R_NW:R_NW + 512] = f(inp["ssd_norm_w"])[0]
    rows[0, R_CONVB:R_CONVB + 1024] = f(inp["conv_b"])[0]
    rows[0, R_BADA:R_BADA + 6144] = f(inp["b_ada"])[0]
    sm = np.concatenate([f(inp["dt_bias_fwd"])[0], f(inp["dt_bias_bwd"])[0], f(inp["a_log_fwd"])[0], f(inp["a_log_bwd"])[0],
                         f(inp["d_skip"])[0], f(inp["ret_decay_fwd"])[0], f(inp["ret_decay_bwd"])[0]])
    rows[0, R_SMALL:R_SMALL + 48] = sm
    shared = {
        "w_in": f(inp["w_in"])[0], "w_out": f(inp["w_out"])[0], "w_gate": f(inp["w_gate"])[0],
        "w_up": f(inp["w_up"])[0], "w_down": f(inp["w_down"])[0], "w_ada": f(inp["w_ada"])[0],
        "rows": rows, "cstf": cf, "cstb": cb,
    }
    xs = f(inp["x_sample"])
    xp = f(inp["x_prompt"])
    c = f(inp["c"])
    cc = f(inp["c_ctx"])
    sr = f(inp["state_ret"])
    ss = f(inp["state_ssd"])
    b_ada = f(inp["b_ada"])[0]
    conv_w = f(inp["conv_w"])[0]
    conv_b = f(inp["conv_b"])[0]
    maps = []
    for i in range(8):
        cols = np.zeros((128, C_N), np.float32)
        cv = np.stack([c[i], cc], 0).reshape(2, 8, 128).transpose(2, 1, 0)
        cols[:, C_CVEC:C_CVEC + 16] = cv.reshape(128, 16)
        cols[:, C_BADA:C_BADA + 48] = b_ada.reshape(48, 128).T
        cols[:, C_CONVW:C_CONVW + 40] = conv_w.reshape(5, 8, 128).transpose(2, 1, 0).reshape(128, 40)
        cols[:, C_CONVB:C_CONVB + 8] = conv_b.reshape(8, 128).T
        cols[:, C_LN1G:C_LN1G + 8] = f(inp["ln1_g"])[0].reshape(8, 128).T
        cols[:, C_LN1B:C_LN1B + 8] = f(inp["ln1_b"])[0].reshape(8, 128).T
        m = dict(shared)
        m["x"] = np.ascontiguousarray(np.concatenate([xs[i], xp[2 * i], xp[2 * i + 1]], 0))
        m["sret0"] = np.ascontiguousarray(sr[i, 0].transpose(2, 0, 1, 3).reshape(128, 1024))
        m["sssd0"] = np.ascontiguousarray(ss[i, 0].transpose(2, 0, 1, 3).reshape(128, 1024))
        m["cols"] = cols
        maps.append(m)
    return maps


def kernel(**inputs):
    if "nc" not in _CACHE:
        _CACHE["nc"] = build()
    nc = _CACHE["nc"]
    maps = _prep_inputs(inputs)
    res = run_bass_kernel_spmd(nc, maps, core_ids=list(range(8)))
    r = res.results
    y_s = np.stack([r[i]["y"][:1024] for i in range(8)], 0)
    y_p = np.concatenate([r[i]["y"][1024:].reshape(2, 256, 1024) for i in range(8)], 0)
    nret = np.concatenate([r[i]["oret"] for i in range(8)], 0)[:, None]
    nssd = np.concatenate([r[i]["ossd"] for i in range(8)], 0)[:, None]
    return (y_p.astype(np.float32), y_s.astype(np.float32), nret.astype(np.float32), nssd.astype(np.float32))
```

```python
import contextlib
import math
import numpy as np
import concourse.bass as bass
import concourse.mybir as mybir
from concourse.bass_utils import run_bass_kernel_spmd

F32 = mybir.dt.float32
BF16 = mybir.dt.bfloat16
U8 = mybir.dt.uint8
AF = mybir.ActivationFunctionType
ALU = mybir.AluOpType
SZ = {F32: 4, BF16: 2}

PE, ACT, DVE, POOL, SP = "tensor", "scalar", "vector", "gpsimd", "sync"
ENGS = [PE, ACT, DVE, POOL, SP]

D = 1024
NT = 1536
NTT = 12
INC = 3592
DFF = 2816
EPS = 1e-6
ALPHA = 2.0 ** 0.25
NEG = -32768.0
SEQS = [(0, 8), (8, 2), (10, 2)]


class Op:
    __slots__ = ("idx", "eng", "fn", "is_dma", "dsem", "dval", "deps", "inc", "cval", "cost", "fin", "npend", "succ", "start", "crit", "tag", "aps")

    def __init__(self, idx, eng, fn, is_dma):
        self.idx, self.eng, self.fn, self.is_dma = idx, eng, fn, is_dma
        self.dsem, self.dval, self.deps, self.inc, self.cval = None, 0, [], False, 0
        self.cost, self.fin, self.npend, self.succ = 0.3, 0.0, 0, []
        self.start, self.crit, self.tag, self.aps = 0.0, None, '', None


class Prog:
    def __init__(self, nc):
        self.nc = nc
        self.ops = []
        self.res = {}
        self.dma_sems = {}
        self.out_dma_ops = []
        self.cur_tag = ""
        self.last_dma = {}

    @staticmethod
    def _split(key):
        if isinstance(key, tuple):
            return key[0], tuple(key[1:])
        return key, ()

    @staticmethod
    def _related(p, q):
        n = min(len(p), len(q))
        return p[:n] == q[:n]

    def _record(self, op, reads, writes):
        deps = set()
        for k in reads:
            name, p = self._split(k)
            d = self.res.setdefault(name, {})
            for q, e in d.items():
                if self._related(p, q) and e[0] is not None:
                    deps.add(e[0])
        for k in writes:
            name, p = self._split(k)
            d = self.res.setdefault(name, {})
            for q, e in d.items():
                if self._related(p, q):
                    if e[0] is not None:
                        deps.add(e[0])
                    deps.update(e[1])
        deps.discard(op)
        op.deps = sorted(deps, key=lambda o: o.idx)
        for k in reads:
            name, p = self._split(k)
            d = self.res[name]
            if p not in d:
                d[p] = [None, []]
            d[p][1].append(op)
        for k in writes:
            name, p = self._split(k)
            d = self.res[name]
            for q in [q for q in d if len(q) >= len(p) and q[:len(p)] == p]:
                del d[q]
            d[p] = [op, []]

    def op(self, eng, fn, reads=(), writes=(), cost=0.3):
        o = Op(len(self.ops), eng, fn, False)
        o.cost = cost
        o.tag = self.cur_tag
        self.ops.append(o)
        self._record(o, list(reads), list(writes))
        return o

    def dma(self, eng, fn, semkey, reads=(), writes=(), is_output=False, cost=3.0, chain=True):
        o = Op(len(self.ops), eng, fn, True)
        o.cost = cost
        o.tag = self.cur_tag
        ent = self.dma_sems.setdefault(semkey, [None, 0])
        ent[1] += 16
        o.dsem, o.dval = semkey, ent[1]
        self.ops.append(o)
        self._record(o, list(reads), list(writes))
        prev = self.last_dma.get(semkey)
        if chain and prev is not None and prev not in o.deps:
            o.deps.append(prev)
            o.deps.sort(key=lambda q: q.idx)
        self.last_dma[semkey] = o
        if is_output:
            self.out_dma_ops.append(o)
        return o

    def schedule(self):
        import heapq
        ops = self.ops
        for o in ops:
            o.succ = []
        for o in ops:
            o.npend = len(o.deps)
            for d in o.deps:
                d.succ.append(o)

        def lat(d, o):
            return 0.05 if (d.eng == PE and o.eng == PE and not d.is_dma and not o.is_dma) else 0.25

        bl = [0.0] * len(ops)
        for o in reversed(ops):
            m = 0.0
            for s_ in o.succ:
                v = bl[s_.idx] + lat(o, s_)
                if v > m:
                    m = v
            bl[o.idx] = o.cost + m
        inorder = {SP: False, POOL: False, PE: False, ACT: False, DVE: False}
        pend = {e: [] for e in ENGS}
        avail = {e: [] for e in ENGS}
        tcur = {e: 0.0 for e in ENGS}
        ready_t = {}
        nxt = {e: 0 for e in ENGS}
        eng_ops = {e: [o for o in ops if o.eng == e] for e in ENGS}
        order = {e: [] for e in ENGS}

        def push(o):
            rt = 0.0
            for d in o.deps:
                rt = max(rt, d.fin + lat(d, o))
            ready_t[o.idx] = rt
            heapq.heappush(pend[o.eng], (rt, o.idx, o))

        for o in ops:
            if o.npend == 0:
                push(o)
        done, n = 0, len(ops)
        while done < n:
            best = None
            for e in ENGS:
                if inorder[e]:
                    if nxt[e] >= len(eng_ops[e]):
                        continue
                    o = eng_ops[e][nxt[e]]
                    if o.npend != 0 or o.idx not in ready_t:
                        continue
                    st_ = max(tcur[e], ready_t[o.idx])
                    cand = (st_, o.idx, e, o)
                else:
                    p, a = pend[e], avail[e]
                    while p and p[0][0] <= tcur[e]:
                        rt, idx, o = heapq.heappop(p)
                        heapq.heappush(a, (-bl[idx], idx, o))
                    if a:
                        o = a[0][2]
                        cand = (tcur[e], o.idx, e, o)
                    elif p:
                        rt, idx, o = p[0]
                        cand = (rt, idx, e, o)
                    else:
                        continue
                if best is None or cand[:2] < best[:2]:
                    best = cand
            assert best is not None, "scheduler deadlock"
            st_, _, e, o = best
            if inorder[e]:
                nxt[e] += 1
            else:
                if avail[e] and avail[e][0][2] is o:
                    heapq.heappop(avail[e])
                else:
                    heapq.heappop(pend[e])
            o.start = st_
            o.crit = ("eng", order[e][-1]) if (order[e] and tcur[e] >= ready_t[o.idx]) else \
                ("dep", max(o.deps, key=lambda d: d.fin) if o.deps else None)
            if o.is_dma:
                tcur[e] = st_ + (1.0 if e == POOL else 0.1)
                o.fin = st_ + o.cost
            else:
                tcur[e] = st_ + o.cost
                o.fin = tcur[e]
            order[e].append(o)
            done += 1
            for s_ in o.succ:
                s_.npend -= 1
                if s_.npend == 0:
                    push(s_)
        self.est_total = max(o.fin for o in ops)
        return order

    def emit(self, stack, reorder=True):
        nc = self.nc
        if reorder:
            order = self.schedule()
        else:
            order = {e: [o for o in self.ops if o.eng == e] for e in ENGS}
        esem = {e: stack.enter_context(nc.semaphore("c_" + e)) for e in ENGS}
        for i, (k, ent) in enumerate(self.dma_sems.items()):
            ent[0] = stack.enter_context(nc.semaphore("d%d" % i))
        final_waits = {}
        for o in self.out_dma_ops:
            final_waits[o.dsem] = max(final_waits.get(o.dsem, 0), o.dval)

        def skip(d, o):
            return (not d.is_dma) and d.eng == PE and o.eng == PE and not o.is_dma

        pos = {}
        for e in ENGS:
            for i_, o in enumerate(order[e]):
                pos[o.idx] = i_
        need = {}
        for e in ENGS:
            seenpos = {}
            for o in order[e]:
                last = {}
                for d in o.deps:
                    if d.is_dma or skip(d, o):
                        continue
                    m = last.get(d.eng)
                    if m is None or pos[d.idx] > pos[m.idx]:
                        last[d.eng] = d
                lst = []
                for pe_, m in last.items():
                    if seenpos.get(pe_, -1) >= pos[m.idx]:
                        continue
                    seenpos[pe_] = pos[m.idx]
                    m.inc = True
                    lst.append(m)
                need[o.idx] = lst
        cnt = {e: 0 for e in ENGS}
        for e in ENGS:
            for o in order[e]:
                if not o.is_dma and o.inc:
                    cnt[e] += 1
                    o.cval = cnt[e]
        seen = {e: {} for e in ENGS}
        per_eng = {e: [] for e in ENGS}
        for o in [o for e in ENGS for o in order[e]]:
            wl = [(esem[m.eng], m.cval) for m in need[o.idx]]
            waits = {}
            for d in o.deps:
                if d.is_dma:
                    key, val = d.dsem, d.dval
                    if val > waits.get(key, 0):
                        waits[key] = val
            s = seen[o.eng]
            for key, val in waits.items():
                if s.get(key, 0) >= val:
                    continue
                s[key] = val
                wl.append((self.dma_sems[key][0], val))
            per_eng[o.eng].append((o, wl))
        self.n_incs = dict(cnt)
        block = stack.enter_context(nc.Block())
        dma_sems = self.dma_sems

        def make(engname):
            def body(eng):
                for o, wl in per_eng[engname]:
                    for sem, val in wl:
                        eng.wait_ge(sem, val)
                    ins = o.fn(eng)
                    if o.is_dma:
                        ins.then_inc(dma_sems[o.dsem][0], 16)
                    elif o.inc:
                        ins.then_inc(esem[engname], 1)
                if engname == SP:
                    for k, v in final_waits.items():
                        eng.wait_ge(dma_sems[k][0], v)
            return body

        block.tensor(make(PE))
        block.scalar(make(ACT))
        block.vector(make(DVE))
        block.gpsimd(make(POOL))
        block.sync(make(SP))


CF_IDENT, CF_UTRI, CF_ONES = 0, 128, 256
CF_IOTAF, CF_IOTAB = 384, 1408
CF_COS, CF_SIN = 2432, 3456
CF_POS, CF_NEG = 4480, 6528
CF_TAILF, CF_TAILB = 8576, 8578
CF_N = 8580
CB_IDENT, CB_ONES, CB_MF, CB_MB, CB_SELROW, CB_SELBIAS, CB_N = 0, 128, 256, 768, 1280, 3328, 5376


def _host_consts():
    p = np.arange(128)[:, None].astype(np.float64)
    cf = np.zeros((128, CF_N), np.float32)
    j = np.arange(128)[None, :]
    cf[:, CF_IDENT:CF_IDENT + 128] = (p == j)
    cf[:, CF_UTRI:CF_UTRI + 128] = (p <= j)
    cf[:, CF_ONES:CF_ONES + 128] = 1.0
    t = np.arange(1024)[None, :]
    cf[:, CF_IOTAF:CF_IOTAF + 1024] = t + 1
    cf[:, CF_IOTAB:CF_IOTAB + 1024] = 1024 - t
    tt = np.arange(1024)
    t_row = (tt // 64).astype(np.float64)
    t_col = (tt % 64).astype(np.float64)
    inv = 10000.0 ** (-np.arange(32, dtype=np.float64) / 32.0)
    ang = np.concatenate([t_row[:, None] * inv[None, :], t_col[:, None] * inv[None, :]], axis=-1)
    cos = np.cos(ang).T
    sin = np.sin(ang).T
    cf[:, CF_COS:CF_COS + 1024] = np.concatenate([cos, cos], 0)
    cf[:, CF_SIN:CF_SIN + 1024] = np.concatenate([-sin, sin], 0)
    u = np.arange(2048)[None, :]
    delta = u - p - 1024
    cf[:, CF_POS:CF_POS + 2048] = np.maximum(delta, 0)
    cf[:, CF_NEG:CF_NEG + 2048] = np.minimum(delta, 0)
    jj = np.arange(2)[None, :]
    cf[:, CF_TAILF:CF_TAILF + 2] = 255 - 128 * jj - p
    cf[:, CF_TAILB:CF_TAILB + 2] = 128 * jj + p
    cb = np.zeros((128, CB_N), np.float32)
    cb[:, CB_IDENT:CB_IDENT + 128] = (p == j)
    cb[:, CB_ONES:CB_ONES + 128] = 1.0
    mf = np.where(p <= j, 0.0, NEG)
    mb = np.where(p > j, 0.0, NEG)
    cb[:, CB_MF:CB_MF + 512] = np.tile(mf, (1, 4))
    cb[:, CB_MB:CB_MB + 512] = np.tile(mb, (1, 4))
    k = np.arange(128)
    selrow = np.zeros((128, 16, 128), np.float32)
    for hd in range(16):
        selrow[(k < 96) & (k % 32 == hd), hd, :] = 1.0
    cb[:, CB_SELROW:CB_SELROW + 2048] = selrow.reshape(128, 2048)
    selb = np.zeros((128, 2, 8, 128), np.float32)
    for d in range(2):
        for h in range(8):
            selb[(k < 96) & (k % 32 == 16 + d * 8 + h), d, h, :] = 1.0
    cb[:, CB_SELBIAS:CB_SELBIAS + 2048] = selb.reshape(128, 2048)
    return cf, cb


R_LN1G, R_LN1B, R_LN2G, R_LN2B, R_NW, R_CONVB, R_BADA, R_SMALL, R_N = 0, 1024, 2048, 3072, 4096, 4608, 5632, 11776, 11824
C_CVEC, C_BADA, C_CONVW, C_CONVB, C_LN1G, C_LN1B, C_N = 0, 16, 64, 104, 112, 120, 128


def build(debug=()):
    nc = bass.Bass("TRN2", target_bir_lowering=False)
    P = Prog(nc)
    di = lambda name, shape: nc.dram_tensor(name, list(shape), F32, kind="ExternalInput").ap()
    do = lambda name, shape: nc.dram_tensor(name, list(shape), F32, kind="ExternalOutput").ap()
    x_d = di("x", [NT, D])
    sret_d = di("sret0", [128, 1024])
    sssd_d = di("sssd0", [128, 1024])
    w_in_d = di("w_in", [D, INC])
    w_out_d = di("w_out", [D, D])
    w_gate_d = di("w_gate", [D, DFF])
    w_up_d = di("w_up", [D, DFF])
    w_down_d = di("w_down", [DFF, D])
    w_ada_d = di("w_ada", [D, 6 * D])
    rows_d = di("rows", [1, R_N])
    cols_d = di("cols", [128, C_N])
    cf_d = di("cstf", [128, CF_N])
    cb_d = di("cstb", [128, CB_N])
    y_d = do("y", [NT, D])
    oret_d = do("oret", [2, 2, 4, 128, 128])
    ossd_d = do("ossd", [2, 2, 8, 128, 64])

    st = contextlib.ExitStack()
    ARENA = 212800
    arena = nc.alloc_sbuf_tensor("arena", [128, ARENA], U8)
    psum = nc.alloc_psum_tensor("psum", [128, 4096], F32)
    K = 1024

    def V(off, n, dt):
        off = int(off)
        assert off % 4 == 0 and off + n * SZ[dt] <= ARENA, (off, n)
        return arena[:, off:off + n * SZ[dt]].bitcast(dt)

    def PS(b, n=512, off=0):
        return psum[:, b * 512 + off:b * 512 + off + n]

    def pk(b):
        return "ps%d" % b

    o_const, o_scr, o_lnp, o_ring = 0, 5 * K, 15 * K, 31 * K
    SLOT = 8320
    o_A, o_B, o_C, o_D, o_E, o_F, o_G = 64 * K, 88 * K, 112 * K, 124 * K, 136 * K, 148 * K, 173 * K

    ident_f = V(0, 128, F32)
    utri_f = V(512, 128, F32)
    ones_f = V(1024, 128, F32)
    ident_b = V(1536, 128, BF16)
    ones_b = V(1792, 128, BF16)
    m = [2048]

    def small(n, dt=F32):
        a = V(m[0], n, dt)
        m[0] += (n * SZ[dt] + 3) // 4 * 4
        assert m[0] <= 5 * K, m[0]
        return a

    cols = small(C_N)
    smalls = small(48)
    modT = small(96)
    scv = small(16, BF16)
    lg = small(8)
    nlg = small(8)
    tmp8 = small(8)
    nA = small(16)
    tailpos = small(4)
    tailw = small(16)
    sc1p = small(16)
    sc2p = small(16)
    lscb = small(1)
    fencew = small(1)
    dtraw = small(96)
    dskb = small(8)

    def fs(ap):
        n = 1
        for d in ap.shape[1:]:
            n *= int(d)
        return n

    def inps(ap):
        return ap.tensor.name == "psum"

    def mm(out, lhsT, rhs, start, stop, reads, writes):
        n_ = fs(rhs)
        c = ((0.035 + n_ / 2560.0) if n_ >= 256 else (0.03 + n_ / 1400.0)) * (4.0 if rhs.dtype == F32 else 1.0)
        o_ = P.op(PE, lambda e: e.matmul(out, lhsT=lhsT, rhs=rhs, start=start, stop=stop), reads, writes, cost=c)
        o_.aps = ([lhsT, rhs], [out])
        return o_

    def tr(out, in_, reads, writes):
        o_ = P.op(PE, lambda e: e.transpose(out=out, in_=in_, identity=ident_f[:]), list(reads) + ["const"], writes, cost=0.1)
        o_.aps = ([in_, ident_f[:]], [out])
        return o_

    def act(out, in_, func, reads, writes, bias=None, scale=None, accum_out=None):
        kw = {}
        c = 0.2 + fs(in_) / 1400.0
        if fs(in_) <= 8:
            c = 0.6
        if bias is not None:
            kw["bias"] = bias
            c += 0.05
        if scale is not None:
            kw["scale"] = scale
        if accum_out is not None:
            kw["accum_out"] = accum_out
            c += 0.1
        o_ = P.op(ACT, lambda e: e.activation(out=out, in_=in_, func=func, **kw), reads, writes, cost=c)
        o_.aps = ([in_] + [v_ for v_ in (bias, scale) if v_ is not None and not isinstance(v_, float)], [out] + ([accum_out] if accum_out is not None else []))
        return o_

    def vcost(eng, n, f):
        if n <= 8:
            return 0.6
        return (0.1 + n / 490.0) if eng == POOL else (0.09 + n * f / 1060.0)

    def tt(out, in0, in1, op, reads, writes, eng=DVE):
        f = 1.0 if (inps(in0) or inps(in1)) else 2.0
        if in0.dtype == BF16 and in1.dtype == BF16 and out.dtype == BF16 and f == 2.0:
            f = 0.6
        o_ = P.op(eng, lambda e: e.tensor_tensor(out=out, in0=in0, in1=in1, op=op), reads, writes, cost=vcost(eng, fs(out), f))
        o_.aps = ([in0, in1], [out])
        return o_

    def ts(out, in0, s1, op0, reads, writes, s2=None, op1=None, eng=DVE):
        c = vcost(eng, fs(out), 1.0)
        if op1 is None:
            o_ = P.op(eng, lambda e: e.tensor_scalar(out=out, in0=in0, scalar1=s1, scalar2=None, op0=op0), reads, writes, cost=c)
            o_.aps = ([in0] + [v_ for v_ in (s1,) if not isinstance(v_, (float, int))], [out])
            return o_
        o_ = P.op(eng, lambda e: e.tensor_scalar(out=out, in0=in0, scalar1=s1, scalar2=s2, op0=op0, op1=op1), reads, writes, cost=c)
        o_.aps = ([in0] + [v_ for v_ in (s1, s2) if not isinstance(v_, (float, int))], [out])
        return o_

    def stt(out, in0, scalar, in1, op0, op1, reads, writes):
        f = 1.0 if (inps(in0) or inps(in1)) else 2.0
        o_ = P.op(DVE, lambda e: e.scalar_tensor_tensor(out=out, in0=in0, scalar=scalar, in1=in1, op0=op0, op1=op1), reads, writes,
                  cost=vcost(DVE, fs(out), f))
        o_.aps = ([in0, in1] + [v_ for v_ in (scalar,) if not isinstance(v_, (float, int))], [out])
        return o_

    def cp(out, in_, reads, writes, eng=DVE):
        o_ = P.op(eng, lambda e: e.tensor_copy(out=out, in_=in_), reads, writes, cost=vcost(eng, fs(out), 1.0))
        o_.aps = ([in_], [out])
        return o_

    def memset(ap, val, writes, eng=DVE):
        o_ = P.op(eng, lambda e: e.memset(ap, val), [], writes, cost=vcost(eng, fs(ap), 0.5))
        o_.aps = ([], [ap])
        return o_

    def dma(eng, out, in_, key, reads, writes, is_output=False, chain=True):
        nb = 128 * fs(out) * 4
        o_ = P.dma(eng, lambda e: e.dma_start(out=out, in_=in_), key, reads, writes, is_output, cost=2.0 + nb / 150e3, chain=chain)
        o_.aps = ([in_], [out])
        return o_

    P.marks = []

    def fence(regions, name=None):
        o = P.op(DVE, lambda e: e.memset(fencew[:], 0.0), [], list(regions), cost=0.1)
        P.marks.append((name or ("f%d" % len(P.marks)), o))

    def bcast_row(off, n):
        return rows_d[0:1, off:off + n].partition_broadcast(128).rearrange("p a n -> p (a n)")

    dbg_n = [0]

    def tap(name, ap, shape, reads):
        if name in debug:
            dd = do("dbg_" + name, shape)
            dbg_n[0] += 1
            dma(POOL, dd, ap, "dbg%d" % dbg_n[0], reads, [], True)

    ring_n = [0]

    def ring_load(parts, slot=None, after=()):
        if slot is None:
            s = ring_n[0] % 4
            ring_n[0] += 1
        else:
            s = slot
        base = o_ring + s * SLOT
        for ip, (src, dst_off_elems, nk, ncol) in enumerate(parts):
            dst = V(base + dst_off_elems * 2, nk * ncol, BF16).rearrange("p (k n) -> p k n", k=nk)
            dma(POOL, dst, src, "ring%d" % s, list(after), [("ring", s, ip)], chain=(ip == 0))
        return s, base

    def rstd_from(dst, src_ps_or_sb, scale, reads, writes):
        act(dst, src_ps_or_sb, AF.Ln, list(reads) + ["epsb"], writes, bias=epsb[:, 0:1], scale=scale)
        act(dst, dst, AF.Exp, writes, writes, scale=-0.5)

    epsb = small(1)
    dgs = [V(o_scr + 8 * K + i * 512, 128, F32) for i in range(2)]
    memset(epsb[:], EPS, ["epsb"])

    dma(SP, V(0, 384, F32), cf_d[:, 0:384], "c0", [], ["const"])
    dma(POOL, V(1536, 256, BF16), cb_d[:, 0:256], "c1", [], ["constb"])
    dma(SP, cols[:], cols_d[:, :], "c2", [], ["cols"])
    dma(SP, smalls[:], bcast_row(R_SMALL, 48), "c3", [], ["smalls"])
    dma(SP, tailpos[:], cf_d[:, CF_TAILF:CF_TAILF + 4], "c7", [], ["tailpos"])
    cos_t = V(o_lnp, 1024, F32)
    sin_t = V(o_lnp + 4 * K, 1024, F32)
    scb = V(o_scr + 4 * K, 2048, BF16)
    dma(SP, V(o_lnp, 2048, F32), cf_d[:, CF_COS:CF_COS + 2048], "c4", [], [("LNP", "rope")])
    act(scv[:], cols[:, C_CVEC:C_CVEC + 16], AF.Silu, ["cols"], ["scv"])
    scv3 = scv[:].rearrange("p (k c) -> p k c", c=2)

    def make_scb():
        cp(scb[:].rearrange("p (a m) -> p a m", m=128), scv[:].unsqueeze(2).to_broadcast([128, 16, 128]),
           ["scv"], [("SCR", "scb")])
    scb4 = scb[:].rearrange("p (k c m) -> p k c m", k=8, c=2)

    def mod_fm(ft0, col0, nslots):
        for sl in range(nslots):
            c0 = col0 + sl * 512
            s, base = ring_load([(w_ada_d[:, c0:c0 + 512].rearrange("(k p) n -> p k n", p=128), 0, 8, 512)])
            wv = V(base, 8 * 512, BF16).rearrange("p (k n) -> p k n", k=8)
            for j in range(4):
                ft = ft0 + sl * 4 + j
                for kt in range(8):
                    mm(PS(7, 2, ft * 2), wv[:, kt, j * 128:(j + 1) * 128], scv3[:, kt, :], kt == 0, kt == 7,
                       [("ring", s), "scv"], ["ps7"])
        n = nslots * 4
        tt(modT[:, ft0 * 2:(ft0 + n) * 2].rearrange("p (f c) -> p f c", c=2),
           PS(7, n * 2, ft0 * 2).rearrange("p (f c) -> p f c", c=2),
           cols[:, C_BADA + ft0:C_BADA + ft0 + n].unsqueeze(2).to_broadcast([128, n, 2]), ALU.add,
           ["ps7", "cols"], [("modT", ft0)])

    def gate_load(col0):
        slots = []
        for sl in range(2):
            c0 = col0 + sl * 512
            s, base = ring_load([(w_ada_d[:, c0:c0 + 512].rearrange("(k p) n -> p k n", p=128), 0, 8, 512)])
            slots.append((s, V(base, 8 * 512, BF16).rearrange("p (k n) -> p k n", k=8)))
        return slots

    def gate_compute(slots, dst_off, rowoff):
        make_scb()
        btmp = V(o_scr, 1024, F32)
        dma(SP, btmp[:], bcast_row(rowoff, 1024), "gb", [], [("SCR", "btmp")])
        for sl in range(2):
            s, wv = slots[sl]
            for c in range(2):
                b = 5 + c
                for kt in range(8):
                    mm(PS(b), scb4[:, kt, c, :], wv[:, kt, :], kt == 0, kt == 7, [("ring", s), ("SCR", "scb")], [pk(b)])
                tt(V(dst_off + (c * 1024 + sl * 512) * 4, 512, F32)[:], PS(b), btmp[:, sl * 512:(sl + 1) * 512], ALU.add,
                   [pk(b), ("SCR", "btmp")], [("LNP", "gate", c, sl)])

    def gate_bcast(dst_off, col0, rowoff):
        gate_compute(gate_load(col0), dst_off, rowoff)

    xTf = V(o_F, 8 * NT, F32).rearrange("p (k t) -> p k t", k=8)
    xst = [V(o_B + i * 4 * K, 1024, F32) for i in range(8)]
    x_t = x_d.rearrange("(n p) f -> n p f", p=128)

    def x_load(ck, after=()):
        h_ = ck % 2
        dst = V(o_B + h_ * 16 * K, 4096, F32).rearrange("p (j f) -> p j f", j=4)
        src = x_d[ck * 512:(ck + 1) * 512, :].rearrange("(j p) f -> p j f", p=128)
        dma(SP, dst, src, "xst%d" % h_, list(after), [("B", "xst", h_ * 4 + j) for j in range(4)])
    x_load(0)
    mod_fm(0, 0, 4)
    ts(sc1p[:], modT[:, 16:32], 1.0, ALU.add, [("modT", 0)], ["sc1p"])
    wada_done = [("ring", 0), ("ring", 1), ("ring", 2), ("ring", 3)]
    x_load(1, after=wada_done[:2])
    n_ = 0
    for ck in range(3):
        if ck == 1:
            x_load(2, after=wada_done)
        for kt in range(8):
            b = n_ % 4
            n_ += 1
            for j in range(4):
                jj = (ck % 2) * 4 + j
                tr(PS(b, 128, j * 128), xst[jj][:, kt * 128:(kt + 1) * 128], [("B", "xst", jj)], [pk(b)])
            if n_ % 2 == 0:
                act(xTf[:, kt, ck * 512:(ck + 1) * 512], PS(b), AF.Copy, [pk(b)], [("F", "xTf", kt, ck)])
            else:
                cp(xTf[:, kt, ck * 512:(ck + 1) * 512], PS(b), [pk(b)], [("F", "xTf", kt, ck)])
    sh1 = modT[:, 0:16].rearrange("p (k c) -> p k c", c=2)
    sc1p3 = sc1p[:].rearrange("p (k c) -> p k c", c=2)
    hT = V(o_A, 8 * NT, BF16).rearrange("p (k t) -> p k t", k=8)
    n_ = 0
    for ck in range(3):
        cond = 0 if ck < 2 else 1
        for kt in range(8):
            if n_ % 2 == 0:
                act(hT[:, kt, ck * 512:(ck + 1) * 512], xTf[:, kt, ck * 512:(ck + 1) * 512], AF.Identity,
                    [("F", "xTf", kt, ck), "sc1p", ("modT", 0)], [("A", ck, "hT", kt)],
                    bias=sh1[:, kt, cond:cond + 1], scale=sc1p3[:, kt, cond:cond + 1])
            else:
                ts(hT[:, kt, ck * 512:(ck + 1) * 512], xTf[:, kt, ck * 512:(ck + 1) * 512], sc1p3[:, kt, cond:cond + 1], ALU.mult,
                   [("F", "xTf", kt, ck), "sc1p", ("modT", 0)], [("A", ck, "hT", kt)], s2=sh1[:, kt, cond:cond + 1], op1=ALU.add)
            n_ += 1
    fence(["B", "C", "F", "G"])
    def w_in_chunk(c0, ncol, after=()):
        s, base = ring_load([(w_in_d[:, c0:c0 + ncol].rearrange("(k p) n -> p k n", p=128), 0, 8, ncol)], after=after)
        return s, V(base, 8 * ncol, BF16).rearrange("p (k n) -> p k n", k=8)

    pre_in = {1024: w_in_chunk(1024, 512)}
    for c0 in (0, 512):
        pre_in[c0] = w_in_chunk(c0, 512, after=[("A", 0, "hT", 7)])

    def w_in_get(c0, ncol):
        return pre_in.pop(c0) if c0 in pre_in else w_in_chunk(c0, ncol)

    delta = V(o_F, 2048, F32)
    E1 = V(o_F + 8 * K, 2048, F32)
    E2 = V(o_F + 16 * K, 2048, F32)
    P.op(POOL, lambda e: e.iota(delta[:], pattern=[[1, 2048]], base=-1024, channel_multiplier=-1,
                                allow_small_or_imprecise_dtypes=True), [], [("F", "delta")], cost=4.5)
    Th = V(o_G, 4 * 2048, BF16).rearrange("p (h u) -> p h u", h=4)
    iotaFB = V(o_G + 16 * K, 2048, F32)
    rowtab = V(o_G + 24 * K, 1024, BF16)
    mscr = [V(o_G + 26 * K, 512, F32), V(o_G + 32 * K, 512, F32)]
    ktm = V(o_G + 28 * K, 2048, BF16).rearrange("p (j d f) -> p j d f", j=2, d=2)
    P.op(POOL, lambda e: e.iota(iotaFB[:, 0:1024], pattern=[[1, 1024]], base=1, channel_multiplier=0,
                                allow_small_or_imprecise_dtypes=True), [], [("G", "iota", 0)], cost=2.3)
    P.op(POOL, lambda e: e.iota(iotaFB[:, 1024:2048], pattern=[[-1, 1024]], base=1024, channel_multiplier=0,
                                allow_small_or_imprecise_dtypes=True), [], [("G", "iota", 1)], cost=2.3)

    u8 = small(8)
    l8 = small(8)
    act(tmp8[:], smalls[:, 40:48], AF.Exp, ["smalls"], ["tmp8"], scale=-1.0)
    ts(u8[:], tmp8[:], 1.0, ALU.add, ["tmp8"], ["u8"])
    act(l8[:], u8[:], AF.Ln, ["u8"], ["l8"])
    ts(u8[:], u8[:], -1.0, ALU.add, ["u8"], ["u8"], s2=1e-30, op1=ALU.max)
    P.op(DVE, lambda e: e.reciprocal(out=u8[:], in_=u8[:]), ["u8"], ["u8"], cost=0.2)
    tt(l8[:], l8[:], u8[:], ALU.mult, ["l8", "u8"], ["l8"])
    tt(tmp8[:], tmp8[:], l8[:], ALU.mult, ["tmp8", "l8"], ["tmp8"])
    ts(lg[:], tmp8[:], -1.0, ALU.mult, ["tmp8"], ["lg"])
    cp(nlg[:], tmp8[:], ["tmp8"], ["nlg"])
    act(nA[:], smalls[:, 16:32], AF.Exp, ["smalls"], ["nA"])
    ts(nA[:], nA[:], -1.0, ALU.mult, ["nA"], ["nA"])
    cp(dskb[:], smalls[:, 32:40], ["smalls"], ["dskb"])
    memset(lscb[:], -0.5 * math.log(128.0), ["lscb"])
    for h in range(4):
        act(E1[:], delta[:], AF.Exp, [("F", "delta"), "lg", "lscb"], [("F", "E1")], bias=lscb[:, 0:1], scale=lg[:, h:h + 1])
        act(E2[:], delta[:], AF.Exp, [("F", "delta"), "nlg", "lscb"], [("F", "E2")], bias=lscb[:, 0:1], scale=nlg[:, 4 + h:5 + h])
        for q in range(4):
            qs = slice(q * 512, (q + 1) * 512)
            tt(Th[:, h, qs], E1[:, qs], E2[:, qs], ALU.min, [("F", "E1"), ("F", "E2")], [("G", "Th", h, q)])
    for d in range(2):
        for j in range(2):
            ts(tailw[:, d * 8 + j * 4:d * 8 + j * 4 + 4], lg[:, d * 4:d * 4 + 4], tailpos[:, d * 2 + j:d * 2 + j + 1],
               ALU.mult, ["lg", "tailpos"], [("tailw", d, j)])
    act(tailw[:], tailw[:], AF.Exp, ["tailw", "lscb"], ["tailw"], bias=lscb[:, 0:1])


    qT = V(o_B, 4 * NT, BF16).rearrange("p (h t) -> p h t", h=4)
    kT = V(o_B + 12 * K, 4 * NT, BF16).rearrange("p (h t) -> p h t", h=4)
    v_tm = V(o_C, NTT * 512, BF16).rearrange("p (n f) -> p n f", n=NTT)
    gT = V(o_D, 4 * NT, BF16).rearrange("p (h t) -> p h t", h=4)
    z_tm = V(o_E, NTT * 512, BF16).rearrange("p (n f) -> p n f", n=NTT)
    XW = 1548
    XOFF = [2, 1030, 1290]
    xbc = V(o_F, 8 * XW, BF16).rearrange("p (c t) -> p c t", c=8)
    rope_tmpB = [[V(o_scr, 512, F32), V(o_scr + 2 * K, 512, F32)], [V(o_lnp + 8 * K, 512, F32), V(o_lnp + 10 * K, 512, F32)]]
    nrope = [0]
    stS = V(o_scr + 4 * K, 1024, F32)

    def tok2xcol(ck):
        if ck < 2:
            return [(0, 512, XOFF[0] + ck * 512)]
        return [(0, 256, XOFF[1]), (256, 256, XOFF[2])]

    def fm_proj(s, wv, j, ck, bank):
        for kt in range(8):
            mm(PS(bank), wv[:, kt, j * 128:(j + 1) * 128], hT[:, kt, ck * 512:(ck + 1) * 512], kt == 0, kt == 7,
               [("ring", s), ("A", ck, "hT", kt)], [pk(bank)])

    def tm_proj(s, wv, t, bank, ncol=512, c0=0):
        for kt in range(8):
            mm(PS(bank, ncol), hT[:, kt, t * 128:(t + 1) * 128], wv[:, kt, c0:c0 + ncol], kt == 0, kt == 7,
               [("ring", s), ("A", t // 4, "hT", kt)], [pk(bank)])

    bk = [0]

    def nb4():
        b = bk[0] % 4
        bk[0] += 1
        return b

    s, wv = w_in_get(1024, 512)
    for t in range(NTT):
        b = nb4()
        tm_proj(s, wv, t, b)
        if t % 2 == 0:
            act(v_tm[:, t, :], PS(b), AF.Copy, [pk(b)], [("C", "v", t)])
        else:
            cp(v_tm[:, t, :], PS(b), [pk(b)], [("C", "v", t)])
    for which, dstT in ((0, qT), (1, kT)):
        s, wv = w_in_get(which * 512, 512)
        for j in range(4):
            for ck in range(3):
                b = nb4()
                fm_proj(s, wv, j, ck, b)
                dst = dstT[:, j, ck * 512:(ck + 1) * 512]
                wr = [("B", "qk", which, j, ck)]
                if ck == 2:
                    act(dst, PS(b), AF.Copy, [pk(b)], wr)
                else:
                    rq = nrope[0] % 2
                    nrope[0] += 1
                    tA, tB = rope_tmpB[rq]
                    rn = "SCR" if rq == 0 else "LNP"
                    tsl = slice(ck * 512, (ck + 1) * 512)
                    tt(tA[:], PS(b), cos_t[:, tsl], ALU.mult, [pk(b), ("LNP", "rope")], [(rn, "ropeA")])
                    tt(tB[0:64, :], PS(b)[64:128, :], sin_t[0:64, tsl], ALU.mult, [pk(b), ("LNP", "rope")], [(rn, "ropeB", 0)])
                    tt(tB[64:128, :], PS(b)[0:64, :], sin_t[64:128, tsl], ALU.mult, [pk(b), ("LNP", "rope")], [(rn, "ropeB", 1)])
                    tt(dst, tA[:], tB[:], ALU.add, [(rn, "ropeA"), (rn, "ropeB", 0), (rn, "ropeB", 1)], wr, eng=POOL)
        if which == 1:
            for sq in range(2):
                for jb in range(2):
                    t = 8 + sq * 2 + jb
                    b = nb4()
                    tm_proj(s, wv, t, b)
                    for d in range(2):
                        for h in range(4):
                            c = d * 8 + jb * 4 + h
                            act(ktm[:, jb, d, h * 128:(h + 1) * 128], PS(b, 128, h * 128), AF.Copy, [pk(b), "tailw"],
                                [("G", "ktm", jb, d, h)], scale=tailw[:, c:c + 1])
                for d in range(2):
                    for h in range(4):
                        for jb in range(2):
                            mm(PS(4 + d, 128, h * 128), ktm[:, jb, d, h * 128:(h + 1) * 128],
                               v_tm[:, 8 + sq * 2 + jb, h * 128:(h + 1) * 128], jb == 0, jb == 1,
                               [("G", "ktm", jb, d, h), ("C", "v", 8 + sq * 2 + jb)], [pk(4 + d)])
                    cp(stS[:, d * 512:(d + 1) * 512], PS(4 + d), [pk(4 + d)], [("SCR", "stS", d)])
                dma(SP, oret_d[sq].rearrange("d h p e -> p d h e"), stS[:].rearrange("p (d h e) -> p d h e", d=2, h=4),
                    "oret", [("SCR", "stS", 0), ("SCR", "stS", 1)], [], True)
    tap("qT", qT, [128, 4, NT], ["B"])
    tap("kT", kT, [128, 4, NT], ["B"])
    tap("v", v_tm, [128, NTT, 512], ["C"])
    s, wv = w_in_chunk(1536, 512)
    for j in range(4):
        for ck in range(3):
            b = nb4()
            fm_proj(s, wv, j, ck, b)
            act(gT[:, j, ck * 512:(ck + 1) * 512], PS(b), AF.Silu, [pk(b)], [("D", "g", j, ck)])
    mod_fm(24, 3072, 2)
    s, wv = w_in_chunk(2048, 512)
    for t in range(NTT):
        b = nb4()
        tm_proj(s, wv, t, b)
        act(z_tm[:, t, :], PS(b), AF.Silu, [pk(b)], [("E", "z", t)])
    mod_fm(32, 4096, 2)
    ts(sc2p[:], modT[:, 64:80], 1.0, ALU.add, [("modT", 32)], ["sc2p"])
    fence(["F"])
    for (c0_, c1_) in ((0, 2), (1026, 1030), (1286, 1290), (1546, 1548)):
        memset(xbc[:, :, c0_:c1_], 0.0, [("F", "xbc", "pad", c0_)], eng=POOL)
    for half in range(2):
        ncol = 512 if half == 0 else 520
        s, wv = w_in_chunk(2560 + half * 512, ncol)
        for j in range(4):
            ct = half * 4 + j
            for ck in range(3):
                b = nb4()
                fm_proj(s, wv, j, ck, b)
                for (o, n, xc) in tok2xcol(ck):
                    if (j + ck) % 2 == 0:
                        act(xbc[:, ct, xc:xc + n], PS(b, n, o), AF.Copy, [pk(b)], [("F", "xbc", ct, xc)])
                    else:
                        cp(xbc[:, ct, xc:xc + n], PS(b, n, o), [pk(b)], [("F", "xbc", ct, xc)])
        if half == 0:
            mod_fm(16, 2048, 2)
        if half == 1:
            for t in range(NTT):
                for kt in range(8):
                    mm(PS(7, 8, t * 8), hT[:, kt, t * 128:(t + 1) * 128], wv[:, kt, 512:520], kt == 0, kt == 7,
                       [("ring", s), ("A", t // 4, "hT", kt)], ["ps7"])
            cp(dtraw[:], PS(7, 96), ["ps7"], ["dtraw"])
    mod_fm(40, 5120, 2)
    pre_wo = []
    for half in range(2):
        s, base = ring_load([(w_out_d[:, half * 512:(half + 1) * 512].rearrange("(k p) n -> p k n", p=128), 0, 8, 512)], slot=2 + half)
        pre_wo.append((s, V(base, 8 * 512, BF16).rearrange("p (k n) -> p k n", k=8)))

    def ffn_slot(sl):
        c0 = sl * 256
        return ring_load([(w_gate_d[:, c0:c0 + 256].rearrange("(k p) n -> p k n", p=128), 0, 8, 256),
                          (w_up_d[:, c0:c0 + 256].rearrange("(k p) n -> p k n", p=128), 2048, 8, 256)], slot=sl % 2)
    ffn_pre = [ffn_slot(0), ffn_slot(1)]

    def gate_from_mod(ft0, dsts, dkeys):
        n_ = 0
        for c in range(2):
            for half in range(2):
                b = 4 + (n_ % 2)
                n_ += 1
                for k4 in range(4):
                    kt = half * 4 + k4
                    q = kt % 2
                    ts(dgs[q][:], ident_f[:], modT[:, (ft0 + kt) * 2 + c:(ft0 + kt) * 2 + c + 1], ALU.mult,
                       ["const", ("modT", ft0)], [("SCR", "dgs", q)])
                    mm(PS(b, 128, k4 * 128), ones_f[:], dgs[q][:], True, True, ["const", ("SCR", "dgs", q)], [pk(b)])
                cp(dsts[c][:, half * 512:(half + 1) * 512], PS(b), [pk(b)], [dkeys[c] + (half,)])

    fence(["A", "LNP", "SCR", ("G", "ktm")])
    oT = V(o_A, 8 * NT, BF16).rearrange("p (k t) -> p k t", k=8)
    PTb = [V(o_lnp + i * 8 * K, 8 * 512, BF16).rearrange("p (i t) -> p i t", i=8) for i in range(2)]
    rs_fB = [V(o_scr + 4 * K + i * 2 * K, 512, F32) for i in range(2)]
    sq_bB = [V(o_G + 28 * K + i * K, 512, BF16) for i in range(2)]
    qfb = [V(o_G + 30 * K + i * K, 512, BF16) for i in range(2)]
    S0b = V(o_scr + 8 * K, 1024, BF16).rearrange("p (d h e) -> p d h e", d=2, h=4)
    PTp = [V(o_scr + i * K, 512, BF16).rearrange("p (i t) -> p i t", i=2) for i in range(2)]
    rs_p = [V(o_scr + 2 * K + i * K, 256, F32) for i in range(2)]
    sq_p = [V(o_G + 34 * K, 256, BF16)] * 2
    dma(POOL, S0b, sret_d[:, :].rearrange("p (d h e) -> p d h e", d=2, h=4), "s0r", [], [("SCR", "S0b")])
    nmask = [0]
    ucount = {True: 0, False: 0}

    def ret_unit(t0, nb, h, r0, W):
        is_sample = nb == 8
        tok0 = t0 * 128
        q_ = ucount[is_sample] % 2
        ucount[is_sample] += 1
        if is_sample:
            pt, ptkey, ob, msb = PTb[q_], ("LNP", "PT", q_), 2 + q_, 4 + q_
            rsf_, sqb_, rkey, skey = rs_fB[q_], sq_bB[q_], ("SCR", "rs_f", q_), ("G", "ktm", "sq", q_)
        else:
            pt, ptkey, ob, msb = PTp[q_], ("SCR", "PTp", q_), 6, 7
            rsf_, sqb_, rkey, skey = rs_p[q_], sq_p[q_], ("SCR", "rs_p", q_), ("G", "ktm", "sqp")
        for i in range(nb):
            b = i % 2
            mm(PS(b, W), kT[:, h, tok0 + i * 128:tok0 + (i + 1) * 128], qT[:, h, tok0 + r0:tok0 + r0 + W], True, True,
               [("B", "qk")], [pk(b)])
            u0 = r0 - 128 * i + 1024
            if is_sample and i % 3 == 2:
                mq = nmask[0] % 2
                nmask[0] += 1
                act(mscr[mq][:, 0:W], PS(b, W), AF.Copy, [pk(b)], [("G", "mscr", mq)])
                tt(pt[:, i, 0:W], mscr[mq][:, 0:W], Th[:, h, u0:u0 + W], ALU.mult, [("G", "mscr", mq), ("G", "Th", h)],
                   [ptkey + (i,)], eng=POOL)
            else:
                tt(pt[:, i, 0:W], PS(b, W), Th[:, h, u0:u0 + W], ALU.mult, [pk(b), ("G", "Th", h)], [ptkey + (i,)])
        if is_sample:
            for d in range(2):
                act(rowtab[:, 0:W], iotaFB[:, d * 1024 + r0:d * 1024 + r0 + W], AF.Exp, [("G", "iota"), "lg"],
                    [("G", "rowtab")], scale=lg[:, d * 4 + h:d * 4 + h + 1])
                tt(qfb[d][:, 0:W], qT[:, h, tok0 + r0:tok0 + r0 + W], rowtab[:, 0:W], ALU.mult,
                   [("B", "qk"), ("G", "rowtab")], [("G", "ktm", "qf", d)])
        nmm = nb + (2 if is_sample else 0)
        for i in range(nb):
            mm(PS(ob, W), v_tm[:, t0 + i, h * 128:(h + 1) * 128], pt[:, i, 0:W], i == 0, i == nmm - 1,
               [("C", "v", t0 + i), ptkey + (i,)], [pk(ob)])
        if is_sample:
            for d in range(2):
                mm(PS(ob, W), S0b[:, d, h, :], qfb[d][:, 0:W], False, d == 1,
                   [("SCR", "S0b"), ("G", "ktm", "qf", d)], [pk(ob)])
        act(sqb_[:, 0:W], PS(ob, W), AF.Square, [pk(ob)], [skey])
        mm(PS(msb, W), ones_b[:], sqb_[:, 0:W], True, True, [skey, "constb"], [pk(msb)])
        rstd_from(rsf_[:, 0:W], PS(msb, W), 1.0 / 128.0, [pk(msb)], [rkey])
        tt(rsf_[:, 0:W], PS(ob, W), rsf_[:, 0:W], ALU.mult, [pk(ob), rkey], [rkey])
        tt(oT[:, h, tok0 + r0:tok0 + r0 + W], rsf_[:, 0:W], gT[:, h, tok0 + r0:tok0 + r0 + W], ALU.mult,
           [rkey, ("D", "g")], [("A", (tok0 + r0) // 512, "oT", h, t0, r0)], eng=POOL)

    s_units = [(0, 8, h, r0, 512) for h in range(4) for r0 in (0, 512)]
    p_units = [(t0, 2, h, 0, 256) for t0 in (8, 10) for h in range(4)]
    for su, pu in zip(s_units, p_units):
        ret_unit(*su)
        ret_unit(*pu)
    tap("oTr", oT[:, 0:4, :], [128, 4, NT], ["A"])

    fence(["B", "C", "D", "G", "SCR", "LNP"])
    cstS = V(o_lnp, 5120, BF16)
    dma(POOL, cstS[:], cb_d[:, CB_MF:CB_MF + 5120], "cstS", [], [("LNP", "cstS")])
    xs_tm = V(o_B, NTT * 512, BF16).rearrange("p (n f) -> p n f", n=NTT)
    B_tm = V(o_B + 12 * K, NTT * 256, BF16).rearrange("p (n f) -> p n f", n=NTT)
    BT = V(o_B + 18 * K, 2 * NT, BF16).rearrange("p (g t) -> p g t", g=2)
    CT = V(o_C, 2 * NT, BF16).rearrange("p (g t) -> p g t", g=2)
    diagw = V(o_G, 8 * 5 * 128, BF16).rearrange("p (c j m) -> p c j m", c=8, j=5)
    xsT = V(o_G + 12 * K, 4 * NT, BF16).rearrange("p (c t) -> p c t", c=4)
    nd_ = 0
    for ct in range(8):
        for j in range(5):
            sc_ = cols[:, C_CONVW + ct * 5 + j:C_CONVW + ct * 5 + j + 1]
            if nd_ % 3 == 0:
                ts(diagw[:, ct, j, :], ident_f[:], sc_, ALU.mult, ["const", "cols"], [("G", "diagw", ct, j)])
            elif nd_ % 3 == 1:
                act(diagw[:, ct, j, :], ident_f[:], AF.Copy, ["const", "cols"], [("G", "diagw", ct, j)], scale=sc_)
            else:
                ts(diagw[:, ct, j, :], ident_f[:], sc_, ALU.mult, ["const", "cols"], [("G", "diagw", ct, j)], eng=POOL)
            nd_ += 1
    nbk = 0
    for ct in range(8):
        for ck in range(3):
            for (o, n, xc) in tok2xcol(ck):
                bank = nbk % 4
                nbk += 1
                for j in range(5):
                    mm(PS(bank, n), diagw[:, ct, j, :], xbc[:, ct, xc + j - 2:xc + j - 2 + n], j == 0, j == 4,
                       [("F", "xbc"), ("G", "diagw", ct)], [pk(bank)])
                if ct < 4:
                    dstT, g, key = xsT, ct, ("G", "xsT", ct, ck, o)
                elif ct < 6:
                    dstT, g, key = BT, ct - 4, ("B", "BT", ct - 4, ck, o)
                else:
                    dstT, g, key = CT, ct - 6, ("C", "CT", ct - 6, ck, o)
                act(dstT[:, g, ck * 512 + o:ck * 512 + o + n], PS(bank, n), AF.Silu, [pk(bank), "cols"], [key],
                    bias=cols[:, C_CONVB + ct:C_CONVB + ct + 1])
    for t in range(NTT):
        bx = 4 + 2 * (t % 2)
        for c in range(4):
            mm(PS(bx, 128, c * 128), xsT[:, c, t * 128:(t + 1) * 128], ident_b[:], True, True,
               [("G", "xsT", c), "constb"], [pk(bx)])
        for c in range(2):
            mm(PS(bx + 1, 128, c * 128), BT[:, c, t * 128:(t + 1) * 128], ident_b[:], True, True,
               [("B", "BT", c), "constb"], [pk(bx + 1)])
        cp(xs_tm[:, t, :], PS(bx), [pk(bx)], [("B", "xs", t)])
        act(B_tm[:, t, :], PS(bx + 1, 256), AF.Copy, [pk(bx + 1)], [("B", "Btm", t)])
    tap("xsb", V(o_B, NTT * 768, BF16).rearrange("p (n f) -> p n f", n=NTT), [128, NTT, 768], ["B"]) if False else None
    tap("BCT", V(o_B + 18 * K, 4 * NT, BF16).rearrange("p (g t) -> p g t", g=4), [128, 4, NT], ["B", "C"])

    rsT = V(o_scr + 7 * K, NT, BF16)
    pool_off = [o_G + 24 * K]

    def pl(n, dt=F32):
        a = V(pool_off[0], n, dt)
        pool_off[0] += n * SZ[dt]
        assert pool_off[0] <= o_G + 35 * K - 1024
        return a
    X = pl(192).rearrange("p (b d) -> p b d", b=NTT)
    AX = pl(192).rearrange("p (b d) -> p b d", b=NTT)
    DT = pl(192).rearrange("p (b d) -> p b d", b=NTT)
    LA = pl(192).rearrange("p (b d) -> p b d", b=NTT)
    LNDT = pl(192).rearrange("p (b d) -> p b d", b=NTT)
    CUM = pl(192).rearrange("p (b d) -> p b d", b=NTT)
    TOT = pl(192).rearrange("p (b d) -> p b d", b=NTT)
    RSRC = pl(384).rearrange("p (b d) -> p b d", b=NTT)
    EXPO = V(o_scr + 4 * K, 576, F32).rearrange("p (b d) -> p b d", b=NTT)
    EXPIN = V(o_D, 576, F32).rearrange("p (b d) -> p b d", b=NTT)
    SPL3 = V(o_D + 2304, 1152, F32).rearrange("p (b d) -> p b d", b=NTT)
    R1 = V(o_D + 2304 + 4608, 384, F32).rearrange("p (b d) -> p b d", b=NTT)
    H1B = V(o_D + 2304 + 4608 + 1536, 384, BF16).rearrange("p (b d) -> p b d", b=NTT)
    PF = [("G", "pool")]
    PD = [("D", "pool")]
    dtr3 = dtraw[:].rearrange("p (b h) -> p b h", b=NTT)
    X4 = X.rearrange("p b (d h) -> p b d h", d=2)
    tt(X4, dtr3.unsqueeze(2).to_broadcast([128, NTT, 2, 8]),
       smalls[:, 0:16].rearrange("p (d h) -> p d h", d=2).unsqueeze(1).to_broadcast([128, NTT, 2, 8]), ALU.add,
       ["dtraw", "smalls"], PF)
    UU = pl(192).rearrange("p (b d) -> p b d", b=NTT)
    LL = pl(192).rearrange("p (b d) -> p b d", b=NTT)
    act(AX, X, AF.Abs, PF, PF)
    act(AX, AX, AF.Exp, PF, PF, scale=-1.0)
    ts(UU, AX, 1.0, ALU.add, PF, PF)
    act(LL, UU, AF.Ln, PF, PF)
    ts(UU, UU, -1.0, ALU.add, PF, PF, s2=1e-30, op1=ALU.max)
    P.op(DVE, lambda e: e.reciprocal(out=UU, in_=UU), PF, PF, cost=0.3)
    tt(LL, LL, UU, ALU.mult, PF, PF)
    tt(AX, AX, LL, ALU.mult, PF, PF)
    ts(X, X, 0.0, ALU.max, PF, PF)
    tt(DT, X, AX, ALU.add, PF, PF)
    ts(DT, DT, 1e-30, ALU.max, PF, PF)
    tt(LA, DT, nA[:].unsqueeze(1).to_broadcast([128, NTT, 16]), ALU.mult, PF + ["nA"], PF)
    act(LNDT, DT, AF.Ln, PF, PF)
    for b in range(NTT):
        mm(PS(0, 16, b * 16), utri_f[:], LA[:, b, :], True, True, PF + ["const"], ["ps0"])
        mm(PS(1, 16, b * 16), ones_f[:], LA[:, b, :], True, True, PF + ["const"], ["ps1"])
    cp(CUM, PS(0, 192).rearrange("p (b d) -> p b d", b=NTT), ["ps0"], PF)
    cp(TOT, PS(1, 192).rearrange("p (b d) -> p b d", b=NTT), ["ps1"], PF)
    cp(RSRC[:, :, 0:8], CUM[:, :, 0:8], PF, PF)
    tt(RSRC[:, :, 8:16], LA[:, :, 8:16], CUM[:, :, 8:16], ALU.subtract, PF, PF)
    tt(RSRC[:, :, 16:24], LNDT[:, :, 0:8], CUM[:, :, 0:8], ALU.subtract, PF, PF)
    tt(RSRC[:, :, 24:32], LNDT[:, :, 8:16], RSRC[:, :, 8:16], ALU.subtract, PF, PF)
    cp(EXPIN[:, :, 0:8], RSRC[:, :, 0:8], PF, PD)
    tt(EXPIN[:, :, 8:16], TOT[:, :, 8:16], RSRC[:, :, 8:16], ALU.add, PF, PD)
    tt(EXPIN[:, :, 16:24], TOT[:, :, 0:8], RSRC[:, :, 16:24], ALU.add, PF, PD)
    cp(EXPIN[:, :, 24:32], RSRC[:, :, 24:32], PF, PD)
    cp(EXPIN[:, :, 32:48], TOT, PF, PD)
    act(EXPO, EXPIN, AF.Exp, PD, [("SCR", "expo")])
    cp(H1B, RSRC, PF, PD)
    cp(SPL3[:, :, 0:32], H1B, PD, PD)
    tt(R1, RSRC, SPL3[:, :, 0:32], ALU.subtract, PF + PD, PD)
    cp(H1B, R1, PD, PD)
    cp(SPL3[:, :, 32:64], H1B, PD, PD)
    tt(R1, R1, SPL3[:, :, 32:64], ALU.subtract, PD, PD)
    cp(H1B, R1, PD, PD)
    cp(SPL3[:, :, 64:96], H1B, PD, PD)
    for ck in range(3):
        for j in range(4):
            tr(PS(2 + ck % 2, 128, j * 128)[0:96, :], SPL3[:, ck * 4 + j, :], PD, [pk(2 + ck % 2)])
        cp(rsT[0:96, ck * 512:(ck + 1) * 512], PS(2 + ck % 2)[0:96, :], [pk(2 + ck % 2)], [("SCR", "rsT", ck)])
    fence(["D", "F", "G", "SCR"])
    MF4, MB4 = cstS[:, 0:512], cstS[:, 512:1024]
    selrow = cstS[:, 1024:3072].rearrange("p (a m) -> p a m", a=16)
    selbias = cstS[:, 3072:5120].rearrange("p (d n) -> p d n", d=2)
    dI = V(o_G + 30 * K, 1024, BF16).rearrange("p (h m) -> p h m", h=8)
    for h in range(8):
        ts(dI[:, h, :], ident_f[:], dskb[:, h:h + 1], ALU.mult, ["const", "dskb"], [("G", "dI")])
    nwb = V(o_G + 32 * K, 512, F32)
    dma(SP, nwb[:], bcast_row(R_NW, 512), "nwb", [], [("G", "nwb")])
    WfB = [V(o_G + i * 11 * K, 1024, BF16) for i in range(2)]
    WbB = [V(o_G + i * 11 * K + 2 * K, 1024, BF16) for i in range(2)]
    PmB = [V(o_G + i * 11 * K + 4 * K, 1024, BF16).rearrange("p (h t) -> p h t", h=8) for i in range(2)]
    y1B = [V(o_G + i * 11 * K + 6 * K, 512, F32) for i in range(2)]
    y2B = [V(o_G + i * 11 * K + 8 * K, 512, F32) for i in range(2)]
    jkB = [V(o_G + i * 11 * K + 10 * K, 512, BF16) for i in range(2)]
    xswB = [[V(o_G + 22 * K + (d * 2 + i) * K, 512, BF16) for i in range(2)] for d in range(2)]
    Sst = [V(o_G + 26 * K, 512, F32), V(o_G + 28 * K, 512, F32)]
    ssqB = [small(1) for _ in range(2)]
    rstdB = [small(1) for _ in range(2)]
    fence(["D"], "ssdpre")
    Rall = V(o_D, 8 * 512, BF16).rearrange("p (b f) -> p b f", b=8)
    SallA = V(o_D + 8 * K, 4 * 512, BF16).rearrange("p (b f) -> p b f", b=4)
    SallB = V(o_C + 6 * K, 4 * 512, BF16).rearrange("p (b f) -> p b f", b=4)

    def Sall(i):
        return SallA[:, i, :] if i < 4 else SallB[:, i - 4, :]

    def Skey(i):
        return ("D", "S", i) if i < 4 else ("C", "S", i)
    psum2 = lambda b0: psum[:, b0 * 512:b0 * 512 + 1024]

    def bc8(ap8):
        return ap8.unsqueeze(2).to_broadcast([128, 8, 64])

    def v8(ap512):
        return ap512.rearrange("p (h e) -> p h e", h=8)

    nxsw = [0, 0]

    def state_update(d, b, bank):
        w = EXPO[:, b, 16 + d * 8:24 + d * 8]
        k = nxsw[d] % 2
        nxsw[d] += 1
        xw = xswB[d][k]
        tt(v8(xw[:]), v8(xs_tm[:, b, :]), bc8(w), ALU.mult, [("B", "xs", b), ("SCR", "expo")], [("G", "xsw", d, k)], eng=POOL)
        for g in range(2):
            mm(PS(bank, 256, g * 256), B_tm[:, b, g * 128:(g + 1) * 128], xw[:, g * 256:(g + 1) * 256], True, True,
               [("B", "Btm", b), ("G", "xsw", d, k)], [pk(bank)])
        tt(v8(Sst[d][:]), v8(Sst[d][:]), bc8(EXPO[:, b, 32 + d * 8:40 + d * 8]), ALU.mult, [("G", "S", d), ("SCR", "expo")],
           [("G", "S", d)], eng=POOL)
        tt(Sst[d][:], Sst[d][:], PS(bank), ALU.add, [("G", "S", d), pk(bank)], [("G", "S", d)])

    RallS = V(o_scr, 2 * 512, BF16).rearrange("p (b f) -> p b f", b=2)
    SallS = V(o_scr + 2 * K, 2 * 512, BF16).rearrange("p (b f) -> p b f", b=2)

    def Sv(big, i):
        return (Sall(i), Skey(i)) if big else (SallS[:, i, :], ("SCR", "S", i))

    def Rv(big, i):
        return (Rall[:, i, :], ("D", "R", i)) if big else (RallS[:, i, :], ("SCR", "R", i))

    def chain_steps(si):
        t0, nb = SEQS[si]
        big = nb == 8
        if big:
            dma(SP, Sst[0][:], sssd_d[:, 0:512], "s0s0", [], [("G", "S", 0)])
            dma(SP, Sst[1][:], sssd_d[:, 512:1024], "s0s1", [], [("G", "S", 1)])
        else:
            memset(Sst[0][:], 0.0, [("G", "S", 0)])
            memset(Sst[1][:], 0.0, [("G", "S", 1)])
        for step in range(nb):
            i_f = step
            i_b = nb - 1 - step
            sv, sk = Sv(big, i_f)
            rv, rk = Rv(big, i_b)
            act(sv, Sst[0][:], AF.Copy, [("G", "S", 0)], [sk])
            act(rv, Sst[1][:], AF.Copy, [("G", "S", 1)], [rk])
            if i_f < nb - 1 or not big:
                state_update(0, t0 + i_f, 6)
            if i_b > 0 or not big:
                state_update(1, t0 + i_b, 7)
            yield
        if not big:
            sq = si - 1
            for d in range(2):
                dma(SP, ossd_d[sq, d].rearrange("h n e -> n h e"), Sst[d][:].rearrange("p (h e) -> p h e", h=8), "ossd%d" % d,
                    [("G", "S", d)], [], True)
        yield

    nblk = [0]
    kbof = {}

    def stageA(si, i):
        t0, nb = SEQS[si]
        b = t0 + i
        tok = b * 128
        kb = nblk[0] % 2
        nblk[0] += 1
        kbof[b] = kb
        Wf, Wb, Pm = WfB[kb], WbB[kb], PmB[kb]
        bk_ = lambda n: ("G", n, kb)
        for g in range(2):
            mm(PS(4, 128, g * 128), BT[:, g, tok:tok + 128], CT[:, g, tok:tok + 128], True, True,
               [("B", "BT", g), ("C", "CT", g)], ["ps4"])
        for d in range(2):
            bank0 = 2 * d
            wr = [pk(bank0), pk(bank0 + 1)]
            Md = MF4 if d == 0 else MB4
            for half in range(2):
                mm(PS(bank0 + half), ident_b[:], Md, True, False, ["constb", ("LNP", "cstS")], wr)
                mm(PS(bank0 + half), rsT[0:96, tok:tok + 128], selbias[0:96, d, half * 512:(half + 1) * 512], False, False,
                   [("SCR", "rsT", b // 4), ("LNP", "cstS")], wr)
            for h in range(8):
                mm(PS(bank0 + h // 4, 128, (h % 4) * 128), selrow[0:96, d * 8 + h, :], rsT[0:96, tok:tok + 128], False, h % 4 == 3,
                   [("SCR", "rsT", b // 4), ("LNP", "cstS")], wr)
            act((Wf if d == 0 else Wb)[:], psum2(bank0), AF.Exp, wr, [bk_("W%d" % d)])
        tt(Wf[:], Wf[:], Wb[:], ALU.add, [bk_("W0"), bk_("W1")], [bk_("W0")])
        tt(Pm.rearrange("p (g q) t -> p g q t", g=2), Wf[:].rearrange("p (g q t) -> p g q t", g=2, q=4),
           PS(4, 256).rearrange("p (g t) -> p g t", g=2).unsqueeze(2).to_broadcast([128, 2, 4, 128]), ALU.mult,
           [bk_("W0"), "ps4"], [bk_("Pm")])

    def stageB(si, i):
        t0, nb = SEQS[si]
        big = nb == 8
        b = t0 + i
        tok = b * 128
        kb = kbof[b]
        Pm, y1, y2, jk = PmB[kb], y1B[kb], y2B[kb], jkB[kb]
        ssq_, rstd_ = ssqB[kb], rstdB[kb]
        bk_ = lambda n: ("G", n, kb)
        sv, sk = Sv(big, i)
        rv, rk = Rv(big, i)
        for h in range(8):
            mm(PS(5, 64, h * 64), Pm[:, h, :], xs_tm[:, b, h * 64:(h + 1) * 64], True, False,
               [bk_("Pm"), ("B", "xs", b)], ["ps5"])
            mm(PS(5, 64, h * 64), dI[:, h, :], xs_tm[:, b, h * 64:(h + 1) * 64], False, True,
               [("G", "dI"), ("B", "xs", b)], ["ps5"])
        for g in range(2):
            mm(PS(6, 256, g * 256), CT[:, g, tok:tok + 128], sv[:, g * 256:(g + 1) * 256], True, True,
               [("C", "CT", g), sk], ["ps6"])
            mm(PS(7, 256, g * 256), CT[:, g, tok:tok + 128], rv[:, g * 256:(g + 1) * 256], True, True,
               [("C", "CT", g), rk], ["ps7"])
        tt(v8(y1[:]), v8(PS(6)), bc8(EXPO[:, b, 0:8]), ALU.mult, ["ps6", ("SCR", "expo")], [bk_("y1")])
        tt(v8(y2[:]), v8(PS(7)), bc8(EXPO[:, b, 8:16]), ALU.mult, ["ps7", ("SCR", "expo")], [bk_("y2")])
        tt(y1[:], y1[:], PS(5), ALU.add, [bk_("y1"), "ps5"], [bk_("y1")])
        tt(y1[:], y1[:], y2[:], ALU.add, [bk_("y1"), bk_("y2")], [bk_("y1")])
        tt(y1[:], y1[:], z_tm[:, b, :], ALU.mult, [bk_("y1"), ("E", "z", b)], [bk_("y1")], eng=POOL)
        act(jk[:], y1[:], AF.Square, [bk_("y1")], [bk_("jk"), ("ssq", kb)], accum_out=ssq_[:, 0:1])
        rstd_from(rstd_[:, 0:1], ssq_[:, 0:1], 1.0 / 512.0, [("ssq", kb)], [("rstd", kb)])
        stt(y2[:], y1[:], rstd_[:, 0:1], nwb[:], ALU.mult, ALU.mult, [bk_("y1"), ("rstd", kb), ("G", "nwb")], [bk_("y2")])

    def stageC(si, i):
        t0, nb = SEQS[si]
        b = t0 + i
        tok = b * 128
        kb = kbof[b]
        y2 = y2B[kb]
        for j in range(4):
            tr(PS(4, 128, j * 128), y2[:, j * 128:(j + 1) * 128], [("G", "y2", kb)], ["ps4"])
        act(oT[:, 4:8, tok:tok + 128], PS(4).rearrange("p (j t) -> p j t", j=4), AF.Copy, ["ps4"], [("A", b // 4, "oT", 4, b)])

    seq_order = [1, 0, 2]
    blocks = [(si, i) for si in seq_order for i in range(SEQS[si][1])]
    first_of = {}
    for n_, (si, i) in enumerate(blocks):
        first_of.setdefault(si, n_)
    for _ in chain_steps(seq_order[0]):
        pass
    pending_chain = None
    nbk = len(blocks)
    for step in range(nbk + 2):
        if step < nbk:
            si, i = blocks[step]
            if i == 0:
                pos = seq_order.index(si)
                if pos + 1 < len(seq_order):
                    pending_chain = chain_steps(seq_order[pos + 1])
            stageA(si, i)
        if 1 <= step <= nbk:
            stageB(*blocks[step - 1])
        if 2 <= step:
            stageC(*blocks[step - 2])
        if pending_chain is not None:
            si_cur = blocks[min(step, nbk - 1)][0]
            nsteps = 4 if SEQS[si_cur][1] == 2 else 1
            for _ in range(nsteps):
                try:
                    next(pending_chain)
                except StopIteration:
                    pending_chain = None
                    break
    tap("oT", oT, [128, 8, NT], ["A"])

    fence(["B", "C", "D", "E", "F", "G", "LNP", "SCR"])
    x1 = [V(o_B + t * 4 * K, 1024, F32) for t in range(NTT)]
    xs2 = [V(o_G + j * 4 * K, 1024, F32) for j in range(4)]
    xs2k = [[("G", "xs2", j)] for j in range(4)]
    lng = V(o_lnp, 1024, F32)
    lnb = V(o_lnp + 4 * K, 1024, F32)
    gateb = [V(o_lnp + 8 * K + c * 4 * K, 1024, F32) for c in range(2)]

    rstdL = [small(1) for _ in range(2)]
    nmrB = [small(1) for _ in range(2)]

    s1B = [small(1) for _ in range(2)]
    s2B = [small(1) for _ in range(2)]
    mB = [small(1) for _ in range(2)]
    vB = [small(1) for _ in range(2)]
    ljunk = V(o_scr, 1024, BF16)

    def ln_affine(t, dst, dst_key, lng, lnb, lkey):
        tt(dst[:], dst[:], lng[:], ALU.mult, [dst_key, lkey + ("g",)], [dst_key], eng=POOL)
        tt(dst[:], dst[:], lnb[:], ALU.add, [dst_key, lkey + ("b",)], [dst_key], eng=POOL)

    def layer_norm_tile(t, banks, xres, xres_key, dst, dst_key, lng, lnb, gateb, gkey, lkey, defer_affine=False):
        cond = 0 if t < 8 else 1
        q = t % 2
        s1, s2, m_, v_, rstd, nmr = s1B[q], s2B[q], mB[q], vB[q], rstdL[q], nmrB[q]
        u = dst
        xk = xres_key if isinstance(xres_key, list) else [xres_key]
        for half in range(2):
            hs = slice(half * 512, (half + 1) * 512)
            tt(u[:, hs], PS(banks[half]), gateb[cond][:, hs], ALU.mult, [pk(banks[half]), gkey + (cond, half)], [dst_key + (half,)])
            stt(u[:, hs], xres[:, hs], ALPHA, u[:, hs], ALU.mult, ALU.add, xk + [dst_key + (half,)], [dst_key + (half,)])
        act(ljunk[:], u[:], AF.Copy, [dst_key], [("SCR", "ljunk"), ("s1", q)], accum_out=s1[:, 0:1])
        act(ljunk[:], u[:], AF.Square, [dst_key], [("SCR", "ljunk"), ("s2", q)], accum_out=s2[:, 0:1])
        ts(m_[:, 0:1], s1[:, 0:1], 1.0 / 1024.0, ALU.mult, [("s1", q)], [("m", q)])
        tt(v_[:, 0:1], m_[:, 0:1], m_[:, 0:1], ALU.mult, [("m", q)], [("v", q)])
        stt(v_[:, 0:1], s2[:, 0:1], 1.0 / 1024.0, v_[:, 0:1], ALU.mult, ALU.subtract, [("s2", q), ("v", q)], [("v", q)])
        rstd_from(rstd[:, 0:1], v_[:, 0:1], 1.0, [("v", q)], [("rstdL", q)])
        ts(nmr[:, 0:1], m_[:, 0:1], rstd[:, 0:1], ALU.mult, [("m", q), ("rstdL", q)], [("nmr", q)], s2=-1.0, op1=ALU.mult)
        act(u[:], u[:], AF.Identity, [dst_key, ("rstdL", q), ("nmr", q)], [dst_key], bias=nmr[:, 0:1], scale=rstd[:, 0:1])
        if not defer_affine:
            ln_affine(t, dst, dst_key, lng, lnb, lkey)

    dma(SP, lng[:], bcast_row(R_LN1G, 1024), "lnpg", [], [("LNP", "ln", "g")])
    dma(SP, lnb[:], bcast_row(R_LN1B, 1024), "lnpb", [], [("LNP", "ln", "b")])
    gate_from_mod(16, gateb, [("LNP", "gate", 0), ("LNP", "gate", 1)])
    wo = pre_wo
    for t in range(NTT):
        j = t % 4
        dma(SP, xs2[j][:], x_t[t], "xs2%d" % j, [], xs2k[j])
        banks = (2 * (t % 4), 2 * (t % 4) + 1)
        for half in range(2):
            s, wv = wo[half]
            for kt in range(8):
                mm(PS(banks[half]), oT[:, kt, t * 128:(t + 1) * 128], wv[:, kt, :], kt == 0, kt == 7,
                   [("ring", s), ("A", t // 4)], [pk(banks[half])])
        layer_norm_tile(t, banks, xs2[j], xs2k[j], x1[t], ("B", "x1", t), lng, lnb, gateb, ("LNP", "gate"), ("LNP", "ln"), defer_affine=True)
    tap("x1", V(o_B, NTT * 1024, F32).rearrange("p (n f) -> p n f", n=NTT), [128, NTT, 1024], ["B"])

    sc2p3 = sc2p[:].rearrange("p (k c) -> p k c", c=2)
    sh2 = modT[:, 48:64].rearrange("p (k c) -> p k c", c=2)
    scale2 = small(16)
    bias2 = small(16)
    tt(scale2[:].rearrange("p (k c) -> p k c", c=2), sc2p3, cols[:, C_LN1G:C_LN1G + 8].unsqueeze(2).to_broadcast([128, 8, 2]), ALU.mult,
       ["sc2p", "cols"], ["scale2"])
    tt(bias2[:].rearrange("p (k c) -> p k c", c=2), sc2p3, cols[:, C_LN1B:C_LN1B + 8].unsqueeze(2).to_broadcast([128, 8, 2]), ALU.mult,
       ["sc2p", "cols"], ["bias2"])
    tt(bias2[:], bias2[:], modT[:, 48:64], ALU.add, ["bias2", ("modT", 24)], ["bias2"])
    scale23 = scale2[:].rearrange("p (k c) -> p k c", c=2)
    bias23 = bias2[:].rearrange("p (k c) -> p k c", c=2)
    h2T = V(o_A, 8 * NT, BF16).rearrange("p (k t) -> p k t", k=8)
    aT = V(o_E, 22 * NT, BF16).rearrange("p (f t) -> p f t", f=22)
    sg = [V(o_scr + 2 * K + i * K, 512, BF16) for i in range(2)]
    def h2t_chunk(ck, extra=()):
        cond = 0 if ck < 2 else 1
        fence([("A", ck)])
        for kt in range(8):
            b = kt % 4
            for j in range(4):
                tr(PS(b, 128, j * 128), x1[ck * 4 + j][:, kt * 128:(kt + 1) * 128], [("B", "x1", ck * 4 + j)] + list(extra), [pk(b)])
            if kt % 2 == 0:
                act(h2T[:, kt, ck * 512:(ck + 1) * 512], PS(b), AF.Identity, [pk(b), "scale2", "bias2"],
                    [("A", ck, "h2T", kt)], bias=bias23[:, kt, cond:cond + 1], scale=scale23[:, kt, cond:cond + 1])
            else:
                ts(h2T[:, kt, ck * 512:(ck + 1) * 512], PS(b), scale23[:, kt, cond:cond + 1], ALU.mult, [pk(b), "scale2", "bias2"],
                   [("A", ck, "h2T", kt)], s2=bias23[:, kt, cond:cond + 1], op1=ALU.add)

    npair = [0]

    def ffn_group(s, wg, wu, ft, jt, ck, order_key=None):
        bg = 2 * (npair[0] % 4)
        bu = bg + 1
        si_ = npair[0] % 2
        npair[0] += 1
        for kt in range(8):
            mm(PS(bg), wg[:, kt, jt * 128:(jt + 1) * 128], h2T[:, kt, ck * 512:(ck + 1) * 512], kt == 0, kt == 7,
               [("ring", s), ("A", ck, "h2T", kt)], [pk(bg)])
        for kt in range(8):
            mm(PS(bu), wu[:, kt, jt * 128:(jt + 1) * 128], h2T[:, kt, ck * 512:(ck + 1) * 512], kt == 0, kt == 7,
               [("ring", s), ("A", ck, "h2T", kt)], [pk(bu)] + ([order_key] if (order_key and kt == 7) else []))
        act(sg[si_][:], PS(bg), AF.Silu, [pk(bg)], [("SCR", "sg", si_)])
        tt(aT[:, ft, ck * 512:(ck + 1) * 512], sg[si_][:], PS(bu), ALU.mult, [("SCR", "sg", si_), pk(bu)], [("E", "aT", ft, ck)])

    def ffn_views(base):
        return (V(base, 2048, BF16).rearrange("p (k n) -> p k n", k=8), V(base + 4096, 2048, BF16).rearrange("p (k n) -> p k n", k=8))

    h2t_chunk(0)
    h2t_chunk(1)
    s0_, base0_ = ffn_pre[0]
    wg0_, wu0_ = ffn_views(base0_)
    for jt in range(2):
        for ck in range(2):
            ffn_group(s0_, wg0_, wu0_, jt, jt, ck, order_key=("ffnorder",))
    h2t_chunk(2, extra=[("ffnorder",)])
    assert o_E + 12 * 3072 <= o_G and o_scr + 2 * K >= o_scr + 2048
    for t in range(NTT):
        ln_affine(t, x1[t], ("B", "x1", t), lng, lnb, ("LNP", "ln"))
    fence(["LNP"])
    def wd_load(sl):
        nft = 4 if sl < 5 else 2
        src = w_down_d[sl * 512:sl * 512 + nft * 128, :].rearrange("(k p) n -> p k n", p=128)
        if sl in (2, 3):
            base = o_lnp + (sl - 2) * 8 * K
            key = ("LNP", "wd", sl)
            dma(POOL, V(base, nft * 1024, BF16).rearrange("p (k n) -> p k n", k=nft), src, "wd%d" % sl, [], [key])
        else:
            s, base = ring_load([(src, 0, nft, 1024)], slot={0: 2, 1: 3, 4: 0, 5: 1}[sl])
            key = ("ring", s)
        wvd = V(base, nft * 1024, BF16).rearrange("p (k n) -> p k n", k=nft)
        return [(key, wvd[:, k_, :]) for k_ in range(nft)]
    wd_part = {sl: wd_load(sl) for sl in (0, 1, 2, 3)}
    for jt in range(2):
        ffn_group(s0_, wg0_, wu0_, jt, jt, 2)
    for sl in range(1, 11):
        if sl == 6:
            fence(["E", "G"])
        s, base = ffn_pre[sl] if sl < 2 else ffn_slot(sl)
        wg, wu = ffn_views(base)
        for jt in range(2):
            for ck in range(3):
                ffn_group(s, wg, wu, sl * 2 + jt, jt, ck)

    fence(["A", "SCR"])
    lng2 = V(o_A + 8 * K, 1024, F32)
    lnb2 = V(o_A + 12 * K, 1024, F32)
    gateb2 = [V(o_A + 16 * K + c_ * 4 * K, 1024, F32) for c_ in range(2)]
    dma(SP, lng2[:], bcast_row(R_LN2G, 1024), "lnpg", [], [("A", "ln", "g")])
    dma(SP, lnb2[:], bcast_row(R_LN2B, 1024), "lnpb", [], [("A", "ln", "b")])
    gate_from_mod(40, gateb2, [("A", "gate", 0), ("A", "gate", 1)])
    ystage = [V(o_A + j * 4 * K, 1024, F32) for j in range(2)]
    for sl in (4, 5):
        wd_part[sl] = wd_load(sl)
    wd = []
    for sl in range(6):
        wd += wd_part[sl]
    y_t = y_d.rearrange("(n p) f -> n p f", p=128)
    for t in range(NTT):
        j = t % 2
        banks = (4 * j + 2, 4 * j + 3)
        for half in range(2):
            for ft in range(22):
                key, w = wd[ft]
                mm(PS(banks[half]), aT[:, ft, t * 128:(t + 1) * 128], w[:, half * 512:(half + 1) * 512], ft == 0, ft == 21,
                   [key, ("E", "aT", ft, t // 4)], [pk(banks[half])])
        if t < NTT - 1:
            layer_norm_tile(t, banks, x1[t], ("B", "x1", t), ystage[j], ("A", "ystage", j), lng2, lnb2, gateb2, ("A", "gate"), ("A", "ln"))
            dma(SP, y_t[t], ystage[j][:], "yout%d" % j, [("A", "ystage", j)], [], True)
        else:
            layer_norm_tile(t, banks, x1[t], ("B", "x1", t), ystage[j], ("A", "ystage", j), lng2, lnb2, gateb2, ("A", "gate"), ("A", "ln"),
                            defer_affine=True)
            for half in range(2):
                hs = slice(half * 512, (half + 1) * 512)
                eng_ = POOL if half == 0 else DVE
                kh = ("A", "ystage", j, half)
                tt(ystage[j][:, hs], ystage[j][:, hs], lng2[:, hs], ALU.mult, [kh, ("A", "ln", "g")], [kh], eng=eng_)
                tt(ystage[j][:, hs], ystage[j][:, hs], lnb2[:, hs], ALU.add, [kh, ("A", "ln", "b")], [kh], eng=eng_)
                dma(SP, y_t[t][:, hs], ystage[j][:, hs], "ylast%d" % half, [kh], [], True)
    P.emit(st)
    build.P = P
    return nc


_CACHE = {}


def _prep_inputs(inp):
    f = lambda a: np.ascontiguousarray(np.asarray(a, dtype=np.float32))
    cf, cb = _host_consts()
    rows = np.zeros((1, R_N), np.float32)
    rows[0, R_LN1G:R_LN1G + 1024] = f(inp["ln1_g"])[0]
    rows[0, R_LN1B:R_LN1B + 1024] = f(inp["ln1_b"])[0]
    rows[0, R_LN2G:R_LN2G + 1024] = f(inp["ln2_g"])[0]
    rows[0, R_LN2B:R_LN2B + 1024] = f(inp["ln2_b"])[0]
    rows[0, R_NW:R_NW + 512] = f(inp["ssd_norm_w"])[0]
    rows[0, R_CONVB:R_CONVB + 1024] = f(inp["conv_b"])[0]
    rows[0, R_BADA:R_BADA + 6144] = f(inp["b_ada"])[0]
    sm = np.concatenate([f(inp["dt_bias_fwd"])[0], f(inp["dt_bias_bwd"])[0], f(inp["a_log_fwd"])[0], f(inp["a_log_bwd"])[0],
                         f(inp["d_skip"])[0], f(inp["ret_decay_fwd"])[0], f(inp["ret_decay_bwd"])[0]])
    rows[0, R_SMALL:R_SMALL + 48] = sm
    shared = {
        "w_in": f(inp["w_in"])[0], "w_out": f(inp["w_out"])[0], "w_gate": f(inp["w_gate"])[0],
        "w_up": f(inp["w_up"])[0], "w_down": f(inp["w_down"])[0], "w_ada": f(inp["w_ada"])[0],
        "rows": rows, "cstf": cf, "cstb": cb,
    }
    xs = f(inp["x_sample"])
    xp = f(inp["x_prompt"])
    c = f(inp["c"])
    cc = f(inp["c_ctx"])
    sr = f(inp["state_ret"])
    ss = f(inp["state_ssd"])
    b_ada = f(inp["b_ada"])[0]
    conv_w = f(inp["conv_w"])[0]
    conv_b = f(inp["conv_b"])[0]
    maps = []
    for i in range(8):
        cols = np.zeros((128, C_N), np.float32)
        cv = np.stack([c[i], cc], 0).reshape(2, 8, 128).transpose(2, 1, 0)
        cols[:, C_CVEC:C_CVEC + 16] = cv.reshape(128, 16)
        cols[:, C_BADA:C_BADA + 48] = b_ada.reshape(48, 128).T
        cols[:, C_CONVW:C_CONVW + 40] = conv_w.reshape(5, 8, 128).transpose(2, 1, 0).reshape(128, 40)
        cols[:, C_CONVB:C_CONVB + 8] = conv_b.reshape(8, 128).T
        cols[:, C_LN1G:C_LN1G + 8] = f(inp["ln1_g"])[0].reshape(8, 128).T
        cols[:, C_LN1B:C_LN1B + 8] = f(inp["ln1_b"])[0].reshape(8, 128).T
        m = dict(shared)
        m["x"] = np.ascontiguousarray(np.concatenate([xs[i], xp[2 * i], xp[2 * i + 1]], 0))
        m["sret0"] = np.ascontiguousarray(sr[i, 0].transpose(2, 0, 1, 3).reshape(128, 1024))
        m["sssd0"] = np.ascontiguousarray(ss[i, 0].transpose(2, 0, 1, 3).reshape(128, 1024))
        m["cols"] = cols
        maps.append(m)
    return maps


def kernel(**inputs):
    if "nc" not in _CACHE:
        _CACHE["nc"] = build()
    nc = _CACHE["nc"]
    maps = _prep_inputs(inputs)
    res = run_bass_kernel_spmd(nc, maps, core_ids=list(range(8)))
    r = res.results
    y_s = np.stack([r[i]["y"][:1024] for i in range(8)], 0)
    y_p = np.concatenate([r[i]["y"][1024:].reshape(2, 256, 1024) for i in range(8)], 0)
    nret = np.concatenate([r[i]["oret"] for i in range(8)], 0)[:, None]
    nssd = np.concatenate([r[i]["ossd"] for i in range(8)], 0)[:, None]
    return (y_p.astype(np.float32), y_s.astype(np.float32), nret.astype(np.float32), nssd.astype(np.float32))
```
